# Optimizing a Trainium2 kernel written in Bass

```python
import math
import jax, jax.numpy as jnp
from jax import lax
import numpy as np

D_MODEL = 1024
BATCH = 2
SEQ = 8192
DEPTH = 1

N_MEM = 256
MAX_POS_OFFSET = 1024
BLOCK = 128
RMS_EPS = 1e-6
NEG_INF = -1e30

MLA_HEADS = 8
MLA_NOPE = 64
MLA_ROPE = 32
MLA_V = 64
MLA_QK_DIM = MLA_NOPE + MLA_ROPE
MLA_Q_RANK = 384
MLA_KV_RANK = 256
ROPE_THETA = 10000.0
MLA_WIDTH = MLA_HEADS * MLA_V

DIL_PAIRS = ((128, 1), (512, 4), (2048, 16))
DIL_GROUPS = 3
DIL_HEADS_PER_GROUP = 4
DIL_HEADS = DIL_GROUPS * DIL_HEADS_PER_GROUP
DIL_HEAD_DIM = 128
DIL_WIDTH = DIL_HEADS_PER_GROUP * DIL_HEAD_DIM

MEM_HEADS = 4
MEM_HEAD_DIM = 128
MEM_WIDTH = MEM_HEADS * MEM_HEAD_DIM

N_BRANCH = 3

D_FF = 2816
CONV_WIDTH = 3

OFF_Q = MLA_Q_RANK
OFF_KV = OFF_Q + MLA_KV_RANK
OFF_KR = OFF_KV + MLA_ROPE
OFF_DIL = OFF_KR + 3 * DIL_HEADS * DIL_HEAD_DIM
OFF_MEMQ = OFF_DIL + MEM_WIDTH
D_IN = OFF_MEMQ + N_BRANCH * D_MODEL

kernel_name = "hybrid_mla_dilated_memory_convffn"


def _rms_norm(x, g):
    xf = x.astype(jnp.float32)
    y = xf * lax.rsqrt(jnp.mean(xf * xf, axis=-1, keepdims=True) + RMS_EPS)
    return (y * g.astype(jnp.float32)).astype(x.dtype)


def _rope(t, positions):
    half = t.shape[-1] // 2
    inv_freq = ROPE_THETA ** (-jnp.arange(half, dtype=jnp.float32) / half)
    ang = positions.astype(jnp.float32)[:, :, None, None] * inv_freq
    cos, sin = jnp.cos(ang), jnp.sin(ang)
    t1 = t[..., :half].astype(jnp.float32)
    t2 = t[..., half:].astype(jnp.float32)
    return jnp.concatenate([t1 * cos - t2 * sin, t2 * cos + t1 * sin], axis=-1).astype(t.dtype)


def _alibi_slopes(n):
    return jnp.exp2(-8.0 * jnp.arange(1, n + 1, dtype=jnp.float32) / n)


def _mla(c_q, c_kv, k_rope, positions, q_norm, w_uq, kv_norm, w_ukv):
    B, S, _ = c_q.shape
    q = (_rms_norm(c_q, q_norm) @ w_uq).reshape(B, S, MLA_HEADS, MLA_QK_DIM)
    q = jnp.concatenate([q[..., :MLA_NOPE], _rope(q[..., MLA_NOPE:], positions)], axis=-1)
    kv = (_rms_norm(c_kv, kv_norm) @ w_ukv).reshape(B, S, MLA_HEADS, MLA_NOPE + MLA_V)
    k_pe = jnp.broadcast_to(_rope(k_rope[:, :, None, :], positions), (B, S, MLA_HEADS, MLA_ROPE))
    k = jnp.concatenate([kv[..., :MLA_NOPE], k_pe], axis=-1)
    v = kv[..., MLA_NOPE:]
    n_blk = S // BLOCK
    q_blocks = (q * MLA_QK_DIM ** -0.5).reshape(B, n_blk, BLOCK, MLA_HEADS, MLA_QK_DIM).transpose(1, 0, 2, 3, 4)
    key_idx = jnp.arange(S)

    def attend(args):
        q_blk, blk = args
        s = jnp.einsum('bqhd,bkhd->bhqk', q_blk, k).astype(jnp.float32)
        q_idx = blk * BLOCK + jnp.arange(BLOCK)
        s = jnp.where(key_idx[None, :] <= q_idx[:, None], s, NEG_INF)
        p = jax.nn.softmax(s, axis=-1).astype(v.dtype)
        return jnp.einsum('bhqk,bkhd->bqhd', p, v)

    o = lax.map(attend, (q_blocks, jnp.arange(n_blk)))
    return o.transpose(1, 0, 2, 3, 4).reshape(B, S, MLA_WIDTH)


def _dilated_group(q, k, v, window, dilation, slopes):
    B, S, H, dh = q.shape
    span = window // dilation
    L = S // dilation
    n_blk = -(-L // BLOCK)
    Lp = n_blk * BLOCK

    def to_blocks(t):
        t = t.reshape(B, L, dilation, H, dh).transpose(0, 2, 3, 1, 4)
        t = jnp.pad(t, ((0, 0), (0, 0), (0, 0), (0, Lp - L), (0, 0)))
        return t.reshape(B, dilation, H, n_blk, BLOCK, dh)

    def band(t):
        prev = jnp.pad(t, ((0, 0), (0, 0), (0, 0), (1, 0), (0, 0), (0, 0)))[:, :, :, :-1]
        return jnp.concatenate([prev, t], axis=4)

    qb = to_blocks(q) * dh ** -0.5
    kb = band(to_blocks(k))
    vb = band(to_blocks(v))
    s = jnp.einsum('bdhnqe,bdhnke->bdhnqk', qb, kb).astype(jnp.float32)
    dist = jnp.arange(BLOCK)[:, None] + BLOCK - jnp.arange(2 * BLOCK)[None, :]
    key_sub = jnp.arange(n_blk)[:, None, None] * BLOCK - BLOCK + jnp.arange(2 * BLOCK)[None, None, :]
    valid = (dist >= 0) & (dist <= span) & (key_sub >= 0)
    alibi = -slopes.astype(jnp.float32)[:, None, None, None] * (dist * dilation).astype(jnp.float32)
    s = jnp.where(valid, s + alibi, NEG_INF)
    m = jnp.max(s, axis=-1, keepdims=True)
    e = jnp.exp(s - m)
    den = jnp.sum(e, axis=-1, keepdims=True)
    o = jnp.einsum('bdhnqk,bdhnke->bdhnqe', (e / den).astype(v.dtype), vb)
    lse = (m + jnp.log(den))[..., 0]
    o = o.reshape(B, dilation, H, Lp, dh)[:, :, :, :L].transpose(0, 3, 1, 2, 4).reshape(B, S, H, dh)
    lse = lse.reshape(B, dilation, H, Lp)[..., :L].transpose(0, 3, 1, 2).reshape(B, S, H)
    return o, lse


def _dilated_mixture(dil_qkv):
    B, S, _ = dil_qkv.shape
    qkv = dil_qkv.reshape(B, S, 3, DIL_GROUPS, DIL_HEADS_PER_GROUP, DIL_HEAD_DIM)
    slopes = _alibi_slopes(DIL_HEADS).reshape(DIL_HEADS_PER_GROUP, DIL_GROUPS).T
    outs, lses = [], []
    for g, (window, dilation) in enumerate(DIL_PAIRS):
        o, lse = _dilated_group(qkv[:, :, 0, g], qkv[:, :, 1, g], qkv[:, :, 2, g], window, dilation, slopes[g])
        outs.append(o)
        lses.append(lse)
    w = jax.nn.softmax(jnp.stack(lses, axis=0), axis=0)
    o_stack = jnp.stack(outs, axis=0)
    o = jnp.sum(w[..., None].astype(o_stack.dtype) * o_stack, axis=0)
    return o.reshape(B, S, DIL_WIDTH)


def _mem_attention(q, mem, g_mem, w_mem_kv):
    B, S, _ = q.shape
    M = mem.shape[1]
    kv = (_rms_norm(mem, g_mem) @ w_mem_kv).reshape(B, M, 2, MEM_HEADS, MEM_HEAD_DIM)
    qh = q.reshape(B, S, MEM_HEADS, MEM_HEAD_DIM) * MEM_HEAD_DIM ** -0.5
    s = jnp.einsum('bshd,bmhd->bhsm', qh, kv[:, :, 0]).astype(jnp.float32)
    p = jax.nn.softmax(s, axis=-1).astype(q.dtype)
    return jnp.einsum('bhsm,bmhd->bshd', p, kv[:, :, 1]).reshape(B, S, MEM_WIDTH)


def _conv_ffn(h, w_up, conv_w, conv_b, w_down):
    S = h.shape[1]
    u = h @ w_up
    u_pad = jnp.pad(u, ((0, 0), (CONV_WIDTH - 1, 0), (0, 0)))
    z = conv_b + conv_w[0] * u_pad[:, 0:S]
    for j in range(1, CONV_WIDTH):
        z = z + conv_w[j] * u_pad[:, j:j + S]
    gate, val = z[..., :D_FF], z[..., D_FF:]
    return (jax.nn.silu(gate) * val) @ w_down


def _layer(x, mem, positions, g_pre_mix, w_in, b_gate, mla_q_norm, w_uq, mla_kv_norm, w_ukv, g_mem, w_mem_kv,
           w_br_mla, w_br_dil, w_br_mem, w_o, g_post_mix, g_pre_ffn, w_ffn_up, conv_w, conv_b, w_ffn_down, g_post_ffn):
    B, S, _ = x.shape
    h = _rms_norm(x, g_pre_mix)
    proj = h @ w_in
    y_mla = _mla(proj[..., :OFF_Q], proj[..., OFF_Q:OFF_KV], proj[..., OFF_KV:OFF_KR], positions,
                 mla_q_norm, w_uq, mla_kv_norm, w_ukv)
    y_dil = _dilated_mixture(proj[..., OFF_KR:OFF_DIL])
    y_mem = _mem_attention(proj[..., OFF_DIL:OFF_MEMQ], mem, g_mem, w_mem_kv)
    gates = jax.nn.sigmoid((proj[..., OFF_MEMQ:] + b_gate).astype(jnp.float32)).astype(x.dtype)
    gates = gates.reshape(B, S, N_BRANCH, D_MODEL)
    merged = (gates[:, :, 0] * (y_mla @ w_br_mla)
              + gates[:, :, 1] * (y_dil @ w_br_dil)
              + gates[:, :, 2] * (y_mem @ w_br_mem))
    x = x + _rms_norm(merged @ w_o, g_post_mix)
    h2 = _rms_norm(x, g_pre_ffn)
    x = x + _rms_norm(_conv_ffn(h2, w_ffn_up, conv_w, conv_b, w_ffn_down), g_post_ffn)
    return x


def setup_inputs(seed: int = 0) -> dict:
    key = jax.random.key(seed)
    ks = jax.random.split(key, 24)
    f32 = jnp.float32

    def dense(k, fan_in, fan_out):
        return jax.random.normal(k, (DEPTH, fan_in, fan_out), f32) * fan_in ** -0.5

    def gain(k, n):
        return 1.0 + 0.05 * jax.random.normal(k, (DEPTH, n), f32)

    x = jax.random.normal(ks[0], (BATCH, SEQ, D_MODEL), f32)
    mem = jax.random.normal(ks[1], (BATCH, N_MEM, D_MODEL), f32)
    offset = jax.random.randint(ks[2], (BATCH, 1), 0, MAX_POS_OFFSET, dtype=jnp.int32)
    positions = (offset + jnp.arange(SEQ, dtype=jnp.int32)[None, :]).astype(jnp.int32)
    return {
        "x": x,
        "mem": mem,
        "positions": positions,
        "g_pre_mix": gain(ks[3], D_MODEL),
        "w_in": dense(ks[4], D_MODEL, D_IN),
        "b_gate": 0.1 * jax.random.normal(ks[5], (DEPTH, N_BRANCH * D_MODEL), f32),
        "mla_q_norm": gain(ks[6], MLA_Q_RANK),
        "w_uq": dense(ks[7], MLA_Q_RANK, MLA_HEADS * MLA_QK_DIM),
        "mla_kv_norm": gain(ks[8], MLA_KV_RANK),
        "w_ukv": dense(ks[9], MLA_KV_RANK, MLA_HEADS * (MLA_NOPE + MLA_V)),
        "g_mem": gain(ks[10], D_MODEL),
        "w_mem_kv": dense(ks[11], D_MODEL, 2 * MEM_WIDTH),
        "w_br_mla": dense(ks[12], MLA_WIDTH, D_MODEL),
        "w_br_dil": dense(ks[13], DIL_WIDTH, D_MODEL),
        "w_br_mem": dense(ks[14], MEM_WIDTH, D_MODEL),
        "w_o": dense(ks[15], D_MODEL, D_MODEL),
        "g_post_mix": gain(ks[16], D_MODEL),
        "g_pre_ffn": gain(ks[17], D_MODEL),
        "w_ffn_up": dense(ks[18], D_MODEL, 2 * D_FF),
        "conv_w": jax.random.normal(ks[19], (DEPTH, CONV_WIDTH, 2 * D_FF), f32) * CONV_WIDTH ** -0.5,
        "conv_b": 0.01 * jax.random.normal(ks[20], (DEPTH, 2 * D_FF), f32),
        "w_ffn_down": dense(ks[21], D_FF, D_MODEL),
        "g_post_ffn": gain(ks[22], D_MODEL),
    }


def reference(x, mem, positions, g_pre_mix, w_in, b_gate, mla_q_norm, w_uq, mla_kv_norm, w_ukv, g_mem, w_mem_kv,
              w_br_mla, w_br_dil, w_br_mem, w_o, g_post_mix, g_pre_ffn, w_ffn_up, conv_w, conv_b, w_ffn_down,
              g_post_ffn):
    for l in range(DEPTH):
        x = _layer(x, mem, positions, g_pre_mix[l], w_in[l], b_gate[l], mla_q_norm[l], w_uq[l], mla_kv_norm[l],
                   w_ukv[l], g_mem[l], w_mem_kv[l], w_br_mla[l], w_br_dil[l], w_br_mem[l], w_o[l], g_post_mix[l],
                   g_pre_ffn[l], w_ffn_up[l], conv_w[l], conv_b[l], w_ffn_down[l], g_post_ffn[l])
    return x
```

```python
from contextlib import ExitStack
import numpy as np
import concourse.bass as bass
import concourse.mybir as mybir
from concourse.bass_utils import run_bass_kernel_spmd

F32, BF16, I32 = mybir.dt.float32, mybir.dt.bfloat16, mybir.dt.int32
ALU = mybir.AluOpType
AF = mybir.ActivationFunctionType
NEG = -30000.0
D = 1024
SEQ = 8192
CH = 2048
NE = 2176
NT = 17
DFF = 2816
DIN = 8864
OFF_Q, OFF_KV, OFF_KR, OFF_DIL, OFF_MEMQ = 384, 640, 672, 672 + 4608, 672 + 4608 + 512
DIL_D = (1, 4, 16)
EPS = 1e-6
TWO_PI = 6.283185307179586
(C_GPRE, C_GMEM, C_GFFN, C_QN, C_KVN, C_BG, C_CW, C_CB, C_VB, C_ZERO, C_INVF, C_DVB0, C_DVB1, C_UFLAG,
 NCV) = (0, 8, 16, 24, 27, 29, 53, 185, 229, 232, 233, 249, 270, 291, 292)
GROUPS = [(0, 128), (128, 512), (640, 512), (1152, 512), (1664, 512)]
DEBUG = False


class Sched:
    CE = ('pe', 'act', 'dve', 'pool')

    def __init__(s, nc, sems, dsems):
        s.eng = {'pe': nc.tensor, 'act': nc.scalar, 'dve': nc.vector, 'pool': nc.gpsimd, 'sp': nc.sync}
        s.prog = {e: [] for e in s.eng}
        s.sem = sems
        s.dsems = dsems
        s.tick = {e: 0 for e in s.CE}
        s.seen = {e: {} for e in s.eng}
        s.bufs = {}
        s.dcount = {q: 0 for q in dsems}

    def _semh(s, k):
        return s.sem[k] if isinstance(k, str) else s.dsems[k[1]][k[2]]

    def _deps(s, eng, r, w):
        need = {}

        def add(k, v):
            if need.get(k, 0) < v:
                need[k] = v
        for key in r:
            b = s.bufs.get(key)
            if b and b['w']:
                add(*b['w'])
        for key in w:
            b = s.bufs.get(key)
            if b:
                if b['w'] and b['w'][0] != eng:
                    add(*b['w'])
                for k, v in b['r'].items():
                    if k != eng:
                        add(k, v)
        return need

    def _waits(s, q, need):
        for k, v in need.items():
            if s.seen[q].get(k, 0) >= v:
                continue
            s.seen[q][k] = v
            sem = s._semh(k)
            s.prog[q].append(lambda E, sem=sem, v=v: E.wait_ge(sem, v))

    def _record(s, ev, r, w):
        for key in r:
            b = s.bufs.setdefault(key, {'w': None, 'r': {}})
            if b['r'].get(ev[0], 0) < ev[1]:
                b['r'][ev[0]] = ev[1]
        for key in w:
            s.bufs[key] = {'w': ev, 'r': {}}

    def op(s, eng, fn, r=(), w=()):
        s._waits(eng, s._deps(eng, r, w))
        s.tick[eng] += 1
        sem = s.sem[eng]
        s.prog[eng].append(lambda E, fn=fn, sem=sem: fn(E).then_inc(sem, 1))
        s._record((eng, s.tick[eng]), r, w)

    def dma(s, q, fn, r=(), w=()):
        i = s.dcount[q]
        s.dcount[q] += 1
        ns = len(s.dsems[q])
        j, val = i % ns, 16 * (i // ns + 1)
        need = s._deps(None, r, w)
        if i >= ns:
            k = ('d', q, j)
            need[k] = max(need.get(k, 0), val - 16)
        s._waits(q, need)
        sem = s.dsems[q][j]
        s.prog[q].append(lambda E, fn=fn, sem=sem: fn(E).then_inc(sem, 16))
        s._record((('d', q, j), val), r, w)

    def barrier(s):
        for e in s.eng:
            need = {}
            for c in s.CE:
                if s.tick[c] > 0 and c != e:
                    need[c] = s.tick[c]
            for q in s.dsems:
                n, ns = s.dcount[q], len(s.dsems[q])
                for j in range(min(n, ns)):
                    need[('d', q, j)] = 16 * ((n - 1 - j) // ns + 1)
            s._waits(e, need)


def dil_geom(d):
    nq_tot = NE // d
    qb = []
    a = 0
    while a < nq_tot:
        qb.append((a, min(128, nq_tot - a)))
        a += 128
    kt = [(0, 128)] + [(128 + a0, n) for (a0, n) in qb]
    return qb, kt


def build_program():
    nc = bass.Bass("TRN2", target_bir_lowering=False, dynamic_dma_scratch_size=4096)

    def din(name, shape, dt=F32):
        return nc.dram_tensor(name, list(shape), dt, kind="ExternalInput").ap()
    xc = din("xc", [SEQ, D])
    posc = din("posc", [128, 64], I32)
    memx = din("memx", [256, D])
    cvd = din("cv", [128, NCV])
    gbcd = din("gbc", [2, D])
    dbd = din("dbias", [12, 128, 256])
    trid = din("tri", [128, 128])
    identd = din("ident", [128, 128])
    w_in = din("w_in", [D, DIN])
    w_uq = din("w_uq", [384, 768])
    w_ukv = din("w_ukv", [256, 1024])
    w_memkv = din("w_mem_kv", [D, 1024])
    w_br = [din("w_br_mla", [512, D]), din("w_br_dil", [512, D]), din("w_br_mem", [512, D])]
    w_o = din("w_o", [D, D])
    w_up = din("w_ffn_up", [D, 2 * DFF])
    w_dn = din("w_ffn_down", [DFF, D])
    outd = nc.dram_tensor("out", [CH, D], F32, kind="ExternalOutput").ap()
    wupb = nc.dram_tensor("wupb_scratch", [D, 2 * DFF], BF16).ap()
    wdnb = nc.dram_tensor("wdnb_scratch", [DFF, D], BF16).ap()
    dbg = {}
    if DEBUG:
        for nm in ("dbg_ydil", "dbg_ymla", "dbg_ymem", "dbg_merged"):
            dbg[nm] = nc.dram_tensor(nm, [128, 8 if nm == "dbg_merged" else 4, NE], BF16, kind="ExternalOutput").ap()
        dbg["dbg_h"] = nc.dram_tensor("dbg_h", [128, 8, NE], BF16, kind="ExternalOutput").ap()

    END = 212992
    cur = [4608]

    def sbt(name, shape, dt, at=None):
        esz = 4 if dt in (F32, I32) else 2
        n = 1
        for v in shape[1:]:
            n *= v
        nbytes = (n * esz + 63) // 64 * 64
        if at is None:
            off = cur[0]
            cur[0] += nbytes
        else:
            off = at[0]
            at[0] += nbytes
        assert off + nbytes <= END, (name, off, nbytes)
        return nc.alloc_sbuf_tensor_at(name, list(shape), dt, offset=off)

    ident = sbt("ident", [128, 128], BF16)
    ones = sbt("ones", [128, 128], BF16)
    tri = sbt("tri_sb", [128, 128], BF16)
    cv = sbt("cv_sb", [128, NCV], F32)
    gbc = sbt("gbc_sb", [128, 2, D], F32)
    cosT = sbt("cosT", [128, 64, 16], F32)
    sinT = sbt("sinT", [128, 64, 16], F32)
    cosQ = sbt("cosQ", [128, NT, 16], F32)
    sinQ = sbt("sinQ", [128, NT, 16], F32)
    small = sbt("small", [128, 64], F32)
    junk = sbt("junk", [128, D], BF16)
    hT_own = sbt("hT_own", [128, 8, NE], BF16)
    W_OFF = cur[0]
    wsl = [sbt(f"wslot{i}", [128, 4096], BF16) for i in range(4)]
    Y_OFF = cur[0]
    y_dilT = sbt("y_dilT", [128, 4, NE], BF16)
    y_mlaT = sbt("y_mlaT", [128, 4, NE], BF16)
    y_memT = sbt("y_memT", [128, 4, NE], BF16)
    T_OFF = cur[0]
    YB = 17408

    psf = nc.alloc_psum_tensor("psf", [128, 3584], F32)
    psb = nc.alloc_psum_tensor("psb", [128, 1024], BF16)

    def PF(b, n=512, off=0):
        return psf[:, b * 512 + off: b * 512 + off + n]

    def cvc(c, p=128):
        return cv[0:p, c:c + 1]

    es = ExitStack()
    sems = {e: es.enter_context(nc.semaphore(f"s_{e}")) for e in Sched.CE}
    dsems = {q: [es.enter_context(nc.semaphore(f"d_{q}{i}")) for i in range(8)] for q in ('sp', 'pool')}
    S = Sched(nc, sems, dsems)
    name_ctr = [0]

    wctr = [0]

    def wload(parts):
        i = wctr[0] % 4
        wctr[0] += 1
        views = []
        off = 0
        for src in parts:
            k, n = src.shape[1], src.shape[2]
            v = wsl[i][:, off:off + k * n].rearrange("p (k n) -> p k n", n=n)
            off += k * n
            assert off <= 4096
            S.dma('pool', lambda E, v=v, src=src: E.dma_start(out=v, in_=src), w=[('w', i)])
            views.append(v)
        return views, ('w', i)

    def wload_b(src, rkeys):
        i = wctr[0] % 4
        wctr[0] += 1
        k, n = src.shape[1], src.shape[2]
        v = wsl[i][:, 0:k * n].rearrange("p (k n) -> p k n", n=n)
        S.dma('sp', lambda E: E.dma_start(out=v, in_=src), r=rkeys, w=[('w', i)])
        return v, ('w', i)

    def wk(w, c0, n, kp=None):
        return w[:, c0:c0 + n].rearrange("(k p) n -> p k n", p=128)

    def mm(out, pairs, w, r):
        def f(E):
            last = None
            for i, (l, rr) in enumerate(pairs):
                last = E.matmul(out, lhsT=l, rhs=rr, start=(i == 0), stop=(i == len(pairs) - 1))
            return last
        S.op('pe', f, r=r, w=w)

    def act(out, in_, func, r, w, **kw):
        S.op('act', lambda E: E.activation(out=out, in_=in_, func=func, **kw), r=r, w=w)

    def rstd_from_ss(ss_ap, out_ap, tmp_ap, n_feat, rkeys, wkeys, tkeys):
        act(tmp_ap, ss_ap, AF.Ln, r=rkeys, w=tkeys, scale=1.0 / n_feat, bias=EPS)
        act(out_ap, tmp_ap, AF.Exp, r=tkeys, w=wkeys, scale=-0.5)

    def rms_T(tiles, gcol, dst, dkey, xhat, xkey):
        n = len(tiles)
        for i, (ap, key) in enumerate(tiles):
            S.op('act', lambda E, ap=ap, i=i: E.activation(out=junk[:, :], in_=ap, func=AF.Square,
                                                           accum_out=small[:, i:i + 1]),
                 r=[key], w=['junk', ('sm', i)])
        rstd_from_ss(small[:, 0:n], small[:, 16:16 + n], small[:, 8:8 + n], 1024.0,
                     [('sm', i) for i in range(n)], ['smr'], ['sml'])
        for i, (ap, key) in enumerate(tiles):
            S.op('dve', lambda E, ap=ap, i=i: E.tensor_scalar(out=xhat[:, i, :], in0=ap,
                                                              scalar1=small[:, 16 + i:17 + i], scalar2=None,
                                                              op0=ALU.mult),
                 r=[key, 'smr'], w=[(xkey, i)])
        for kc2 in range(4):
            def f(E, kc2=kc2):
                last = None
                for j in range(2):
                    kc = kc2 * 2 + j
                    for i in range(n):
                        last = E.transpose(psb[:, j * 512 + i * 128: j * 512 + (i + 1) * 128],
                                           xhat[:, i, kc * 128:(kc + 1) * 128], ident[:, :])
                return last
            S.op('pe', f, r=[(xkey, i) for i in range(n)] + ['ident'], w=['psb'])
            for j in range(2):
                kc = kc2 * 2 + j
                if j == 0:
                    act(dst(kc), psb[:, j * 512: j * 512 + n * 128], AF.Copy, r=['psb'], w=[dkey],
                        scale=cvc(gcol + kc))
                else:
                    S.op('dve', lambda E, kc=kc, j=j: E.tensor_scalar(out=dst(kc), in0=psb[:, j * 512: j * 512 + n * 128],
                                                                      scalar1=cvc(gcol + kc), scalar2=None, op0=ALU.mult),
                         r=['psb', 'cv'], w=[dkey])

    def rms_pipe(calls, gcol, xbufs, xhats, xname):
        def stageA(c):
            call = calls[c]
            par = c % 2
            base = par * 24
            n = len(call['srcs'])
            tiles = [xload(xbufs, src) for src in call['srcs']]
            for i, (ap, key) in enumerate(tiles):
                S.op('act', lambda E, ap=ap, i=i: E.activation(out=junk[:, :], in_=ap, func=AF.Square,
                                                               accum_out=small[:, base + i:base + i + 1]),
                     r=[key], w=['junk', ('sm', base + i)])
            rstd_from_ss(small[:, base:base + n], small[:, base + 16:base + 16 + n], small[:, base + 8:base + 8 + n],
                         1024.0, [('sm', base + i) for i in range(n)], [('smr', par)], [('sml', par)])
            for i, (ap, key) in enumerate(tiles):
                S.op('dve', lambda E, ap=ap, i=i: E.tensor_scalar(out=xhats[par][:, i, :], in0=ap,
                                                                  scalar1=small[:, base + 16 + i:base + 17 + i],
                                                                  scalar2=None, op0=ALU.mult),
                     r=[key, ('smr', par)], w=[(xname, par, i)])

        def stageB(c):
            call = calls[c]
            par = c % 2
            n = len(call['srcs'])
            dst, dkey = call['dst'], call['dkey']
            xh = xhats[par]
            for kc2 in range(4):
                def f(E, kc2=kc2):
                    last = None
                    for j in range(2):
                        kc = kc2 * 2 + j
                        for i in range(n):
                            last = E.transpose(psb[:, j * 512 + i * 128: j * 512 + (i + 1) * 128],
                                               xh[:, i, kc * 128:(kc + 1) * 128], ident[:, :])
                    return last
                S.op('pe', f, r=[(xname, par, i) for i in range(n)] + ['ident'], w=['psb'])
                for j in range(2):
                    kc = kc2 * 2 + j
                    if j == 0:
                        act(dst(kc), psb[:, j * 512: j * 512 + n * 128], AF.Copy, r=['psb'], w=[dkey],
                            scale=cvc(gcol + kc))
                    else:
                        S.op('dve', lambda E, kc=kc, j=j: E.tensor_scalar(out=dst(kc), in0=psb[:, j * 512: j * 512 + n * 128],
                                                                          scalar1=cvc(gcol + kc), scalar2=None,
                                                                          op0=ALU.mult), r=['psb', 'cv'], w=[dkey])
            if call.get('after'):
                call['after']()

        stageA(0)
        for c in range(len(calls)):
            if c + 1 < len(calls):
                stageA(c + 1)
            stageB(c)

    xctr = [0]

    def xload(xbufs, src):
        i = xctr[0] % len(xbufs)
        xctr[0] += 1
        dst_ = xbufs[i][:, :]
        S.dma('sp', lambda E: E.dma_start(out=dst_, in_=src), w=[('xb', i)])
        return dst_, ('xb', i)

    tp = [T_OFF]
    tmp32 = sbt("setup_tmp", [128, 128], F32, at=tp)
    S.dma('sp', lambda E: E.dma_start(out=cv[:, :], in_=cvd[:, :]), w=['cv'])
    S.dma('sp', lambda E: E.dma_start(out=gbc[:, :, :], in_=gbcd.partition_broadcast(128)), w=['gbc'])
    S.dma('pool', lambda E: E.dma_start(out=ident[:, :], in_=identd[:, :]), w=['ident'])
    S.dma('pool', lambda E: E.dma_start(out=tri[:, :], in_=trid[:, :]), w=['tri'])
    S.op('dve', lambda E: E.memset(ones[:, :], 1.0), w=['ones'])
    posi = sbt("posi", [128, 64], I32, at=tp)
    posf = sbt("posf", [128, 64], F32, at=tp)
    ang = sbt("ang", [128, 64, 16], F32, at=tp)
    kf = sbt("kf", [128, 64, 16], F32, at=tp)
    ki = sbt("ki", [128, 64, 16], I32, at=tp)
    S.dma('sp', lambda E: E.dma_start(out=posi[:, :], in_=posc[:, :]), w=['posi'])
    S.op('dve', lambda E: E.tensor_copy(out=posf[:, :], in_=posi[:, :]), r=['posi'], w=['posf'])
    S.op('dve', lambda E: E.tensor_tensor(out=ang[:, :, :], in0=posf[:, :].unsqueeze(2).to_broadcast([128, 64, 16]),
                                          in1=cv[:, C_INVF:C_INVF + 16].unsqueeze(1).to_broadcast([128, 64, 16]),
                                          op=ALU.mult), r=['posf', 'cv'], w=['ang'])
    for tab, shift in ((sinT, 0.0), (cosT, np.pi / 2)):
        S.op('dve', lambda E, shift=shift: E.tensor_scalar(out=kf[:, :, :], in0=ang[:, :, :], scalar1=float(shift),
                                                           scalar2=float(1.0 / TWO_PI), op0=ALU.add, op1=ALU.mult),
             r=['ang'], w=['kf'])
        S.op('dve', lambda E: E.tensor_copy(out=ki[:, :, :], in_=kf[:, :, :]), r=['kf'], w=['ki'])
        S.op('dve', lambda E: E.tensor_copy(out=kf[:, :, :], in_=ki[:, :, :]), r=['ki'], w=['kf'])
        S.op('dve', lambda E: E.scalar_tensor_tensor(out=kf[:, :, :], in0=kf[:, :, :], scalar=float(-TWO_PI),
                                                     in1=ang[:, :, :], op0=ALU.mult, op1=ALU.add),
             r=['kf', 'ang'], w=['kf'])
        S.op('dve', lambda E, shift=shift: E.tensor_scalar(out=kf[:, :, :], in0=kf[:, :, :], scalar1=float(shift),
                                                           scalar2=3.1415925, op0=ALU.add, op1=ALU.min),
             r=['kf'], w=['kf'])
        S.op('dve', lambda E: E.tensor_scalar(out=kf[:, :, :], in0=kf[:, :, :], scalar1=-3.1415925, scalar2=None,
                                              op0=ALU.max), r=['kf'], w=['kf'])
        act(tab[:, :, :], kf[:, :, :], AF.Sin, r=['kf'], w=['rope'])
    qs = 96.0 ** -0.5
    S.op('dve', lambda E: E.tensor_scalar(out=cosQ[:, :, :], in0=cosT[:, 47:64, :], scalar1=qs, scalar2=None,
                                          op0=ALU.mult), r=['rope'], w=['ropeq'])
    S.op('dve', lambda E: E.tensor_scalar(out=sinQ[:, :, :], in0=sinT[:, 47:64, :], scalar1=qs, scalar2=None,
                                          op0=ALU.mult), r=['rope'], w=['ropeq'])
    S.barrier()

    tp = [Y_OFF + YB]
    xbufA = [sbt(f"xbufA{i}", [128, D], F32, at=tp) for i in range(2)]
    xhatA = sbt("xhatA", [128, 4, D], BF16, at=tp)
    hT_halo = sbt("hT_halo", [128, 8, 2048], BF16, at=tp)
    QTd = sbt("QTd", [128, NE], BF16, at=tp)
    KTd = sbt("KTd", [128, 4224], BF16, at=tp)
    Vd = sbt("Vd", [128, 48, 128], BF16, at=tp)
    accO = sbt("accO", [128, NE], F32, at=tp)
    accD = sbt("accD", [128, NE], F32, at=tp)
    PTd = [sbt(f"PTd{i}", [128, 256], BF16, at=tp) for i in range(4)]
    ssb = [sbt(f"ssb{i}", [128, 256], F32, at=tp) for i in range(2)]
    dbt = [sbt(f"dbt{i}", [128, 256], F32, at=tp) for i in range(2)]

    def prep_all(row0, ntiles, dstbuf, dkey):
        t = 0
        while t < ntiles:
            n = min(2, ntiles - t)
            tiles = [xload(xbufA, xc[row0 + (t + i) * 128: row0 + (t + i + 1) * 128, :]) for i in range(n)]
            t0 = t
            rms_T(tiles, C_GPRE, lambda kc, t0=t0, n=n: dstbuf[:, kc, t0 * 128:(t0 + n) * 128], dkey, xhatA, 'xhA')
            t += n

    xhatsA = [xhatA[:, 0:2, :], xhatA[:, 2:4, :]]
    callsA = []
    for (row0, ntiles, dstbuf, dkey) in ((3968, 16, hT_halo, 'hTh'), (6016, NT, hT_own, 'hT')):
        t = 0
        while t < ntiles:
            n = min(2, ntiles - t)
            callsA.append(dict(srcs=[xc[row0 + (t + i) * 128: row0 + (t + i + 1) * 128, :] for i in range(n)],
                               dst=(lambda kc, t=t, n=n, dstbuf=dstbuf: dstbuf[:, kc, t * 128:(t + n) * 128]), dkey=dkey))
            t += n
    xbufA4 = xbufA + [accO[:, 0:D], accO[:, D:2 * D]]
    rms_pipe(callsA, C_GPRE, xbufA4, xhatsA, 'xhA')
    S.barrier()
    if DEBUG:
        S.dma('sp', lambda E: E.dma_start(out=dbg["dbg_h"][:, :, :], in_=hT_own[:, :, :]), r=['hT'])

    dscale = 128.0 ** -0.5
    bank_rr = [0]

    def nb():
        bank_rr[0] = (bank_rr[0] + 1) % 7
        return bank_rr[0]

    for hd in range(4):
        for g in range(3):
            d = DIL_D[g]
            gi = g * 4 + hd
            qb, kt = dil_geom(d)
            nql = NE // d
            nkl = 128 + nql
            c0 = OFF_KR + g * 512 + hd * 128
            (wq_, wk_, wv_), wkey = wload([wk(w_in, c0, 128), wk(w_in, c0 + 1536, 128), wk(w_in, c0 + 3072, 128)])
            bi = gi % 2
            S.dma('sp', lambda E, bi=bi, gi=gi: E.dma_start(out=dbt[bi][:, :], in_=dbd[gi, :, :]), w=[('dbt', bi)])
            QV = QTd[:, :].rearrange("p (r l) -> p r l", r=d)
            KV = KTd[:, 0:d * nkl].rearrange("p (r l) -> p r l", r=d)
            for (e0, n) in GROUPS:
                for which, wv3, dstv, loff, sc in ((0, wq_, QV, 0, dscale), (1, wk_, KV, 128, 1.0)):
                    b = nb()
                    mm(PF(b, n), [(wv3[:, kc, :], hT_own[:, kc, e0:e0 + n]) for kc in range(8)],
                       w=[('pf', b)], r=[wkey, 'hT'])
                    act(dstv[:, :, loff + e0 // d: loff + (e0 + n) // d],
                        PF(b, n).rearrange("p (l r) -> p r l", r=d), AF.Copy,
                        r=[('pf', b)], w=['QTd' if which == 0 else 'KTd'], scale=sc)
            h0 = 2048 - 128 * d
            while h0 < 2048:
                n = min(512, 2048 - h0)
                b = nb()
                mm(PF(b, n), [(wk_[:, kc, :], hT_halo[:, kc, h0:h0 + n]) for kc in range(8)],
                   w=[('pf', b)], r=[wkey, 'hTh'])
                l0 = (h0 - (2048 - 128 * d)) // d
                act(KV[:, :, l0: l0 + n // d], PF(b, n).rearrange("p (l r) -> p r l", r=d), AF.Copy,
                    r=[('pf', b)], w=['KTd'])
                h0 += n
            ntile = len(kt)
            for r_ in range(d):
                for j, (a, nk) in enumerate(kt):
                    if j == 0:
                        s0 = 2048 - 128 * d + r_
                        src = lambda kc, s0=s0, nk=nk: hT_halo[:, kc, s0: s0 + (nk - 1) * d + 1: d]
                        rk = 'hTh'
                    else:
                        s0 = (a - 128) * d + r_
                        src = lambda kc, s0=s0, nk=nk: hT_own[:, kc, s0: s0 + (nk - 1) * d + 1: d]
                        rk = 'hT'
                    b = nb()
                    mm(psf[0:nk, b * 512: b * 512 + 128], [(src(kc), wv_[:, kc, :]) for kc in range(8)],
                       w=[('pf', b)], r=[wkey, rk])
                    ti = r_ * ntile + j
                    S.op('dve', lambda E, nk=nk, b=b, ti=ti: E.tensor_copy(out=Vd[0:nk, ti, :],
                                                                         in_=psf[0:nk, b * 512: b * 512 + 128]),
                         r=[('pf', b)], w=['Vd'])
            for r_ in range(d):
                info = {}

                def dil_pv(m, r_=r_, info=info, d=d, g=g, qb=qb, ntile=ntile):
                    qa, qn = qb[m]
                    pp, pcol, _ = info[m]
                    pi, _, nk = info[m + 1]
                    b2 = nb()
                    tprev = r_ * ntile + m
                    tcur = r_ * ntile + m + 1

                    def f(E):
                        o = psf[:, b2 * 512: b2 * 512 + qn]
                        dd = psf[:, b2 * 512 + 128: b2 * 512 + 128 + qn]
                        E.matmul(o, lhsT=Vd[:, tprev, :], rhs=PTd[pp][:, pcol:pcol + qn], start=True, stop=False)
                        E.matmul(o, lhsT=Vd[0:nk, tcur, :], rhs=PTd[pi][0:nk, 0:qn], start=False, stop=True)
                        E.matmul(dd, lhsT=ones[:, :], rhs=PTd[pp][:, pcol:pcol + qn], start=False, stop=False,
                                 skip_group_check=True)
                        return E.matmul(dd, lhsT=ones[0:nk, :], rhs=PTd[pi][0:nk, 0:qn], start=False, stop=True,
                                        skip_group_check=True)
                    S.op('pe', f, r=['Vd', ('PTd', pp), ('PTd', pi), 'ones'], w=[('pf', b2)])
                    e_s = qa * d + r_
                    e_e = e_s + (qn - 1) * d + 1
                    for accb, off, akey in ((accO, 0, 'accO'), (accD, 128, 'accD')):
                        if g == 0:
                            S.op('dve', lambda E, accb=accb, off=off: E.tensor_copy(
                                out=accb[:, e_s:e_e:d], in_=psf[:, b2 * 512 + off: b2 * 512 + off + qn]),
                                r=[('pf', b2)], w=[akey])
                        else:
                            S.op('dve', lambda E, accb=accb, off=off: E.tensor_tensor(
                                out=accb[:, e_s:e_e:d], in0=psf[:, b2 * 512 + off: b2 * 512 + off + qn],
                                in1=accb[:, e_s:e_e:d], op=ALU.add), r=[('pf', b2), akey], w=[akey])

                for j, (a, nk) in enumerate(kt):
                    has_diag = j >= 1
                    has_prev = j < len(qb)
                    ncol = (qb[j - 1][1] if has_diag else 0) + (qb[j][1] if has_prev else 0)
                    qlo = qb[j - 1][0] if has_diag else qb[j][0]
                    boff = 0 if has_diag else 128
                    b = nb()
                    mm(psf[0:nk, b * 512: b * 512 + ncol], [(KV[:, r_, a:a + nk], QV[:, r_, qlo:qlo + ncol])],
                       w=[('pf', b)], r=['KTd', 'QTd'])
                    si = j % 2
                    S.op('dve', lambda E, nk=nk, b=b, ncol=ncol, si=si, boff=boff, bi=bi: E.tensor_tensor(
                        out=ssb[si][0:nk, 0:ncol], in0=psf[0:nk, b * 512: b * 512 + ncol],
                        in1=dbt[bi][0:nk, boff:boff + ncol], op=ALU.add),
                        r=[('pf', b), ('dbt', bi)], w=[('ssb', si)])
                    ridx = (0, 1, 5)[g] + r_
                    bcol = C_DVB0 + ridx if j == 0 else (C_DVB1 + ridx if j == 1 else C_ZERO)
                    pi = j % 4
                    act(PTd[pi][0:nk, 0:ncol], ssb[si][0:nk, 0:ncol], AF.Exp, r=[('ssb', si), 'cv'],
                        w=[('PTd', pi)], bias=cvc(bcol, nk))
                    info[j] = (pi, (qb[j - 1][1] if has_diag else 0), nk)
                    if j >= 2:
                        dil_pv(j - 2)
                dil_pv(len(kt) - 2)
        S.op('dve', lambda E: E.tensor_scalar(out=accD[:, :], in0=accD[:, :], scalar1=1e-30, scalar2=None,
                                              op0=ALU.max), r=['accD'], w=['accD'])
        S.op('dve', lambda E: E.reciprocal(out=accD[:, :], in_=accD[:, :]), r=['accD'], w=['accD'])
        S.op('dve', lambda E, hd=hd: E.tensor_tensor(out=y_dilT[:, hd, :], in0=accO[:, :], in1=accD[:, :],
                                                     op=ALU.mult), r=['accO', 'accD'], w=['ydil'])
    if DEBUG:
        S.dma('sp', lambda E: E.dma_start(out=dbg["dbg_ydil"][:, :, :], in_=y_dilT[:, :, :]), r=['ydil'])
    S.barrier()


    ckvT = nc.alloc_sbuf_tensor_at("ckvT", [128, 2, SEQ], BF16, offset=W_OFF)
    tp = [Y_OFF + 2 * YB]
    KT = sbt("KT", [128, SEQ], BF16, at=tp)
    cqT = sbt("cqT", [128, 3, NE], BF16, at=tp)
    Wuq = sbt("Wuq", [128, 3, 768], BF16, at=tp)
    Wukv = sbt("Wukv", [128, 2, 1024], BF16, at=tp)
    OV = tp[0]
    tp = [OV]
    xbufB = [nc.alloc_sbuf_tensor_at(f"xbufB{i}", [128, D], F32, offset=Y_OFF + YB + i * 4096) for i in range(4)]
    Wq = sbt("Wq", [128, 8, 384], BF16, at=tp)
    xhatB = sbt("xhatB", [128, 4, D], BF16, at=tp)
    hTgs = [sbt(f"hTg{i}", [128, 8, 512], BF16, at=tp) for i in range(2)]
    Wkvr = sbt("Wkvr", [128, 8, 288], BF16, at=tp)
    sq = sbt("sq", [128, 3, 512], BF16, at=tp)
    rbc = sbt("rbc", [128, 512], F32, at=tp)
    kpe = sbt("kpe", [128, 4, 96], BF16, at=tp)
    kt4 = [sbt(f"kt4_{i}", [128, 4, 16], F32, at=tp) for i in range(4)]

    S.dma('pool', lambda E: E.dma_start(out=Wkvr[:, :, :], in_=wk(w_in, OFF_Q, 288)), w=['Wkvr'])
    S.dma('pool', lambda E: E.dma_start(out=Wq[:, :, :], in_=wk(w_in, 0, 384)), w=['Wq'])
    S.dma('pool', lambda E: E.dma_start(out=Wuq[:, :, :], in_=w_uq.rearrange("(k p) n -> p k n", p=128)), w=['Wuq'])
    S.dma('pool', lambda E: E.dma_start(out=Wukv[:, :, :], in_=w_ukv.rearrange("(k p) n -> p k n", p=128)),
          w=['Wukv'])
    S.op('dve', lambda E: E.memset(kpe[:, :, :], 0.0), w=['kpe'])

    def latent_norm(banks, n, nchunk, gcol, dstf, dkey):
        for kc in range(nchunk):
            act(sq[:, kc, 0:n], PF(banks[kc], n), AF.Square, r=[('pf', banks[kc])], w=[('sq', kc)])
        bs = nb()
        mm(PF(bs, n), [(ones[:, :], sq[:, kc, 0:n]) for kc in range(nchunk)], w=[('pf', bs)],
           r=['ones'] + [('sq', kc) for kc in range(nchunk)])
        act(rbc[:, 0:n], PF(bs, n), AF.Ln, r=[('pf', bs)], w=['rbc'], scale=1.0 / (128 * nchunk), bias=EPS)
        act(rbc[:, 0:n], rbc[:, 0:n], AF.Exp, r=['rbc'], w=['rbc'], scale=-0.5)
        for kc in range(nchunk):
            S.op('dve', lambda E, kc=kc: E.scalar_tensor_tensor(out=dstf(kc), in0=PF(banks[kc], n),
                                                                scalar=cvc(gcol + kc), in1=rbc[:, 0:n],
                                                                op0=ALU.mult, op1=ALU.mult),
                 r=[('pf', banks[kc]), 'rbc', 'cv'], w=[dkey])

    def rope_tm(src3, cos3, sin3, dst3, nt, rkeys, wkey):
        a, b_ = src3[:, :, 0:16], src3[:, :, 16:32]
        t = [k[:, 0:nt, :] for k in kt4]
        S.op('dve', lambda E: E.tensor_tensor(out=t[0], in0=a, in1=cos3, op=ALU.mult), r=rkeys, w=[('kt4', 0)])
        S.op('dve', lambda E: E.tensor_tensor(out=t[1], in0=b_, in1=sin3, op=ALU.mult), r=rkeys, w=[('kt4', 1)])
        S.op('dve', lambda E: E.tensor_tensor(out=dst3[:, :, 0:16], in0=t[0], in1=t[1], op=ALU.subtract),
             r=[('kt4', 0), ('kt4', 1)], w=[wkey])
        S.op('dve', lambda E: E.tensor_tensor(out=t[2], in0=b_, in1=cos3, op=ALU.mult), r=rkeys, w=[('kt4', 2)])
        S.op('dve', lambda E: E.tensor_tensor(out=t[3], in0=a, in1=sin3, op=ALU.mult), r=rkeys, w=[('kt4', 3)])
        S.op('dve', lambda E: E.tensor_tensor(out=dst3[:, :, 16:32], in0=t[2], in1=t[3], op=ALU.add),
             r=[('kt4', 2), ('kt4', 3)], w=[wkey])

    xctr[0] = 0

    def ctx_latent(tg):
        hTg = hTgs[tg % 2]
        hk = ('hTg', tg % 2)
        banks = [nb(), nb()]
        for kc2 in range(2):
            mm(PF(banks[kc2]), [(Wkvr[:, kc, kc2 * 128:(kc2 + 1) * 128], hTg[:, kc, :]) for kc in range(8)],
               w=[('pf', banks[kc2])], r=['Wkvr', hk])
        latent_norm(banks, 512, 2, C_KVN, lambda kc: ckvT[:, kc, tg * 512:(tg + 1) * 512], 'ckvT')
        bk = nb()

        def fk(E):
            last = None
            for i in range(4):
                for kc in range(8):
                    last = E.matmul(psf[:, bk * 512 + i * 32: bk * 512 + (i + 1) * 32],
                                    lhsT=hTg[:, kc, i * 128:(i + 1) * 128], rhs=Wkvr[:, kc, 256:288],
                                    start=(kc == 0), stop=(kc == 7))
            return last
        S.op('pe', fk, r=['Wkvr', hk], w=[('pf', bk)])
        src3 = psf[:, bk * 512: bk * 512 + 128].rearrange("p (t c) -> p t c", c=32)
        rope_tm(src3, cosT[:, tg * 4:(tg + 1) * 4, :], sinT[:, tg * 4:(tg + 1) * 4, :], kpe[:, :, 64:96], 4,
                [('pf', bk), 'rope'], 'kpe')

        def ft(E):
            last = None
            for i in range(4):
                last = E.transpose(psb[0:96, i * 128:(i + 1) * 128], kpe[:, i, :], ident[:, :])
            return last
        S.op('pe', ft, r=['kpe', 'ident'], w=['psb'])
        act(KT[64:96, tg * 512:(tg + 1) * 512], psb[64:96, 0:512], AF.Copy, r=['psb'], w=['KTr'])

    xhatsB = [xhatB[:, 0:2, :], xhatB[:, 2:4, :]]
    callsB = []
    for tg in range(16):
        for hh in range(2):
            callsB.append(dict(
                srcs=[xc[(tg * 4 + hh * 2 + i) * 128:(tg * 4 + hh * 2 + i + 1) * 128, :] for i in range(2)],
                dst=(lambda kc, tg=tg, hh=hh: hTgs[tg % 2][:, kc, hh * 256:(hh + 1) * 256]), dkey=('hTg', tg % 2),
                after=((lambda tg=tg: ctx_latent(tg)) if hh == 1 else None)))
    rms_pipe(callsB, C_GPRE, xbufB, xhatsB, 'xhB')
    for (e0, n) in GROUPS:
        banks = [nb(), nb(), nb()]
        for kc3 in range(3):
            mm(PF(banks[kc3], n), [(Wq[:, kc, kc3 * 128:(kc3 + 1) * 128], hT_own[:, kc, e0:e0 + n]) for kc in range(8)],
               w=[('pf', banks[kc3])], r=['Wq', 'hT'])
        latent_norm(banks, n, 3, C_QN, lambda kc, e0=e0, n=n: cqT[:, kc, e0:e0 + n], 'cqT')
    S.barrier()

    for c in range(4):
        S.dma('pool', lambda E, c=c: E.dma_start(out=wupb[:, c * 1408:(c + 1) * 1408], in_=w_up[:, c * 1408:(c + 1) * 1408]),
              w=[('wupb', c)])
    for r_ in range(2):
        S.dma('pool', lambda E, r_=r_: E.dma_start(out=wdnb[r_ * 1408:(r_ + 1) * 1408, :], in_=w_dn[r_ * 1408:(r_ + 1) * 1408, :]),
              w=[('wdnb', r_)])
    tp = [OV]
    Vts = [sbt(f"Vt{i}", [128, 64, 65], BF16, at=tp) for i in range(2)]
    QTs = [sbt(f"QT{i}", [128, NE], BF16, at=tp) for i in range(2)]
    qtm = sbt("qtm", [128, NT, 96], BF16, at=tp)
    ypair = sbt("ypair", [128, NT, 128], BF16, at=tp)
    PTm = [sbt(f"PTm{i}", [128, 1024], BF16, at=tp) for i in range(3)]
    rden = sbt("rden", [128, 8], F32, at=tp)
    kt4 = [sbt(f"kq4_{i}", [128, 5, 16], F32, at=tp) for i in range(4)]
    for Vt_ in Vts:
        S.op('dve', lambda E, Vt_=Vt_: E.memset(Vt_[:, :, 64:65], 1.0), w=['Vones'])
    evac_rr = [0]

    def evac(out, in_, r, w, **kw):
        evac_rr[0] += 1
        if not kw:
            S.op('dve', lambda E: E.tensor_copy(out=out, in_=in_), r=r, w=w)
        else:
            act(out, in_, AF.Copy, r=r, w=w, **kw)

    pctr = [0]
    for h in range(8):
        for tg in range(16):
            b = nb()
            mm(psf[0:64, b * 512:(b + 1) * 512],
               [(Wukv[:, kc, h * 128: h * 128 + 64], ckvT[:, kc, tg * 512:(tg + 1) * 512]) for kc in range(2)],
               w=[('pf', b)], r=['Wukv', 'ckvT'])
            evac(KT[0:64, tg * 512:(tg + 1) * 512], psf[0:64, b * 512:(b + 1) * 512], r=[('pf', b)], w=['KTn'])
        def build_vq(h, bankf):
            Vt, QT = Vts[h % 2], QTs[h % 2]
            vk, qk = ('Vt', h % 2), ('QT', h % 2)
            chunks = []

            def vchunk(t8):
                b = bankf()

                def fv(E):
                    last = None
                    for i in range(8):
                        for kc in range(2):
                            last = E.matmul(psf[:, b * 512 + i * 64: b * 512 + (i + 1) * 64],
                                            lhsT=ckvT[:, kc, (t8 * 8 + i) * 128:(t8 * 8 + i + 1) * 128],
                                            rhs=Wukv[:, kc, h * 128 + 64: h * 128 + 128], start=(kc == 0), stop=(kc == 1))
                    return last
                S.op('pe', fv, r=['Wukv', 'ckvT'], w=[('pf', b)])
                evac(Vt[:, t8 * 8:(t8 + 1) * 8, 0:64], PF(b).rearrange("p (t e) -> p t e", e=64), r=[('pf', b)], w=[vk])

            def qchunk(t0):
                nt = min(5, NT - t0)
                b = bankf()

                def fq(E):
                    last = None
                    for i in range(nt):
                        for kc in range(3):
                            last = E.matmul(psf[:, b * 512 + i * 96: b * 512 + (i + 1) * 96],
                                            lhsT=cqT[:, kc, (t0 + i) * 128:(t0 + i + 1) * 128],
                                            rhs=Wuq[:, kc, h * 96:(h + 1) * 96], start=(kc == 0), stop=(kc == 2))
                    return last
                S.op('pe', fq, r=['Wuq', 'cqT'], w=[('pf', b)])
                p3 = psf[:, b * 512: b * 512 + nt * 96].rearrange("p (t c) -> p t c", c=96)
                S.op('dve', lambda E: E.tensor_scalar(out=qtm[:, t0:t0 + nt, 0:64], in0=p3[:, :, 0:64],
                                                      scalar1=qs, scalar2=None, op0=ALU.mult),
                     r=[('pf', b)], w=['qtm'])
                rope_tm(p3[:, :, 64:96], cosQ[:, t0:t0 + nt, :], sinQ[:, t0:t0 + nt, :], qtm[:, t0:t0 + nt, 64:96], nt,
                        [('pf', b), 'ropeq'], 'qtm')

            def tchunk(t0):
                nt = min(8, NT - t0)

                def ftq(E):
                    last = None
                    for i in range(nt):
                        last = E.transpose(psb[0:96, i * 128:(i + 1) * 128], qtm[:, t0 + i, :], ident[:, :])
                    return last
                S.op('pe', ftq, r=['qtm', 'ident'], w=['psb'])
                S.op('dve', lambda E: E.tensor_copy(out=QT[0:96, t0 * 128:(t0 + nt) * 128],
                                                    in_=psb[0:96, 0:nt * 128]), r=['psb'], w=[qk])
            for t8 in range(8):
                chunks.append(lambda t8=t8: vchunk(t8))
            for t0 in range(0, NT, 5):
                chunks.append(lambda t0=t0: qchunk(t0))
            for t0 in range(0, NT, 8):
                chunks.append(lambda t0=t0: tchunk(t0))
            return chunks

        if h == 0:
            for ch in build_vq(0, nb):
                ch()
        Vt, QT = Vts[h % 2], QTs[h % 2]
        vk, qk = ('Vt', h % 2), ('QT', h % 2)
        items = []
        for gq, (e0, n) in enumerate(GROUPS):
            tq0, ntq = e0 // 128, n // 128
            kfirst = 47 + tq0
            per = 1024 // n
            units = []
            kb = 0
            while kb < kfirst:
                lim = min(kfirst, (kb // 16 + 1) * 16, kb + per)
                units.append((list(range(kb, lim)), None))
                kb = lim
            for m in range(ntq):
                units.append(([kfirst + m], m))
            for ui, (kbs, m) in enumerate(units):
                items.append(dict(e0=e0, n=n, tq0=tq0, ntq=ntq, ob=4 + (gq % 2), kbs=kbs, m=m, first=(ui == 0),
                                  last=(ui == len(units) - 1)))

        def emit_S(it, i):
            sp, n, e0, m = ((pctr[0] + i) % 2) * 2, it['n'], it['e0'], it['m']
            if m is None:
                def fs(E, it=it, sp=sp, n=n, e0=e0, QT=QT):
                    last = None
                    for idx, kbk in enumerate(it['kbs']):
                        last = E.matmul(psf[:, sp * 512 + idx * n: sp * 512 + (idx + 1) * n],
                                        lhsT=KT[0:96, kbk * 128:(kbk + 1) * 128], rhs=QT[0:96, e0:e0 + n],
                                        start=True, stop=True)
                    return last
                S.op('pe', fs, r=['KTn', 'KTr', qk], w=[('pf', sp), ('pf', sp + 1)])
            else:
                kbk = it['kbs'][0]
                cols = n - 128 * m
                mm(psf[:, sp * 512: sp * 512 + cols],
                   [(KT[0:96, kbk * 128:(kbk + 1) * 128], QT[0:96, e0 + 128 * m:e0 + n])],
                   w=[('pf', sp), ('pf', sp + 1)], r=['KTn', 'KTr', qk])

        def emit_A(it, i):
            sp, pi = ((pctr[0] + i) % 2) * 2, (pctr[0] + i) % 3
            n, m = it['n'], it['m']
            kb0 = it['kbs'][0]
            tot = len(it['kbs']) * n if m is None else n - 128 * m
            act(PTm[pi][:, 0:tot], psf[:, sp * 512: sp * 512 + tot], AF.Exp,
                r=[('pf', sp), ('pf', sp + 1), 'cv'], w=[('PTm', pi)], bias=cvc(C_VB + min(kb0 // 16, 3)))
            if m is not None:
                S.op('dve', lambda E, pi=pi: E.tensor_tensor(out=PTm[pi][:, 0:128], in0=PTm[pi][:, 0:128],
                                                             in1=tri[:, :], op=ALU.mult),
                     r=[('PTm', pi), 'tri'], w=[('PTm', pi)])

        def emit_PV(it, i):
            pi = (pctr[0] + i) % 3
            n, m, ntq, ob, tq0 = it['n'], it['m'], it['ntq'], it['ob'], it['tq0']
            if m is None:
                blocks = [(idx * n, kbk, list(range(ntq))) for idx, kbk in enumerate(it['kbs'])]
            else:
                blocks = [(0, it['kbs'][0], list(range(m, ntq)))]

            def pv(E, pi=pi, blocks=blocks, ob=ob, first=it['first'], Vt=Vt):
                last = None
                st = first
                for (col0, kbk, qts) in blocks:
                    for qt in qts:
                        last = E.matmul(psf[:, ob * 512 + qt * 65: ob * 512 + qt * 65 + 65],
                                        lhsT=PTm[pi][:, col0 + (qt - qts[0]) * 128: col0 + (qt - qts[0] + 1) * 128],
                                        rhs=Vt[:, kbk, :], start=st, stop=False, skip_group_check=True)
                        st = False
                return last
            S.op('pe', pv, r=[('PTm', pi), vk, 'Vones'], w=[('pf', ob)])
            if it['last']:
                o3 = psf[:, ob * 512: ob * 512 + ntq * 65].rearrange("p (t c) -> p t c", c=65)
                S.op('dve', lambda E, o3=o3, ntq=ntq: E.tensor_scalar(out=rden[:, 0:ntq], in0=o3[:, :, 64],
                                                                      scalar1=1e-30, scalar2=None, op0=ALU.max),
                     r=[('pf', ob)], w=['rden'])
                S.op('dve', lambda E, ntq=ntq: E.reciprocal(out=rden[:, 0:ntq], in_=rden[:, 0:ntq]),
                     r=['rden'], w=['rden'])
                hoff = (h % 2) * 64
                S.op('dve', lambda E, o3=o3, ntq=ntq, tq0=tq0, hoff=hoff: E.tensor_tensor(
                    out=ypair[:, tq0:tq0 + ntq, hoff:hoff + 64], in0=o3[:, :, 0:64],
                    in1=rden[:, 0:ntq].unsqueeze(2).to_broadcast([128, ntq, 64]), op=ALU.mult),
                    r=[('pf', ob), 'rden'], w=['ypair'])

        for i, it in enumerate(items):
            emit_S(it, i)
            emit_A(it, i)
            if i >= 1:
                emit_PV(items[i - 1], i - 1)
            if i == 12 and h < 7:
                pending = build_vq(h + 1, lambda: 6)
            if i > 12 and h < 7 and pending:
                pending.pop(0)()
        emit_PV(items[-1], len(items) - 1)
        while h < 7 and pending:
            pending.pop(0)()
        pctr[0] += len(items)
        if h % 2 == 1:
            for t0 in range(0, NT, 8):
                nt = min(8, NT - t0)

                def fty(E, t0=t0, nt=nt):
                    last = None
                    for i in range(nt):
                        last = E.transpose(psb[:, i * 128:(i + 1) * 128], ypair[:, t0 + i, :], ident[:, :])
                    return last
                S.op('pe', fty, r=['ypair', 'ident'], w=['psb'])
                S.op('dve', lambda E, t0=t0, nt=nt, h=h: E.tensor_copy(out=y_mlaT[:, h // 2, t0 * 128:(t0 + nt) * 128],
                                                                       in_=psb[:, 0:nt * 128]), r=['psb'], w=['ymla'])
    if DEBUG:
        S.dma('sp', lambda E: E.dma_start(out=dbg["dbg_ymla"][:, :, :], in_=y_mlaT[:, :, :]), r=['ymla'])
    S.barrier()


    MERG = END - 34816
    mergedT = nc.alloc_sbuf_tensor_at("mergedT", [128, 8, NE], BF16, offset=MERG)
    tp = [T_OFF]
    memT = sbt("memT", [128, 8, 256], BF16, at=tp)
    KmT = sbt("KmT", [128, 4, 256], BF16, at=tp)
    VmT = sbt("VmT", [128, 2, 512], BF16, at=tp)
    OVC = tp[0]
    xbufC = [sbt(f"xbufC{i}", [128, D], F32, at=tp) for i in range(2)]
    xhatC = sbt("xhatC", [128, 2, D], BF16, at=tp)
    assert tp[0] <= MERG
    xctr[0] = 0
    tiles = [xload(xbufC, memx[i * 128:(i + 1) * 128, :]) for i in range(2)]
    rms_T(tiles, C_GMEM, lambda kc: memT[:, kc, :], 'memT', xhatC, 'xhC')
    (wkm,), kkm = wload([wk(w_memkv, 0, 512)])
    for h in range(4):
        b = nb()
        mm(PF(b, 256), [(wkm[:, kc, h * 128:(h + 1) * 128], memT[:, kc, :]) for kc in range(8)],
           w=[('pf', b)], r=[kkm, 'memT'])
        act(KmT[:, h, :], PF(b, 256), AF.Copy, r=[('pf', b)], w=['KmT'])
    (wvm,), kvm = wload([wk(w_memkv, 512, 512)])
    for mt in range(2):
        b = nb()
        mm(PF(b), [(memT[:, kc, mt * 128:(mt + 1) * 128], wvm[:, kc, :]) for kc in range(8)],
           w=[('pf', b)], r=[kvm, 'memT'])
        act(VmT[:, mt, :], PF(b), AF.Copy, r=[('pf', b)], w=['VmT'])
    S.barrier()
    tp = [OVC]
    qm = sbt("qm", [128, 512], BF16, at=tp)
    PTc = sbt("PTc", [128, 2, 512], BF16, at=tp)
    recc = sbt("recc", [128, 512], F32, at=tp)
    gsb = [sbt(f"gsb{i}", [128, 512], BF16, at=tp) for i in range(3)]
    mtc = [sbt(f"mtc{i}", [128, 512], F32, at=tp) for i in range(2)]
    assert tp[0] <= MERG
    (wqm,), kqm = wload([wk(w_in, OFF_DIL, 512)])
    for h in range(4):
        for (e0, n) in GROUPS:
            b = nb()
            mm(PF(b, n), [(wqm[:, kc, h * 128:(h + 1) * 128], hT_own[:, kc, e0:e0 + n]) for kc in range(8)],
               w=[('pf', b)], r=[kqm, 'hT'])
            act(qm[:, 0:n], PF(b, n), AF.Copy, r=[('pf', b)], w=['qm'], scale=dscale)
            for mt in range(2):
                b = nb()
                mm(PF(b, n), [(KmT[:, h, mt * 128:(mt + 1) * 128], qm[:, 0:n])], w=[('pf', b)], r=['KmT', 'qm'])
                act(PTc[:, mt, 0:n], PF(b, n), AF.Exp, r=[('pf', b)], w=[('PTc', mt)])
            bo, bd = nb(), nb()
            mm(PF(bo, n), [(VmT[:, mt, h * 128:(h + 1) * 128], PTc[:, mt, 0:n]) for mt in range(2)],
               w=[('pf', bo)], r=['VmT', ('PTc', 0), ('PTc', 1)])
            mm(PF(bd, n), [(ones[:, :], PTc[:, mt, 0:n]) for mt in range(2)],
               w=[('pf', bd)], r=['ones', ('PTc', 0), ('PTc', 1)])
            S.op('dve', lambda E, bd=bd, n=n: E.reciprocal(out=recc[:, 0:n], in_=PF(bd, n)), r=[('pf', bd)], w=['recc'])
            S.op('dve', lambda E, bo=bo, n=n, e0=e0, h=h: E.tensor_tensor(out=y_memT[:, h, e0:e0 + n], in0=PF(bo, n),
                                                                        in1=recc[:, 0:n], op=ALU.mult),
                 r=[('pf', bo), 'recc'], w=['ymem'])
    if DEBUG:
        S.dma('sp', lambda E: E.dma_start(out=dbg["dbg_ymem"][:, :, :], in_=y_memT[:, :, :]), r=['ymem'])
    yT = [y_mlaT, y_dilT, y_memT]
    ykeys = ['ymla', 'ydil', 'ymem']
    for oc in range(8):
        wg, kg = wload([wk(w_in, OFF_MEMQ + br * 1024 + oc * 128, 128) for br in range(3)])
        wb_, kb_ = wload([w_br[br][:, oc * 128:(oc + 1) * 128].rearrange("(k p) n -> p k n", p=128) for br in range(3)])
        for (e0, n) in GROUPS:
            for br in range(3):
                bg = nb()
                mm(PF(bg, n), [(wg[br][:, kc, :], hT_own[:, kc, e0:e0 + n]) for kc in range(8)],
                   w=[('pf', bg)], r=[kg, 'hT'])
                act(gsb[br][:, 0:n], PF(bg, n), AF.Sigmoid, r=[('pf', bg), 'cv'], w=[('gsb', br)],
                    bias=cvc(C_BG + br * 8 + oc))
                bb = nb()
                mm(PF(bb, n), [(wb_[br][:, kc, :], yT[br][:, kc, e0:e0 + n]) for kc in range(4)],
                   w=[('pf', bb)], r=[kb_, ykeys[br]])
                di = 0 if br == 0 else 1
                S.op('dve', lambda E, bb=bb, n=n, br=br, di=di: E.tensor_tensor(out=mtc[di][:, 0:n], in0=PF(bb, n),
                                                                               in1=gsb[br][:, 0:n], op=ALU.mult),
                     r=[('pf', bb), ('gsb', br)], w=[('mtc', di)])
                if br == 1:
                    S.op('dve', lambda E, n=n: E.tensor_tensor(out=mtc[0][:, 0:n], in0=mtc[0][:, 0:n],
                                                               in1=mtc[1][:, 0:n], op=ALU.add),
                         r=[('mtc', 0), ('mtc', 1)], w=[('mtc', 0)])
                if br == 2:
                    S.op('dve', lambda E, n=n, oc=oc, e0=e0: E.tensor_tensor(out=mergedT[:, oc, e0:e0 + n],
                                                                            in0=mtc[0][:, 0:n], in1=mtc[1][:, 0:n],
                                                                            op=ALU.add),
                         r=[('mtc', 0), ('mtc', 1)], w=['merged'])
    if DEBUG:
        S.dma('sp', lambda E: E.dma_start(out=dbg["dbg_merged"][:, :, :], in_=mergedT[:, :, :]), r=['merged'])
    S.barrier()

    x1 = nc.alloc_sbuf_tensor_at("x1", [128, 16, D], F32, offset=Y_OFF)
    tp = [Y_OFF + 65536]
    xbufD = [sbt("xbufD0", [128, D], F32, at=tp)]
    tmpD = sbt("tmpD", [128, D], F32, at=tp)
    x1pre = sbt("x1pre", [128, D], F32, at=tp)
    xhatD = sbt("xhatD", [128, 1, D], BF16, at=tp)
    assert tp[0] <= MERG
    h2T = hT_own
    (wo0,), ko0 = wload([wk(w_o, 0, 512)])
    (wo1,), ko1 = wload([wk(w_o, 512, 512)])
    xctr[0] = 0
    for t in range(NT):
        p = t % 3
        for half, wo_, ko in ((0, wo0, ko0), (1, wo1, ko1)):
            mm(psf[:, p * 1024 + half * 512: p * 1024 + (half + 1) * 512],
               [(mergedT[:, kc, t * 128:(t + 1) * 128], wo_[:, kc, :]) for kc in range(8)],
               w=[('pf', 2 * p + half)], r=[ko, 'merged'])
        pk = [('pf', 2 * p), ('pf', 2 * p + 1)]
        pfull = psf[:, p * 1024:(p + 1) * 1024]
        S.op('act', lambda E, pfull=pfull: E.activation(out=junk[:, :], in_=pfull, func=AF.Square,
                                                        accum_out=small[:, 32:33]), r=pk, w=['junk', ('sm', 32)])
        rstd_from_ss(small[:, 32:33], small[:, 34:35], small[:, 33:34], 1024.0, [('sm', 32)], [('sm', 34)], [('sm', 33)])
        xa, xk = xload(xbufD, xc[6016 + t * 128: 6016 + (t + 1) * 128, :])
        S.op('dve', lambda E, pfull=pfull: E.scalar_tensor_tensor(out=tmpD[:, :], in0=pfull, scalar=small[:, 34:35],
                                                                  in1=gbc[:, 0, :], op0=ALU.mult, op1=ALU.mult),
             r=pk + [('sm', 34), 'gbc'], w=['tmpD'])
        dst = x1pre[:, :] if t == 0 else x1[:, t - 1, :]
        dk = ('x1', t)
        S.op('dve', lambda E, dst=dst, xa=xa: E.tensor_tensor(out=dst, in0=tmpD[:, :], in1=xa, op=ALU.add),
             r=['tmpD', xk], w=[dk])
        rms_T([(dst, dk)], C_GFFN, lambda kc, t=t: h2T[:, kc, t * 128:(t + 1) * 128], 'h2T', xhatD, 'xhD')
    S.barrier()

    tp = [Y_OFF + 65536]
    gbuf = sbt("gbuf", [128, 22, 512], BF16, at=tp)
    ubuf = [sbt(f"ubuf{i}", [128, 528], F32, at=tp) for i in range(4)]
    zt = [sbt(f"zt{i}", [128, 512], F32, at=tp) for i in range(4)]
    sgs = [sbt(f"sg{i}", [128, 512], F32, at=tp) for i in range(2)]
    carry = sbt("carry", [128, 44, 2], F32, at=tp)
    fo0 = sbt("fo0", [128, 4, 512], F32, at=tp)
    tmpE = zt[0]
    eb = [0]

    def nbe():
        eb[0] = (eb[0] + 1) % 3
        return eb[0]
    for j in range(4):
        e0 = 128 + 512 * j
        for fc0 in range(0, 22, 4):
            nf = min(4, 22 - fc0)
            wupk = [('wupb', c) for c in range(4)]
            wgt, kgt = wload_b(wk(wupb, fc0 * 128, nf * 128), wupk)
            wvt, kvt = wload_b(wk(wupb, DFF + fc0 * 128, nf * 128), wupk)
            for fi in range(nf):
                fc = fc0 + fi
                if j > 0:
                    for fcn in ([0, 1] if fc == 0 else ([fc + 1] if fc + 1 < 22 else [])):
                        for half_ in range(2):
                            S.op('dve', lambda E, fcn=fcn, half_=half_: E.tensor_copy(
                                out=ubuf[half_ * 2 + fcn % 2][:, 0:2], in_=carry[:, half_ * 22 + fcn, :]),
                                r=[('carry', half_ * 22 + fcn)], w=[('ubc', half_ * 2 + fcn % 2)])
                for half, wt, kw_ in ((0, wgt, kgt), (1, wvt, kvt)):
                    ci = half * 22 + fc
                    b = nbe()
                    mm(PF(b), [(wt[:, kc, fi * 128:(fi + 1) * 128], h2T[:, kc, e0:e0 + 512]) for kc in range(8)],
                       w=[('pf', b)], r=[kw_, 'h2T'])
                    ub = ubuf[half * 2 + fc % 2]
                    uk = ('ub', half * 2 + fc % 2)
                    uck = ('ubc', half * 2 + fc % 2)
                    ztb = zt[half * 2 + fc % 2]
                    if j == 0:
                        pb_ = 3 + (ci % 4)
                        mm(psf[:, pb_ * 512: pb_ * 512 + 2],
                           [(wt[:, kc, fi * 128:(fi + 1) * 128], h2T[:, kc, 126:128]) for kc in range(8)],
                           w=[('pf', pb_)], r=[kw_, 'h2T'])
                        S.op('dve', lambda E, ub=ub, pb_=pb_: E.tensor_scalar(out=ub[:, 0:2], in0=psf[:, pb_ * 512: pb_ * 512 + 2],
                                                                             scalar1=cvc(C_UFLAG), scalar2=None, op0=ALU.mult),
                             r=[('pf', pb_), 'cv'], w=[uck])
                    act(ub[:, 2:514], PF(b), AF.Copy, r=[('pf', b)], w=[uk])
                    S.op('dve', lambda E, ub=ub, ci=ci: E.tensor_copy(out=carry[:, ci, :], in_=ub[:, 512:514]),
                         r=[uk], w=[('carry', ci)])
                    zk = ('zt', half * 2 + fc % 2)
                    act(ztb[:, :], ub[:, 0:512], AF.Identity, r=[uk, uck, 'cv'], w=[zk], scale=cvc(C_CW + ci),
                        bias=cvc(C_CB + ci))
                    S.op('dve', lambda E, ub=ub, ci=ci, ztb=ztb: E.scalar_tensor_tensor(
                        out=ztb[:, :], in0=ub[:, 1:513], scalar=cvc(C_CW + 44 + ci), in1=ztb[:, :],
                        op0=ALU.mult, op1=ALU.add), r=[uk, uck, zk, 'cv'], w=[zk])
                    S.op('dve', lambda E, ub=ub, ci=ci, ztb=ztb: E.scalar_tensor_tensor(
                        out=ztb[:, :], in0=ub[:, 2:514], scalar=cvc(C_CW + 88 + ci), in1=ztb[:, :],
                        op0=ALU.mult, op1=ALU.add), r=[uk, zk, 'cv'], w=[zk])
                sg = sgs[fc % 2]
                sk = ('sg', fc % 2)
                zg, zv = zt[fc % 2], zt[2 + fc % 2]
                act(sg[:, :], zg[:, :], AF.Silu, r=[('zt', fc % 2)], w=[sk])
                S.op('pool', lambda E, fc=fc, sg=sg, zv=zv: E.tensor_tensor(out=gbuf[:, fc, :], in0=sg[:, :], in1=zv[:, :],
                                                                          op=ALU.mult),
                     r=[sk, ('zt', 2 + fc % 2)], w=['gbuf'])
        for half in range(2):
            for (f0, nfk) in ((0, 8), (8, 8), (16, 6)):
                wd, kd = wload_b(wdnb[f0 * 128:(f0 + nfk) * 128, half * 512:(half + 1) * 512]
                                 .rearrange("(k p) n -> p k n", p=128), [('wdnb', 0), ('wdnb', 1)])
                for tt in range(4):
                    def fd(E, wd=wd, f0=f0, nfk=nfk, tt=tt):
                        last = None
                        for k in range(nfk):
                            last = E.matmul(PF(3 + tt), lhsT=gbuf[:, f0 + k, tt * 128:(tt + 1) * 128], rhs=wd[:, k, :],
                                            start=(f0 == 0 and k == 0), stop=(f0 == 16 and k == nfk - 1))
                        return last
                    S.op('pe', fd, r=[kd, 'gbuf'], w=[('pf', 3 + tt)])
            for tt in range(4):
                col = 40 + half * 4 + tt
                S.op('act', lambda E, tt=tt, col=col: E.activation(out=junk[:, 0:512], in_=PF(3 + tt), func=AF.Square,
                                                                   accum_out=small[:, col:col + 1]),
                     r=[('pf', 3 + tt)], w=['junk', ('sm', col)])
                if half == 0:
                    act(fo0[:, tt, :], PF(3 + tt), AF.Copy, r=[('pf', 3 + tt)], w=[('fo0', tt)])
            if half == 1:
                S.op('dve', lambda E: E.tensor_tensor(out=small[:, 48:52], in0=small[:, 40:44], in1=small[:, 44:48],
                                                      op=ALU.add), r=[('sm', c) for c in range(40, 48)], w=[('sm', 48)])
                rstd_from_ss(small[:, 48:52], small[:, 56:60], small[:, 52:56], 1024.0, [('sm', 48)], [('sm', 56)],
                             [('sm', 52)])
                for tt in range(4):
                    xt = x1[:, 4 * j + tt, :]
                    xk = ('x1', 4 * j + tt + 1)
                    rs = small[:, 56 + tt:57 + tt]
                    S.op('dve', lambda E, tt=tt, rs=rs: E.scalar_tensor_tensor(
                        out=fo0[:, tt, :], in0=fo0[:, tt, :], scalar=rs, in1=gbc[:, 1, 0:512], op0=ALU.mult,
                        op1=ALU.mult), r=[('fo0', tt), ('sm', 56), 'gbc'], w=[('fo0', tt)])
                    S.op('dve', lambda E, tt=tt, xt=xt: E.tensor_tensor(out=xt[:, 0:512], in0=xt[:, 0:512],
                                                                        in1=fo0[:, tt, :], op=ALU.add),
                         r=[('fo0', tt), xk], w=[xk])
                    S.op('dve', lambda E, tt=tt, rs=rs: E.scalar_tensor_tensor(
                        out=tmpE[:, :], in0=PF(3 + tt), scalar=rs, in1=gbc[:, 1, 512:1024], op0=ALU.mult,
                        op1=ALU.mult), r=[('pf', 3 + tt), ('sm', 56), 'gbc'], w=[('zt', 0)])
                    S.op('dve', lambda E, tt=tt, xt=xt: E.tensor_tensor(out=xt[:, 512:1024], in0=xt[:, 512:1024],
                                                                        in1=tmpE[:, :], op=ALU.add),
                         r=[('zt', 0), xk], w=[xk])
                    row = (4 * j + tt) * 128
                    S.dma('sp', lambda E, row=row, xt=xt: E.dma_start(out=outd[row:row + 128, :], in_=xt), r=[xk])

    return nc, es, S


def _finish(nc, es, S):
    S.barrier()
    with nc.Block() as block:
        @block.tensor
        def _(E):
            for f in S.prog['pe']:
                f(E)

        @block.scalar
        def _(E):
            for f in S.prog['act']:
                f(E)

        @block.vector
        def _(E):
            for f in S.prog['dve']:
                f(E)

        @block.gpsimd
        def _(E):
            for f in S.prog['pool']:
                f(E)

        @block.sync
        def _(E):
            for f in S.prog['sp']:
                f(E)
    es.close()
    return nc


def host_inputs(inputs):
    import ml_dtypes
    x = np.asarray(inputs["x"], np.float32)
    mem = np.asarray(inputs["mem"], np.float32)
    pos = np.asarray(inputs["positions"], np.int32)

    def P(k):
        return np.asarray(inputs[k], np.float32)[0]
    shared = {
        "gbc": np.stack([P("g_post_mix"), P("g_post_ffn")]).astype(np.float32),
        "tri": np.triu(np.ones((128, 128), np.float32)),
        "ident": np.eye(128, dtype=np.float32),
        "w_in": P("w_in"), "w_uq": P("w_uq"), "w_ukv": P("w_ukv"), "w_mem_kv": P("w_mem_kv"),
        "w_br_mla": P("w_br_mla"), "w_br_dil": P("w_br_dil"), "w_br_mem": P("w_br_mem"), "w_o": P("w_o"),
        "w_ffn_up": P("w_ffn_up"), "w_ffn_down": P("w_ffn_down"),
    }
    db = np.zeros((12, 128, 256), np.float32)
    slopes = np.exp2(-8.0 * np.arange(1, 13, dtype=np.float32) / 12).reshape(4, 3).T
    k = np.arange(128)[:, None]
    i = np.arange(128)[None, :]
    for g in range(3):
        for hd in range(4):
            a = slopes[g, hd] * DIL_D[g]
            dist = i - k
            db[g * 4 + hd, :, 0:128] = np.where(dist >= 0, -a * dist, NEG)
            dist2 = 128 + i - k
            db[g * 4 + hd, :, 128:256] = np.where(k >= i, -a * dist2, NEG)
    shared["dbias"] = db

    def cols(v, n):
        return v.reshape(n, 128).T
    cv0 = np.zeros((128, NCV), np.float32)
    cv0[:, C_GPRE:C_GPRE + 8] = cols(P("g_pre_mix"), 8)
    cv0[:, C_GMEM:C_GMEM + 8] = cols(P("g_mem"), 8)
    cv0[:, C_GFFN:C_GFFN + 8] = cols(P("g_pre_ffn"), 8)
    cv0[:, C_QN:C_QN + 3] = cols(P("mla_q_norm"), 3)
    cv0[:, C_KVN:C_KVN + 2] = cols(P("mla_kv_norm"), 2)
    cv0[:, C_BG:C_BG + 24] = cols(P("b_gate"), 24)
    cw = P("conv_w")
    for j in range(3):
        cv0[:, C_CW + j * 44: C_CW + (j + 1) * 44] = cols(cw[j], 44)
    cv0[:, C_CB:C_CB + 44] = cols(P("conv_b"), 44)
    cv0[:, C_INVF:C_INVF + 16] = (np.float32(10000.0) ** (-np.arange(16, dtype=np.float32) / np.float32(16)))[None, :]
    in_maps = []
    for c in range(8):
        b, q = c // 4, c % 4
        xcx = np.zeros((SEQ, D), np.float32)
        pc = np.zeros((SEQ,), np.int32)
        lo = 6144 - CH * q
        xcx[lo:] = x[b, 0: CH * (q + 1)]
        pc[lo:] = pos[b, 0: CH * (q + 1)]
        cvq = cv0.copy()
        for cidx in range(3):
            cvq[:, C_VB + cidx] = 0.0 if (cidx + q) >= 3 else NEG
        ridx = 0
        for g in range(3):
            d = DIL_D[g]
            for r_ in range(d):
                kk = np.arange(128)
                tau0 = 3968 + (2048 - 128 * d) + kk * d + r_
                cvq[:, C_DVB0 + ridx] = np.where(tau0 >= lo, 0.0, NEG)
                tau1 = 3968 + 2048 + kk * d + r_
                cvq[:, C_DVB1 + ridx] = np.where(tau1 >= lo, 0.0, NEG)
                ridx += 1
        cvq[:, C_UFLAG] = 1.0 if q > 0 else 0.0
        m = dict(shared)
        m["xc"] = xcx
        m["posc"] = np.ascontiguousarray(pc.reshape(64, 128).T)
        m["memx"] = np.ascontiguousarray(mem[b])
        m["cv"] = cvq
        in_maps.append(m)
    return in_maps


def kernel(**inputs):
    in_maps = host_inputs(inputs)
    nc, es, S = build_program()
    nc = _finish(nc, es, S)
    res = run_bass_kernel_spmd(nc, in_maps, core_ids=list(range(8)))
    out = np.zeros((2, SEQ, D), np.float32)
    for c in range(8):
        b, q = c // 4, c % 4
        out[b, q * CH:(q + 1) * CH] = np.asarray(res.results[c]["out"], np.float32)
    return out
```

```python
from contextlib import ExitStack
import numpy as np
import concourse.bass as bass
import concourse.mybir as mybir
from concourse.bass_utils import run_bass_kernel_spmd

F32, BF16, I32 = mybir.dt.float32, mybir.dt.bfloat16, mybir.dt.int32
ALU = mybir.AluOpType
AF = mybir.ActivationFunctionType
NEG = -30000.0
D = 1024
SEQ = 8192
CH = 2048
NE = 2176
NT = 17
DFF = 2816
DIN = 8864
OFF_Q, OFF_KV, OFF_KR, OFF_DIL, OFF_MEMQ = 384, 640, 672, 672 + 4608, 672 + 4608 + 512
DIL_D = (1, 4, 16)
EPS = 1e-6
TWO_PI = 6.283185307179586
(C_GPRE, C_GMEM, C_GFFN, C_QN, C_KVN, C_BG, C_CW, C_CB, C_VB, C_ZERO, C_INVF, C_DVB0, C_DVB1, C_UFLAG,
 NCV) = (0, 8, 16, 24, 27, 29, 53, 185, 229, 232, 233, 249, 270, 291, 292)
GROUPS = [(0, 128), (128, 512), (640, 512), (1152, 512), (1664, 512)]
DEBUG = False


class Sched:
    CE = ('pe', 'act', 'dve', 'pool')

    def __init__(s, nc, sems, dsems):
        s.eng = {'pe': nc.tensor, 'act': nc.scalar, 'dve': nc.vector, 'pool': nc.gpsimd, 'sp': nc.sync}
        s.prog = {e: [] for e in s.eng}
        s.sem = sems
        s.dsems = dsems
        s.tick = {e: 0 for e in s.CE}
        s.seen = {e: {} for e in s.eng}
        s.bufs = {}
        s.dcount = {q: 0 for q in dsems}

    def _semh(s, k):
        return s.sem[k] if isinstance(k, str) else s.dsems[k[1]][k[2]]

    def _deps(s, eng, r, w):
        need = {}

        def add(k, v):
            if need.get(k, 0) < v:
                need[k] = v
        for key in r:
            b = s.bufs.get(key)
            if b and b['w']:
                add(*b['w'])
        for key in w:
            b = s.bufs.get(key)
            if b:
                if b['w'] and b['w'][0] != eng:
                    add(*b['w'])
                for k, v in b['r'].items():
                    if k != eng:
                        add(k, v)
        return need

    def _waits(s, q, need):
        for k, v in need.items():
            if s.seen[q].get(k, 0) >= v:
                continue
            s.seen[q][k] = v
            sem = s._semh(k)
            s.prog[q].append(lambda E, sem=sem, v=v: E.wait_ge(sem, v))

    def _record(s, ev, r, w):
        for key in r:
            b = s.bufs.setdefault(key, {'w': None, 'r': {}})
            if b['r'].get(ev[0], 0) < ev[1]:
                b['r'][ev[0]] = ev[1]
        for key in w:
            s.bufs[key] = {'w': ev, 'r': {}}

    def op(s, eng, fn, r=(), w=()):
        s._waits(eng, s._deps(eng, r, w))
        s.tick[eng] += 1
        sem = s.sem[eng]
        s.prog[eng].append(lambda E, fn=fn, sem=sem: fn(E).then_inc(sem, 1))
        s._record((eng, s.tick[eng]), r, w)

    def dma(s, q, fn, r=(), w=()):
        i = s.dcount[q]
        s.dcount[q] += 1
        ns = len(s.dsems[q])
        j, val = i % ns, 16 * (i // ns + 1)
        need = s._deps(None, r, w)
        if i >= ns:
            k = ('d', q, j)
            need[k] = max(need.get(k, 0), val - 16)
        s._waits(q, need)
        sem = s.dsems[q][j]
        s.prog[q].append(lambda E, fn=fn, sem=sem: fn(E).then_inc(sem, 16))
        s._record((('d', q, j), val), r, w)

    def barrier(s):
        for e in s.eng:
            need = {}
            for c in s.CE:
                if s.tick[c] > 0 and c != e:
                    need[c] = s.tick[c]
            for q in s.dsems:
                n, ns = s.dcount[q], len(s.dsems[q])
                for j in range(min(n, ns)):
                    need[('d', q, j)] = 16 * ((n - 1 - j) // ns + 1)
            s._waits(e, need)


def dil_geom(d):
    nq_tot = NE // d
    qb = []
    a = 0
    while a < nq_tot:
        qb.append((a, min(128, nq_tot - a)))
        a += 128
    kt = [(0, 128)] + [(128 + a0, n) for (a0, n) in qb]
    return qb, kt


def build_program():
    nc = bass.Bass("TRN2", target_bir_lowering=False, dynamic_dma_scratch_size=4096)

    def din(name, shape, dt=F32):
        return nc.dram_tensor(name, list(shape), dt, kind="ExternalInput").ap()
    xc = din("xc", [SEQ, D])
    posc = din("posc", [128, 64], I32)
    memx = din("memx", [256, D])
    cvd = din("cv", [128, NCV])
    gbcd = din("gbc", [2, D])
    dbd = din("dbias", [12, 128, 256])
    trid = din("tri", [128, 128])
    identd = din("ident", [128, 128])
    w_in = din("w_in", [D, DIN])
    w_uq = din("w_uq", [384, 768])
    w_ukv = din("w_ukv", [256, 1024])
    w_memkv = din("w_mem_kv", [D, 1024])
    w_br = [din("w_br_mla", [512, D]), din("w_br_dil", [512, D]), din("w_br_mem", [512, D])]
    w_o = din("w_o", [D, D])
    w_up = din("w_ffn_up", [D, 2 * DFF])
    w_dn = din("w_ffn_down", [DFF, D])
    outd = nc.dram_tensor("out", [CH, D], F32, kind="ExternalOutput").ap()
    wupb = nc.dram_tensor("wupb_scratch", [D, 2 * DFF], BF16).ap()
    wdnb = nc.dram_tensor("wdnb_scratch", [DFF, D], BF16).ap()
    dbg = {}
    if DEBUG:
        for nm in ("dbg_ydil", "dbg_ymla", "dbg_ymem", "dbg_merged"):
            dbg[nm] = nc.dram_tensor(nm, [128, 8 if nm == "dbg_merged" else 4, NE], BF16, kind="ExternalOutput").ap()
        dbg["dbg_h"] = nc.dram_tensor("dbg_h", [128, 8, NE], BF16, kind="ExternalOutput").ap()

    END = 212992
    cur = [4608]

    def sbt(name, shape, dt, at=None):
        esz = 4 if dt in (F32, I32) else 2
        n = 1
        for v in shape[1:]:
            n *= v
        nbytes = (n * esz + 63) // 64 * 64
        if at is None:
            off = cur[0]
            cur[0] += nbytes
        else:
            off = at[0]
            at[0] += nbytes
        assert off + nbytes <= END, (name, off, nbytes)
        return nc.alloc_sbuf_tensor_at(name, list(shape), dt, offset=off)

    ident = sbt("ident", [128, 128], BF16)
    ones = sbt("ones", [128, 128], BF16)
    tri = sbt("tri_sb", [128, 128], BF16)
    cv = sbt("cv_sb", [128, NCV], F32)
    gbc = sbt("gbc_sb", [128, 2, D], F32)
    cosT = sbt("cosT", [128, 64, 16], F32)
    sinT = sbt("sinT", [128, 64, 16], F32)
    cosQ = sbt("cosQ", [128, NT, 16], F32)
    sinQ = sbt("sinQ", [128, NT, 16], F32)
    small = sbt("small", [128, 64], F32)
    junk = sbt("junk", [128, D], BF16)
    hT_own = sbt("hT_own", [128, 8, NE], BF16)
    W_OFF = cur[0]
    wsl = [sbt(f"wslot{i}", [128, 4096], BF16) for i in range(4)]
    Y_OFF = cur[0]
    y_dilT = sbt("y_dilT", [128, 4, NE], BF16)
    y_mlaT = sbt("y_mlaT", [128, 4, NE], BF16)
    y_memT = sbt("y_memT", [128, 4, NE], BF16)
    T_OFF = cur[0]
    YB = 17408

    psf = nc.alloc_psum_tensor("psf", [128, 3584], F32)
    psb = nc.alloc_psum_tensor("psb", [128, 1024], BF16)

    def PF(b, n=512, off=0):
        return psf[:, b * 512 + off: b * 512 + off + n]

    def cvc(c, p=128):
        return cv[0:p, c:c + 1]

    es = ExitStack()
    sems = {e: es.enter_context(nc.semaphore(f"s_{e}")) for e in Sched.CE}
    dsems = {q: [es.enter_context(nc.semaphore(f"d_{q}{i}")) for i in range(8)] for q in ('sp', 'pool')}
    S = Sched(nc, sems, dsems)
    name_ctr = [0]

    wctr = [0]

    def wload(parts):
        i = wctr[0] % 4
        wctr[0] += 1
        views = []
        off = 0
        for src in parts:
            k, n = src.shape[1], src.shape[2]
            v = wsl[i][:, off:off + k * n].rearrange("p (k n) -> p k n", n=n)
            off += k * n
            assert off <= 4096
            S.dma('pool', lambda E, v=v, src=src: E.dma_start(out=v, in_=src), w=[('w', i)])
            views.append(v)
        return views, ('w', i)

    def wload_b(src, rkeys):
        i = wctr[0] % 4
        wctr[0] += 1
        k, n = src.shape[1], src.shape[2]
        v = wsl[i][:, 0:k * n].rearrange("p (k n) -> p k n", n=n)
        S.dma('sp', lambda E: E.dma_start(out=v, in_=src), r=rkeys, w=[('w', i)])
        return v, ('w', i)

    def wk(w, c0, n, kp=None):
        return w[:, c0:c0 + n].rearrange("(k p) n -> p k n", p=128)

    def mm(out, pairs, w, r):
        def f(E):
            last = None
            for i, (l, rr) in enumerate(pairs):
                last = E.matmul(out, lhsT=l, rhs=rr, start=(i == 0), stop=(i == len(pairs) - 1))
            return last
        S.op('pe', f, r=r, w=w)

    def act(out, in_, func, r, w, **kw):
        S.op('act', lambda E: E.activation(out=out, in_=in_, func=func, **kw), r=r, w=w)

    def rstd_from_ss(ss_ap, out_ap, tmp_ap, n_feat, rkeys, wkeys, tkeys):
        act(tmp_ap, ss_ap, AF.Ln, r=rkeys, w=tkeys, scale=1.0 / n_feat, bias=EPS)
        act(out_ap, tmp_ap, AF.Exp, r=tkeys, w=wkeys, scale=-0.5)

    def rms_T(tiles, gcol, dst, dkey, xhat, xkey):
        n = len(tiles)
        for i, (ap, key) in enumerate(tiles):
            S.op('act', lambda E, ap=ap, i=i: E.activation(out=junk[:, :], in_=ap, func=AF.Square,
                                                           accum_out=small[:, i:i + 1]),
                 r=[key], w=['junk', ('sm', i)])
        rstd_from_ss(small[:, 0:n], small[:, 16:16 + n], small[:, 8:8 + n], 1024.0,
                     [('sm', i) for i in range(n)], ['smr'], ['sml'])
        for i, (ap, key) in enumerate(tiles):
            S.op('dve', lambda E, ap=ap, i=i: E.tensor_scalar(out=xhat[:, i, :], in0=ap,
                                                              scalar1=small[:, 16 + i:17 + i], scalar2=None,
                                                              op0=ALU.mult),
                 r=[key, 'smr'], w=[(xkey, i)])
        for kc2 in range(4):
            def f(E, kc2=kc2):
                last = None
                for j in range(2):
                    kc = kc2 * 2 + j
                    for i in range(n):
                        last = E.transpose(psb[:, j * 512 + i * 128: j * 512 + (i + 1) * 128],
                                           xhat[:, i, kc * 128:(kc + 1) * 128], ident[:, :])
                return last
            S.op('pe', f, r=[(xkey, i) for i in range(n)] + ['ident'], w=['psb'])
            for j in range(2):
                kc = kc2 * 2 + j
                if j == 0:
                    act(dst(kc), psb[:, j * 512: j * 512 + n * 128], AF.Copy, r=['psb'], w=[dkey],
                        scale=cvc(gcol + kc))
                else:
                    S.op('dve', lambda E, kc=kc, j=j: E.tensor_scalar(out=dst(kc), in0=psb[:, j * 512: j * 512 + n * 128],
                                                                      scalar1=cvc(gcol + kc), scalar2=None, op0=ALU.mult),
                         r=['psb', 'cv'], w=[dkey])

    def rms_pipe(calls, gcol, xbufs, xhats, xname):
        def stageA(c):
            call = calls[c]
            par = c % 2
            base = par * 24
            n = len(call['srcs'])
            tiles = [xload(xbufs, src) for src in call['srcs']]
            for i, (ap, key) in enumerate(tiles):
                S.op('act', lambda E, ap=ap, i=i: E.activation(out=junk[:, :], in_=ap, func=AF.Square,
                                                               accum_out=small[:, base + i:base + i + 1]),
                     r=[key], w=['junk', ('sm', base + i)])
            rstd_from_ss(small[:, base:base + n], small[:, base + 16:base + 16 + n], small[:, base + 8:base + 8 + n],
                         1024.0, [('sm', base + i) for i in range(n)], [('smr', par)], [('sml', par)])
            for i, (ap, key) in enumerate(tiles):
                S.op('dve', lambda E, ap=ap, i=i: E.tensor_scalar(out=xhats[par][:, i, :], in0=ap,
                                                                  scalar1=small[:, base + 16 + i:base + 17 + i],
                                                                  scalar2=None, op0=ALU.mult),
                     r=[key, ('smr', par)], w=[(xname, par, i)])

        def stageB(c):
            call = calls[c]
            par = c % 2
            n = len(call['srcs'])
            dst, dkey = call['dst'], call['dkey']
            xh = xhats[par]
            for kc2 in range(4):
                def f(E, kc2=kc2):
                    last = None
                    for j in range(2):
                        kc = kc2 * 2 + j
                        for i in range(n):
                            last = E.transpose(psb[:, j * 512 + i * 128: j * 512 + (i + 1) * 128],
                                               xh[:, i, kc * 128:(kc + 1) * 128], ident[:, :])
                    return last
                S.op('pe', f, r=[(xname, par, i) for i in range(n)] + ['ident'], w=['psb'])
                for j in range(2):
                    kc = kc2 * 2 + j
                    if j == 0:
                        act(dst(kc), psb[:, j * 512: j * 512 + n * 128], AF.Copy, r=['psb'], w=[dkey],
                            scale=cvc(gcol + kc))
                    else:
                        S.op('dve', lambda E, kc=kc, j=j: E.tensor_scalar(out=dst(kc), in0=psb[:, j * 512: j * 512 + n * 128],
                                                                          scalar1=cvc(gcol + kc), scalar2=None,
                                                                          op0=ALU.mult), r=['psb', 'cv'], w=[dkey])
            if call.get('after'):
                call['after']()

        stageA(0)
        for c in range(len(calls)):
            if c + 1 < len(calls):
                stageA(c + 1)
            stageB(c)

    xctr = [0]

    def xload(xbufs, src):
        i = xctr[0] % len(xbufs)
        xctr[0] += 1
        dst_ = xbufs[i][:, :]
        S.dma('sp', lambda E: E.dma_start(out=dst_, in_=src), w=[('xb', i)])
        return dst_, ('xb', i)

    tp = [T_OFF]
    tmp32 = sbt("setup_tmp", [128, 128], F32, at=tp)
    S.dma('sp', lambda E: E.dma_start(out=cv[:, :], in_=cvd[:, :]), w=['cv'])
    S.dma('sp', lambda E: E.dma_start(out=gbc[:, :, :], in_=gbcd.partition_broadcast(128)), w=['gbc'])
    S.dma('pool', lambda E: E.dma_start(out=ident[:, :], in_=identd[:, :]), w=['ident'])
    S.dma('pool', lambda E: E.dma_start(out=tri[:, :], in_=trid[:, :]), w=['tri'])
    S.op('dve', lambda E: E.memset(ones[:, :], 1.0), w=['ones'])
    posi = sbt("posi", [128, 64], I32, at=tp)
    posf = sbt("posf", [128, 64], F32, at=tp)
    ang = sbt("ang", [128, 64, 16], F32, at=tp)
    kf = sbt("kf", [128, 64, 16], F32, at=tp)
    ki = sbt("ki", [128, 64, 16], I32, at=tp)
    S.dma('sp', lambda E: E.dma_start(out=posi[:, :], in_=posc[:, :]), w=['posi'])
    S.op('dve', lambda E: E.tensor_copy(out=posf[:, :], in_=posi[:, :]), r=['posi'], w=['posf'])
    S.op('dve', lambda E: E.tensor_tensor(out=ang[:, :, :], in0=posf[:, :].unsqueeze(2).to_broadcast([128, 64, 16]),
                                          in1=cv[:, C_INVF:C_INVF + 16].unsqueeze(1).to_broadcast([128, 64, 16]),
                                          op=ALU.mult), r=['posf', 'cv'], w=['ang'])
    for tab, shift in ((sinT, 0.0), (cosT, np.pi / 2)):
        S.op('dve', lambda E, shift=shift: E.tensor_scalar(out=kf[:, :, :], in0=ang[:, :, :], scalar1=float(shift),
                                                           scalar2=float(1.0 / TWO_PI), op0=ALU.add, op1=ALU.mult),
             r=['ang'], w=['kf'])
        S.op('dve', lambda E: E.tensor_copy(out=ki[:, :, :], in_=kf[:, :, :]), r=['kf'], w=['ki'])
        S.op('dve', lambda E: E.tensor_copy(out=kf[:, :, :], in_=ki[:, :, :]), r=['ki'], w=['kf'])
        S.op('dve', lambda E: E.scalar_tensor_tensor(out=kf[:, :, :], in0=kf[:, :, :], scalar=float(-TWO_PI),
                                                     in1=ang[:, :, :], op0=ALU.mult, op1=ALU.add),
             r=['kf', 'ang'], w=['kf'])
        S.op('dve', lambda E, shift=shift: E.tensor_scalar(out=kf[:, :, :], in0=kf[:, :, :], scalar1=float(shift),
                                                           scalar2=3.1415925, op0=ALU.add, op1=ALU.min),
             r=['kf'], w=['kf'])
        S.op('dve', lambda E: E.tensor_scalar(out=kf[:, :, :], in0=kf[:, :, :], scalar1=-3.1415925, scalar2=None,
                                              op0=ALU.max), r=['kf'], w=['kf'])
        act(tab[:, :, :], kf[:, :, :], AF.Sin, r=['kf'], w=['rope'])
    qs = 96.0 ** -0.5
    S.op('dve', lambda E: E.tensor_scalar(out=cosQ[:, :, :], in0=cosT[:, 47:64, :], scalar1=qs, scalar2=None,
                                          op0=ALU.mult), r=['rope'], w=['ropeq'])
    S.op('dve', lambda E: E.tensor_scalar(out=sinQ[:, :, :], in0=sinT[:, 47:64, :], scalar1=qs, scalar2=None,
                                          op0=ALU.mult), r=['rope'], w=['ropeq'])
    S.barrier()

    tp = [Y_OFF + YB]
    xbufA = [sbt(f"xbufA{i}", [128, D], F32, at=tp) for i in range(2)]
    xhatA = sbt("xhatA", [128, 4, D], BF16, at=tp)
    hT_halo = sbt("hT_halo", [128, 8, 2048], BF16, at=tp)
    QTd = sbt("QTd", [128, NE], BF16, at=tp)
    KTd = sbt("KTd", [128, 4224], BF16, at=tp)
    Vd = sbt("Vd", [128, 48, 128], BF16, at=tp)
    accO = sbt("accO", [128, NE], F32, at=tp)
    accD = sbt("accD", [128, NE], F32, at=tp)
    PTd = [sbt(f"PTd{i}", [128, 256], BF16, at=tp) for i in range(4)]
    ssb = [sbt(f"ssb{i}", [128, 256], F32, at=tp) for i in range(2)]
    dbt = [sbt(f"dbt{i}", [128, 256], F32, at=tp) for i in range(2)]

    def prep_all(row0, ntiles, dstbuf, dkey):
        t = 0
        while t < ntiles:
            n = min(2, ntiles - t)
            tiles = [xload(xbufA, xc[row0 + (t + i) * 128: row0 + (t + i + 1) * 128, :]) for i in range(n)]
            t0 = t
            rms_T(tiles, C_GPRE, lambda kc, t0=t0, n=n: dstbuf[:, kc, t0 * 128:(t0 + n) * 128], dkey, xhatA, 'xhA')
            t += n

    xhatsA = [xhatA[:, 0:2, :], xhatA[:, 2:4, :]]
    callsA = []
    for (row0, ntiles, dstbuf, dkey) in ((3968, 16, hT_halo, 'hTh'), (6016, NT, hT_own, 'hT')):
        t = 0
        while t < ntiles:
            n = min(2, ntiles - t)
            callsA.append(dict(srcs=[xc[row0 + (t + i) * 128: row0 + (t + i + 1) * 128, :] for i in range(n)],
                               dst=(lambda kc, t=t, n=n, dstbuf=dstbuf: dstbuf[:, kc, t * 128:(t + n) * 128]), dkey=dkey))
            t += n
    xbufA4 = xbufA + [accO[:, 0:D], accO[:, D:2 * D]]
    rms_pipe(callsA, C_GPRE, xbufA4, xhatsA, 'xhA')
    S.barrier()
    if DEBUG:
        S.dma('sp', lambda E: E.dma_start(out=dbg["dbg_h"][:, :, :], in_=hT_own[:, :, :]), r=['hT'])

    dscale = 128.0 ** -0.5
    bank_rr = [0]

    def nb():
        bank_rr[0] = (bank_rr[0] + 1) % 7
        return bank_rr[0]

    for hd in range(4):
        for g in range(3):
            d = DIL_D[g]
            gi = g * 4 + hd
            qb, kt = dil_geom(d)
            nql = NE // d
            nkl = 128 + nql
            c0 = OFF_KR + g * 512 + hd * 128
            (wq_, wk_, wv_), wkey = wload([wk(w_in, c0, 128), wk(w_in, c0 + 1536, 128), wk(w_in, c0 + 3072, 128)])
            bi = gi % 2
            S.dma('sp', lambda E, bi=bi, gi=gi: E.dma_start(out=dbt[bi][:, :], in_=dbd[gi, :, :]), w=[('dbt', bi)])
            QV = QTd[:, :].rearrange("p (r l) -> p r l", r=d)
            KV = KTd[:, 0:d * nkl].rearrange("p (r l) -> p r l", r=d)
            for (e0, n) in GROUPS:
                for which, wv3, dstv, loff, sc in ((0, wq_, QV, 0, dscale), (1, wk_, KV, 128, 1.0)):
                    b = nb()
                    mm(PF(b, n), [(wv3[:, kc, :], hT_own[:, kc, e0:e0 + n]) for kc in range(8)],
                       w=[('pf', b)], r=[wkey, 'hT'])
                    act(dstv[:, :, loff + e0 // d: loff + (e0 + n) // d],
                        PF(b, n).rearrange("p (l r) -> p r l", r=d), AF.Copy,
                        r=[('pf', b)], w=['QTd' if which == 0 else 'KTd'], scale=sc)
            h0 = 2048 - 128 * d
            while h0 < 2048:
                n = min(512, 2048 - h0)
                b = nb()
                mm(PF(b, n), [(wk_[:, kc, :], hT_halo[:, kc, h0:h0 + n]) for kc in range(8)],
                   w=[('pf', b)], r=[wkey, 'hTh'])
                l0 = (h0 - (2048 - 128 * d)) // d
                act(KV[:, :, l0: l0 + n // d], PF(b, n).rearrange("p (l r) -> p r l", r=d), AF.Copy,
                    r=[('pf', b)], w=['KTd'])
                h0 += n
            ntile = len(kt)
            for r_ in range(d):
                for j, (a, nk) in enumerate(kt):
                    if j == 0:
                        s0 = 2048 - 128 * d + r_
                        src = lambda kc, s0=s0, nk=nk: hT_halo[:, kc, s0: s0 + (nk - 1) * d + 1: d]
                        rk = 'hTh'
                    else:
                        s0 = (a - 128) * d + r_
                        src = lambda kc, s0=s0, nk=nk: hT_own[:, kc, s0: s0 + (nk - 1) * d + 1: d]
                        rk = 'hT'
                    b = nb()
                    mm(psf[0:nk, b * 512: b * 512 + 128], [(src(kc), wv_[:, kc, :]) for kc in range(8)],
                       w=[('pf', b)], r=[wkey, rk])
                    ti = r_ * ntile + j
                    S.op('dve', lambda E, nk=nk, b=b, ti=ti: E.tensor_copy(out=Vd[0:nk, ti, :],
                                                                         in_=psf[0:nk, b * 512: b * 512 + 128]),
                         r=[('pf', b)], w=['Vd'])
            for r_ in range(d):
                info = {}

                def dil_pv(m, r_=r_, info=info, d=d, g=g, qb=qb, ntile=ntile):
                    qa, qn = qb[m]
                    pp, pcol, _ = info[m]
                    pi, _, nk = info[m + 1]
                    b2 = nb()
                    tprev = r_ * ntile + m
                    tcur = r_ * ntile + m + 1

                    def f(E):
                        o = psf[:, b2 * 512: b2 * 512 + qn]
                        dd = psf[:, b2 * 512 + 128: b2 * 512 + 128 + qn]
                        E.matmul(o, lhsT=Vd[:, tprev, :], rhs=PTd[pp][:, pcol:pcol + qn], start=True, stop=False)
                        E.matmul(o, lhsT=Vd[0:nk, tcur, :], rhs=PTd[pi][0:nk, 0:qn], start=False, stop=True)
                        E.matmul(dd, lhsT=ones[:, :], rhs=PTd[pp][:, pcol:pcol + qn], start=False, stop=False,
                                 skip_group_check=True)
                        return E.matmul(dd, lhsT=ones[0:nk, :], rhs=PTd[pi][0:nk, 0:qn], start=False, stop=True,
                                        skip_group_check=True)
                    S.op('pe', f, r=['Vd', ('PTd', pp), ('PTd', pi), 'ones'], w=[('pf', b2)])
                    e_s = qa * d + r_
                    e_e = e_s + (qn - 1) * d + 1
                    for accb, off, akey in ((accO, 0, 'accO'), (accD, 128, 'accD')):
                        if g == 0:
                            S.op('dve', lambda E, accb=accb, off=off: E.tensor_copy(
                                out=accb[:, e_s:e_e:d], in_=psf[:, b2 * 512 + off: b2 * 512 + off + qn]),
                                r=[('pf', b2)], w=[akey])
                        else:
                            S.op('dve', lambda E, accb=accb, off=off: E.tensor_tensor(
                                out=accb[:, e_s:e_e:d], in0=psf[:, b2 * 512 + off: b2 * 512 + off + qn],
                                in1=accb[:, e_s:e_e:d], op=ALU.add), r=[('pf', b2), akey], w=[akey])

                for j, (a, nk) in enumerate(kt):
                    has_diag = j >= 1
                    has_prev = j < len(qb)
                    ncol = (qb[j - 1][1] if has_diag else 0) + (qb[j][1] if has_prev else 0)
                    qlo = qb[j - 1][0] if has_diag else qb[j][0]
                    boff = 0 if has_diag else 128
                    b = nb()
                    mm(psf[0:nk, b * 512: b * 512 + ncol], [(KV[:, r_, a:a + nk], QV[:, r_, qlo:qlo + ncol])],
                       w=[('pf', b)], r=['KTd', 'QTd'])
                    si = j % 2
                    S.op('dve', lambda E, nk=nk, b=b, ncol=ncol, si=si, boff=boff, bi=bi: E.tensor_tensor(
                        out=ssb[si][0:nk, 0:ncol], in0=psf[0:nk, b * 512: b * 512 + ncol],
                        in1=dbt[bi][0:nk, boff:boff + ncol], op=ALU.add),
                        r=[('pf', b), ('dbt', bi)], w=[('ssb', si)])
                    ridx = (0, 1, 5)[g] + r_
                    bcol = C_DVB0 + ridx if j == 0 else (C_DVB1 + ridx if j == 1 else C_ZERO)
                    pi = j % 4
                    act(PTd[pi][0:nk, 0:ncol], ssb[si][0:nk, 0:ncol], AF.Exp, r=[('ssb', si), 'cv'],
                        w=[('PTd', pi)], bias=cvc(bcol, nk))
                    info[j] = (pi, (qb[j - 1][1] if has_diag else 0), nk)
                    if j >= 2:
                        dil_pv(j - 2)
                dil_pv(len(kt) - 2)
        S.op('dve', lambda E: E.tensor_scalar(out=accD[:, :], in0=accD[:, :], scalar1=1e-30, scalar2=None,
                                              op0=ALU.max), r=['accD'], w=['accD'])
        S.op('dve', lambda E: E.reciprocal(out=accD[:, :], in_=accD[:, :]), r=['accD'], w=['accD'])
        S.op('dve', lambda E, hd=hd: E.tensor_tensor(out=y_dilT[:, hd, :], in0=accO[:, :], in1=accD[:, :],
                                                     op=ALU.mult), r=['accO', 'accD'], w=['ydil'])
    if DEBUG:
        S.dma('sp', lambda E: E.dma_start(out=dbg["dbg_ydil"][:, :, :], in_=y_dilT[:, :, :]), r=['ydil'])
    S.barrier()


    ckvT = nc.alloc_sbuf_tensor_at("ckvT", [128, 2, SEQ], BF16, offset=W_OFF)
    tp = [Y_OFF + 2 * YB]
    KT = sbt("KT", [128, SEQ], BF16, at=tp)
    cqT = sbt("cqT", [128, 3, NE], BF16, at=tp)
    Wuq = sbt("Wuq", [128, 3, 768], BF16, at=tp)
    Wukv = sbt("Wukv", [128, 2, 1024], BF16, at=tp)
    OV = tp[0]
    tp = [OV]
    xbufB = [nc.alloc_sbuf_tensor_at(f"xbufB{i}", [128, D], F32, offset=Y_OFF + YB + i * 4096) for i in range(4)]
    Wq = sbt("Wq", [128, 8, 384], BF16, at=tp)
    xhatB = sbt("xhatB", [128, 4, D], BF16, at=tp)
    hTgs = [sbt(f"hTg{i}", [128, 8, 512], BF16, at=tp) for i in range(2)]
    Wkvr = sbt("Wkvr", [128, 8, 288], BF16, at=tp)
    sq = sbt("sq", [128, 3, 512], BF16, at=tp)
    rbc = sbt("rbc", [128, 512], F32, at=tp)
    kpe = sbt("kpe", [128, 4, 96], BF16, at=tp)
    kt4 = [sbt(f"kt4_{i}", [128, 4, 16], F32, at=tp) for i in range(4)]

    S.dma('pool', lambda E: E.dma_start(out=Wkvr[:, :, :], in_=wk(w_in, OFF_Q, 288)), w=['Wkvr'])
    S.dma('pool', lambda E: E.dma_start(out=Wq[:, :, :], in_=wk(w_in, 0, 384)), w=['Wq'])
    S.dma('pool', lambda E: E.dma_start(out=Wuq[:, :, :], in_=w_uq.rearrange("(k p) n -> p k n", p=128)), w=['Wuq'])
    S.dma('pool', lambda E: E.dma_start(out=Wukv[:, :, :], in_=w_ukv.rearrange("(k p) n -> p k n", p=128)),
          w=['Wukv'])
    S.op('dve', lambda E: E.memset(kpe[:, :, :], 0.0), w=['kpe'])

    def latent_norm(banks, n, nchunk, gcol, dstf, dkey):
        for kc in range(nchunk):
            act(sq[:, kc, 0:n], PF(banks[kc], n), AF.Square, r=[('pf', banks[kc])], w=[('sq', kc)])
        bs = nb()
        mm(PF(bs, n), [(ones[:, :], sq[:, kc, 0:n]) for kc in range(nchunk)], w=[('pf', bs)],
           r=['ones'] + [('sq', kc) for kc in range(nchunk)])
        act(rbc[:, 0:n], PF(bs, n), AF.Ln, r=[('pf', bs)], w=['rbc'], scale=1.0 / (128 * nchunk), bias=EPS)
        act(rbc[:, 0:n], rbc[:, 0:n], AF.Exp, r=['rbc'], w=['rbc'], scale=-0.5)
        for kc in range(nchunk):
            S.op('dve', lambda E, kc=kc: E.scalar_tensor_tensor(out=dstf(kc), in0=PF(banks[kc], n),
                                                                scalar=cvc(gcol + kc), in1=rbc[:, 0:n],
                                                                op0=ALU.mult, op1=ALU.mult),
                 r=[('pf', banks[kc]), 'rbc', 'cv'], w=[dkey])

    def rope_tm(src3, cos3, sin3, dst3, nt, rkeys, wkey):
        a, b_ = src3[:, :, 0:16], src3[:, :, 16:32]
        t = [k[:, 0:nt, :] for k in kt4]
        S.op('dve', lambda E: E.tensor_tensor(out=t[0], in0=a, in1=cos3, op=ALU.mult), r=rkeys, w=[('kt4', 0)])
        S.op('dve', lambda E: E.tensor_tensor(out=t[1], in0=b_, in1=sin3, op=ALU.mult), r=rkeys, w=[('kt4', 1)])
        S.op('dve', lambda E: E.tensor_tensor(out=dst3[:, :, 0:16], in0=t[0], in1=t[1], op=ALU.subtract),
             r=[('kt4', 0), ('kt4', 1)], w=[wkey])
        S.op('dve', lambda E: E.tensor_tensor(out=t[2], in0=b_, in1=cos3, op=ALU.mult), r=rkeys, w=[('kt4', 2)])
        S.op('dve', lambda E: E.tensor_tensor(out=t[3], in0=a, in1=sin3, op=ALU.mult), r=rkeys, w=[('kt4', 3)])
        S.op('dve', lambda E: E.tensor_tensor(out=dst3[:, :, 16:32], in0=t[2], in1=t[3], op=ALU.add),
             r=[('kt4', 2), ('kt4', 3)], w=[wkey])

    xctr[0] = 0

    def ctx_latent(tg):
        hTg = hTgs[tg % 2]
        hk = ('hTg', tg % 2)
        banks = [nb(), nb()]
        for kc2 in range(2):
            mm(PF(banks[kc2]), [(Wkvr[:, kc, kc2 * 128:(kc2 + 1) * 128], hTg[:, kc, :]) for kc in range(8)],
               w=[('pf', banks[kc2])], r=['Wkvr', hk])
        latent_norm(banks, 512, 2, C_KVN, lambda kc: ckvT[:, kc, tg * 512:(tg + 1) * 512], 'ckvT')
        bk = nb()

        def fk(E):
            last = None
            for i in range(4):
                for kc in range(8):
                    last = E.matmul(psf[:, bk * 512 + i * 32: bk * 512 + (i + 1) * 32],
                                    lhsT=hTg[:, kc, i * 128:(i + 1) * 128], rhs=Wkvr[:, kc, 256:288],
                                    start=(kc == 0), stop=(kc == 7))
            return last
        S.op('pe', fk, r=['Wkvr', hk], w=[('pf', bk)])
        src3 = psf[:, bk * 512: bk * 512 + 128].rearrange("p (t c) -> p t c", c=32)
        rope_tm(src3, cosT[:, tg * 4:(tg + 1) * 4, :], sinT[:, tg * 4:(tg + 1) * 4, :], kpe[:, :, 64:96], 4,
                [('pf', bk), 'rope'], 'kpe')

        def ft(E):
            last = None
            for i in range(4):
                last = E.transpose(psb[0:96, i * 128:(i + 1) * 128], kpe[:, i, :], ident[:, :])
            return last
        S.op('pe', ft, r=['kpe', 'ident'], w=['psb'])
        act(KT[64:96, tg * 512:(tg + 1) * 512], psb[64:96, 0:512], AF.Copy, r=['psb'], w=['KTr'])

    xhatsB = [xhatB[:, 0:2, :], xhatB[:, 2:4, :]]
    callsB = []
    for tg in range(16):
        for hh in range(2):
            callsB.append(dict(
                srcs=[xc[(tg * 4 + hh * 2 + i) * 128:(tg * 4 + hh * 2 + i + 1) * 128, :] for i in range(2)],
                dst=(lambda kc, tg=tg, hh=hh: hTgs[tg % 2][:, kc, hh * 256:(hh + 1) * 256]), dkey=('hTg', tg % 2),
                after=((lambda tg=tg: ctx_latent(tg)) if hh == 1 else None)))
    rms_pipe(callsB, C_GPRE, xbufB, xhatsB, 'xhB')
    for (e0, n) in GROUPS:
        banks = [nb(), nb(), nb()]
        for kc3 in range(3):
            mm(PF(banks[kc3], n), [(Wq[:, kc, kc3 * 128:(kc3 + 1) * 128], hT_own[:, kc, e0:e0 + n]) for kc in range(8)],
               w=[('pf', banks[kc3])], r=['Wq', 'hT'])
        latent_norm(banks, n, 3, C_QN, lambda kc, e0=e0, n=n: cqT[:, kc, e0:e0 + n], 'cqT')
    S.barrier()

    for c in range(4):
        S.dma('pool', lambda E, c=c: E.dma_start(out=wupb[:, c * 1408:(c + 1) * 1408], in_=w_up[:, c * 1408:(c + 1) * 1408]),
              w=[('wupb', c)])
    for r_ in range(2):
        S.dma('pool', lambda E, r_=r_: E.dma_start(out=wdnb[r_ * 1408:(r_ + 1) * 1408, :], in_=w_dn[r_ * 1408:(r_ + 1) * 1408, :]),
              w=[('wdnb', r_)])
    tp = [OV]
    Vts = [sbt(f"Vt{i}", [128, 64, 65], BF16, at=tp) for i in range(2)]
    QTs = [sbt(f"QT{i}", [128, NE], BF16, at=tp) for i in range(2)]
    qtm = sbt("qtm", [128, NT, 96], BF16, at=tp)
    ypair = sbt("ypair", [128, NT, 128], BF16, at=tp)
    PTm = [sbt(f"PTm{i}", [128, 1024], BF16, at=tp) for i in range(3)]
    rden = sbt("rden", [128, 8], F32, at=tp)
    kt4 = [sbt(f"kq4_{i}", [128, 5, 16], F32, at=tp) for i in range(4)]
    for Vt_ in Vts:
        S.op('dve', lambda E, Vt_=Vt_: E.memset(Vt_[:, :, 64:65], 1.0), w=['Vones'])
    evac_rr = [0]

    def evac(out, in_, r, w, **kw):
        evac_rr[0] += 1
        if not kw:
            S.op('dve', lambda E: E.tensor_copy(out=out, in_=in_), r=r, w=w)
        else:
            act(out, in_, AF.Copy, r=r, w=w, **kw)

    pctr = [0]
    for h in range(8):
        for tg in range(16):
            b = nb()
            mm(psf[0:64, b * 512:(b + 1) * 512],
               [(Wukv[:, kc, h * 128: h * 128 + 64], ckvT[:, kc, tg * 512:(tg + 1) * 512]) for kc in range(2)],
               w=[('pf', b)], r=['Wukv', 'ckvT'])
            evac(KT[0:64, tg * 512:(tg + 1) * 512], psf[0:64, b * 512:(b + 1) * 512], r=[('pf', b)], w=['KTn'])
        def build_vq(h, bankf):
            Vt, QT = Vts[h % 2], QTs[h % 2]
            vk, qk = ('Vt', h % 2), ('QT', h % 2)
            chunks = []

            def vchunk(t8):
                b = bankf()

                def fv(E):
                    last = None
                    for i in range(8):
                        for kc in range(2):
                            last = E.matmul(psf[:, b * 512 + i * 64: b * 512 + (i + 1) * 64],
                                            lhsT=ckvT[:, kc, (t8 * 8 + i) * 128:(t8 * 8 + i + 1) * 128],
                                            rhs=Wukv[:, kc, h * 128 + 64: h * 128 + 128], start=(kc == 0), stop=(kc == 1))
                    return last
                S.op('pe', fv, r=['Wukv', 'ckvT'], w=[('pf', b)])
                evac(Vt[:, t8 * 8:(t8 + 1) * 8, 0:64], PF(b).rearrange("p (t e) -> p t e", e=64), r=[('pf', b)], w=[vk])

            def qchunk(t0):
                nt = min(5, NT - t0)
                b = bankf()

                def fq(E):
                    last = None
                    for i in range(nt):
                        for kc in range(3):
                            last = E.matmul(psf[:, b * 512 + i * 96: b * 512 + (i + 1) * 96],
                                            lhsT=cqT[:, kc, (t0 + i) * 128:(t0 + i + 1) * 128],
                                            rhs=Wuq[:, kc, h * 96:(h + 1) * 96], start=(kc == 0), stop=(kc == 2))
                    return last
                S.op('pe', fq, r=['Wuq', 'cqT'], w=[('pf', b)])
                p3 = psf[:, b * 512: b * 512 + nt * 96].rearrange("p (t c) -> p t c", c=96)
                S.op('dve', lambda E: E.tensor_scalar(out=qtm[:, t0:t0 + nt, 0:64], in0=p3[:, :, 0:64],
                                                      scalar1=qs, scalar2=None, op0=ALU.mult),
                     r=[('pf', b)], w=['qtm'])
                rope_tm(p3[:, :, 64:96], cosQ[:, t0:t0 + nt, :], sinQ[:, t0:t0 + nt, :], qtm[:, t0:t0 + nt, 64:96], nt,
                        [('pf', b), 'ropeq'], 'qtm')

            def tchunk(t0):
                nt = min(8, NT - t0)

                def ftq(E):
                    last = None
                    for i in range(nt):
                        last = E.transpose(psb[0:96, i * 128:(i + 1) * 128], qtm[:, t0 + i, :], ident[:, :])
                    return last
                S.op('pe', ftq, r=['qtm', 'ident'], w=['psb'])
                S.op('dve', lambda E: E.tensor_copy(out=QT[0:96, t0 * 128:(t0 + nt) * 128],
                                                    in_=psb[0:96, 0:nt * 128]), r=['psb'], w=[qk])
            for t8 in range(8):
                chunks.append(lambda t8=t8: vchunk(t8))
            for t0 in range(0, NT, 5):
                chunks.append(lambda t0=t0: qchunk(t0))
            for t0 in range(0, NT, 8):
                chunks.append(lambda t0=t0: tchunk(t0))
            return chunks

        if h == 0:
            for ch in build_vq(0, nb):
                ch()
        Vt, QT = Vts[h % 2], QTs[h % 2]
        vk, qk = ('Vt', h % 2), ('QT', h % 2)
        items = []
        for gq, (e0, n) in enumerate(GROUPS):
            tq0, ntq = e0 // 128, n // 128
            kfirst = 47 + tq0
            per = 1024 // n
            units = []
            kb = 0
            while kb < kfirst:
                lim = min(kfirst, (kb // 16 + 1) * 16, kb + per)
                units.append((list(range(kb, lim)), None))
                kb = lim
            for m in range(ntq):
                units.append(([kfirst + m], m))
            for ui, (kbs, m) in enumerate(units):
                items.append(dict(e0=e0, n=n, tq0=tq0, ntq=ntq, ob=4 + (gq % 2), kbs=kbs, m=m, first=(ui == 0),
                                  last=(ui == len(units) - 1)))

        def emit_S(it, i):
            sp, n, e0, m = ((pctr[0] + i) % 2) * 2, it['n'], it['e0'], it['m']
            if m is None:
                def fs(E, it=it, sp=sp, n=n, e0=e0, QT=QT):
                    last = None
                    for idx, kbk in enumerate(it['kbs']):
                        last = E.matmul(psf[:, sp * 512 + idx * n: sp * 512 + (idx + 1) * n],
                                        lhsT=KT[0:96, kbk * 128:(kbk + 1) * 128], rhs=QT[0:96, e0:e0 + n],
                                        start=True, stop=True)
                    return last
                S.op('pe', fs, r=['KTn', 'KTr', qk], w=[('pf', sp), ('pf', sp + 1)])
            else:
                kbk = it['kbs'][0]
                cols = n - 128 * m
                mm(psf[:, sp * 512: sp * 512 + cols],
                   [(KT[0:96, kbk * 128:(kbk + 1) * 128], QT[0:96, e0 + 128 * m:e0 + n])],
                   w=[('pf', sp), ('pf', sp + 1)], r=['KTn', 'KTr', qk])

        def emit_A(it, i):
            sp, pi = ((pctr[0] + i) % 2) * 2, (pctr[0] + i) % 3
            n, m = it['n'], it['m']
            kb0 = it['kbs'][0]
            tot = len(it['kbs']) * n if m is None else n - 128 * m
            act(PTm[pi][:, 0:tot], psf[:, sp * 512: sp * 512 + tot], AF.Exp,
                r=[('pf', sp), ('pf', sp + 1), 'cv'], w=[('PTm', pi)], bias=cvc(C_VB + min(kb0 // 16, 3)))
            if m is not None:
                S.op('dve', lambda E, pi=pi: E.tensor_tensor(out=PTm[pi][:, 0:128], in0=PTm[pi][:, 0:128],
                                                             in1=tri[:, :], op=ALU.mult),
                     r=[('PTm', pi), 'tri'], w=[('PTm', pi)])

        def emit_PV(it, i):
            pi = (pctr[0] + i) % 3
            n, m, ntq, ob, tq0 = it['n'], it['m'], it['ntq'], it['ob'], it['tq0']
            if m is None:
                blocks = [(idx * n, kbk, list(range(ntq))) for idx, kbk in enumerate(it['kbs'])]
            else:
                blocks = [(0, it['kbs'][0], list(range(m, ntq)))]

            def pv(E, pi=pi, blocks=blocks, ob=ob, first=it['first'], Vt=Vt):
                last = None
                st = first
                for (col0, kbk, qts) in blocks:
                    for qt in qts:
                        last = E.matmul(psf[:, ob * 512 + qt * 65: ob * 512 + qt * 65 + 65],
                                        lhsT=PTm[pi][:, col0 + (qt - qts[0]) * 128: col0 + (qt - qts[0] + 1) * 128],
                                        rhs=Vt[:, kbk, :], start=st, stop=False, skip_group_check=True)
                        st = False
                return last
            S.op('pe', pv, r=[('PTm', pi), vk, 'Vones'], w=[('pf', ob)])
            if it['last']:
                o3 = psf[:, ob * 512: ob * 512 + ntq * 65].rearrange("p (t c) -> p t c", c=65)
                S.op('dve', lambda E, o3=o3, ntq=ntq: E.tensor_scalar(out=rden[:, 0:ntq], in0=o3[:, :, 64],
                                                                      scalar1=1e-30, scalar2=None, op0=ALU.max),
                     r=[('pf', ob)], w=['rden'])
                S.op('dve', lambda E, ntq=ntq: E.reciprocal(out=rden[:, 0:ntq], in_=rden[:, 0:ntq]),
                     r=['rden'], w=['rden'])
                hoff = (h % 2) * 64
                S.op('dve', lambda E, o3=o3, ntq=ntq, tq0=tq0, hoff=hoff: E.tensor_tensor(
                    out=ypair[:, tq0:tq0 + ntq, hoff:hoff + 64], in0=o3[:, :, 0:64],
                    in1=rden[:, 0:ntq].unsqueeze(2).to_broadcast([128, ntq, 64]), op=ALU.mult),
                    r=[('pf', ob), 'rden'], w=['ypair'])

        for i, it in enumerate(items):
            emit_S(it, i)
            emit_A(it, i)
            if i >= 1:
                emit_PV(items[i - 1], i - 1)
            if i == 12 and h < 7:
                pending = build_vq(h + 1, lambda: 6)
            if i > 12 and h < 7 and pending:
                pending.pop(0)()
        emit_PV(items[-1], len(items) - 1)
        while h < 7 and pending:
            pending.pop(0)()
        pctr[0] += len(items)
        if h % 2 == 1:
            for t0 in range(0, NT, 8):
                nt = min(8, NT - t0)

                def fty(E, t0=t0, nt=nt):
                    last = None
                    for i in range(nt):
                        last = E.transpose(psb[:, i * 128:(i + 1) * 128], ypair[:, t0 + i, :], ident[:, :])
                    return last
                S.op('pe', fty, r=['ypair', 'ident'], w=['psb'])
                S.op('dve', lambda E, t0=t0, nt=nt, h=h: E.tensor_copy(out=y_mlaT[:, h // 2, t0 * 128:(t0 + nt) * 128],
                                                                       in_=psb[:, 0:nt * 128]), r=['psb'], w=['ymla'])
    if DEBUG:
        S.dma('sp', lambda E: E.dma_start(out=dbg["dbg_ymla"][:, :, :], in_=y_mlaT[:, :, :]), r=['ymla'])
    S.barrier()


    MERG = END - 34816
    mergedT = nc.alloc_sbuf_tensor_at("mergedT", [128, 8, NE], BF16, offset=MERG)
    tp = [T_OFF]
    memT = sbt("memT", [128, 8, 256], BF16, at=tp)
    KmT = sbt("KmT", [128, 4, 256], BF16, at=tp)
    VmT = sbt("VmT", [128, 2, 512], BF16, at=tp)
    OVC = tp[0]
    xbufC = [sbt(f"xbufC{i}", [128, D], F32, at=tp) for i in range(2)]
    xhatC = sbt("xhatC", [128, 2, D], BF16, at=tp)
    assert tp[0] <= MERG
    xctr[0] = 0
    tiles = [xload(xbufC, memx[i * 128:(i + 1) * 128, :]) for i in range(2)]
    rms_T(tiles, C_GMEM, lambda kc: memT[:, kc, :], 'memT', xhatC, 'xhC')
    (wkm,), kkm = wload([wk(w_memkv, 0, 512)])
    for h in range(4):
        b = nb()
        mm(PF(b, 256), [(wkm[:, kc, h * 128:(h + 1) * 128], memT[:, kc, :]) for kc in range(8)],
           w=[('pf', b)], r=[kkm, 'memT'])
        act(KmT[:, h, :], PF(b, 256), AF.Copy, r=[('pf', b)], w=['KmT'])
    (wvm,), kvm = wload([wk(w_memkv, 512, 512)])
    for mt in range(2):
        b = nb()
        mm(PF(b), [(memT[:, kc, mt * 128:(mt + 1) * 128], wvm[:, kc, :]) for kc in range(8)],
           w=[('pf', b)], r=[kvm, 'memT'])
        act(VmT[:, mt, :], PF(b), AF.Copy, r=[('pf', b)], w=['VmT'])
    S.barrier()
    tp = [OVC]
    qm = sbt("qm", [128, 512], BF16, at=tp)
    PTc = sbt("PTc", [128, 2, 512], BF16, at=tp)
    recc = sbt("recc", [128, 512], F32, at=tp)
    gsb = [sbt(f"gsb{i}", [128, 512], BF16, at=tp) for i in range(3)]
    mtc = [sbt(f"mtc{i}", [128, 512], F32, at=tp) for i in range(2)]
    assert tp[0] <= MERG
    (wqm,), kqm = wload([wk(w_in, OFF_DIL, 512)])
    qmB = sbt("qmB", [128, 512], BF16, at=tp)
    PTcB = sbt("PTcB", [128, 2, 512], BF16, at=tp)
    assert tp[0] <= MERG
    qm2, PTc2 = [qm, qmB], [PTc, PTcB]
    itsC = [(h, e0, n) for h in range(4) for (e0, n) in GROUPS]

    def c_s1(i):
        h, e0, n = itsC[i]
        b = nb()
        mm(PF(b, n), [(wqm[:, kc, h * 128:(h + 1) * 128], hT_own[:, kc, e0:e0 + n]) for kc in range(8)],
           w=[('pf', b)], r=[kqm, 'hT'])
        act(qm2[i % 2][:, 0:n], PF(b, n), AF.Copy, r=[('pf', b)], w=[('qm', i % 2)], scale=dscale)

    def c_s2(i):
        h, e0, n = itsC[i]
        for mt in range(2):
            b = nb()
            mm(PF(b, n), [(KmT[:, h, mt * 128:(mt + 1) * 128], qm2[i % 2][:, 0:n])], w=[('pf', b)],
               r=['KmT', ('qm', i % 2)])
            act(PTc2[i % 2][:, mt, 0:n], PF(b, n), AF.Exp, r=[('pf', b)], w=[('PTc', i % 2, mt)])

    def c_s3(i):
        h, e0, n = itsC[i]
        P_ = PTc2[i % 2]
        pk = [('PTc', i % 2, 0), ('PTc', i % 2, 1)]
        bo, bd = nb(), nb()
        mm(PF(bo, n), [(VmT[:, mt, h * 128:(h + 1) * 128], P_[:, mt, 0:n]) for mt in range(2)],
           w=[('pf', bo)], r=['VmT'] + pk)
        mm(PF(bd, n), [(ones[:, :], P_[:, mt, 0:n]) for mt in range(2)], w=[('pf', bd)], r=['ones'] + pk)
        S.op('dve', lambda E: E.reciprocal(out=recc[:, 0:n], in_=PF(bd, n)), r=[('pf', bd)], w=['recc'])
        S.op('dve', lambda E: E.tensor_tensor(out=y_memT[:, h, e0:e0 + n], in0=PF(bo, n), in1=recc[:, 0:n],
                                              op=ALU.mult), r=[('pf', bo), 'recc'], w=['ymem'])
    NC_ = len(itsC)
    c_s1(0)
    c_s1(1)
    c_s2(0)
    for i in range(NC_):
        if i + 2 < NC_:
            c_s1(i + 2)
        if i + 1 < NC_:
            c_s2(i + 1)
        c_s3(i)
    if DEBUG:
        S.dma('sp', lambda E: E.dma_start(out=dbg["dbg_ymem"][:, :, :], in_=y_memT[:, :, :]), r=['ymem'])
    yT = [y_mlaT, y_dilT, y_memT]
    ykeys = ['ymla', 'ydil', 'ymem']
    for oc in range(8):
        wg, kg = wload([wk(w_in, OFF_MEMQ + br * 1024 + oc * 128, 128) for br in range(3)])
        wb_, kb_ = wload([w_br[br][:, oc * 128:(oc + 1) * 128].rearrange("(k p) n -> p k n", p=128) for br in range(3)])
        for (e0, n) in GROUPS:
            for br in range(3):
                bg = nb()
                mm(PF(bg, n), [(wg[br][:, kc, :], hT_own[:, kc, e0:e0 + n]) for kc in range(8)],
                   w=[('pf', bg)], r=[kg, 'hT'])
                act(gsb[br][:, 0:n], PF(bg, n), AF.Sigmoid, r=[('pf', bg), 'cv'], w=[('gsb', br)],
                    bias=cvc(C_BG + br * 8 + oc))
                bb = nb()
                mm(PF(bb, n), [(wb_[br][:, kc, :], yT[br][:, kc, e0:e0 + n]) for kc in range(4)],
                   w=[('pf', bb)], r=[kb_, ykeys[br]])
                di = 0 if br == 0 else 1
                S.op('dve', lambda E, bb=bb, n=n, br=br, di=di: E.tensor_tensor(out=mtc[di][:, 0:n], in0=PF(bb, n),
                                                                               in1=gsb[br][:, 0:n], op=ALU.mult),
                     r=[('pf', bb), ('gsb', br)], w=[('mtc', di)])
                if br == 1:
                    S.op('dve', lambda E, n=n: E.tensor_tensor(out=mtc[0][:, 0:n], in0=mtc[0][:, 0:n],
                                                               in1=mtc[1][:, 0:n], op=ALU.add),
                         r=[('mtc', 0), ('mtc', 1)], w=[('mtc', 0)])
                if br == 2:
                    S.op('dve', lambda E, n=n, oc=oc, e0=e0: E.tensor_tensor(out=mergedT[:, oc, e0:e0 + n],
                                                                            in0=mtc[0][:, 0:n], in1=mtc[1][:, 0:n],
                                                                            op=ALU.add),
                         r=[('mtc', 0), ('mtc', 1)], w=['merged'])
    if DEBUG:
        S.dma('sp', lambda E: E.dma_start(out=dbg["dbg_merged"][:, :, :], in_=mergedT[:, :, :]), r=['merged'])
    S.barrier()

    x1 = nc.alloc_sbuf_tensor_at("x1", [128, 16, D], F32, offset=Y_OFF)
    tp = [Y_OFF + 65536]
    xbufD = [sbt("xbufD0", [128, D], F32, at=tp)]
    tmpD = sbt("tmpD", [128, D], F32, at=tp)
    x1pre = sbt("x1pre", [128, D], F32, at=tp)
    xhatD = sbt("xhatD", [128, 1, D], BF16, at=tp)
    assert tp[0] <= MERG
    h2T = hT_own
    (wo0,), ko0 = wload([wk(w_o, 0, 512)])
    (wo1,), ko1 = wload([wk(w_o, 512, 512)])
    xctr[0] = 0
    dstD = {}

    def d_s1(t):
        p = t % 3
        for half, wo_, ko in ((0, wo0, ko0), (1, wo1, ko1)):
            mm(psf[:, p * 1024 + half * 512: p * 1024 + (half + 1) * 512],
               [(mergedT[:, kc, t * 128:(t + 1) * 128], wo_[:, kc, :]) for kc in range(8)],
               w=[('pf', 2 * p + half)], r=[ko, 'merged'])
        pk = [('pf', 2 * p), ('pf', 2 * p + 1)]
        pfull = psf[:, p * 1024:(p + 1) * 1024]
        S.op('act', lambda E: E.activation(out=junk[:, :], in_=pfull, func=AF.Square,
                                           accum_out=small[:, 48:49]), r=pk, w=['junk', ('sm', 48)])
        rstd_from_ss(small[:, 48:49], small[:, 50:51], small[:, 49:50], 1024.0, [('sm', 48)], [('sm', 50)], [('sm', 49)])
        xa, xk = xload(xbufD, xc[6016 + t * 128: 6016 + (t + 1) * 128, :])
        S.op('dve', lambda E: E.scalar_tensor_tensor(out=tmpD[:, :], in0=pfull, scalar=small[:, 50:51],
                                                     in1=gbc[:, 0, :], op0=ALU.mult, op1=ALU.mult),
             r=pk + [('sm', 50), 'gbc'], w=['tmpD'])
        dst = x1pre[:, :] if t == 0 else x1[:, t - 1, :]
        dk = ('x1', t)
        S.op('dve', lambda E: E.tensor_tensor(out=dst, in0=tmpD[:, :], in1=xa, op=ALU.add),
             r=['tmpD', xk], w=[dk])
        dstD[t] = (dst, dk)

    def d_s2(t):
        rms_T([dstD[t]], C_GFFN, lambda kc: h2T[:, kc, t * 128:(t + 1) * 128], 'h2T', xhatD, 'xhD')

    d_s1(0)
    for t in range(NT):
        if t + 1 < NT:
            d_s1(t + 1)
        d_s2(t)
    S.barrier()

    tp = [Y_OFF + 65536]
    gbuf = sbt("gbuf", [128, 22, 512], BF16, at=tp)
    ubuf = [sbt(f"ubuf{i}", [128, 528], F32, at=tp) for i in range(4)]
    zt = [sbt(f"zt{i}", [128, 512], F32, at=tp) for i in range(4)]
    sgs = [sbt(f"sg{i}", [128, 512], F32, at=tp) for i in range(2)]
    carry = sbt("carry", [128, 44, 2], F32, at=tp)
    fo0 = sbt("fo0", [128, 4, 512], F32, at=tp)
    tmpE = zt[0]
    eb = [0]

    def nbe():
        eb[0] = (eb[0] + 1) % 3
        return eb[0]
    for j in range(4):
        e0 = 128 + 512 * j
        for fc0 in range(0, 22, 4):
            nf = min(4, 22 - fc0)
            wupk = [('wupb', c) for c in range(4)]
            wgt, kgt = wload_b(wk(wupb, fc0 * 128, nf * 128), wupk)
            wvt, kvt = wload_b(wk(wupb, DFF + fc0 * 128, nf * 128), wupk)
            for fi in range(nf):
                fc = fc0 + fi
                if j > 0:
                    for fcn in ([0, 1] if fc == 0 else ([fc + 1] if fc + 1 < 22 else [])):
                        for half_ in range(2):
                            S.op('dve', lambda E, fcn=fcn, half_=half_: E.tensor_copy(
                                out=ubuf[half_ * 2 + fcn % 2][:, 0:2], in_=carry[:, half_ * 22 + fcn, :]),
                                r=[('carry', half_ * 22 + fcn)], w=[('ubc', half_ * 2 + fcn % 2)])
                for half, wt, kw_ in ((0, wgt, kgt), (1, wvt, kvt)):
                    ci = half * 22 + fc
                    b = nbe()
                    mm(PF(b), [(wt[:, kc, fi * 128:(fi + 1) * 128], h2T[:, kc, e0:e0 + 512]) for kc in range(8)],
                       w=[('pf', b)], r=[kw_, 'h2T'])
                    ub = ubuf[half * 2 + fc % 2]
                    uk = ('ub', half * 2 + fc % 2)
                    uck = ('ubc', half * 2 + fc % 2)
                    ztb = zt[half * 2 + fc % 2]
                    if j == 0:
                        pb_ = 3 + (ci % 4)
                        mm(psf[:, pb_ * 512: pb_ * 512 + 2],
                           [(wt[:, kc, fi * 128:(fi + 1) * 128], h2T[:, kc, 126:128]) for kc in range(8)],
                           w=[('pf', pb_)], r=[kw_, 'h2T'])
                        S.op('dve', lambda E, ub=ub, pb_=pb_: E.tensor_scalar(out=ub[:, 0:2], in0=psf[:, pb_ * 512: pb_ * 512 + 2],
                                                                             scalar1=cvc(C_UFLAG), scalar2=None, op0=ALU.mult),
                             r=[('pf', pb_), 'cv'], w=[uck])
                    act(ub[:, 2:514], PF(b), AF.Copy, r=[('pf', b)], w=[uk])
                    S.op('dve', lambda E, ub=ub, ci=ci: E.tensor_copy(out=carry[:, ci, :], in_=ub[:, 512:514]),
                         r=[uk], w=[('carry', ci)])
                    zk = ('zt', half * 2 + fc % 2)
                    act(ztb[:, :], ub[:, 0:512], AF.Identity, r=[uk, uck, 'cv'], w=[zk], scale=cvc(C_CW + ci),
                        bias=cvc(C_CB + ci))
                    S.op('dve', lambda E, ub=ub, ci=ci, ztb=ztb: E.scalar_tensor_tensor(
                        out=ztb[:, :], in0=ub[:, 1:513], scalar=cvc(C_CW + 44 + ci), in1=ztb[:, :],
                        op0=ALU.mult, op1=ALU.add), r=[uk, uck, zk, 'cv'], w=[zk])
                    S.op('dve', lambda E, ub=ub, ci=ci, ztb=ztb: E.scalar_tensor_tensor(
                        out=ztb[:, :], in0=ub[:, 2:514], scalar=cvc(C_CW + 88 + ci), in1=ztb[:, :],
                        op0=ALU.mult, op1=ALU.add), r=[uk, zk, 'cv'], w=[zk])
                sg = sgs[fc % 2]
                sk = ('sg', fc % 2)
                zg, zv = zt[fc % 2], zt[2 + fc % 2]
                act(sg[:, :], zg[:, :], AF.Silu, r=[('zt', fc % 2)], w=[sk])
                S.op('pool', lambda E, fc=fc, sg=sg, zv=zv: E.tensor_tensor(out=gbuf[:, fc, :], in0=sg[:, :], in1=zv[:, :],
                                                                          op=ALU.mult),
                     r=[sk, ('zt', 2 + fc % 2)], w=['gbuf'])
        for half in range(2):
            for (f0, nfk) in ((0, 8), (8, 8), (16, 6)):
                wd, kd = wload_b(wdnb[f0 * 128:(f0 + nfk) * 128, half * 512:(half + 1) * 512]
                                 .rearrange("(k p) n -> p k n", p=128), [('wdnb', 0), ('wdnb', 1)])
                for tt in range(4):
                    def fd(E, wd=wd, f0=f0, nfk=nfk, tt=tt):
                        last = None
                        for k in range(nfk):
                            last = E.matmul(PF(3 + tt), lhsT=gbuf[:, f0 + k, tt * 128:(tt + 1) * 128], rhs=wd[:, k, :],
                                            start=(f0 == 0 and k == 0), stop=(f0 == 16 and k == nfk - 1))
                        return last
                    S.op('pe', fd, r=[kd, 'gbuf'], w=[('pf', 3 + tt)])
            for tt in range(4):
                col = 40 + half * 4 + tt
                S.op('act', lambda E, tt=tt, col=col: E.activation(out=junk[:, 0:512], in_=PF(3 + tt), func=AF.Square,
                                                                   accum_out=small[:, col:col + 1]),
                     r=[('pf', 3 + tt)], w=['junk', ('sm', col)])
                if half == 0:
                    act(fo0[:, tt, :], PF(3 + tt), AF.Copy, r=[('pf', 3 + tt)], w=[('fo0', tt)])
            if half == 1:
                S.op('dve', lambda E: E.tensor_tensor(out=small[:, 48:52], in0=small[:, 40:44], in1=small[:, 44:48],
                                                      op=ALU.add), r=[('sm', c) for c in range(40, 48)], w=[('sm', 48)])
                rstd_from_ss(small[:, 48:52], small[:, 56:60], small[:, 52:56], 1024.0, [('sm', 48)], [('sm', 56)],
                             [('sm', 52)])
                for tt in range(4):
                    xt = x1[:, 4 * j + tt, :]
                    xk = ('x1', 4 * j + tt + 1)
                    rs = small[:, 56 + tt:57 + tt]
                    S.op('dve', lambda E, tt=tt, rs=rs: E.scalar_tensor_tensor(
                        out=fo0[:, tt, :], in0=fo0[:, tt, :], scalar=rs, in1=gbc[:, 1, 0:512], op0=ALU.mult,
                        op1=ALU.mult), r=[('fo0', tt), ('sm', 56), 'gbc'], w=[('fo0', tt)])
                    S.op('dve', lambda E, tt=tt, xt=xt: E.tensor_tensor(out=xt[:, 0:512], in0=xt[:, 0:512],
                                                                        in1=fo0[:, tt, :], op=ALU.add),
                         r=[('fo0', tt), xk], w=[xk])
                    S.op('dve', lambda E, tt=tt, rs=rs: E.scalar_tensor_tensor(
                        out=tmpE[:, :], in0=PF(3 + tt), scalar=rs, in1=gbc[:, 1, 512:1024], op0=ALU.mult,
                        op1=ALU.mult), r=[('pf', 3 + tt), ('sm', 56), 'gbc'], w=[('zt', 0)])
                    S.op('dve', lambda E, tt=tt, xt=xt: E.tensor_tensor(out=xt[:, 512:1024], in0=xt[:, 512:1024],
                                                                        in1=tmpE[:, :], op=ALU.add),
                         r=[('zt', 0), xk], w=[xk])
                    row = (4 * j + tt) * 128
                    S.dma('sp', lambda E, row=row, xt=xt: E.dma_start(out=outd[row:row + 128, :], in_=xt), r=[xk])

    return nc, es, S


def _finish(nc, es, S):
    S.barrier()
    with nc.Block() as block:
        @block.tensor
        def _(E):
            for f in S.prog['pe']:
                f(E)

        @block.scalar
        def _(E):
            for f in S.prog['act']:
                f(E)

        @block.vector
        def _(E):
            for f in S.prog['dve']:
                f(E)

        @block.gpsimd
        def _(E):
            for f in S.prog['pool']:
                f(E)

        @block.sync
        def _(E):
            for f in S.prog['sp']:
                f(E)
    es.close()
    return nc


def host_inputs(inputs):
    import ml_dtypes
    x = np.asarray(inputs["x"], np.float32)
    mem = np.asarray(inputs["mem"], np.float32)
    pos = np.asarray(inputs["positions"], np.int32)

    def P(k):
        return np.asarray(inputs[k], np.float32)[0]
    shared = {
        "gbc": np.stack([P("g_post_mix"), P("g_post_ffn")]).astype(np.float32),
        "tri": np.triu(np.ones((128, 128), np.float32)),
        "ident": np.eye(128, dtype=np.float32),
        "w_in": P("w_in"), "w_uq": P("w_uq"), "w_ukv": P("w_ukv"), "w_mem_kv": P("w_mem_kv"),
        "w_br_mla": P("w_br_mla"), "w_br_dil": P("w_br_dil"), "w_br_mem": P("w_br_mem"), "w_o": P("w_o"),
        "w_ffn_up": P("w_ffn_up"), "w_ffn_down": P("w_ffn_down"),
    }
    db = np.zeros((12, 128, 256), np.float32)
    slopes = np.exp2(-8.0 * np.arange(1, 13, dtype=np.float32) / 12).reshape(4, 3).T
    k = np.arange(128)[:, None]
    i = np.arange(128)[None, :]
    for g in range(3):
        for hd in range(4):
            a = slopes[g, hd] * DIL_D[g]
            dist = i - k
            db[g * 4 + hd, :, 0:128] = np.where(dist >= 0, -a * dist, NEG)
            dist2 = 128 + i - k
            db[g * 4 + hd, :, 128:256] = np.where(k >= i, -a * dist2, NEG)
    shared["dbias"] = db

    def cols(v, n):
        return v.reshape(n, 128).T
    cv0 = np.zeros((128, NCV), np.float32)
    cv0[:, C_GPRE:C_GPRE + 8] = cols(P("g_pre_mix"), 8)
    cv0[:, C_GMEM:C_GMEM + 8] = cols(P("g_mem"), 8)
    cv0[:, C_GFFN:C_GFFN + 8] = cols(P("g_pre_ffn"), 8)
    cv0[:, C_QN:C_QN + 3] = cols(P("mla_q_norm"), 3)
    cv0[:, C_KVN:C_KVN + 2] = cols(P("mla_kv_norm"), 2)
    cv0[:, C_BG:C_BG + 24] = cols(P("b_gate"), 24)
    cw = P("conv_w")
    for j in range(3):
        cv0[:, C_CW + j * 44: C_CW + (j + 1) * 44] = cols(cw[j], 44)
    cv0[:, C_CB:C_CB + 44] = cols(P("conv_b"), 44)
    cv0[:, C_INVF:C_INVF + 16] = (np.float32(10000.0) ** (-np.arange(16, dtype=np.float32) / np.float32(16)))[None, :]
    in_maps = []
    for c in range(8):
        b, q = c // 4, c % 4
        xcx = np.zeros((SEQ, D), np.float32)
        pc = np.zeros((SEQ,), np.int32)
        lo = 6144 - CH * q
        xcx[lo:] = x[b, 0: CH * (q + 1)]
        pc[lo:] = pos[b, 0: CH * (q + 1)]
        cvq = cv0.copy()
        for cidx in range(3):
            cvq[:, C_VB + cidx] = 0.0 if (cidx + q) >= 3 else NEG
        ridx = 0
        for g in range(3):
            d = DIL_D[g]
            for r_ in range(d):
                kk = np.arange(128)
                tau0 = 3968 + (2048 - 128 * d) + kk * d + r_
                cvq[:, C_DVB0 + ridx] = np.where(tau0 >= lo, 0.0, NEG)
                tau1 = 3968 + 2048 + kk * d + r_
                cvq[:, C_DVB1 + ridx] = np.where(tau1 >= lo, 0.0, NEG)
                ridx += 1
        cvq[:, C_UFLAG] = 1.0 if q > 0 else 0.0
        m = dict(shared)
        m["xc"] = xcx
        m["posc"] = np.ascontiguousarray(pc.reshape(64, 128).T)
        m["memx"] = np.ascontiguousarray(mem[b])
        m["cv"] = cvq
        in_maps.append(m)
    return in_maps


def kernel(**inputs):
    in_maps = host_inputs(inputs)
    nc, es, S = build_program()
    nc = _finish(nc, es, S)
    res = run_bass_kernel_spmd(nc, in_maps, core_ids=list(range(8)))
    out = np.zeros((2, SEQ, D), np.float32)
    for c in range(8):
        b, q = c // 4, c % 4
        out[b, q * CH:(q + 1) * CH] = np.asarray(res.results[c]["out"], np.float32)
    return out
```

```python
from contextlib import ExitStack
import numpy as np
import concourse.bass as bass
import concourse.mybir as mybir
from concourse.bass_utils import run_bass_kernel_spmd

F32, BF16, I32 = mybir.dt.float32, mybir.dt.bfloat16, mybir.dt.int32
ALU = mybir.AluOpType
AF = mybir.ActivationFunctionType
NEG = -30000.0
D = 1024
SEQ = 8192
CH = 2048
NE = 2176
NT = 17
DFF = 2816
DIN = 8864
OFF_Q, OFF_KV, OFF_KR, OFF_DIL, OFF_MEMQ = 384, 640, 672, 672 + 4608, 672 + 4608 + 512
DIL_D = (1, 4, 16)
EPS = 1e-6
TWO_PI = 6.283185307179586
(C_GPRE, C_GMEM, C_GFFN, C_QN, C_KVN, C_BG, C_CW, C_CB, C_VB, C_ZERO, C_INVF, C_DVB0, C_DVB1, C_UFLAG,
 NCV) = (0, 8, 16, 24, 27, 29, 53, 185, 229, 232, 233, 249, 270, 291, 292)
GROUPS = [(0, 128), (128, 512), (640, 512), (1152, 512), (1664, 512)]
DEBUG = False


class Sched:
    CE = ('pe', 'act', 'dve', 'pool')

    def __init__(s, nc, sems, dsems):
        s.eng = {'pe': nc.tensor, 'act': nc.scalar, 'dve': nc.vector, 'pool': nc.gpsimd, 'sp': nc.sync}
        s.prog = {e: [] for e in s.eng}
        s.sem = sems
        s.dsems = dsems
        s.tick = {e: 0 for e in s.CE}
        s.seen = {e: {} for e in s.eng}
        s.bufs = {}
        s.dcount = {q: 0 for q in dsems}

    def _semh(s, k):
        return s.sem[k] if isinstance(k, str) else s.dsems[k[1]][k[2]]

    def _deps(s, eng, r, w):
        need = {}

        def add(k, v):
            if need.get(k, 0) < v:
                need[k] = v
        for key in r:
            b = s.bufs.get(key)
            if b and b['w']:
                add(*b['w'])
        for key in w:
            b = s.bufs.get(key)
            if b:
                if b['w'] and b['w'][0] != eng:
                    add(*b['w'])
                for k, v in b['r'].items():
                    if k != eng:
                        add(k, v)
        return need

    def _waits(s, q, need):
        for k, v in need.items():
            if s.seen[q].get(k, 0) >= v:
                continue
            s.seen[q][k] = v
            sem = s._semh(k)
            s.prog[q].append(lambda E, sem=sem, v=v: E.wait_ge(sem, v))

    def _record(s, ev, r, w):
        for key in r:
            b = s.bufs.setdefault(key, {'w': None, 'r': {}})
            if b['r'].get(ev[0], 0) < ev[1]:
                b['r'][ev[0]] = ev[1]
        for key in w:
            s.bufs[key] = {'w': ev, 'r': {}}

    def op(s, eng, fn, r=(), w=()):
        s._waits(eng, s._deps(eng, r, w))
        s.tick[eng] += 1
        sem = s.sem[eng]
        s.prog[eng].append(lambda E, fn=fn, sem=sem: fn(E).then_inc(sem, 1))
        s._record((eng, s.tick[eng]), r, w)

    def dma(s, q, fn, r=(), w=()):
        i = s.dcount[q]
        s.dcount[q] += 1
        ns = len(s.dsems[q])
        j, val = i % ns, 16 * (i // ns + 1)
        need = s._deps(None, r, w)
        if i >= ns:
            k = ('d', q, j)
            need[k] = max(need.get(k, 0), val - 16)
        s._waits(q, need)
        sem = s.dsems[q][j]
        s.prog[q].append(lambda E, fn=fn, sem=sem: fn(E).then_inc(sem, 16))
        s._record((('d', q, j), val), r, w)

    def barrier(s):
        for e in s.eng:
            need = {}
            for c in s.CE:
                if s.tick[c] > 0 and c != e:
                    need[c] = s.tick[c]
            for q in s.dsems:
                n, ns = s.dcount[q], len(s.dsems[q])
                for j in range(min(n, ns)):
                    need[('d', q, j)] = 16 * ((n - 1 - j) // ns + 1)
            s._waits(e, need)


def dil_geom(d):
    nq_tot = NE // d
    qb = []
    a = 0
    while a < nq_tot:
        qb.append((a, min(128, nq_tot - a)))
        a += 128
    kt = [(0, 128)] + [(128 + a0, n) for (a0, n) in qb]
    return qb, kt


def build_program():
    nc = bass.Bass("TRN2", target_bir_lowering=False, dynamic_dma_scratch_size=4096)

    def din(name, shape, dt=F32):
        return nc.dram_tensor(name, list(shape), dt, kind="ExternalInput").ap()
    xc = din("xc", [SEQ, D])
    posc = din("posc", [128, 64], I32)
    memx = din("memx", [256, D])
    cvd = din("cv", [128, NCV])
    gbcd = din("gbc", [2, D])
    dbd = din("dbias", [12, 128, 256])
    trid = din("tri", [128, 128])
    identd = din("ident", [128, 128])
    w_in = din("w_in", [D, DIN])
    w_uq = din("w_uq", [384, 768])
    w_ukv = din("w_ukv", [256, 1024])
    w_memkv = din("w_mem_kv", [D, 1024])
    w_br = [din("w_br_mla", [512, D]), din("w_br_dil", [512, D]), din("w_br_mem", [512, D])]
    w_o = din("w_o", [D, D])
    w_up = din("w_ffn_up", [D, 2 * DFF])
    w_dn = din("w_ffn_down", [DFF, D])
    outd = nc.dram_tensor("out", [CH, D], F32, kind="ExternalOutput").ap()
    wupb = nc.dram_tensor("wupb_scratch", [D, 2 * DFF], BF16).ap()
    wdnb = nc.dram_tensor("wdnb_scratch", [DFF, D], BF16).ap()
    dbg = {}
    if DEBUG:
        for nm in ("dbg_ydil", "dbg_ymla", "dbg_ymem", "dbg_merged"):
            dbg[nm] = nc.dram_tensor(nm, [128, 8 if nm == "dbg_merged" else 4, NE], BF16, kind="ExternalOutput").ap()
        dbg["dbg_h"] = nc.dram_tensor("dbg_h", [128, 8, NE], BF16, kind="ExternalOutput").ap()

    END = 212992
    cur = [4608]

    def sbt(name, shape, dt, at=None):
        esz = 4 if dt in (F32, I32) else 2
        n = 1
        for v in shape[1:]:
            n *= v
        nbytes = (n * esz + 63) // 64 * 64
        if at is None:
            off = cur[0]
            cur[0] += nbytes
        else:
            off = at[0]
            at[0] += nbytes
        assert off + nbytes <= END, (name, off, nbytes)
        return nc.alloc_sbuf_tensor_at(name, list(shape), dt, offset=off)

    ident = sbt("ident", [128, 128], BF16)
    ones = sbt("ones", [128, 128], BF16)
    tri = sbt("tri_sb", [128, 128], BF16)
    cv = sbt("cv_sb", [128, NCV], F32)
    gbc = sbt("gbc_sb", [128, 2, D], F32)
    cosT = sbt("cosT", [128, 64, 16], F32)
    sinT = sbt("sinT", [128, 64, 16], F32)
    cosQ = sbt("cosQ", [128, NT, 16], F32)
    sinQ = sbt("sinQ", [128, NT, 16], F32)
    small = sbt("small", [128, 64], F32)
    junk = sbt("junk", [128, D], BF16)
    hT_own = sbt("hT_own", [128, 8, NE], BF16)
    W_OFF = cur[0]
    wsl = [sbt(f"wslot{i}", [128, 4096], BF16) for i in range(4)]
    Y_OFF = cur[0]
    y_dilT = sbt("y_dilT", [128, 4, NE], BF16)
    y_mlaT = sbt("y_mlaT", [128, 4, NE], BF16)
    y_memT = sbt("y_memT", [128, 4, NE], BF16)
    T_OFF = cur[0]
    YB = 17408

    psf = nc.alloc_psum_tensor("psf", [128, 3584], F32)
    psb = nc.alloc_psum_tensor("psb", [128, 1024], BF16)

    def PF(b, n=512, off=0):
        return psf[:, b * 512 + off: b * 512 + off + n]

    def cvc(c, p=128):
        return cv[0:p, c:c + 1]

    es = ExitStack()
    sems = {e: es.enter_context(nc.semaphore(f"s_{e}")) for e in Sched.CE}
    dsems = {q: [es.enter_context(nc.semaphore(f"d_{q}{i}")) for i in range(8)] for q in ('sp', 'pool')}
    S = Sched(nc, sems, dsems)
    name_ctr = [0]

    wctr = [0]

    def wload(parts):
        i = wctr[0] % 4
        wctr[0] += 1
        views = []
        off = 0
        for src in parts:
            k, n = src.shape[1], src.shape[2]
            v = wsl[i][:, off:off + k * n].rearrange("p (k n) -> p k n", n=n)
            off += k * n
            assert off <= 4096
            S.dma('pool', lambda E, v=v, src=src: E.dma_start(out=v, in_=src), w=[('w', i)])
            views.append(v)
        return views, ('w', i)

    def wload_b(src, rkeys):
        i = wctr[0] % 4
        wctr[0] += 1
        k, n = src.shape[1], src.shape[2]
        v = wsl[i][:, 0:k * n].rearrange("p (k n) -> p k n", n=n)
        S.dma('sp', lambda E: E.dma_start(out=v, in_=src), r=rkeys, w=[('w', i)])
        return v, ('w', i)

    def wk(w, c0, n, kp=None):
        return w[:, c0:c0 + n].rearrange("(k p) n -> p k n", p=128)

    def mm(out, pairs, w, r):
        def f(E):
            last = None
            for i, (l, rr) in enumerate(pairs):
                last = E.matmul(out, lhsT=l, rhs=rr, start=(i == 0), stop=(i == len(pairs) - 1))
            return last
        S.op('pe', f, r=r, w=w)

    def act(out, in_, func, r, w, **kw):
        S.op('act', lambda E: E.activation(out=out, in_=in_, func=func, **kw), r=r, w=w)

    def rstd_from_ss(ss_ap, out_ap, tmp_ap, n_feat, rkeys, wkeys, tkeys):
        act(tmp_ap, ss_ap, AF.Ln, r=rkeys, w=tkeys, scale=1.0 / n_feat, bias=EPS)
        act(out_ap, tmp_ap, AF.Exp, r=tkeys, w=wkeys, scale=-0.5)

    def rms_T(tiles, gcol, dst, dkey, xhat, xkey):
        n = len(tiles)
        for i, (ap, key) in enumerate(tiles):
            S.op('act', lambda E, ap=ap, i=i: E.activation(out=junk[:, :], in_=ap, func=AF.Square,
                                                           accum_out=small[:, i:i + 1]),
                 r=[key], w=['junk', ('sm', i)])
        rstd_from_ss(small[:, 0:n], small[:, 16:16 + n], small[:, 8:8 + n], 1024.0,
                     [('sm', i) for i in range(n)], ['smr'], ['sml'])
        for i, (ap, key) in enumerate(tiles):
            S.op('dve', lambda E, ap=ap, i=i: E.tensor_scalar(out=xhat[:, i, :], in0=ap,
                                                              scalar1=small[:, 16 + i:17 + i], scalar2=None,
                                                              op0=ALU.mult),
                 r=[key, 'smr'], w=[(xkey, i)])
        for kc2 in range(4):
            def f(E, kc2=kc2):
                last = None
                for j in range(2):
                    kc = kc2 * 2 + j
                    for i in range(n):
                        last = E.transpose(psb[:, j * 512 + i * 128: j * 512 + (i + 1) * 128],
                                           xhat[:, i, kc * 128:(kc + 1) * 128], ident[:, :])
                return last
            S.op('pe', f, r=[(xkey, i) for i in range(n)] + ['ident'], w=['psb'])
            for j in range(2):
                kc = kc2 * 2 + j
                if j == 0:
                    act(dst(kc), psb[:, j * 512: j * 512 + n * 128], AF.Copy, r=['psb'], w=[dkey],
                        scale=cvc(gcol + kc))
                else:
                    S.op('dve', lambda E, kc=kc, j=j: E.tensor_scalar(out=dst(kc), in0=psb[:, j * 512: j * 512 + n * 128],
                                                                      scalar1=cvc(gcol + kc), scalar2=None, op0=ALU.mult),
                         r=['psb', 'cv'], w=[dkey])

    def rms_pipe(calls, gcol, xbufs, xhats, xname):
        def stageA(c):
            call = calls[c]
            par = c % 2
            base = par * 24
            n = len(call['srcs'])
            tiles = [xload(xbufs, src) for src in call['srcs']]
            for i, (ap, key) in enumerate(tiles):
                S.op('act', lambda E, ap=ap, i=i: E.activation(out=junk[:, :], in_=ap, func=AF.Square,
                                                               accum_out=small[:, base + i:base + i + 1]),
                     r=[key], w=['junk', ('sm', base + i)])
            rstd_from_ss(small[:, base:base + n], small[:, base + 16:base + 16 + n], small[:, base + 8:base + 8 + n],
                         1024.0, [('sm', base + i) for i in range(n)], [('smr', par)], [('sml', par)])
            for i, (ap, key) in enumerate(tiles):
                S.op('dve', lambda E, ap=ap, i=i: E.tensor_scalar(out=xhats[par][:, i, :], in0=ap,
                                                                  scalar1=small[:, base + 16 + i:base + 17 + i],
                                                                  scalar2=None, op0=ALU.mult),
                     r=[key, ('smr', par)], w=[(xname, par, i)])

        def stageB(c):
            call = calls[c]
            par = c % 2
            n = len(call['srcs'])
            dst, dkey = call['dst'], call['dkey']
            xh = xhats[par]
            for kc2 in range(4):
                def f(E, kc2=kc2):
                    last = None
                    for j in range(2):
                        kc = kc2 * 2 + j
                        for i in range(n):
                            last = E.transpose(psb[:, j * 512 + i * 128: j * 512 + (i + 1) * 128],
                                               xh[:, i, kc * 128:(kc + 1) * 128], ident[:, :])
                    return last
                S.op('pe', f, r=[(xname, par, i) for i in range(n)] + ['ident'], w=['psb'])
                for j in range(2):
                    kc = kc2 * 2 + j
                    if False:
                        act(dst(kc), psb[:, j * 512: j * 512 + n * 128], AF.Copy, r=['psb'], w=[dkey],
                            scale=cvc(gcol + kc))
                    else:
                        S.op('dve', lambda E, kc=kc, j=j: E.tensor_scalar(out=dst(kc), in0=psb[:, j * 512: j * 512 + n * 128],
                                                                          scalar1=cvc(gcol + kc), scalar2=None,
                                                                          op0=ALU.mult), r=['psb', 'cv'], w=[dkey])
            if call.get('after'):
                call['after']()

        stageA(0)
        for c in range(len(calls)):
            if c + 1 < len(calls):
                stageA(c + 1)
            stageB(c)

    xctr = [0]

    def xload(xbufs, src):
        i = xctr[0] % len(xbufs)
        xctr[0] += 1
        dst_ = xbufs[i][:, :]
        S.dma('sp', lambda E: E.dma_start(out=dst_, in_=src), w=[('xb', i)])
        return dst_, ('xb', i)

    tp = [T_OFF]
    tmp32 = sbt("setup_tmp", [128, 128], F32, at=tp)
    S.dma('sp', lambda E: E.dma_start(out=cv[:, :], in_=cvd[:, :]), w=['cv'])
    S.dma('sp', lambda E: E.dma_start(out=gbc[:, :, :], in_=gbcd.partition_broadcast(128)), w=['gbc'])
    S.dma('pool', lambda E: E.dma_start(out=ident[:, :], in_=identd[:, :]), w=['ident'])
    S.dma('pool', lambda E: E.dma_start(out=tri[:, :], in_=trid[:, :]), w=['tri'])
    S.op('dve', lambda E: E.memset(ones[:, :], 1.0), w=['ones'])
    posi = sbt("posi", [128, 64], I32, at=tp)
    posf = sbt("posf", [128, 64], F32, at=tp)
    ang = sbt("ang", [128, 64, 16], F32, at=tp)
    kf = sbt("kf", [128, 64, 16], F32, at=tp)
    ki = sbt("ki", [128, 64, 16], I32, at=tp)
    S.dma('sp', lambda E: E.dma_start(out=posi[:, :], in_=posc[:, :]), w=['posi'])
    S.op('dve', lambda E: E.tensor_copy(out=posf[:, :], in_=posi[:, :]), r=['posi'], w=['posf'])
    S.op('dve', lambda E: E.tensor_tensor(out=ang[:, :, :], in0=posf[:, :].unsqueeze(2).to_broadcast([128, 64, 16]),
                                          in1=cv[:, C_INVF:C_INVF + 16].unsqueeze(1).to_broadcast([128, 64, 16]),
                                          op=ALU.mult), r=['posf', 'cv'], w=['ang'])
    for tab, shift in ((sinT, 0.0), (cosT, np.pi / 2)):
        S.op('dve', lambda E, shift=shift: E.tensor_scalar(out=kf[:, :, :], in0=ang[:, :, :], scalar1=float(shift),
                                                           scalar2=float(1.0 / TWO_PI), op0=ALU.add, op1=ALU.mult),
             r=['ang'], w=['kf'])
        S.op('dve', lambda E: E.tensor_copy(out=ki[:, :, :], in_=kf[:, :, :]), r=['kf'], w=['ki'])
        S.op('dve', lambda E: E.tensor_copy(out=kf[:, :, :], in_=ki[:, :, :]), r=['ki'], w=['kf'])
        S.op('dve', lambda E: E.scalar_tensor_tensor(out=kf[:, :, :], in0=kf[:, :, :], scalar=float(-TWO_PI),
                                                     in1=ang[:, :, :], op0=ALU.mult, op1=ALU.add),
             r=['kf', 'ang'], w=['kf'])
        S.op('dve', lambda E, shift=shift: E.tensor_scalar(out=kf[:, :, :], in0=kf[:, :, :], scalar1=float(shift),
                                                           scalar2=3.1415925, op0=ALU.add, op1=ALU.min),
             r=['kf'], w=['kf'])
        S.op('dve', lambda E: E.tensor_scalar(out=kf[:, :, :], in0=kf[:, :, :], scalar1=-3.1415925, scalar2=None,
                                              op0=ALU.max), r=['kf'], w=['kf'])
        act(tab[:, :, :], kf[:, :, :], AF.Sin, r=['kf'], w=['rope'])
    qs = 96.0 ** -0.5
    S.op('dve', lambda E: E.tensor_scalar(out=cosQ[:, :, :], in0=cosT[:, 47:64, :], scalar1=qs, scalar2=None,
                                          op0=ALU.mult), r=['rope'], w=['ropeq'])
    S.op('dve', lambda E: E.tensor_scalar(out=sinQ[:, :, :], in0=sinT[:, 47:64, :], scalar1=qs, scalar2=None,
                                          op0=ALU.mult), r=['rope'], w=['ropeq'])
    S.barrier()

    tp = [Y_OFF + YB]
    xbufA = [sbt(f"xbufA{i}", [128, D], F32, at=tp) for i in range(2)]
    xhatA = sbt("xhatA", [128, 4, D], BF16, at=tp)
    hT_halo = sbt("hT_halo", [128, 8, 2048], BF16, at=tp)
    QTd = sbt("QTd", [128, NE], BF16, at=tp)
    KTd = sbt("KTd", [128, 4224], BF16, at=tp)
    Vd = sbt("Vd", [128, 48, 128], BF16, at=tp)
    accO = sbt("accO", [128, NE], F32, at=tp)
    accD = sbt("accD", [128, NE], F32, at=tp)
    PTd = [sbt(f"PTd{i}", [128, 256], BF16, at=tp) for i in range(4)]
    ssb = [sbt(f"ssb{i}", [128, 256], F32, at=tp) for i in range(2)]
    dbt = [sbt(f"dbt{i}", [128, 256], F32, at=tp) for i in range(2)]

    def prep_all(row0, ntiles, dstbuf, dkey):
        t = 0
        while t < ntiles:
            n = min(2, ntiles - t)
            tiles = [xload(xbufA, xc[row0 + (t + i) * 128: row0 + (t + i + 1) * 128, :]) for i in range(n)]
            t0 = t
            rms_T(tiles, C_GPRE, lambda kc, t0=t0, n=n: dstbuf[:, kc, t0 * 128:(t0 + n) * 128], dkey, xhatA, 'xhA')
            t += n

    xhatsA = [xhatA[:, 0:2, :], xhatA[:, 2:4, :]]
    callsA = []
    for (row0, ntiles, dstbuf, dkey) in ((3968, 16, hT_halo, 'hTh'), (6016, NT, hT_own, 'hT')):
        t = 0
        while t < ntiles:
            n = min(2, ntiles - t)
            callsA.append(dict(srcs=[xc[row0 + (t + i) * 128: row0 + (t + i + 1) * 128, :] for i in range(n)],
                               dst=(lambda kc, t=t, n=n, dstbuf=dstbuf: dstbuf[:, kc, t * 128:(t + n) * 128]), dkey=dkey))
            t += n
    xbufA4 = xbufA + [accO[:, 0:D], accO[:, D:2 * D]]
    rms_pipe(callsA, C_GPRE, xbufA4, xhatsA, 'xhA')
    S.barrier()
    if DEBUG:
        S.dma('sp', lambda E: E.dma_start(out=dbg["dbg_h"][:, :, :], in_=hT_own[:, :, :]), r=['hT'])

    dscale = 128.0 ** -0.5
    bank_rr = [0]

    def nb():
        bank_rr[0] = (bank_rr[0] + 1) % 7
        return bank_rr[0]

    for hd in range(4):
        for g in range(3):
            d = DIL_D[g]
            gi = g * 4 + hd
            qb, kt = dil_geom(d)
            nql = NE // d
            nkl = 128 + nql
            c0 = OFF_KR + g * 512 + hd * 128
            (wq_, wk_, wv_), wkey = wload([wk(w_in, c0, 128), wk(w_in, c0 + 1536, 128), wk(w_in, c0 + 3072, 128)])
            bi = gi % 2
            S.dma('sp', lambda E, bi=bi, gi=gi: E.dma_start(out=dbt[bi][:, :], in_=dbd[gi, :, :]), w=[('dbt', bi)])
            QV = QTd[:, :].rearrange("p (r l) -> p r l", r=d)
            KV = KTd[:, 0:d * nkl].rearrange("p (r l) -> p r l", r=d)
            for (e0, n) in GROUPS:
                for which, wv3, dstv, loff, sc in ((0, wq_, QV, 0, dscale), (1, wk_, KV, 128, 1.0)):
                    b = nb()
                    mm(PF(b, n), [(wv3[:, kc, :], hT_own[:, kc, e0:e0 + n]) for kc in range(8)],
                       w=[('pf', b)], r=[wkey, 'hT'])
                    act(dstv[:, :, loff + e0 // d: loff + (e0 + n) // d],
                        PF(b, n).rearrange("p (l r) -> p r l", r=d), AF.Copy,
                        r=[('pf', b)], w=['QTd' if which == 0 else 'KTd'], scale=sc)
            h0 = 2048 - 128 * d
            while h0 < 2048:
                n = min(512, 2048 - h0)
                b = nb()
                mm(PF(b, n), [(wk_[:, kc, :], hT_halo[:, kc, h0:h0 + n]) for kc in range(8)],
                   w=[('pf', b)], r=[wkey, 'hTh'])
                l0 = (h0 - (2048 - 128 * d)) // d
                act(KV[:, :, l0: l0 + n // d], PF(b, n).rearrange("p (l r) -> p r l", r=d), AF.Copy,
                    r=[('pf', b)], w=['KTd'])
                h0 += n
            ntile = len(kt)
            for r_ in range(d):
                for j, (a, nk) in enumerate(kt):
                    if j == 0:
                        s0 = 2048 - 128 * d + r_
                        src = lambda kc, s0=s0, nk=nk: hT_halo[:, kc, s0: s0 + (nk - 1) * d + 1: d]
                        rk = 'hTh'
                    else:
                        s0 = (a - 128) * d + r_
                        src = lambda kc, s0=s0, nk=nk: hT_own[:, kc, s0: s0 + (nk - 1) * d + 1: d]
                        rk = 'hT'
                    b = nb()
                    mm(psf[0:nk, b * 512: b * 512 + 128], [(src(kc), wv_[:, kc, :]) for kc in range(8)],
                       w=[('pf', b)], r=[wkey, rk])
                    ti = r_ * ntile + j
                    S.op('dve', lambda E, nk=nk, b=b, ti=ti: E.tensor_copy(out=Vd[0:nk, ti, :],
                                                                         in_=psf[0:nk, b * 512: b * 512 + 128]),
                         r=[('pf', b)], w=['Vd'])
            for r_ in range(d):
                info = {}

                def dil_pv(m, r_=r_, info=info, d=d, g=g, qb=qb, ntile=ntile):
                    qa, qn = qb[m]
                    pp, pcol, _ = info[m]
                    pi, _, nk = info[m + 1]
                    b2 = nb()
                    tprev = r_ * ntile + m
                    tcur = r_ * ntile + m + 1

                    def f(E):
                        o = psf[:, b2 * 512: b2 * 512 + qn]
                        dd = psf[:, b2 * 512 + 128: b2 * 512 + 128 + qn]
                        E.matmul(o, lhsT=Vd[:, tprev, :], rhs=PTd[pp][:, pcol:pcol + qn], start=True, stop=False)
                        E.matmul(o, lhsT=Vd[0:nk, tcur, :], rhs=PTd[pi][0:nk, 0:qn], start=False, stop=True)
                        E.matmul(dd, lhsT=ones[:, :], rhs=PTd[pp][:, pcol:pcol + qn], start=False, stop=False,
                                 skip_group_check=True)
                        return E.matmul(dd, lhsT=ones[0:nk, :], rhs=PTd[pi][0:nk, 0:qn], start=False, stop=True,
                                        skip_group_check=True)
                    S.op('pe', f, r=['Vd', ('PTd', pp), ('PTd', pi), 'ones'], w=[('pf', b2)])
                    e_s = qa * d + r_
                    e_e = e_s + (qn - 1) * d + 1
                    for accb, off, akey in ((accO, 0, 'accO'), (accD, 128, 'accD')):
                        if g == 0:
                            S.op('dve', lambda E, accb=accb, off=off: E.tensor_copy(
                                out=accb[:, e_s:e_e:d], in_=psf[:, b2 * 512 + off: b2 * 512 + off + qn]),
                                r=[('pf', b2)], w=[akey])
                        else:
                            S.op('dve', lambda E, accb=accb, off=off: E.tensor_tensor(
                                out=accb[:, e_s:e_e:d], in0=psf[:, b2 * 512 + off: b2 * 512 + off + qn],
                                in1=accb[:, e_s:e_e:d], op=ALU.add), r=[('pf', b2), akey], w=[akey])

                for j, (a, nk) in enumerate(kt):
                    has_diag = j >= 1
                    has_prev = j < len(qb)
                    ncol = (qb[j - 1][1] if has_diag else 0) + (qb[j][1] if has_prev else 0)
                    qlo = qb[j - 1][0] if has_diag else qb[j][0]
                    boff = 0 if has_diag else 128
                    b = nb()
                    mm(psf[0:nk, b * 512: b * 512 + ncol], [(KV[:, r_, a:a + nk], QV[:, r_, qlo:qlo + ncol])],
                       w=[('pf', b)], r=['KTd', 'QTd'])
                    si = j % 2
                    S.op('dve', lambda E, nk=nk, b=b, ncol=ncol, si=si, boff=boff, bi=bi: E.tensor_tensor(
                        out=ssb[si][0:nk, 0:ncol], in0=psf[0:nk, b * 512: b * 512 + ncol],
                        in1=dbt[bi][0:nk, boff:boff + ncol], op=ALU.add),
                        r=[('pf', b), ('dbt', bi)], w=[('ssb', si)])
                    ridx = (0, 1, 5)[g] + r_
                    bcol = C_DVB0 + ridx if j == 0 else (C_DVB1 + ridx if j == 1 else C_ZERO)
                    pi = j % 4
                    act(PTd[pi][0:nk, 0:ncol], ssb[si][0:nk, 0:ncol], AF.Exp, r=[('ssb', si), 'cv'],
                        w=[('PTd', pi)], bias=cvc(bcol, nk))
                    info[j] = (pi, (qb[j - 1][1] if has_diag else 0), nk)
                    if j >= 2:
                        dil_pv(j - 2)
                dil_pv(len(kt) - 2)
        S.op('dve', lambda E: E.tensor_scalar(out=accD[:, :], in0=accD[:, :], scalar1=1e-30, scalar2=None,
                                              op0=ALU.max), r=['accD'], w=['accD'])
        S.op('dve', lambda E: E.reciprocal(out=accD[:, :], in_=accD[:, :]), r=['accD'], w=['accD'])
        S.op('dve', lambda E, hd=hd: E.tensor_tensor(out=y_dilT[:, hd, :], in0=accO[:, :], in1=accD[:, :],
                                                     op=ALU.mult), r=['accO', 'accD'], w=['ydil'])
    if DEBUG:
        S.dma('sp', lambda E: E.dma_start(out=dbg["dbg_ydil"][:, :, :], in_=y_dilT[:, :, :]), r=['ydil'])
    S.barrier()


    ckvT = nc.alloc_sbuf_tensor_at("ckvT", [128, 2, SEQ], BF16, offset=W_OFF)
    tp = [Y_OFF + 2 * YB]
    KT = sbt("KT", [128, SEQ], BF16, at=tp)
    cqT = sbt("cqT", [128, 3, NE], BF16, at=tp)
    Wuq = sbt("Wuq", [128, 3, 768], BF16, at=tp)
    Wukv = sbt("Wukv", [128, 2, 1024], BF16, at=tp)
    OV = tp[0]
    tp = [OV]
    xbufB = [nc.alloc_sbuf_tensor_at(f"xbufB{i}", [128, D], F32, offset=Y_OFF + YB + i * 4096) for i in range(4)]
    Wq = sbt("Wq", [128, 8, 384], BF16, at=tp)
    xhatB = sbt("xhatB", [128, 4, D], BF16, at=tp)
    hTgs = [sbt(f"hTg{i}", [128, 8, 512], BF16, at=tp) for i in range(2)]
    Wkvr = sbt("Wkvr", [128, 8, 288], BF16, at=tp)
    sq = sbt("sq", [128, 3, 512], BF16, at=tp)
    rbc = sbt("rbc", [128, 512], F32, at=tp)
    kpe = sbt("kpe", [128, 4, 96], BF16, at=tp)
    kt4 = [sbt(f"kt4_{i}", [128, 4, 16], F32, at=tp) for i in range(4)]

    S.dma('pool', lambda E: E.dma_start(out=Wkvr[:, :, :], in_=wk(w_in, OFF_Q, 288)), w=['Wkvr'])
    S.dma('pool', lambda E: E.dma_start(out=Wq[:, :, :], in_=wk(w_in, 0, 384)), w=['Wq'])
    S.dma('pool', lambda E: E.dma_start(out=Wuq[:, :, :], in_=w_uq.rearrange("(k p) n -> p k n", p=128)), w=['Wuq'])
    S.dma('pool', lambda E: E.dma_start(out=Wukv[:, :, :], in_=w_ukv.rearrange("(k p) n -> p k n", p=128)),
          w=['Wukv'])
    S.op('dve', lambda E: E.memset(kpe[:, :, :], 0.0), w=['kpe'])

    def latent_norm(banks, n, nchunk, gcol, dstf, dkey):
        for kc in range(nchunk):
            act(sq[:, kc, 0:n], PF(banks[kc], n), AF.Square, r=[('pf', banks[kc])], w=[('sq', kc)])
        bs = nb()
        mm(PF(bs, n), [(ones[:, :], sq[:, kc, 0:n]) for kc in range(nchunk)], w=[('pf', bs)],
           r=['ones'] + [('sq', kc) for kc in range(nchunk)])
        act(rbc[:, 0:n], PF(bs, n), AF.Ln, r=[('pf', bs)], w=['rbc'], scale=1.0 / (128 * nchunk), bias=EPS)
        act(rbc[:, 0:n], rbc[:, 0:n], AF.Exp, r=['rbc'], w=['rbc'], scale=-0.5)
        for kc in range(nchunk):
            S.op('dve', lambda E, kc=kc: E.scalar_tensor_tensor(out=dstf(kc), in0=PF(banks[kc], n),
                                                                scalar=cvc(gcol + kc), in1=rbc[:, 0:n],
                                                                op0=ALU.mult, op1=ALU.mult),
                 r=[('pf', banks[kc]), 'rbc', 'cv'], w=[dkey])

    def rope_tm(src3, cos3, sin3, dst3, nt, rkeys, wkey):
        a, b_ = src3[:, :, 0:16], src3[:, :, 16:32]
        t = [k[:, 0:nt, :] for k in kt4]
        S.op('dve', lambda E: E.tensor_tensor(out=t[0], in0=a, in1=cos3, op=ALU.mult), r=rkeys, w=[('kt4', 0)])
        S.op('dve', lambda E: E.tensor_tensor(out=t[1], in0=b_, in1=sin3, op=ALU.mult), r=rkeys, w=[('kt4', 1)])
        S.op('dve', lambda E: E.tensor_tensor(out=dst3[:, :, 0:16], in0=t[0], in1=t[1], op=ALU.subtract),
             r=[('kt4', 0), ('kt4', 1)], w=[wkey])
        S.op('dve', lambda E: E.tensor_tensor(out=t[2], in0=b_, in1=cos3, op=ALU.mult), r=rkeys, w=[('kt4', 2)])
        S.op('dve', lambda E: E.tensor_tensor(out=t[3], in0=a, in1=sin3, op=ALU.mult), r=rkeys, w=[('kt4', 3)])
        S.op('dve', lambda E: E.tensor_tensor(out=dst3[:, :, 16:32], in0=t[2], in1=t[3], op=ALU.add),
             r=[('kt4', 2), ('kt4', 3)], w=[wkey])

    xctr[0] = 0

    def ctx_latent(tg):
        hTg = hTgs[tg % 2]
        hk = ('hTg', tg % 2)
        banks = [nb(), nb()]
        for kc2 in range(2):
            mm(PF(banks[kc2]), [(Wkvr[:, kc, kc2 * 128:(kc2 + 1) * 128], hTg[:, kc, :]) for kc in range(8)],
               w=[('pf', banks[kc2])], r=['Wkvr', hk])
        latent_norm(banks, 512, 2, C_KVN, lambda kc: ckvT[:, kc, tg * 512:(tg + 1) * 512], 'ckvT')
        bk = nb()

        def fk(E):
            last = None
            for i in range(4):
                for kc in range(8):
                    last = E.matmul(psf[:, bk * 512 + i * 32: bk * 512 + (i + 1) * 32],
                                    lhsT=hTg[:, kc, i * 128:(i + 1) * 128], rhs=Wkvr[:, kc, 256:288],
                                    start=(kc == 0), stop=(kc == 7))
            return last
        S.op('pe', fk, r=['Wkvr', hk], w=[('pf', bk)])
        src3 = psf[:, bk * 512: bk * 512 + 128].rearrange("p (t c) -> p t c", c=32)
        rope_tm(src3, cosT[:, tg * 4:(tg + 1) * 4, :], sinT[:, tg * 4:(tg + 1) * 4, :], kpe[:, :, 64:96], 4,
                [('pf', bk), 'rope'], 'kpe')

        def ft(E):
            last = None
            for i in range(4):
                last = E.transpose(psb[0:96, i * 128:(i + 1) * 128], kpe[:, i, :], ident[:, :])
            return last
        S.op('pe', ft, r=['kpe', 'ident'], w=['psb'])
        act(KT[64:96, tg * 512:(tg + 1) * 512], psb[64:96, 0:512], AF.Copy, r=['psb'], w=['KTr'])

    xhatsB = [xhatB[:, 0:2, :], xhatB[:, 2:4, :]]
    callsB = []
    for tg in range(16):
        for hh in range(2):
            callsB.append(dict(
                srcs=[xc[(tg * 4 + hh * 2 + i) * 128:(tg * 4 + hh * 2 + i + 1) * 128, :] for i in range(2)],
                dst=(lambda kc, tg=tg, hh=hh: hTgs[tg % 2][:, kc, hh * 256:(hh + 1) * 256]), dkey=('hTg', tg % 2),
                after=((lambda tg=tg: ctx_latent(tg)) if hh == 1 else None)))
    rms_pipe(callsB, C_GPRE, xbufB, xhatsB, 'xhB')
    for (e0, n) in GROUPS:
        banks = [nb(), nb(), nb()]
        for kc3 in range(3):
            mm(PF(banks[kc3], n), [(Wq[:, kc, kc3 * 128:(kc3 + 1) * 128], hT_own[:, kc, e0:e0 + n]) for kc in range(8)],
               w=[('pf', banks[kc3])], r=['Wq', 'hT'])
        latent_norm(banks, n, 3, C_QN, lambda kc, e0=e0, n=n: cqT[:, kc, e0:e0 + n], 'cqT')
    S.barrier()

    for c in range(4):
        S.dma('pool', lambda E, c=c: E.dma_start(out=wupb[:, c * 1408:(c + 1) * 1408], in_=w_up[:, c * 1408:(c + 1) * 1408]),
              w=[('wupb', c)])
    for r_ in range(2):
        S.dma('pool', lambda E, r_=r_: E.dma_start(out=wdnb[r_ * 1408:(r_ + 1) * 1408, :], in_=w_dn[r_ * 1408:(r_ + 1) * 1408, :]),
              w=[('wdnb', r_)])
    tp = [OV]
    Vts = [sbt(f"Vt{i}", [128, 64, 65], BF16, at=tp) for i in range(2)]
    QTs = [sbt(f"QT{i}", [128, NE], BF16, at=tp) for i in range(2)]
    qtm = sbt("qtm", [128, NT, 96], BF16, at=tp)
    ypair = sbt("ypair", [128, NT, 128], BF16, at=tp)
    PTm = [sbt(f"PTm{i}", [128, 1024], BF16, at=tp) for i in range(3)]
    rden = sbt("rden", [128, 8], F32, at=tp)
    kt4 = [sbt(f"kq4_{i}", [128, 5, 16], F32, at=tp) for i in range(4)]
    for Vt_ in Vts:
        S.op('dve', lambda E, Vt_=Vt_: E.memset(Vt_[:, :, 64:65], 1.0), w=['Vones'])
    evac_rr = [0]

    def evac(out, in_, r, w, **kw):
        evac_rr[0] += 1
        if not kw:
            S.op('dve', lambda E: E.tensor_copy(out=out, in_=in_), r=r, w=w)
        else:
            act(out, in_, AF.Copy, r=r, w=w, **kw)

    pctr = [0]
    for h in range(8):
        for tg in range(16):
            b = nb()
            mm(psf[0:64, b * 512:(b + 1) * 512],
               [(Wukv[:, kc, h * 128: h * 128 + 64], ckvT[:, kc, tg * 512:(tg + 1) * 512]) for kc in range(2)],
               w=[('pf', b)], r=['Wukv', 'ckvT'])
            evac(KT[0:64, tg * 512:(tg + 1) * 512], psf[0:64, b * 512:(b + 1) * 512], r=[('pf', b)], w=['KTn'])
        def build_vq(h, bankf):
            Vt, QT = Vts[h % 2], QTs[h % 2]
            vk, qk = ('Vt', h % 2), ('QT', h % 2)
            chunks = []

            def vchunk(t8):
                b = bankf()

                def fv(E):
                    last = None
                    for i in range(8):
                        for kc in range(2):
                            last = E.matmul(psf[:, b * 512 + i * 64: b * 512 + (i + 1) * 64],
                                            lhsT=ckvT[:, kc, (t8 * 8 + i) * 128:(t8 * 8 + i + 1) * 128],
                                            rhs=Wukv[:, kc, h * 128 + 64: h * 128 + 128], start=(kc == 0), stop=(kc == 1))
                    return last
                S.op('pe', fv, r=['Wukv', 'ckvT'], w=[('pf', b)])
                evac(Vt[:, t8 * 8:(t8 + 1) * 8, 0:64], PF(b).rearrange("p (t e) -> p t e", e=64), r=[('pf', b)], w=[vk])

            def qchunk(t0):
                nt = min(5, NT - t0)
                b = bankf()

                def fq(E):
                    last = None
                    for i in range(nt):
                        for kc in range(3):
                            last = E.matmul(psf[:, b * 512 + i * 96: b * 512 + (i + 1) * 96],
                                            lhsT=cqT[:, kc, (t0 + i) * 128:(t0 + i + 1) * 128],
                                            rhs=Wuq[:, kc, h * 96:(h + 1) * 96], start=(kc == 0), stop=(kc == 2))
                    return last
                S.op('pe', fq, r=['Wuq', 'cqT'], w=[('pf', b)])
                p3 = psf[:, b * 512: b * 512 + nt * 96].rearrange("p (t c) -> p t c", c=96)
                S.op('dve', lambda E: E.tensor_scalar(out=qtm[:, t0:t0 + nt, 0:64], in0=p3[:, :, 0:64],
                                                      scalar1=qs, scalar2=None, op0=ALU.mult),
                     r=[('pf', b)], w=['qtm'])
                rope_tm(p3[:, :, 64:96], cosQ[:, t0:t0 + nt, :], sinQ[:, t0:t0 + nt, :], qtm[:, t0:t0 + nt, 64:96], nt,
                        [('pf', b), 'ropeq'], 'qtm')

            def tchunk(t0):
                nt = min(8, NT - t0)

                def ftq(E):
                    last = None
                    for i in range(nt):
                        last = E.transpose(psb[0:96, i * 128:(i + 1) * 128], qtm[:, t0 + i, :], ident[:, :])
                    return last
                S.op('pe', ftq, r=['qtm', 'ident'], w=['psb'])
                S.op('dve', lambda E: E.tensor_copy(out=QT[0:96, t0 * 128:(t0 + nt) * 128],
                                                    in_=psb[0:96, 0:nt * 128]), r=['psb'], w=[qk])
            for t8 in range(8):
                chunks.append(lambda t8=t8: vchunk(t8))
            for t0 in range(0, NT, 5):
                chunks.append(lambda t0=t0: qchunk(t0))
            for t0 in range(0, NT, 8):
                chunks.append(lambda t0=t0: tchunk(t0))
            return chunks

        if h == 0:
            for ch in build_vq(0, nb):
                ch()
        Vt, QT = Vts[h % 2], QTs[h % 2]
        vk, qk = ('Vt', h % 2), ('QT', h % 2)
        items = []
        for gq, (e0, n) in enumerate(GROUPS):
            tq0, ntq = e0 // 128, n // 128
            kfirst = 47 + tq0
            per = 1024 // n
            units = []
            kb = 0
            while kb < kfirst:
                lim = min(kfirst, (kb // 16 + 1) * 16, kb + per)
                units.append((list(range(kb, lim)), None))
                kb = lim
            for m in range(ntq):
                units.append(([kfirst + m], m))
            for ui, (kbs, m) in enumerate(units):
                items.append(dict(e0=e0, n=n, tq0=tq0, ntq=ntq, ob=4 + (gq % 2), kbs=kbs, m=m, first=(ui == 0),
                                  last=(ui == len(units) - 1)))

        def emit_S(it, i):
            sp, n, e0, m = ((pctr[0] + i) % 2) * 2, it['n'], it['e0'], it['m']
            if m is None:
                def fs(E, it=it, sp=sp, n=n, e0=e0, QT=QT):
                    last = None
                    for idx, kbk in enumerate(it['kbs']):
                        last = E.matmul(psf[:, sp * 512 + idx * n: sp * 512 + (idx + 1) * n],
                                        lhsT=KT[0:96, kbk * 128:(kbk + 1) * 128], rhs=QT[0:96, e0:e0 + n],
                                        start=True, stop=True)
                    return last
                S.op('pe', fs, r=['KTn', 'KTr', qk], w=[('pf', sp), ('pf', sp + 1)])
            else:
                kbk = it['kbs'][0]
                cols = n - 128 * m
                mm(psf[:, sp * 512: sp * 512 + cols],
                   [(KT[0:96, kbk * 128:(kbk + 1) * 128], QT[0:96, e0 + 128 * m:e0 + n])],
                   w=[('pf', sp), ('pf', sp + 1)], r=['KTn', 'KTr', qk])

        def emit_A(it, i):
            sp, pi = ((pctr[0] + i) % 2) * 2, (pctr[0] + i) % 3
            n, m = it['n'], it['m']
            kb0 = it['kbs'][0]
            tot = len(it['kbs']) * n if m is None else n - 128 * m
            act(PTm[pi][:, 0:tot], psf[:, sp * 512: sp * 512 + tot], AF.Exp,
                r=[('pf', sp), ('pf', sp + 1), 'cv'], w=[('PTm', pi)], bias=cvc(C_VB + min(kb0 // 16, 3)))
            if m is not None:
                S.op('dve', lambda E, pi=pi: E.tensor_tensor(out=PTm[pi][:, 0:128], in0=PTm[pi][:, 0:128],
                                                             in1=tri[:, :], op=ALU.mult),
                     r=[('PTm', pi), 'tri'], w=[('PTm', pi)])

        def emit_PV(it, i):
            pi = (pctr[0] + i) % 3
            n, m, ntq, ob, tq0 = it['n'], it['m'], it['ntq'], it['ob'], it['tq0']
            if m is None:
                blocks = [(idx * n, kbk, list(range(ntq))) for idx, kbk in enumerate(it['kbs'])]
            else:
                blocks = [(0, it['kbs'][0], list(range(m, ntq)))]

            def pv(E, pi=pi, blocks=blocks, ob=ob, first=it['first'], Vt=Vt):
                last = None
                st = first
                for (col0, kbk, qts) in blocks:
                    for qt in qts:
                        last = E.matmul(psf[:, ob * 512 + qt * 65: ob * 512 + qt * 65 + 65],
                                        lhsT=PTm[pi][:, col0 + (qt - qts[0]) * 128: col0 + (qt - qts[0] + 1) * 128],
                                        rhs=Vt[:, kbk, :], start=st, stop=False, skip_group_check=True)
                        st = False
                return last
            S.op('pe', pv, r=[('PTm', pi), vk, 'Vones'], w=[('pf', ob)])
            if it['last']:
                o3 = psf[:, ob * 512: ob * 512 + ntq * 65].rearrange("p (t c) -> p t c", c=65)
                S.op('dve', lambda E, o3=o3, ntq=ntq: E.tensor_scalar(out=rden[:, 0:ntq], in0=o3[:, :, 64],
                                                                      scalar1=1e-30, scalar2=None, op0=ALU.max),
                     r=[('pf', ob)], w=['rden'])
                S.op('dve', lambda E, ntq=ntq: E.reciprocal(out=rden[:, 0:ntq], in_=rden[:, 0:ntq]),
                     r=['rden'], w=['rden'])
                hoff = (h % 2) * 64
                S.op('dve', lambda E, o3=o3, ntq=ntq, tq0=tq0, hoff=hoff: E.tensor_tensor(
                    out=ypair[:, tq0:tq0 + ntq, hoff:hoff + 64], in0=o3[:, :, 0:64],
                    in1=rden[:, 0:ntq].unsqueeze(2).to_broadcast([128, ntq, 64]), op=ALU.mult),
                    r=[('pf', ob), 'rden'], w=['ypair'])

        for i, it in enumerate(items):
            emit_S(it, i)
            emit_A(it, i)
            if i >= 1:
                emit_PV(items[i - 1], i - 1)
            if i == 12 and h < 7:
                pending = build_vq(h + 1, lambda: 6)
            if i > 12 and h < 7 and pending:
                pending.pop(0)()
        emit_PV(items[-1], len(items) - 1)
        while h < 7 and pending:
            pending.pop(0)()
        pctr[0] += len(items)
        if h % 2 == 1:
            for t0 in range(0, NT, 8):
                nt = min(8, NT - t0)

                def fty(E, t0=t0, nt=nt):
                    last = None
                    for i in range(nt):
                        last = E.transpose(psb[:, i * 128:(i + 1) * 128], ypair[:, t0 + i, :], ident[:, :])
                    return last
                S.op('pe', fty, r=['ypair', 'ident'], w=['psb'])
                S.op('dve', lambda E, t0=t0, nt=nt, h=h: E.tensor_copy(out=y_mlaT[:, h // 2, t0 * 128:(t0 + nt) * 128],
                                                                       in_=psb[:, 0:nt * 128]), r=['psb'], w=['ymla'])
    if DEBUG:
        S.dma('sp', lambda E: E.dma_start(out=dbg["dbg_ymla"][:, :, :], in_=y_mlaT[:, :, :]), r=['ymla'])
    S.barrier()


    MERG = END - 34816
    mergedT = nc.alloc_sbuf_tensor_at("mergedT", [128, 8, NE], BF16, offset=MERG)
    tp = [T_OFF]
    memT = sbt("memT", [128, 8, 256], BF16, at=tp)
    KmT = sbt("KmT", [128, 4, 256], BF16, at=tp)
    VmT = sbt("VmT", [128, 2, 512], BF16, at=tp)
    OVC = tp[0]
    xbufC = [sbt(f"xbufC{i}", [128, D], F32, at=tp) for i in range(2)]
    xhatC = sbt("xhatC", [128, 2, D], BF16, at=tp)
    assert tp[0] <= MERG
    xctr[0] = 0
    tiles = [xload(xbufC, memx[i * 128:(i + 1) * 128, :]) for i in range(2)]
    rms_T(tiles, C_GMEM, lambda kc: memT[:, kc, :], 'memT', xhatC, 'xhC')
    (wkm,), kkm = wload([wk(w_memkv, 0, 512)])
    for h in range(4):
        b = nb()
        mm(PF(b, 256), [(wkm[:, kc, h * 128:(h + 1) * 128], memT[:, kc, :]) for kc in range(8)],
           w=[('pf', b)], r=[kkm, 'memT'])
        act(KmT[:, h, :], PF(b, 256), AF.Copy, r=[('pf', b)], w=['KmT'])
    (wvm,), kvm = wload([wk(w_memkv, 512, 512)])
    for mt in range(2):
        b = nb()
        mm(PF(b), [(memT[:, kc, mt * 128:(mt + 1) * 128], wvm[:, kc, :]) for kc in range(8)],
           w=[('pf', b)], r=[kvm, 'memT'])
        act(VmT[:, mt, :], PF(b), AF.Copy, r=[('pf', b)], w=['VmT'])
    S.barrier()
    tp = [OVC]
    qm = sbt("qm", [128, 512], BF16, at=tp)
    PTc = sbt("PTc", [128, 2, 512], BF16, at=tp)
    recc = sbt("recc", [128, 512], F32, at=tp)
    gsb = [sbt(f"gsb{i}", [128, 512], BF16, at=tp) for i in range(3)]
    mtc = [sbt(f"mtc{i}", [128, 512], F32, at=tp) for i in range(2)]
    assert tp[0] <= MERG
    (wqm,), kqm = wload([wk(w_in, OFF_DIL, 512)])
    qmB = sbt("qmB", [128, 512], BF16, at=tp)
    PTcB = sbt("PTcB", [128, 2, 512], BF16, at=tp)
    assert tp[0] <= MERG
    qm2, PTc2 = [qm, qmB], [PTc, PTcB]
    itsC = [(h, e0, n) for h in range(4) for (e0, n) in GROUPS]

    def c_s1(i):
        h, e0, n = itsC[i]
        b = nb()
        mm(PF(b, n), [(wqm[:, kc, h * 128:(h + 1) * 128], hT_own[:, kc, e0:e0 + n]) for kc in range(8)],
           w=[('pf', b)], r=[kqm, 'hT'])
        act(qm2[i % 2][:, 0:n], PF(b, n), AF.Copy, r=[('pf', b)], w=[('qm', i % 2)], scale=dscale)

    def c_s2(i):
        h, e0, n = itsC[i]
        for mt in range(2):
            b = nb()
            mm(PF(b, n), [(KmT[:, h, mt * 128:(mt + 1) * 128], qm2[i % 2][:, 0:n])], w=[('pf', b)],
               r=['KmT', ('qm', i % 2)])
            act(PTc2[i % 2][:, mt, 0:n], PF(b, n), AF.Exp, r=[('pf', b)], w=[('PTc', i % 2, mt)])

    def c_s3(i):
        h, e0, n = itsC[i]
        P_ = PTc2[i % 2]
        pk = [('PTc', i % 2, 0), ('PTc', i % 2, 1)]
        bo, bd = nb(), nb()
        mm(PF(bo, n), [(VmT[:, mt, h * 128:(h + 1) * 128], P_[:, mt, 0:n]) for mt in range(2)],
           w=[('pf', bo)], r=['VmT'] + pk)
        mm(PF(bd, n), [(ones[:, :], P_[:, mt, 0:n]) for mt in range(2)], w=[('pf', bd)], r=['ones'] + pk)
        S.op('dve', lambda E: E.reciprocal(out=recc[:, 0:n], in_=PF(bd, n)), r=[('pf', bd)], w=['recc'])
        S.op('dve', lambda E: E.tensor_tensor(out=y_memT[:, h, e0:e0 + n], in0=PF(bo, n), in1=recc[:, 0:n],
                                              op=ALU.mult), r=[('pf', bo), 'recc'], w=['ymem'])
    NC_ = len(itsC)
    c_s1(0)
    c_s1(1)
    c_s2(0)
    for i in range(NC_):
        if i + 2 < NC_:
            c_s1(i + 2)
        if i + 1 < NC_:
            c_s2(i + 1)
        c_s3(i)
    if DEBUG:
        S.dma('sp', lambda E: E.dma_start(out=dbg["dbg_ymem"][:, :, :], in_=y_memT[:, :, :]), r=['ymem'])
    yT = [y_mlaT, y_dilT, y_memT]
    ykeys = ['ymla', 'ydil', 'ymem']
    for oc in range(8):
        wg, kg = wload([wk(w_in, OFF_MEMQ + br * 1024 + oc * 128, 128) for br in range(3)])
        wb_, kb_ = wload([w_br[br][:, oc * 128:(oc + 1) * 128].rearrange("(k p) n -> p k n", p=128) for br in range(3)])
        for (e0, n) in GROUPS:
            for br in range(3):
                bg = nb()
                mm(PF(bg, n), [(wg[br][:, kc, :], hT_own[:, kc, e0:e0 + n]) for kc in range(8)],
                   w=[('pf', bg)], r=[kg, 'hT'])
                act(gsb[br][:, 0:n], PF(bg, n), AF.Sigmoid, r=[('pf', bg), 'cv'], w=[('gsb', br)],
                    bias=cvc(C_BG + br * 8 + oc))
                bb = nb()
                mm(PF(bb, n), [(wb_[br][:, kc, :], yT[br][:, kc, e0:e0 + n]) for kc in range(4)],
                   w=[('pf', bb)], r=[kb_, ykeys[br]])
                di = 0 if br == 0 else 1
                S.op('dve', lambda E, bb=bb, n=n, br=br, di=di: E.tensor_tensor(out=mtc[di][:, 0:n], in0=PF(bb, n),
                                                                               in1=gsb[br][:, 0:n], op=ALU.mult),
                     r=[('pf', bb), ('gsb', br)], w=[('mtc', di)])
                if br == 1:
                    S.op('dve', lambda E, n=n: E.tensor_tensor(out=mtc[0][:, 0:n], in0=mtc[0][:, 0:n],
                                                               in1=mtc[1][:, 0:n], op=ALU.add),
                         r=[('mtc', 0), ('mtc', 1)], w=[('mtc', 0)])
                if br == 2:
                    S.op('dve', lambda E, n=n, oc=oc, e0=e0: E.tensor_tensor(out=mergedT[:, oc, e0:e0 + n],
                                                                            in0=mtc[0][:, 0:n], in1=mtc[1][:, 0:n],
                                                                            op=ALU.add),
                         r=[('mtc', 0), ('mtc', 1)], w=['merged'])
    if DEBUG:
        S.dma('sp', lambda E: E.dma_start(out=dbg["dbg_merged"][:, :, :], in_=mergedT[:, :, :]), r=['merged'])
    S.barrier()

    x1 = nc.alloc_sbuf_tensor_at("x1", [128, 16, D], F32, offset=Y_OFF)
    tp = [Y_OFF + 65536]
    xbufD = [sbt("xbufD0", [128, D], F32, at=tp)]
    tmpD = sbt("tmpD", [128, D], F32, at=tp)
    x1pre = sbt("x1pre", [128, D], F32, at=tp)
    xhatD = sbt("xhatD", [128, 1, D], BF16, at=tp)
    assert tp[0] <= MERG
    h2T = hT_own
    (wo0,), ko0 = wload([wk(w_o, 0, 512)])
    (wo1,), ko1 = wload([wk(w_o, 512, 512)])
    xctr[0] = 0
    dstD = {}

    def d_s1(t):
        p = t % 3
        for half, wo_, ko in ((0, wo0, ko0), (1, wo1, ko1)):
            mm(psf[:, p * 1024 + half * 512: p * 1024 + (half + 1) * 512],
               [(mergedT[:, kc, t * 128:(t + 1) * 128], wo_[:, kc, :]) for kc in range(8)],
               w=[('pf', 2 * p + half)], r=[ko, 'merged'])
        pk = [('pf', 2 * p), ('pf', 2 * p + 1)]
        pfull = psf[:, p * 1024:(p + 1) * 1024]
        S.op('act', lambda E: E.activation(out=junk[:, :], in_=pfull, func=AF.Square,
                                           accum_out=small[:, 48:49]), r=pk, w=['junk', ('sm', 48)])
        rstd_from_ss(small[:, 48:49], small[:, 50:51], small[:, 49:50], 1024.0, [('sm', 48)], [('sm', 50)], [('sm', 49)])
        xa, xk = xload(xbufD, xc[6016 + t * 128: 6016 + (t + 1) * 128, :])
        S.op('dve', lambda E: E.scalar_tensor_tensor(out=tmpD[:, :], in0=pfull, scalar=small[:, 50:51],
                                                     in1=gbc[:, 0, :], op0=ALU.mult, op1=ALU.mult),
             r=pk + [('sm', 50), 'gbc'], w=['tmpD'])
        dst = x1pre[:, :] if t == 0 else x1[:, t - 1, :]
        dk = ('x1', t)
        S.op('dve', lambda E: E.tensor_tensor(out=dst, in0=tmpD[:, :], in1=xa, op=ALU.add),
             r=['tmpD', xk], w=[dk])
        dstD[t] = (dst, dk)

    def d_s2(t):
        rms_T([dstD[t]], C_GFFN, lambda kc: h2T[:, kc, t * 128:(t + 1) * 128], 'h2T', xhatD, 'xhD')

    d_s1(0)
    for t in range(NT):
        if t + 1 < NT:
            d_s1(t + 1)
        d_s2(t)
    S.barrier()

    tp = [Y_OFF + 65536]
    gbuf = sbt("gbuf", [128, 22, 512], BF16, at=tp)
    ubuf = [sbt(f"ubuf{i}", [128, 528], F32, at=tp) for i in range(4)]
    zt = [sbt(f"zt{i}", [128, 512], F32, at=tp) for i in range(4)]
    sgs = [sbt(f"sg{i}", [128, 512], F32, at=tp) for i in range(2)]
    carry = sbt("carry", [128, 44, 2], F32, at=tp)
    fo0 = sbt("fo0", [128, 4, 512], F32, at=tp)
    tmpE = zt[0]
    eb = [0]

    def nbe():
        eb[0] = (eb[0] + 1) % 3
        return eb[0]
    for j in range(4):
        e0 = 128 + 512 * j
        for fc0 in range(0, 22, 4):
            nf = min(4, 22 - fc0)
            wupk = [('wupb', c) for c in range(4)]
            wgt, kgt = wload_b(wk(wupb, fc0 * 128, nf * 128), wupk)
            wvt, kvt = wload_b(wk(wupb, DFF + fc0 * 128, nf * 128), wupk)
            for fi in range(nf):
                fc = fc0 + fi
                if j > 0:
                    for fcn in ([0, 1] if fc == 0 else ([fc + 1] if fc + 1 < 22 else [])):
                        for half_ in range(2):
                            S.op('dve', lambda E, fcn=fcn, half_=half_: E.tensor_copy(
                                out=ubuf[half_ * 2 + fcn % 2][:, 0:2], in_=carry[:, half_ * 22 + fcn, :]),
                                r=[('carry', half_ * 22 + fcn)], w=[('ubc', half_ * 2 + fcn % 2)])
                for half, wt, kw_ in ((0, wgt, kgt), (1, wvt, kvt)):
                    ci = half * 22 + fc
                    b = nbe()
                    mm(PF(b), [(wt[:, kc, fi * 128:(fi + 1) * 128], h2T[:, kc, e0:e0 + 512]) for kc in range(8)],
                       w=[('pf', b)], r=[kw_, 'h2T'])
                    ub = ubuf[half * 2 + fc % 2]
                    uk = ('ub', half * 2 + fc % 2)
                    uck = ('ubc', half * 2 + fc % 2)
                    ztb = zt[half * 2 + fc % 2]
                    if j == 0:
                        pb_ = 3 + (ci % 4)
                        mm(psf[:, pb_ * 512: pb_ * 512 + 2],
                           [(wt[:, kc, fi * 128:(fi + 1) * 128], h2T[:, kc, 126:128]) for kc in range(8)],
                           w=[('pf', pb_)], r=[kw_, 'h2T'])
                        S.op('dve', lambda E, ub=ub, pb_=pb_: E.tensor_scalar(out=ub[:, 0:2], in0=psf[:, pb_ * 512: pb_ * 512 + 2],
                                                                             scalar1=cvc(C_UFLAG), scalar2=None, op0=ALU.mult),
                             r=[('pf', pb_), 'cv'], w=[uck])
                    act(ub[:, 2:514], PF(b), AF.Copy, r=[('pf', b)], w=[uk])
                    S.op('dve', lambda E, ub=ub, ci=ci: E.tensor_copy(out=carry[:, ci, :], in_=ub[:, 512:514]),
                         r=[uk], w=[('carry', ci)])
                    zk = ('zt', half * 2 + fc % 2)
                    act(ztb[:, :], ub[:, 0:512], AF.Identity, r=[uk, uck, 'cv'], w=[zk], scale=cvc(C_CW + ci),
                        bias=cvc(C_CB + ci))
                    S.op('dve', lambda E, ub=ub, ci=ci, ztb=ztb: E.scalar_tensor_tensor(
                        out=ztb[:, :], in0=ub[:, 1:513], scalar=cvc(C_CW + 44 + ci), in1=ztb[:, :],
                        op0=ALU.mult, op1=ALU.add), r=[uk, uck, zk, 'cv'], w=[zk])
                    S.op('dve', lambda E, ub=ub, ci=ci, ztb=ztb: E.scalar_tensor_tensor(
                        out=ztb[:, :], in0=ub[:, 2:514], scalar=cvc(C_CW + 88 + ci), in1=ztb[:, :],
                        op0=ALU.mult, op1=ALU.add), r=[uk, zk, 'cv'], w=[zk])
                sg = sgs[fc % 2]
                sk = ('sg', fc % 2)
                zg, zv = zt[fc % 2], zt[2 + fc % 2]
                act(sg[:, :], zg[:, :], AF.Silu, r=[('zt', fc % 2)], w=[sk])
                S.op('pool', lambda E, fc=fc, sg=sg, zv=zv: E.tensor_tensor(out=gbuf[:, fc, :], in0=sg[:, :], in1=zv[:, :],
                                                                          op=ALU.mult),
                     r=[sk, ('zt', 2 + fc % 2)], w=['gbuf'])
        for half in range(2):
            for (f0, nfk) in ((0, 8), (8, 8), (16, 6)):
                wd, kd = wload_b(wdnb[f0 * 128:(f0 + nfk) * 128, half * 512:(half + 1) * 512]
                                 .rearrange("(k p) n -> p k n", p=128), [('wdnb', 0), ('wdnb', 1)])
                for tt in range(4):
                    def fd(E, wd=wd, f0=f0, nfk=nfk, tt=tt):
                        last = None
                        for k in range(nfk):
                            last = E.matmul(PF(3 + tt), lhsT=gbuf[:, f0 + k, tt * 128:(tt + 1) * 128], rhs=wd[:, k, :],
                                            start=(f0 == 0 and k == 0), stop=(f0 == 16 and k == nfk - 1))
                        return last
                    S.op('pe', fd, r=[kd, 'gbuf'], w=[('pf', 3 + tt)])
            for tt in range(4):
                col = 40 + half * 4 + tt
                S.op('act', lambda E, tt=tt, col=col: E.activation(out=junk[:, 0:512], in_=PF(3 + tt), func=AF.Square,
                                                                   accum_out=small[:, col:col + 1]),
                     r=[('pf', 3 + tt)], w=['junk', ('sm', col)])
                if half == 0:
                    act(fo0[:, tt, :], PF(3 + tt), AF.Copy, r=[('pf', 3 + tt)], w=[('fo0', tt)])
            if half == 1:
                S.op('dve', lambda E: E.tensor_tensor(out=small[:, 48:52], in0=small[:, 40:44], in1=small[:, 44:48],
                                                      op=ALU.add), r=[('sm', c) for c in range(40, 48)], w=[('sm', 48)])
                rstd_from_ss(small[:, 48:52], small[:, 56:60], small[:, 52:56], 1024.0, [('sm', 48)], [('sm', 56)],
                             [('sm', 52)])
                for tt in range(4):
                    xt = x1[:, 4 * j + tt, :]
                    xk = ('x1', 4 * j + tt + 1)
                    rs = small[:, 56 + tt:57 + tt]
                    S.op('dve', lambda E, tt=tt, rs=rs: E.scalar_tensor_tensor(
                        out=fo0[:, tt, :], in0=fo0[:, tt, :], scalar=rs, in1=gbc[:, 1, 0:512], op0=ALU.mult,
                        op1=ALU.mult), r=[('fo0', tt), ('sm', 56), 'gbc'], w=[('fo0', tt)])
                    S.op('dve', lambda E, tt=tt, xt=xt: E.tensor_tensor(out=xt[:, 0:512], in0=xt[:, 0:512],
                                                                        in1=fo0[:, tt, :], op=ALU.add),
                         r=[('fo0', tt), xk], w=[xk])
                    S.op('dve', lambda E, tt=tt, rs=rs: E.scalar_tensor_tensor(
                        out=tmpE[:, :], in0=PF(3 + tt), scalar=rs, in1=gbc[:, 1, 512:1024], op0=ALU.mult,
                        op1=ALU.mult), r=[('pf', 3 + tt), ('sm', 56), 'gbc'], w=[('zt', 0)])
                    S.op('dve', lambda E, tt=tt, xt=xt: E.tensor_tensor(out=xt[:, 512:1024], in0=xt[:, 512:1024],
                                                                        in1=tmpE[:, :], op=ALU.add),
                         r=[('zt', 0), xk], w=[xk])
                    row = (4 * j + tt) * 128
                    S.dma('sp', lambda E, row=row, xt=xt: E.dma_start(out=outd[row:row + 128, :], in_=xt), r=[xk])

    return nc, es, S


def _finish(nc, es, S):
    S.barrier()
    with nc.Block() as block:
        @block.tensor
        def _(E):
            for f in S.prog['pe']:
                f(E)

        @block.scalar
        def _(E):
            for f in S.prog['act']:
                f(E)

        @block.vector
        def _(E):
            for f in S.prog['dve']:
                f(E)

        @block.gpsimd
        def _(E):
            for f in S.prog['pool']:
                f(E)

        @block.sync
        def _(E):
            for f in S.prog['sp']:
                f(E)
    es.close()
    return nc


def host_inputs(inputs):
    import ml_dtypes
    x = np.asarray(inputs["x"], np.float32)
    mem = np.asarray(inputs["mem"], np.float32)
    pos = np.asarray(inputs["positions"], np.int32)

    def P(k):
        return np.asarray(inputs[k], np.float32)[0]
    shared = {
        "gbc": np.stack([P("g_post_mix"), P("g_post_ffn")]).astype(np.float32),
        "tri": np.triu(np.ones((128, 128), np.float32)),
        "ident": np.eye(128, dtype=np.float32),
        "w_in": P("w_in"), "w_uq": P("w_uq"), "w_ukv": P("w_ukv"), "w_mem_kv": P("w_mem_kv"),
        "w_br_mla": P("w_br_mla"), "w_br_dil": P("w_br_dil"), "w_br_mem": P("w_br_mem"), "w_o": P("w_o"),
        "w_ffn_up": P("w_ffn_up"), "w_ffn_down": P("w_ffn_down"),
    }
    db = np.zeros((12, 128, 256), np.float32)
    slopes = np.exp2(-8.0 * np.arange(1, 13, dtype=np.float32) / 12).reshape(4, 3).T
    k = np.arange(128)[:, None]
    i = np.arange(128)[None, :]
    for g in range(3):
        for hd in range(4):
            a = slopes[g, hd] * DIL_D[g]
            dist = i - k
            db[g * 4 + hd, :, 0:128] = np.where(dist >= 0, -a * dist, NEG)
            dist2 = 128 + i - k
            db[g * 4 + hd, :, 128:256] = np.where(k >= i, -a * dist2, NEG)
    shared["dbias"] = db

    def cols(v, n):
        return v.reshape(n, 128).T
    cv0 = np.zeros((128, NCV), np.float32)
    cv0[:, C_GPRE:C_GPRE + 8] = cols(P("g_pre_mix"), 8)
    cv0[:, C_GMEM:C_GMEM + 8] = cols(P("g_mem"), 8)
    cv0[:, C_GFFN:C_GFFN + 8] = cols(P("g_pre_ffn"), 8)
    cv0[:, C_QN:C_QN + 3] = cols(P("mla_q_norm"), 3)
    cv0[:, C_KVN:C_KVN + 2] = cols(P("mla_kv_norm"), 2)
    cv0[:, C_BG:C_BG + 24] = cols(P("b_gate"), 24)
    cw = P("conv_w")
    for j in range(3):
        cv0[:, C_CW + j * 44: C_CW + (j + 1) * 44] = cols(cw[j], 44)
    cv0[:, C_CB:C_CB + 44] = cols(P("conv_b"), 44)
    cv0[:, C_INVF:C_INVF + 16] = (np.float32(10000.0) ** (-np.arange(16, dtype=np.float32) / np.float32(16)))[None, :]
    in_maps = []
    for c in range(8):
        b, q = c // 4, c % 4
        xcx = np.zeros((SEQ, D), np.float32)
        pc = np.zeros((SEQ,), np.int32)
        lo = 6144 - CH * q
        xcx[lo:] = x[b, 0: CH * (q + 1)]
        pc[lo:] = pos[b, 0: CH * (q + 1)]
        cvq = cv0.copy()
        for cidx in range(3):
            cvq[:, C_VB + cidx] = 0.0 if (cidx + q) >= 3 else NEG
        ridx = 0
        for g in range(3):
            d = DIL_D[g]
            for r_ in range(d):
                kk = np.arange(128)
                tau0 = 3968 + (2048 - 128 * d) + kk * d + r_
                cvq[:, C_DVB0 + ridx] = np.where(tau0 >= lo, 0.0, NEG)
                tau1 = 3968 + 2048 + kk * d + r_
                cvq[:, C_DVB1 + ridx] = np.where(tau1 >= lo, 0.0, NEG)
                ridx += 1
        cvq[:, C_UFLAG] = 1.0 if q > 0 else 0.0
        m = dict(shared)
        m["xc"] = xcx
        m["posc"] = np.ascontiguousarray(pc.reshape(64, 128).T)
        m["memx"] = np.ascontiguousarray(mem[b])
        m["cv"] = cvq
        in_maps.append(m)
    return in_maps


def kernel(**inputs):
    in_maps = host_inputs(inputs)
    nc, es, S = build_program()
    nc = _finish(nc, es, S)
    res = run_bass_kernel_spmd(nc, in_maps, core_ids=list(range(8)))
    out = np.zeros((2, SEQ, D), np.float32)
    for c in range(8):
        b, q = c // 4, c % 4
        out[b, q * CH:(q + 1) * CH] = np.asarray(res.results[c]["out"], np.float32)
    return out
```

```python
from contextlib import ExitStack
import numpy as np
import concourse.bass as bass
import concourse.mybir as mybir
from concourse.bass_utils import run_bass_kernel_spmd

F32, BF16, I32 = mybir.dt.float32, mybir.dt.bfloat16, mybir.dt.int32
ALU = mybir.AluOpType
AF = mybir.ActivationFunctionType
NEG = -30000.0
D = 1024
SEQ = 8192
CH = 2048
NE = 2176
NT = 17
DFF = 2816
DIN = 8864
OFF_Q, OFF_KV, OFF_KR, OFF_DIL, OFF_MEMQ = 384, 640, 672, 672 + 4608, 672 + 4608 + 512
DIL_D = (1, 4, 16)
EPS = 1e-6
TWO_PI = 6.283185307179586
(C_GPRE, C_GMEM, C_GFFN, C_QN, C_KVN, C_BG, C_CW, C_CB, C_VB, C_ZERO, C_INVF, C_DVB0, C_DVB1, C_UFLAG,
 NCV) = (0, 8, 16, 24, 27, 29, 53, 185, 229, 232, 233, 249, 270, 291, 292)
GROUPS = [(0, 128), (128, 512), (640, 512), (1152, 512), (1664, 512)]
DEBUG = False


class Sched:
    CE = ('pe', 'act', 'dve', 'pool')

    def __init__(s, nc, sems, dsems):
        s.eng = {'pe': nc.tensor, 'act': nc.scalar, 'dve': nc.vector, 'pool': nc.gpsimd, 'sp': nc.sync}
        s.prog = {e: [] for e in s.eng}
        s.sem = sems
        s.dsems = dsems
        s.tick = {e: 0 for e in s.CE}
        s.seen = {e: {} for e in s.eng}
        s.bufs = {}
        s.dcount = {q: 0 for q in dsems}

    def _semh(s, k):
        return s.sem[k] if isinstance(k, str) else s.dsems[k[1]][k[2]]

    def _deps(s, eng, r, w):
        need = {}

        def add(k, v):
            if need.get(k, 0) < v:
                need[k] = v
        for key in r:
            b = s.bufs.get(key)
            if b and b['w']:
                add(*b['w'])
        for key in w:
            b = s.bufs.get(key)
            if b:
                if b['w'] and b['w'][0] != eng:
                    add(*b['w'])
                for k, v in b['r'].items():
                    if k != eng:
                        add(k, v)
        return need

    def _waits(s, q, need):
        for k, v in need.items():
            if s.seen[q].get(k, 0) >= v:
                continue
            s.seen[q][k] = v
            sem = s._semh(k)
            s.prog[q].append(lambda E, sem=sem, v=v: E.wait_ge(sem, v))

    def _record(s, ev, r, w):
        for key in r:
            b = s.bufs.setdefault(key, {'w': None, 'r': {}})
            if b['r'].get(ev[0], 0) < ev[1]:
                b['r'][ev[0]] = ev[1]
        for key in w:
            s.bufs[key] = {'w': ev, 'r': {}}

    def op(s, eng, fn, r=(), w=()):
        s._waits(eng, s._deps(eng, r, w))
        s.tick[eng] += 1
        sem = s.sem[eng]
        s.prog[eng].append(lambda E, fn=fn, sem=sem: fn(E).then_inc(sem, 1))
        s._record((eng, s.tick[eng]), r, w)

    def dma(s, q, fn, r=(), w=()):
        i = s.dcount[q]
        s.dcount[q] += 1
        ns = len(s.dsems[q])
        j, val = i % ns, 16 * (i // ns + 1)
        need = s._deps(None, r, w)
        if i >= ns:
            k = ('d', q, j)
            need[k] = max(need.get(k, 0), val - 16)
        s._waits(q, need)
        sem = s.dsems[q][j]
        s.prog[q].append(lambda E, fn=fn, sem=sem: fn(E).then_inc(sem, 16))
        s._record((('d', q, j), val), r, w)

    def barrier(s):
        for e in s.eng:
            need = {}
            for c in s.CE:
                if s.tick[c] > 0 and c != e:
                    need[c] = s.tick[c]
            for q in s.dsems:
                n, ns = s.dcount[q], len(s.dsems[q])
                for j in range(min(n, ns)):
                    need[('d', q, j)] = 16 * ((n - 1 - j) // ns + 1)
            s._waits(e, need)


def dil_geom(d):
    nq_tot = NE // d
    qb = []
    a = 0
    while a < nq_tot:
        qb.append((a, min(128, nq_tot - a)))
        a += 128
    kt = [(0, 128)] + [(128 + a0, n) for (a0, n) in qb]
    return qb, kt


def build_program():
    nc = bass.Bass("TRN2", target_bir_lowering=False, dynamic_dma_scratch_size=4096)

    def din(name, shape, dt=F32):
        return nc.dram_tensor(name, list(shape), dt, kind="ExternalInput").ap()
    xc = din("xc", [SEQ, D])
    posc = din("posc", [128, 64], I32)
    memx = din("memx", [256, D])
    cvd = din("cv", [128, NCV])
    gbcd = din("gbc", [2, D])
    dbd = din("dbias", [12, 128, 256])
    trid = din("tri", [128, 128])
    identd = din("ident", [128, 128])
    w_in = din("w_in", [D, DIN])
    w_uq = din("w_uq", [384, 768])
    w_ukv = din("w_ukv", [256, 1024])
    w_memkv = din("w_mem_kv", [D, 1024])
    w_br = [din("w_br_mla", [512, D]), din("w_br_dil", [512, D]), din("w_br_mem", [512, D])]
    w_o = din("w_o", [D, D])
    w_up = din("w_ffn_up", [D, 2 * DFF])
    w_dn = din("w_ffn_down", [DFF, D])
    outd = nc.dram_tensor("out", [CH, D], F32, kind="ExternalOutput").ap()
    wupb = nc.dram_tensor("wupb_scratch", [D, 2 * DFF], BF16).ap()
    wdnb = nc.dram_tensor("wdnb_scratch", [DFF, D], BF16).ap()
    dbg = {}
    if DEBUG:
        for nm in ("dbg_ydil", "dbg_ymla", "dbg_ymem", "dbg_merged"):
            dbg[nm] = nc.dram_tensor(nm, [128, 8 if nm == "dbg_merged" else 4, NE], BF16, kind="ExternalOutput").ap()
        dbg["dbg_h"] = nc.dram_tensor("dbg_h", [128, 8, NE], BF16, kind="ExternalOutput").ap()

    END = 212992
    cur = [4608]

    def sbt(name, shape, dt, at=None):
        esz = 4 if dt in (F32, I32) else 2
        n = 1
        for v in shape[1:]:
            n *= v
        nbytes = (n * esz + 63) // 64 * 64
        if at is None:
            off = cur[0]
            cur[0] += nbytes
        else:
            off = at[0]
            at[0] += nbytes
        assert off + nbytes <= END, (name, off, nbytes)
        return nc.alloc_sbuf_tensor_at(name, list(shape), dt, offset=off)

    ident = sbt("ident", [128, 128], BF16)
    ones = sbt("ones", [128, 128], BF16)
    tri = sbt("tri_sb", [128, 128], BF16)
    cv = sbt("cv_sb", [128, NCV], F32)
    gbc = sbt("gbc_sb", [128, 2, D], F32)
    cosT = sbt("cosT", [128, 64, 16], F32)
    sinT = sbt("sinT", [128, 64, 16], F32)
    cosQ = sbt("cosQ", [128, NT, 16], F32)
    sinQ = sbt("sinQ", [128, NT, 16], F32)
    small = sbt("small", [128, 64], F32)
    junk = sbt("junk", [128, D], BF16)
    hT_own = sbt("hT_own", [128, 8, NE], BF16)
    W_OFF = cur[0]
    wsl = [sbt(f"wslot{i}", [128, 4096], BF16) for i in range(4)]
    Y_OFF = cur[0]
    y_dilT = sbt("y_dilT", [128, 4, NE], BF16)
    y_mlaT = sbt("y_mlaT", [128, 4, NE], BF16)
    y_memT = sbt("y_memT", [128, 4, NE], BF16)
    T_OFF = cur[0]
    YB = 17408

    psf = nc.alloc_psum_tensor("psf", [128, 3584], F32)
    psb = nc.alloc_psum_tensor("psb", [128, 1024], BF16)

    def PF(b, n=512, off=0):
        return psf[:, b * 512 + off: b * 512 + off + n]

    def cvc(c, p=128):
        return cv[0:p, c:c + 1]

    es = ExitStack()
    sems = {e: es.enter_context(nc.semaphore(f"s_{e}")) for e in Sched.CE}
    dsems = {q: [es.enter_context(nc.semaphore(f"d_{q}{i}")) for i in range(8)] for q in ('sp', 'pool')}
    S = Sched(nc, sems, dsems)
    name_ctr = [0]

    wctr = [0]

    def wload(parts):
        i = wctr[0] % 4
        wctr[0] += 1
        views = []
        off = 0
        for src in parts:
            k, n = src.shape[1], src.shape[2]
            v = wsl[i][:, off:off + k * n].rearrange("p (k n) -> p k n", n=n)
            off += k * n
            assert off <= 4096
            S.dma('pool', lambda E, v=v, src=src: E.dma_start(out=v, in_=src), w=[('w', i)])
            views.append(v)
        return views, ('w', i)

    def wload_b(src, rkeys):
        i = wctr[0] % 4
        wctr[0] += 1
        k, n = src.shape[1], src.shape[2]
        v = wsl[i][:, 0:k * n].rearrange("p (k n) -> p k n", n=n)
        S.dma('sp', lambda E: E.dma_start(out=v, in_=src), r=rkeys, w=[('w', i)])
        return v, ('w', i)

    def wk(w, c0, n, kp=None):
        return w[:, c0:c0 + n].rearrange("(k p) n -> p k n", p=128)

    def mm(out, pairs, w, r):
        def f(E):
            last = None
            for i, (l, rr) in enumerate(pairs):
                last = E.matmul(out, lhsT=l, rhs=rr, start=(i == 0), stop=(i == len(pairs) - 1))
            return last
        S.op('pe', f, r=r, w=w)

    def act(out, in_, func, r, w, **kw):
        S.op('act', lambda E: E.activation(out=out, in_=in_, func=func, **kw), r=r, w=w)

    def rstd_from_ss(ss_ap, out_ap, tmp_ap, n_feat, rkeys, wkeys, tkeys):
        act(tmp_ap, ss_ap, AF.Ln, r=rkeys, w=tkeys, scale=1.0 / n_feat, bias=EPS)
        act(out_ap, tmp_ap, AF.Exp, r=tkeys, w=wkeys, scale=-0.5)

    def rms_T(tiles, gcol, dst, dkey, xhat, xkey):
        n = len(tiles)
        for i, (ap, key) in enumerate(tiles):
            S.op('act', lambda E, ap=ap, i=i: E.activation(out=junk[:, :], in_=ap, func=AF.Square,
                                                           accum_out=small[:, i:i + 1]),
                 r=[key], w=['junk', ('sm', i)])
        rstd_from_ss(small[:, 0:n], small[:, 16:16 + n], small[:, 8:8 + n], 1024.0,
                     [('sm', i) for i in range(n)], ['smr'], ['sml'])
        for i, (ap, key) in enumerate(tiles):
            S.op('dve', lambda E, ap=ap, i=i: E.tensor_scalar(out=xhat[:, i, :], in0=ap,
                                                              scalar1=small[:, 16 + i:17 + i], scalar2=None,
                                                              op0=ALU.mult),
                 r=[key, 'smr'], w=[(xkey, i)])
        for kc2 in range(4):
            def f(E, kc2=kc2):
                last = None
                for j in range(2):
                    kc = kc2 * 2 + j
                    for i in range(n):
                        last = E.transpose(psb[:, j * 512 + i * 128: j * 512 + (i + 1) * 128],
                                           xhat[:, i, kc * 128:(kc + 1) * 128], ident[:, :])
                return last
            S.op('pe', f, r=[(xkey, i) for i in range(n)] + ['ident'], w=['psb'])
            for j in range(2):
                kc = kc2 * 2 + j
                if j == 0:
                    act(dst(kc), psb[:, j * 512: j * 512 + n * 128], AF.Copy, r=['psb'], w=[dkey],
                        scale=cvc(gcol + kc))
                else:
                    S.op('dve', lambda E, kc=kc, j=j: E.tensor_scalar(out=dst(kc), in0=psb[:, j * 512: j * 512 + n * 128],
                                                                      scalar1=cvc(gcol + kc), scalar2=None, op0=ALU.mult),
                         r=['psb', 'cv'], w=[dkey])

    def rms_pipe(calls, gcol, xbufs, xhats, xname):
        def stageA(c):
            call = calls[c]
            par = c % 2
            base = par * 24
            n = len(call['srcs'])
            tiles = [xload(xbufs, src) for src in call['srcs']]
            for i, (ap, key) in enumerate(tiles):
                S.op('act', lambda E, ap=ap, i=i: E.activation(out=junk[:, :], in_=ap, func=AF.Square,
                                                               accum_out=small[:, base + i:base + i + 1]),
                     r=[key], w=['junk', ('sm', base + i)])
            rstd_from_ss(small[:, base:base + n], small[:, base + 16:base + 16 + n], small[:, base + 8:base + 8 + n],
                         1024.0, [('sm', base + i) for i in range(n)], [('smr', par)], [('sml', par)])
            for i, (ap, key) in enumerate(tiles):
                S.op('dve', lambda E, ap=ap, i=i: E.tensor_scalar(out=xhats[par][:, i, :], in0=ap,
                                                                  scalar1=small[:, base + 16 + i:base + 17 + i],
                                                                  scalar2=None, op0=ALU.mult),
                     r=[key, ('smr', par)], w=[(xname, par, i)])

        def stageB(c):
            call = calls[c]
            par = c % 2
            n = len(call['srcs'])
            dst, dkey = call['dst'], call['dkey']
            xh = xhats[par]
            for kc2 in range(4):
                def f(E, kc2=kc2):
                    last = None
                    for j in range(2):
                        kc = kc2 * 2 + j
                        for i in range(n):
                            last = E.transpose(psb[:, j * 512 + i * 128: j * 512 + (i + 1) * 128],
                                               xh[:, i, kc * 128:(kc + 1) * 128], ident[:, :])
                    return last
                S.op('pe', f, r=[(xname, par, i) for i in range(n)] + ['ident'], w=['psb'])
                for j in range(2):
                    kc = kc2 * 2 + j
                    if False:
                        act(dst(kc), psb[:, j * 512: j * 512 + n * 128], AF.Copy, r=['psb'], w=[dkey],
                            scale=cvc(gcol + kc))
                    else:
                        S.op('dve', lambda E, kc=kc, j=j: E.tensor_scalar(out=dst(kc), in0=psb[:, j * 512: j * 512 + n * 128],
                                                                          scalar1=cvc(gcol + kc), scalar2=None,
                                                                          op0=ALU.mult), r=['psb', 'cv'], w=[dkey])
            if call.get('after'):
                call['after']()

        stageA(0)
        for c in range(len(calls)):
            if c + 1 < len(calls):
                stageA(c + 1)
            stageB(c)

    xctr = [0]

    def xload(xbufs, src):
        i = xctr[0] % len(xbufs)
        xctr[0] += 1
        dst_ = xbufs[i][:, :]
        S.dma('sp', lambda E: E.dma_start(out=dst_, in_=src), w=[('xb', i)])
        return dst_, ('xb', i)

    tp = [T_OFF]
    tmp32 = sbt("setup_tmp", [128, 128], F32, at=tp)
    S.dma('sp', lambda E: E.dma_start(out=cv[:, :], in_=cvd[:, :]), w=['cv'])
    S.dma('sp', lambda E: E.dma_start(out=gbc[:, :, :], in_=gbcd.partition_broadcast(128)), w=['gbc'])
    S.dma('pool', lambda E: E.dma_start(out=ident[:, :], in_=identd[:, :]), w=['ident'])
    S.dma('pool', lambda E: E.dma_start(out=tri[:, :], in_=trid[:, :]), w=['tri'])
    S.op('dve', lambda E: E.memset(ones[:, :], 1.0), w=['ones'])
    posi = sbt("posi", [128, 64], I32, at=tp)
    posf = sbt("posf", [128, 64], F32, at=tp)
    ang = sbt("ang", [128, 64, 16], F32, at=tp)
    kf = sbt("kf", [128, 64, 16], F32, at=tp)
    ki = sbt("ki", [128, 64, 16], I32, at=tp)
    S.dma('sp', lambda E: E.dma_start(out=posi[:, :], in_=posc[:, :]), w=['posi'])
    S.op('dve', lambda E: E.tensor_copy(out=posf[:, :], in_=posi[:, :]), r=['posi'], w=['posf'])
    S.op('dve', lambda E: E.tensor_tensor(out=ang[:, :, :], in0=posf[:, :].unsqueeze(2).to_broadcast([128, 64, 16]),
                                          in1=cv[:, C_INVF:C_INVF + 16].unsqueeze(1).to_broadcast([128, 64, 16]),
                                          op=ALU.mult), r=['posf', 'cv'], w=['ang'])
    for tab, shift in ((sinT, 0.0), (cosT, np.pi / 2)):
        S.op('dve', lambda E, shift=shift: E.tensor_scalar(out=kf[:, :, :], in0=ang[:, :, :], scalar1=float(shift),
                                                           scalar2=float(1.0 / TWO_PI), op0=ALU.add, op1=ALU.mult),
             r=['ang'], w=['kf'])
        S.op('dve', lambda E: E.tensor_copy(out=ki[:, :, :], in_=kf[:, :, :]), r=['kf'], w=['ki'])
        S.op('dve', lambda E: E.tensor_copy(out=kf[:, :, :], in_=ki[:, :, :]), r=['ki'], w=['kf'])
        S.op('dve', lambda E: E.scalar_tensor_tensor(out=kf[:, :, :], in0=kf[:, :, :], scalar=float(-TWO_PI),
                                                     in1=ang[:, :, :], op0=ALU.mult, op1=ALU.add),
             r=['kf', 'ang'], w=['kf'])
        S.op('dve', lambda E, shift=shift: E.tensor_scalar(out=kf[:, :, :], in0=kf[:, :, :], scalar1=float(shift),
                                                           scalar2=3.1415925, op0=ALU.add, op1=ALU.min),
             r=['kf'], w=['kf'])
        S.op('dve', lambda E: E.tensor_scalar(out=kf[:, :, :], in0=kf[:, :, :], scalar1=-3.1415925, scalar2=None,
                                              op0=ALU.max), r=['kf'], w=['kf'])
        act(tab[:, :, :], kf[:, :, :], AF.Sin, r=['kf'], w=['rope'])
    qs = 96.0 ** -0.5
    S.op('dve', lambda E: E.tensor_scalar(out=cosQ[:, :, :], in0=cosT[:, 47:64, :], scalar1=qs, scalar2=None,
                                          op0=ALU.mult), r=['rope'], w=['ropeq'])
    S.op('dve', lambda E: E.tensor_scalar(out=sinQ[:, :, :], in0=sinT[:, 47:64, :], scalar1=qs, scalar2=None,
                                          op0=ALU.mult), r=['rope'], w=['ropeq'])
    S.barrier()

    tp = [Y_OFF + YB]
    xbufA = [sbt(f"xbufA{i}", [128, D], F32, at=tp) for i in range(2)]
    xhatA = sbt("xhatA", [128, 4, D], BF16, at=tp)
    hT_halo = sbt("hT_halo", [128, 8, 2048], BF16, at=tp)
    QTd = sbt("QTd", [128, NE], BF16, at=tp)
    KTd = sbt("KTd", [128, 4224], BF16, at=tp)
    Vd = sbt("Vd", [128, 48, 128], BF16, at=tp)
    accO = sbt("accO", [128, NE], F32, at=tp)
    accD = sbt("accD", [128, NE], F32, at=tp)
    PTd = [sbt(f"PTd{i}", [128, 256], BF16, at=tp) for i in range(4)]
    ssb = [sbt(f"ssb{i}", [128, 256], F32, at=tp) for i in range(2)]
    dbt = [sbt(f"dbt{i}", [128, 256], F32, at=tp) for i in range(2)]

    def prep_all(row0, ntiles, dstbuf, dkey):
        t = 0
        while t < ntiles:
            n = min(2, ntiles - t)
            tiles = [xload(xbufA, xc[row0 + (t + i) * 128: row0 + (t + i + 1) * 128, :]) for i in range(n)]
            t0 = t
            rms_T(tiles, C_GPRE, lambda kc, t0=t0, n=n: dstbuf[:, kc, t0 * 128:(t0 + n) * 128], dkey, xhatA, 'xhA')
            t += n

    xhatsA = [xhatA[:, 0:2, :], xhatA[:, 2:4, :]]
    callsA = []
    for (row0, ntiles, dstbuf, dkey) in ((3968, 16, hT_halo, 'hTh'), (6016, NT, hT_own, 'hT')):
        t = 0
        while t < ntiles:
            n = min(2, ntiles - t)
            callsA.append(dict(srcs=[xc[row0 + (t + i) * 128: row0 + (t + i + 1) * 128, :] for i in range(n)],
                               dst=(lambda kc, t=t, n=n, dstbuf=dstbuf: dstbuf[:, kc, t * 128:(t + n) * 128]), dkey=dkey))
            t += n
    xbufA4 = xbufA + [accO[:, 0:D], accO[:, D:2 * D]]
    rms_pipe(callsA, C_GPRE, xbufA4, xhatsA, 'xhA')
    S.barrier()
    if DEBUG:
        S.dma('sp', lambda E: E.dma_start(out=dbg["dbg_h"][:, :, :], in_=hT_own[:, :, :]), r=['hT'])

    dscale = 128.0 ** -0.5
    bank_rr = [0]

    def nb():
        bank_rr[0] = (bank_rr[0] + 1) % 7
        return bank_rr[0]

    for hd in range(4):
        for g in range(3):
            d = DIL_D[g]
            gi = g * 4 + hd
            qb, kt = dil_geom(d)
            nql = NE // d
            nkl = 128 + nql
            c0 = OFF_KR + g * 512 + hd * 128
            (wq_, wk_, wv_), wkey = wload([wk(w_in, c0, 128), wk(w_in, c0 + 1536, 128), wk(w_in, c0 + 3072, 128)])
            bi = gi % 2
            S.dma('sp', lambda E, bi=bi, gi=gi: E.dma_start(out=dbt[bi][:, :], in_=dbd[gi, :, :]), w=[('dbt', bi)])
            QV = QTd[:, :].rearrange("p (r l) -> p r l", r=d)
            KV = KTd[:, 0:d * nkl].rearrange("p (r l) -> p r l", r=d)
            for (e0, n) in GROUPS:
                for which, wv3, dstv, loff, sc in ((0, wq_, QV, 0, dscale), (1, wk_, KV, 128, 1.0)):
                    b = nb()
                    mm(PF(b, n), [(wv3[:, kc, :], hT_own[:, kc, e0:e0 + n]) for kc in range(8)],
                       w=[('pf', b)], r=[wkey, 'hT'])
                    act(dstv[:, :, loff + e0 // d: loff + (e0 + n) // d],
                        PF(b, n).rearrange("p (l r) -> p r l", r=d), AF.Copy,
                        r=[('pf', b)], w=['QTd' if which == 0 else 'KTd'], scale=sc)
            h0 = 2048 - 128 * d
            while h0 < 2048:
                n = min(512, 2048 - h0)
                b = nb()
                mm(PF(b, n), [(wk_[:, kc, :], hT_halo[:, kc, h0:h0 + n]) for kc in range(8)],
                   w=[('pf', b)], r=[wkey, 'hTh'])
                l0 = (h0 - (2048 - 128 * d)) // d
                act(KV[:, :, l0: l0 + n // d], PF(b, n).rearrange("p (l r) -> p r l", r=d), AF.Copy,
                    r=[('pf', b)], w=['KTd'])
                h0 += n
            ntile = len(kt)
            for r_ in range(d):
                for j, (a, nk) in enumerate(kt):
                    if j == 0:
                        s0 = 2048 - 128 * d + r_
                        src = lambda kc, s0=s0, nk=nk: hT_halo[:, kc, s0: s0 + (nk - 1) * d + 1: d]
                        rk = 'hTh'
                    else:
                        s0 = (a - 128) * d + r_
                        src = lambda kc, s0=s0, nk=nk: hT_own[:, kc, s0: s0 + (nk - 1) * d + 1: d]
                        rk = 'hT'
                    b = nb()
                    mm(psf[0:nk, b * 512: b * 512 + 128], [(src(kc), wv_[:, kc, :]) for kc in range(8)],
                       w=[('pf', b)], r=[wkey, rk])
                    ti = r_ * ntile + j
                    S.op('dve', lambda E, nk=nk, b=b, ti=ti: E.tensor_copy(out=Vd[0:nk, ti, :],
                                                                         in_=psf[0:nk, b * 512: b * 512 + 128]),
                         r=[('pf', b)], w=['Vd'])
            for r_ in range(d):
                info = {}

                def dil_pv(m, r_=r_, info=info, d=d, g=g, qb=qb, ntile=ntile):
                    qa, qn = qb[m]
                    pp, pcol, _ = info[m]
                    pi, _, nk = info[m + 1]
                    b2 = nb()
                    tprev = r_ * ntile + m
                    tcur = r_ * ntile + m + 1

                    def f(E):
                        o = psf[:, b2 * 512: b2 * 512 + qn]
                        dd = psf[:, b2 * 512 + 128: b2 * 512 + 128 + qn]
                        E.matmul(o, lhsT=Vd[:, tprev, :], rhs=PTd[pp][:, pcol:pcol + qn], start=True, stop=False)
                        E.matmul(o, lhsT=Vd[0:nk, tcur, :], rhs=PTd[pi][0:nk, 0:qn], start=False, stop=True)
                        E.matmul(dd, lhsT=ones[:, :], rhs=PTd[pp][:, pcol:pcol + qn], start=False, stop=False,
                                 skip_group_check=True)
                        return E.matmul(dd, lhsT=ones[0:nk, :], rhs=PTd[pi][0:nk, 0:qn], start=False, stop=True,
                                        skip_group_check=True)
                    S.op('pe', f, r=['Vd', ('PTd', pp), ('PTd', pi), 'ones'], w=[('pf', b2)])
                    e_s = qa * d + r_
                    e_e = e_s + (qn - 1) * d + 1
                    for accb, off, akey in ((accO, 0, 'accO'), (accD, 128, 'accD')):
                        if g == 0:
                            S.op('dve', lambda E, accb=accb, off=off: E.tensor_copy(
                                out=accb[:, e_s:e_e:d], in_=psf[:, b2 * 512 + off: b2 * 512 + off + qn]),
                                r=[('pf', b2)], w=[akey])
                        else:
                            S.op('dve', lambda E, accb=accb, off=off: E.tensor_tensor(
                                out=accb[:, e_s:e_e:d], in0=psf[:, b2 * 512 + off: b2 * 512 + off + qn],
                                in1=accb[:, e_s:e_e:d], op=ALU.add), r=[('pf', b2), akey], w=[akey])

                for j, (a, nk) in enumerate(kt):
                    has_diag = j >= 1
                    has_prev = j < len(qb)
                    ncol = (qb[j - 1][1] if has_diag else 0) + (qb[j][1] if has_prev else 0)
                    qlo = qb[j - 1][0] if has_diag else qb[j][0]
                    boff = 0 if has_diag else 128
                    b = nb()
                    mm(psf[0:nk, b * 512: b * 512 + ncol], [(KV[:, r_, a:a + nk], QV[:, r_, qlo:qlo + ncol])],
                       w=[('pf', b)], r=['KTd', 'QTd'])
                    si = j % 2
                    S.op('dve', lambda E, nk=nk, b=b, ncol=ncol, si=si, boff=boff, bi=bi: E.tensor_tensor(
                        out=ssb[si][0:nk, 0:ncol], in0=psf[0:nk, b * 512: b * 512 + ncol],
                        in1=dbt[bi][0:nk, boff:boff + ncol], op=ALU.add),
                        r=[('pf', b), ('dbt', bi)], w=[('ssb', si)])
                    ridx = (0, 1, 5)[g] + r_
                    bcol = C_DVB0 + ridx if j == 0 else (C_DVB1 + ridx if j == 1 else C_ZERO)
                    pi = j % 4
                    act(PTd[pi][0:nk, 0:ncol], ssb[si][0:nk, 0:ncol], AF.Exp, r=[('ssb', si), 'cv'],
                        w=[('PTd', pi)], bias=cvc(bcol, nk))
                    info[j] = (pi, (qb[j - 1][1] if has_diag else 0), nk)
                    if j >= 2:
                        dil_pv(j - 2)
                dil_pv(len(kt) - 2)
        S.op('dve', lambda E: E.tensor_scalar(out=accD[:, :], in0=accD[:, :], scalar1=1e-30, scalar2=None,
                                              op0=ALU.max), r=['accD'], w=['accD'])
        S.op('dve', lambda E: E.reciprocal(out=accD[:, :], in_=accD[:, :]), r=['accD'], w=['accD'])
        S.op('dve', lambda E, hd=hd: E.tensor_tensor(out=y_dilT[:, hd, :], in0=accO[:, :], in1=accD[:, :],
                                                     op=ALU.mult), r=['accO', 'accD'], w=['ydil'])
    if DEBUG:
        S.dma('sp', lambda E: E.dma_start(out=dbg["dbg_ydil"][:, :, :], in_=y_dilT[:, :, :]), r=['ydil'])
    S.barrier()


    ckvT = nc.alloc_sbuf_tensor_at("ckvT", [128, 2, SEQ], BF16, offset=W_OFF)
    tp = [Y_OFF + 2 * YB]
    KT = sbt("KT", [128, SEQ], BF16, at=tp)
    cqT = sbt("cqT", [128, 3, NE], BF16, at=tp)
    Wuq = sbt("Wuq", [128, 3, 768], BF16, at=tp)
    Wukv = sbt("Wukv", [128, 2, 1024], BF16, at=tp)
    OV = tp[0]
    tp = [OV]
    xbufB = [nc.alloc_sbuf_tensor_at(f"xbufB{i}", [128, D], F32, offset=Y_OFF + YB + i * 4096) for i in range(4)]
    Wq = sbt("Wq", [128, 8, 384], BF16, at=tp)
    xhatB = sbt("xhatB", [128, 4, D], BF16, at=tp)
    hTgs = [sbt(f"hTg{i}", [128, 8, 512], BF16, at=tp) for i in range(2)]
    Wkvr = sbt("Wkvr", [128, 8, 288], BF16, at=tp)
    sq = sbt("sq", [128, 3, 512], BF16, at=tp)
    rbc = sbt("rbc", [128, 512], F32, at=tp)
    kpe = sbt("kpe", [128, 4, 96], BF16, at=tp)
    kt4 = [sbt(f"kt4_{i}", [128, 4, 16], F32, at=tp) for i in range(4)]

    S.dma('pool', lambda E: E.dma_start(out=Wkvr[:, :, :], in_=wk(w_in, OFF_Q, 288)), w=['Wkvr'])
    S.dma('pool', lambda E: E.dma_start(out=Wq[:, :, :], in_=wk(w_in, 0, 384)), w=['Wq'])
    S.dma('pool', lambda E: E.dma_start(out=Wuq[:, :, :], in_=w_uq.rearrange("(k p) n -> p k n", p=128)), w=['Wuq'])
    S.dma('pool', lambda E: E.dma_start(out=Wukv[:, :, :], in_=w_ukv.rearrange("(k p) n -> p k n", p=128)),
          w=['Wukv'])
    S.op('dve', lambda E: E.memset(kpe[:, :, :], 0.0), w=['kpe'])

    def latent_norm(banks, n, nchunk, gcol, dstf, dkey):
        for kc in range(nchunk):
            act(sq[:, kc, 0:n], PF(banks[kc], n), AF.Square, r=[('pf', banks[kc])], w=[('sq', kc)])
        bs = nb()
        mm(PF(bs, n), [(ones[:, :], sq[:, kc, 0:n]) for kc in range(nchunk)], w=[('pf', bs)],
           r=['ones'] + [('sq', kc) for kc in range(nchunk)])
        act(rbc[:, 0:n], PF(bs, n), AF.Ln, r=[('pf', bs)], w=['rbc'], scale=1.0 / (128 * nchunk), bias=EPS)
        act(rbc[:, 0:n], rbc[:, 0:n], AF.Exp, r=['rbc'], w=['rbc'], scale=-0.5)
        for kc in range(nchunk):
            S.op('dve', lambda E, kc=kc: E.scalar_tensor_tensor(out=dstf(kc), in0=PF(banks[kc], n),
                                                                scalar=cvc(gcol + kc), in1=rbc[:, 0:n],
                                                                op0=ALU.mult, op1=ALU.mult),
                 r=[('pf', banks[kc]), 'rbc', 'cv'], w=[dkey])

    def rope_tm(src3, cos3, sin3, dst3, nt, rkeys, wkey):
        a, b_ = src3[:, :, 0:16], src3[:, :, 16:32]
        t = [k[:, 0:nt, :] for k in kt4]
        S.op('dve', lambda E: E.tensor_tensor(out=t[0], in0=a, in1=cos3, op=ALU.mult), r=rkeys, w=[('kt4', 0)])
        S.op('dve', lambda E: E.tensor_tensor(out=t[1], in0=b_, in1=sin3, op=ALU.mult), r=rkeys, w=[('kt4', 1)])
        S.op('dve', lambda E: E.tensor_tensor(out=dst3[:, :, 0:16], in0=t[0], in1=t[1], op=ALU.subtract),
             r=[('kt4', 0), ('kt4', 1)], w=[wkey])
        S.op('dve', lambda E: E.tensor_tensor(out=t[2], in0=b_, in1=cos3, op=ALU.mult), r=rkeys, w=[('kt4', 2)])
        S.op('dve', lambda E: E.tensor_tensor(out=t[3], in0=a, in1=sin3, op=ALU.mult), r=rkeys, w=[('kt4', 3)])
        S.op('dve', lambda E: E.tensor_tensor(out=dst3[:, :, 16:32], in0=t[2], in1=t[3], op=ALU.add),
             r=[('kt4', 2), ('kt4', 3)], w=[wkey])

    xctr[0] = 0

    lat = {}

    def ctx_latent1(tg):
        hTg = hTgs[tg % 2]
        hk = ('hTg', tg % 2)
        banks = [nb(), nb()]
        for kc2 in range(2):
            mm(PF(banks[kc2]), [(Wkvr[:, kc, kc2 * 128:(kc2 + 1) * 128], hTg[:, kc, :]) for kc in range(8)],
               w=[('pf', banks[kc2])], r=['Wkvr', hk])
        bk = nb()

        def fk(E):
            last = None
            for i in range(4):
                for kc in range(8):
                    last = E.matmul(psf[:, bk * 512 + i * 32: bk * 512 + (i + 1) * 32],
                                    lhsT=hTg[:, kc, i * 128:(i + 1) * 128], rhs=Wkvr[:, kc, 256:288],
                                    start=(kc == 0), stop=(kc == 7))
            return last
        S.op('pe', fk, r=['Wkvr', hk], w=[('pf', bk)])
        for kc in range(2):
            act(sq[:, kc, 0:512], PF(banks[kc]), AF.Square, r=[('pf', banks[kc])], w=[('sq', kc)])
        src3 = psf[:, bk * 512: bk * 512 + 128].rearrange("p (t c) -> p t c", c=32)
        rope_tm(src3, cosT[:, tg * 4:(tg + 1) * 4, :], sinT[:, tg * 4:(tg + 1) * 4, :], kpe[:, :, 64:96], 4,
                [('pf', bk), 'rope'], 'kpe')
        lat[tg] = banks

    def ctx_latent2(tg):
        banks = lat[tg]
        n = 512
        bs = nb()
        mm(PF(bs, n), [(ones[:, :], sq[:, kc, 0:n]) for kc in range(2)], w=[('pf', bs)],
           r=['ones'] + [('sq', kc) for kc in range(2)])
        act(rbc[:, 0:n], PF(bs, n), AF.Ln, r=[('pf', bs)], w=['rbc'], scale=1.0 / 256, bias=EPS)
        act(rbc[:, 0:n], rbc[:, 0:n], AF.Exp, r=['rbc'], w=['rbc'], scale=-0.5)
        for kc in range(2):
            S.op('dve', lambda E, kc=kc: E.scalar_tensor_tensor(out=ckvT[:, kc, tg * 512:(tg + 1) * 512],
                                                                in0=PF(banks[kc], n), scalar=cvc(C_KVN + kc),
                                                                in1=rbc[:, 0:n], op0=ALU.mult, op1=ALU.mult),
                 r=[('pf', banks[kc]), 'rbc', 'cv'], w=['ckvT'])

        def ft(E):
            last = None
            for i in range(4):
                last = E.transpose(psb[0:96, i * 128:(i + 1) * 128], kpe[:, i, :], ident[:, :])
            return last
        S.op('pe', ft, r=['kpe', 'ident'], w=['psb'])
        act(KT[64:96, tg * 512:(tg + 1) * 512], psb[64:96, 0:512], AF.Copy, r=['psb'], w=['KTr'])

    xhatsB = [xhatB[:, 0:2, :], xhatB[:, 2:4, :]]
    callsB = []
    for tg in range(16):
        for hh in range(2):
            callsB.append(dict(
                srcs=[xc[(tg * 4 + hh * 2 + i) * 128:(tg * 4 + hh * 2 + i + 1) * 128, :] for i in range(2)],
                dst=(lambda kc, tg=tg, hh=hh: hTgs[tg % 2][:, kc, hh * 256:(hh + 1) * 256]), dkey=('hTg', tg % 2),
                after=((lambda tg=tg: ctx_latent1(tg)) if hh == 1 else
                       ((lambda tg=tg: ctx_latent2(tg - 1)) if tg >= 1 else None))))
    rms_pipe(callsB, C_GPRE, xbufB, xhatsB, 'xhB')
    ctx_latent2(15)
    for (e0, n) in GROUPS:
        banks = [nb(), nb(), nb()]
        for kc3 in range(3):
            mm(PF(banks[kc3], n), [(Wq[:, kc, kc3 * 128:(kc3 + 1) * 128], hT_own[:, kc, e0:e0 + n]) for kc in range(8)],
               w=[('pf', banks[kc3])], r=['Wq', 'hT'])
        latent_norm(banks, n, 3, C_QN, lambda kc, e0=e0, n=n: cqT[:, kc, e0:e0 + n], 'cqT')
    S.barrier()

    for c in range(4):
        S.dma('pool', lambda E, c=c: E.dma_start(out=wupb[:, c * 1408:(c + 1) * 1408], in_=w_up[:, c * 1408:(c + 1) * 1408]),
              w=[('wupb', c)])
    for r_ in range(2):
        S.dma('pool', lambda E, r_=r_: E.dma_start(out=wdnb[r_ * 1408:(r_ + 1) * 1408, :], in_=w_dn[r_ * 1408:(r_ + 1) * 1408, :]),
              w=[('wdnb', r_)])
    tp = [OV]
    Vts = [sbt(f"Vt{i}", [128, 64, 65], BF16, at=tp) for i in range(2)]
    QTs = [sbt(f"QT{i}", [128, NE], BF16, at=tp) for i in range(2)]
    qtm = sbt("qtm", [128, NT, 96], BF16, at=tp)
    ypair = sbt("ypair", [128, NT, 128], BF16, at=tp)
    PTm = [sbt(f"PTm{i}", [128, 1024], BF16, at=tp) for i in range(3)]
    rden = sbt("rden", [128, 8], F32, at=tp)
    kt4 = [sbt(f"kq4_{i}", [128, 5, 16], F32, at=tp) for i in range(4)]
    for Vt_ in Vts:
        S.op('dve', lambda E, Vt_=Vt_: E.memset(Vt_[:, :, 64:65], 1.0), w=['Vones'])
    evac_rr = [0]

    def evac(out, in_, r, w, **kw):
        evac_rr[0] += 1
        if not kw:
            S.op('dve', lambda E: E.tensor_copy(out=out, in_=in_), r=r, w=w)
        else:
            act(out, in_, AF.Copy, r=r, w=w, **kw)

    pctr = [0]
    for h in range(8):
        for tg in range(16):
            b = nb()
            mm(psf[0:64, b * 512:(b + 1) * 512],
               [(Wukv[:, kc, h * 128: h * 128 + 64], ckvT[:, kc, tg * 512:(tg + 1) * 512]) for kc in range(2)],
               w=[('pf', b)], r=['Wukv', 'ckvT'])
            evac(KT[0:64, tg * 512:(tg + 1) * 512], psf[0:64, b * 512:(b + 1) * 512], r=[('pf', b)], w=['KTn'])
        def build_vq(h, bankf):
            Vt, QT = Vts[h % 2], QTs[h % 2]
            vk, qk = ('Vt', h % 2), ('QT', h % 2)
            chunks = []

            def vchunk(t8):
                b = bankf()

                def fv(E):
                    last = None
                    for i in range(8):
                        for kc in range(2):
                            last = E.matmul(psf[:, b * 512 + i * 64: b * 512 + (i + 1) * 64],
                                            lhsT=ckvT[:, kc, (t8 * 8 + i) * 128:(t8 * 8 + i + 1) * 128],
                                            rhs=Wukv[:, kc, h * 128 + 64: h * 128 + 128], start=(kc == 0), stop=(kc == 1))
                    return last
                S.op('pe', fv, r=['Wukv', 'ckvT'], w=[('pf', b)])
                evac(Vt[:, t8 * 8:(t8 + 1) * 8, 0:64], PF(b).rearrange("p (t e) -> p t e", e=64), r=[('pf', b)], w=[vk])

            def qchunk(t0):
                nt = min(5, NT - t0)
                b = bankf()

                def fq(E):
                    last = None
                    for i in range(nt):
                        for kc in range(3):
                            last = E.matmul(psf[:, b * 512 + i * 96: b * 512 + (i + 1) * 96],
                                            lhsT=cqT[:, kc, (t0 + i) * 128:(t0 + i + 1) * 128],
                                            rhs=Wuq[:, kc, h * 96:(h + 1) * 96], start=(kc == 0), stop=(kc == 2))
                    return last
                S.op('pe', fq, r=['Wuq', 'cqT'], w=[('pf', b)])
                p3 = psf[:, b * 512: b * 512 + nt * 96].rearrange("p (t c) -> p t c", c=96)
                S.op('dve', lambda E: E.tensor_scalar(out=qtm[:, t0:t0 + nt, 0:64], in0=p3[:, :, 0:64],
                                                      scalar1=qs, scalar2=None, op0=ALU.mult),
                     r=[('pf', b)], w=['qtm'])
                rope_tm(p3[:, :, 64:96], cosQ[:, t0:t0 + nt, :], sinQ[:, t0:t0 + nt, :], qtm[:, t0:t0 + nt, 64:96], nt,
                        [('pf', b), 'ropeq'], 'qtm')

            def tchunk(t0):
                nt = min(8, NT - t0)

                def ftq(E):
                    last = None
                    for i in range(nt):
                        last = E.transpose(psb[0:96, i * 128:(i + 1) * 128], qtm[:, t0 + i, :], ident[:, :])
                    return last
                S.op('pe', ftq, r=['qtm', 'ident'], w=['psb'])
                S.op('dve', lambda E: E.tensor_copy(out=QT[0:96, t0 * 128:(t0 + nt) * 128],
                                                    in_=psb[0:96, 0:nt * 128]), r=['psb'], w=[qk])
            for t8 in range(8):
                chunks.append(lambda t8=t8: vchunk(t8))
            for t0 in range(0, NT, 5):
                chunks.append(lambda t0=t0: qchunk(t0))
            for t0 in range(0, NT, 8):
                chunks.append(lambda t0=t0: tchunk(t0))
            return chunks

        if h == 0:
            for ch in build_vq(0, nb):
                ch()
        Vt, QT = Vts[h % 2], QTs[h % 2]
        vk, qk = ('Vt', h % 2), ('QT', h % 2)
        items = []
        for gq, (e0, n) in enumerate(GROUPS):
            tq0, ntq = e0 // 128, n // 128
            kfirst = 47 + tq0
            per = 1024 // n
            units = []
            kb = 0
            while kb < kfirst:
                lim = min(kfirst, (kb // 16 + 1) * 16, kb + per)
                units.append((list(range(kb, lim)), None))
                kb = lim
            for m in range(ntq):
                units.append(([kfirst + m], m))
            for ui, (kbs, m) in enumerate(units):
                items.append(dict(e0=e0, n=n, tq0=tq0, ntq=ntq, ob=4 + (gq % 2), kbs=kbs, m=m, first=(ui == 0),
                                  last=(ui == len(units) - 1)))

        def emit_S(it, i):
            sp, n, e0, m = ((pctr[0] + i) % 2) * 2, it['n'], it['e0'], it['m']
            if m is None:
                def fs(E, it=it, sp=sp, n=n, e0=e0, QT=QT):
                    last = None
                    for idx, kbk in enumerate(it['kbs']):
                        last = E.matmul(psf[:, sp * 512 + idx * n: sp * 512 + (idx + 1) * n],
                                        lhsT=KT[0:96, kbk * 128:(kbk + 1) * 128], rhs=QT[0:96, e0:e0 + n],
                                        start=True, stop=True)
                    return last
                S.op('pe', fs, r=['KTn', 'KTr', qk], w=[('pf', sp), ('pf', sp + 1)])
            else:
                kbk = it['kbs'][0]
                cols = n - 128 * m
                mm(psf[:, sp * 512: sp * 512 + cols],
                   [(KT[0:96, kbk * 128:(kbk + 1) * 128], QT[0:96, e0 + 128 * m:e0 + n])],
                   w=[('pf', sp), ('pf', sp + 1)], r=['KTn', 'KTr', qk])

        def emit_A(it, i):
            sp, pi = ((pctr[0] + i) % 2) * 2, (pctr[0] + i) % 3
            n, m = it['n'], it['m']
            kb0 = it['kbs'][0]
            tot = len(it['kbs']) * n if m is None else n - 128 * m
            act(PTm[pi][:, 0:tot], psf[:, sp * 512: sp * 512 + tot], AF.Exp,
                r=[('pf', sp), ('pf', sp + 1), 'cv'], w=[('PTm', pi)], bias=cvc(C_VB + min(kb0 // 16, 3)))
            if m is not None:
                S.op('dve', lambda E, pi=pi: E.tensor_tensor(out=PTm[pi][:, 0:128], in0=PTm[pi][:, 0:128],
                                                             in1=tri[:, :], op=ALU.mult),
                     r=[('PTm', pi), 'tri'], w=[('PTm', pi)])

        def emit_PV(it, i):
            pi = (pctr[0] + i) % 3
            n, m, ntq, ob, tq0 = it['n'], it['m'], it['ntq'], it['ob'], it['tq0']
            if m is None:
                blocks = [(idx * n, kbk, list(range(ntq))) for idx, kbk in enumerate(it['kbs'])]
            else:
                blocks = [(0, it['kbs'][0], list(range(m, ntq)))]

            def pv(E, pi=pi, blocks=blocks, ob=ob, first=it['first'], Vt=Vt):
                last = None
                st = first
                for (col0, kbk, qts) in blocks:
                    for qt in qts:
                        last = E.matmul(psf[:, ob * 512 + qt * 65: ob * 512 + qt * 65 + 65],
                                        lhsT=PTm[pi][:, col0 + (qt - qts[0]) * 128: col0 + (qt - qts[0] + 1) * 128],
                                        rhs=Vt[:, kbk, :], start=st, stop=False, skip_group_check=True)
                        st = False
                return last
            S.op('pe', pv, r=[('PTm', pi), vk, 'Vones'], w=[('pf', ob)])
            if it['last']:
                o3 = psf[:, ob * 512: ob * 512 + ntq * 65].rearrange("p (t c) -> p t c", c=65)
                S.op('dve', lambda E, o3=o3, ntq=ntq: E.tensor_scalar(out=rden[:, 0:ntq], in0=o3[:, :, 64],
                                                                      scalar1=1e-30, scalar2=None, op0=ALU.max),
                     r=[('pf', ob)], w=['rden'])
                S.op('dve', lambda E, ntq=ntq: E.reciprocal(out=rden[:, 0:ntq], in_=rden[:, 0:ntq]),
                     r=['rden'], w=['rden'])
                hoff = (h % 2) * 64
                S.op('dve', lambda E, o3=o3, ntq=ntq, tq0=tq0, hoff=hoff: E.tensor_tensor(
                    out=ypair[:, tq0:tq0 + ntq, hoff:hoff + 64], in0=o3[:, :, 0:64],
                    in1=rden[:, 0:ntq].unsqueeze(2).to_broadcast([128, ntq, 64]), op=ALU.mult),
                    r=[('pf', ob), 'rden'], w=['ypair'])

        for i, it in enumerate(items):
            emit_S(it, i)
            emit_A(it, i)
            if i >= 1:
                emit_PV(items[i - 1], i - 1)
            if i == 12 and h < 7:
                pending = build_vq(h + 1, lambda: 6)
            if i > 12 and h < 7 and pending:
                pending.pop(0)()
        emit_PV(items[-1], len(items) - 1)
        while h < 7 and pending:
            pending.pop(0)()
        pctr[0] += len(items)
        if h % 2 == 1:
            for t0 in range(0, NT, 8):
                nt = min(8, NT - t0)

                def fty(E, t0=t0, nt=nt):
                    last = None
                    for i in range(nt):
                        last = E.transpose(psb[:, i * 128:(i + 1) * 128], ypair[:, t0 + i, :], ident[:, :])
                    return last
                S.op('pe', fty, r=['ypair', 'ident'], w=['psb'])
                S.op('dve', lambda E, t0=t0, nt=nt, h=h: E.tensor_copy(out=y_mlaT[:, h // 2, t0 * 128:(t0 + nt) * 128],
                                                                       in_=psb[:, 0:nt * 128]), r=['psb'], w=['ymla'])
    if DEBUG:
        S.dma('sp', lambda E: E.dma_start(out=dbg["dbg_ymla"][:, :, :], in_=y_mlaT[:, :, :]), r=['ymla'])
    S.barrier()


    MERG = END - 34816
    mergedT = nc.alloc_sbuf_tensor_at("mergedT", [128, 8, NE], BF16, offset=MERG)
    tp = [T_OFF]
    memT = sbt("memT", [128, 8, 256], BF16, at=tp)
    KmT = sbt("KmT", [128, 4, 256], BF16, at=tp)
    VmT = sbt("VmT", [128, 2, 512], BF16, at=tp)
    OVC = tp[0]
    xbufC = [sbt(f"xbufC{i}", [128, D], F32, at=tp) for i in range(2)]
    xhatC = sbt("xhatC", [128, 2, D], BF16, at=tp)
    assert tp[0] <= MERG
    xctr[0] = 0
    tiles = [xload(xbufC, memx[i * 128:(i + 1) * 128, :]) for i in range(2)]
    rms_T(tiles, C_GMEM, lambda kc: memT[:, kc, :], 'memT', xhatC, 'xhC')
    (wkm,), kkm = wload([wk(w_memkv, 0, 512)])
    for h in range(4):
        b = nb()
        mm(PF(b, 256), [(wkm[:, kc, h * 128:(h + 1) * 128], memT[:, kc, :]) for kc in range(8)],
           w=[('pf', b)], r=[kkm, 'memT'])
        act(KmT[:, h, :], PF(b, 256), AF.Copy, r=[('pf', b)], w=['KmT'])
    (wvm,), kvm = wload([wk(w_memkv, 512, 512)])
    for mt in range(2):
        b = nb()
        mm(PF(b), [(memT[:, kc, mt * 128:(mt + 1) * 128], wvm[:, kc, :]) for kc in range(8)],
           w=[('pf', b)], r=[kvm, 'memT'])
        act(VmT[:, mt, :], PF(b), AF.Copy, r=[('pf', b)], w=['VmT'])
    S.barrier()
    tp = [OVC]
    qm = sbt("qm", [128, 512], BF16, at=tp)
    PTc = sbt("PTc", [128, 2, 512], BF16, at=tp)
    recc = sbt("recc", [128, 512], F32, at=tp)
    gsb = [sbt(f"gsb{i}", [128, 512], BF16, at=tp) for i in range(3)]
    mtc = [sbt(f"mtc{i}", [128, 512], F32, at=tp) for i in range(2)]
    assert tp[0] <= MERG
    (wqm,), kqm = wload([wk(w_in, OFF_DIL, 512)])
    qmB = sbt("qmB", [128, 512], BF16, at=tp)
    PTcB = sbt("PTcB", [128, 2, 512], BF16, at=tp)
    assert tp[0] <= MERG
    qm2, PTc2 = [qm, qmB], [PTc, PTcB]
    itsC = [(h, e0, n) for h in range(4) for (e0, n) in GROUPS]

    def c_s1(i):
        h, e0, n = itsC[i]
        b = nb()
        mm(PF(b, n), [(wqm[:, kc, h * 128:(h + 1) * 128], hT_own[:, kc, e0:e0 + n]) for kc in range(8)],
           w=[('pf', b)], r=[kqm, 'hT'])
        act(qm2[i % 2][:, 0:n], PF(b, n), AF.Copy, r=[('pf', b)], w=[('qm', i % 2)], scale=dscale)

    def c_s2(i):
        h, e0, n = itsC[i]
        for mt in range(2):
            b = nb()
            mm(PF(b, n), [(KmT[:, h, mt * 128:(mt + 1) * 128], qm2[i % 2][:, 0:n])], w=[('pf', b)],
               r=['KmT', ('qm', i % 2)])
            act(PTc2[i % 2][:, mt, 0:n], PF(b, n), AF.Exp, r=[('pf', b)], w=[('PTc', i % 2, mt)])

    def c_s3(i):
        h, e0, n = itsC[i]
        P_ = PTc2[i % 2]
        pk = [('PTc', i % 2, 0), ('PTc', i % 2, 1)]
        bo, bd = nb(), nb()
        mm(PF(bo, n), [(VmT[:, mt, h * 128:(h + 1) * 128], P_[:, mt, 0:n]) for mt in range(2)],
           w=[('pf', bo)], r=['VmT'] + pk)
        mm(PF(bd, n), [(ones[:, :], P_[:, mt, 0:n]) for mt in range(2)], w=[('pf', bd)], r=['ones'] + pk)
        S.op('dve', lambda E: E.reciprocal(out=recc[:, 0:n], in_=PF(bd, n)), r=[('pf', bd)], w=['recc'])
        S.op('dve', lambda E: E.tensor_tensor(out=y_memT[:, h, e0:e0 + n], in0=PF(bo, n), in1=recc[:, 0:n],
                                              op=ALU.mult), r=[('pf', bo), 'recc'], w=['ymem'])
    NC_ = len(itsC)
    c_s1(0)
    c_s1(1)
    c_s2(0)
    for i in range(NC_):
        if i + 2 < NC_:
            c_s1(i + 2)
        if i + 1 < NC_:
            c_s2(i + 1)
        c_s3(i)
    if DEBUG:
        S.dma('sp', lambda E: E.dma_start(out=dbg["dbg_ymem"][:, :, :], in_=y_memT[:, :, :]), r=['ymem'])
    yT = [y_mlaT, y_dilT, y_memT]
    ykeys = ['ymla', 'ydil', 'ymem']
    for oc in range(8):
        wg, kg = wload([wk(w_in, OFF_MEMQ + br * 1024 + oc * 128, 128) for br in range(3)])
        wb_, kb_ = wload([w_br[br][:, oc * 128:(oc + 1) * 128].rearrange("(k p) n -> p k n", p=128) for br in range(3)])
        for (e0, n) in GROUPS:
            for br in range(3):
                bg = nb()
                mm(PF(bg, n), [(wg[br][:, kc, :], hT_own[:, kc, e0:e0 + n]) for kc in range(8)],
                   w=[('pf', bg)], r=[kg, 'hT'])
                act(gsb[br][:, 0:n], PF(bg, n), AF.Sigmoid, r=[('pf', bg), 'cv'], w=[('gsb', br)],
                    bias=cvc(C_BG + br * 8 + oc))
                bb = nb()
                mm(PF(bb, n), [(wb_[br][:, kc, :], yT[br][:, kc, e0:e0 + n]) for kc in range(4)],
                   w=[('pf', bb)], r=[kb_, ykeys[br]])
                di = 0 if br == 0 else 1
                S.op('dve', lambda E, bb=bb, n=n, br=br, di=di: E.tensor_tensor(out=mtc[di][:, 0:n], in0=PF(bb, n),
                                                                               in1=gsb[br][:, 0:n], op=ALU.mult),
                     r=[('pf', bb), ('gsb', br)], w=[('mtc', di)])
                if br == 1:
                    S.op('dve', lambda E, n=n: E.tensor_tensor(out=mtc[0][:, 0:n], in0=mtc[0][:, 0:n],
                                                               in1=mtc[1][:, 0:n], op=ALU.add),
                         r=[('mtc', 0), ('mtc', 1)], w=[('mtc', 0)])
                if br == 2:
                    S.op('dve', lambda E, n=n, oc=oc, e0=e0: E.tensor_tensor(out=mergedT[:, oc, e0:e0 + n],
                                                                            in0=mtc[0][:, 0:n], in1=mtc[1][:, 0:n],
                                                                            op=ALU.add),
                         r=[('mtc', 0), ('mtc', 1)], w=['merged'])
    if DEBUG:
        S.dma('sp', lambda E: E.dma_start(out=dbg["dbg_merged"][:, :, :], in_=mergedT[:, :, :]), r=['merged'])
    S.barrier()

    x1 = nc.alloc_sbuf_tensor_at("x1", [128, 16, D], F32, offset=Y_OFF)
    tp = [Y_OFF + 65536]
    xbufD = [sbt("xbufD0", [128, D], F32, at=tp)]
    tmpD = sbt("tmpD", [128, D], F32, at=tp)
    x1pre = sbt("x1pre", [128, D], F32, at=tp)
    xhatD = sbt("xhatD", [128, 1, D], BF16, at=tp)
    assert tp[0] <= MERG
    h2T = hT_own
    (wo0,), ko0 = wload([wk(w_o, 0, 512)])
    (wo1,), ko1 = wload([wk(w_o, 512, 512)])
    xctr[0] = 0
    dstD = {}

    def d_s1(t):
        p = t % 3
        for half, wo_, ko in ((0, wo0, ko0), (1, wo1, ko1)):
            mm(psf[:, p * 1024 + half * 512: p * 1024 + (half + 1) * 512],
               [(mergedT[:, kc, t * 128:(t + 1) * 128], wo_[:, kc, :]) for kc in range(8)],
               w=[('pf', 2 * p + half)], r=[ko, 'merged'])
        pk = [('pf', 2 * p), ('pf', 2 * p + 1)]
        pfull = psf[:, p * 1024:(p + 1) * 1024]
        S.op('act', lambda E: E.activation(out=junk[:, :], in_=pfull, func=AF.Square,
                                           accum_out=small[:, 48:49]), r=pk, w=['junk', ('sm', 48)])
        rstd_from_ss(small[:, 48:49], small[:, 50:51], small[:, 49:50], 1024.0, [('sm', 48)], [('sm', 50)], [('sm', 49)])
        xa, xk = xload(xbufD, xc[6016 + t * 128: 6016 + (t + 1) * 128, :])
        S.op('dve', lambda E: E.scalar_tensor_tensor(out=tmpD[:, :], in0=pfull, scalar=small[:, 50:51],
                                                     in1=gbc[:, 0, :], op0=ALU.mult, op1=ALU.mult),
             r=pk + [('sm', 50), 'gbc'], w=['tmpD'])
        dst = x1pre[:, :] if t == 0 else x1[:, t - 1, :]
        dk = ('x1', t)
        S.op('dve', lambda E: E.tensor_tensor(out=dst, in0=tmpD[:, :], in1=xa, op=ALU.add),
             r=['tmpD', xk], w=[dk])
        dstD[t] = (dst, dk)

    def d_s2(t):
        rms_T([dstD[t]], C_GFFN, lambda kc: h2T[:, kc, t * 128:(t + 1) * 128], 'h2T', xhatD, 'xhD')

    d_s1(0)
    for t in range(NT):
        if t + 1 < NT:
            d_s1(t + 1)
        d_s2(t)
    S.barrier()

    tp = [Y_OFF + 65536]
    gbuf = sbt("gbuf", [128, 22, 512], BF16, at=tp)
    ubuf = [sbt(f"ubuf{i}", [128, 528], F32, at=tp) for i in range(4)]
    zt = [sbt(f"zt{i}", [128, 512], F32, at=tp) for i in range(4)]
    sgs = [sbt(f"sg{i}", [128, 512], F32, at=tp) for i in range(2)]
    carry = sbt("carry", [128, 44, 2], F32, at=tp)
    fo0 = sbt("fo0", [128, 4, 512], F32, at=tp)
    tmpE = zt[0]
    eb = [0]

    def nbe():
        eb[0] = (eb[0] + 1) % 3
        return eb[0]
    for j in range(4):
        e0 = 128 + 512 * j
        for fc0 in range(0, 22, 4):
            nf = min(4, 22 - fc0)
            wupk = [('wupb', c) for c in range(4)]
            wgt, kgt = wload_b(wk(wupb, fc0 * 128, nf * 128), wupk)
            wvt, kvt = wload_b(wk(wupb, DFF + fc0 * 128, nf * 128), wupk)
            for fi in range(nf):
                fc = fc0 + fi
                if j > 0:
                    for fcn in ([0, 1] if fc == 0 else ([fc + 1] if fc + 1 < 22 else [])):
                        for half_ in range(2):
                            S.op('dve', lambda E, fcn=fcn, half_=half_: E.tensor_copy(
                                out=ubuf[half_ * 2 + fcn % 2][:, 0:2], in_=carry[:, half_ * 22 + fcn, :]),
                                r=[('carry', half_ * 22 + fcn)], w=[('ubc', half_ * 2 + fcn % 2)])
                for half, wt, kw_ in ((0, wgt, kgt), (1, wvt, kvt)):
                    ci = half * 22 + fc
                    b = nbe()
                    mm(PF(b), [(wt[:, kc, fi * 128:(fi + 1) * 128], h2T[:, kc, e0:e0 + 512]) for kc in range(8)],
                       w=[('pf', b)], r=[kw_, 'h2T'])
                    ub = ubuf[half * 2 + fc % 2]
                    uk = ('ub', half * 2 + fc % 2)
                    uck = ('ubc', half * 2 + fc % 2)
                    ztb = zt[half * 2 + fc % 2]
                    if j == 0:
                        pb_ = 3 + (ci % 4)
                        mm(psf[:, pb_ * 512: pb_ * 512 + 2],
                           [(wt[:, kc, fi * 128:(fi + 1) * 128], h2T[:, kc, 126:128]) for kc in range(8)],
                           w=[('pf', pb_)], r=[kw_, 'h2T'])
                        S.op('dve', lambda E, ub=ub, pb_=pb_: E.tensor_scalar(out=ub[:, 0:2], in0=psf[:, pb_ * 512: pb_ * 512 + 2],
                                                                             scalar1=cvc(C_UFLAG), scalar2=None, op0=ALU.mult),
                             r=[('pf', pb_), 'cv'], w=[uck])
                    act(ub[:, 2:514], PF(b), AF.Copy, r=[('pf', b)], w=[uk])
                    S.op('dve', lambda E, ub=ub, ci=ci: E.tensor_copy(out=carry[:, ci, :], in_=ub[:, 512:514]),
                         r=[uk], w=[('carry', ci)])
                    zk = ('zt', half * 2 + fc % 2)
                    act(ztb[:, :], ub[:, 0:512], AF.Identity, r=[uk, uck, 'cv'], w=[zk], scale=cvc(C_CW + ci),
                        bias=cvc(C_CB + ci))
                    S.op('dve', lambda E, ub=ub, ci=ci, ztb=ztb: E.scalar_tensor_tensor(
                        out=ztb[:, :], in0=ub[:, 1:513], scalar=cvc(C_CW + 44 + ci), in1=ztb[:, :],
                        op0=ALU.mult, op1=ALU.add), r=[uk, uck, zk, 'cv'], w=[zk])
                    S.op('dve', lambda E, ub=ub, ci=ci, ztb=ztb: E.scalar_tensor_tensor(
                        out=ztb[:, :], in0=ub[:, 2:514], scalar=cvc(C_CW + 88 + ci), in1=ztb[:, :],
                        op0=ALU.mult, op1=ALU.add), r=[uk, zk, 'cv'], w=[zk])
                sg = sgs[fc % 2]
                sk = ('sg', fc % 2)
                zg, zv = zt[fc % 2], zt[2 + fc % 2]
                act(sg[:, :], zg[:, :], AF.Silu, r=[('zt', fc % 2)], w=[sk])
                S.op('pool', lambda E, fc=fc, sg=sg, zv=zv: E.tensor_tensor(out=gbuf[:, fc, :], in0=sg[:, :], in1=zv[:, :],
                                                                          op=ALU.mult),
                     r=[sk, ('zt', 2 + fc % 2)], w=['gbuf'])
        for half in range(2):
            for (f0, nfk) in ((0, 8), (8, 8), (16, 6)):
                wd, kd = wload_b(wdnb[f0 * 128:(f0 + nfk) * 128, half * 512:(half + 1) * 512]
                                 .rearrange("(k p) n -> p k n", p=128), [('wdnb', 0), ('wdnb', 1)])
                for tt in range(4):
                    def fd(E, wd=wd, f0=f0, nfk=nfk, tt=tt):
                        last = None
                        for k in range(nfk):
                            last = E.matmul(PF(3 + tt), lhsT=gbuf[:, f0 + k, tt * 128:(tt + 1) * 128], rhs=wd[:, k, :],
                                            start=(f0 == 0 and k == 0), stop=(f0 == 16 and k == nfk - 1))
                        return last
                    S.op('pe', fd, r=[kd, 'gbuf'], w=[('pf', 3 + tt)])
            for tt in range(4):
                col = 40 + half * 4 + tt
                S.op('act', lambda E, tt=tt, col=col: E.activation(out=junk[:, 0:512], in_=PF(3 + tt), func=AF.Square,
                                                                   accum_out=small[:, col:col + 1]),
                     r=[('pf', 3 + tt)], w=['junk', ('sm', col)])
                if half == 0:
                    act(fo0[:, tt, :], PF(3 + tt), AF.Copy, r=[('pf', 3 + tt)], w=[('fo0', tt)])
            if half == 1:
                S.op('dve', lambda E: E.tensor_tensor(out=small[:, 48:52], in0=small[:, 40:44], in1=small[:, 44:48],
                                                      op=ALU.add), r=[('sm', c) for c in range(40, 48)], w=[('sm', 48)])
                rstd_from_ss(small[:, 48:52], small[:, 56:60], small[:, 52:56], 1024.0, [('sm', 48)], [('sm', 56)],
                             [('sm', 52)])
                for tt in range(4):
                    xt = x1[:, 4 * j + tt, :]
                    xk = ('x1', 4 * j + tt + 1)
                    rs = small[:, 56 + tt:57 + tt]
                    S.op('dve', lambda E, tt=tt, rs=rs: E.scalar_tensor_tensor(
                        out=fo0[:, tt, :], in0=fo0[:, tt, :], scalar=rs, in1=gbc[:, 1, 0:512], op0=ALU.mult,
                        op1=ALU.mult), r=[('fo0', tt), ('sm', 56), 'gbc'], w=[('fo0', tt)])
                    S.op('dve', lambda E, tt=tt, xt=xt: E.tensor_tensor(out=xt[:, 0:512], in0=xt[:, 0:512],
                                                                        in1=fo0[:, tt, :], op=ALU.add),
                         r=[('fo0', tt), xk], w=[xk])
                    S.op('dve', lambda E, tt=tt, rs=rs: E.scalar_tensor_tensor(
                        out=tmpE[:, :], in0=PF(3 + tt), scalar=rs, in1=gbc[:, 1, 512:1024], op0=ALU.mult,
                        op1=ALU.mult), r=[('pf', 3 + tt), ('sm', 56), 'gbc'], w=[('zt', 0)])
                    S.op('dve', lambda E, tt=tt, xt=xt: E.tensor_tensor(out=xt[:, 512:1024], in0=xt[:, 512:1024],
                                                                        in1=tmpE[:, :], op=ALU.add),
                         r=[('zt', 0), xk], w=[xk])
                    row = (4 * j + tt) * 128
                    S.dma('sp', lambda E, row=row, xt=xt: E.dma_start(out=outd[row:row + 128, :], in_=xt), r=[xk])

    return nc, es, S


def _finish(nc, es, S):
    S.barrier()
    with nc.Block() as block:
        @block.tensor
        def _(E):
            for f in S.prog['pe']:
                f(E)

        @block.scalar
        def _(E):
            for f in S.prog['act']:
                f(E)

        @block.vector
        def _(E):
            for f in S.prog['dve']:
                f(E)

        @block.gpsimd
        def _(E):
            for f in S.prog['pool']:
                f(E)

        @block.sync
        def _(E):
            for f in S.prog['sp']:
                f(E)
    es.close()
    return nc


def host_inputs(inputs):
    import ml_dtypes
    x = np.asarray(inputs["x"], np.float32)
    mem = np.asarray(inputs["mem"], np.float32)
    pos = np.asarray(inputs["positions"], np.int32)

    def P(k):
        return np.asarray(inputs[k], np.float32)[0]
    shared = {
        "gbc": np.stack([P("g_post_mix"), P("g_post_ffn")]).astype(np.float32),
        "tri": np.triu(np.ones((128, 128), np.float32)),
        "ident": np.eye(128, dtype=np.float32),
        "w_in": P("w_in"), "w_uq": P("w_uq"), "w_ukv": P("w_ukv"), "w_mem_kv": P("w_mem_kv"),
        "w_br_mla": P("w_br_mla"), "w_br_dil": P("w_br_dil"), "w_br_mem": P("w_br_mem"), "w_o": P("w_o"),
        "w_ffn_up": P("w_ffn_up"), "w_ffn_down": P("w_ffn_down"),
    }
    db = np.zeros((12, 128, 256), np.float32)
    slopes = np.exp2(-8.0 * np.arange(1, 13, dtype=np.float32) / 12).reshape(4, 3).T
    k = np.arange(128)[:, None]
    i = np.arange(128)[None, :]
    for g in range(3):
        for hd in range(4):
            a = slopes[g, hd] * DIL_D[g]
            dist = i - k
            db[g * 4 + hd, :, 0:128] = np.where(dist >= 0, -a * dist, NEG)
            dist2 = 128 + i - k
            db[g * 4 + hd, :, 128:256] = np.where(k >= i, -a * dist2, NEG)
    shared["dbias"] = db

    def cols(v, n):
        return v.reshape(n, 128).T
    cv0 = np.zeros((128, NCV), np.float32)
    cv0[:, C_GPRE:C_GPRE + 8] = cols(P("g_pre_mix"), 8)
    cv0[:, C_GMEM:C_GMEM + 8] = cols(P("g_mem"), 8)
    cv0[:, C_GFFN:C_GFFN + 8] = cols(P("g_pre_ffn"), 8)
    cv0[:, C_QN:C_QN + 3] = cols(P("mla_q_norm"), 3)
    cv0[:, C_KVN:C_KVN + 2] = cols(P("mla_kv_norm"), 2)
    cv0[:, C_BG:C_BG + 24] = cols(P("b_gate"), 24)
    cw = P("conv_w")
    for j in range(3):
        cv0[:, C_CW + j * 44: C_CW + (j + 1) * 44] = cols(cw[j], 44)
    cv0[:, C_CB:C_CB + 44] = cols(P("conv_b"), 44)
    cv0[:, C_INVF:C_INVF + 16] = (np.float32(10000.0) ** (-np.arange(16, dtype=np.float32) / np.float32(16)))[None, :]
    in_maps = []
    for c in range(8):
        b, q = c // 4, c % 4
        xcx = np.zeros((SEQ, D), np.float32)
        pc = np.zeros((SEQ,), np.int32)
        lo = 6144 - CH * q
        xcx[lo:] = x[b, 0: CH * (q + 1)]
        pc[lo:] = pos[b, 0: CH * (q + 1)]
        cvq = cv0.copy()
        for cidx in range(3):
            cvq[:, C_VB + cidx] = 0.0 if (cidx + q) >= 3 else NEG
        ridx = 0
        for g in range(3):
            d = DIL_D[g]
            for r_ in range(d):
                kk = np.arange(128)
                tau0 = 3968 + (2048 - 128 * d) + kk * d + r_
                cvq[:, C_DVB0 + ridx] = np.where(tau0 >= lo, 0.0, NEG)
                tau1 = 3968 + 2048 + kk * d + r_
                cvq[:, C_DVB1 + ridx] = np.where(tau1 >= lo, 0.0, NEG)
                ridx += 1
        cvq[:, C_UFLAG] = 1.0 if q > 0 else 0.0
        m = dict(shared)
        m["xc"] = xcx
        m["posc"] = np.ascontiguousarray(pc.reshape(64, 128).T)
        m["memx"] = np.ascontiguousarray(mem[b])
        m["cv"] = cvq
        in_maps.append(m)
    return in_maps


def kernel(**inputs):
    in_maps = host_inputs(inputs)
    nc, es, S = build_program()
    nc = _finish(nc, es, S)
    res = run_bass_kernel_spmd(nc, in_maps, core_ids=list(range(8)))
    out = np.zeros((2, SEQ, D), np.float32)
    for c in range(8):
        b, q = c // 4, c % 4
        out[b, q * CH:(q + 1) * CH] = np.asarray(res.results[c]["out"], np.float32)
    return out
```

```python
from contextlib import ExitStack
import numpy as np
import concourse.bass as bass
import concourse.mybir as mybir
from concourse.bass_utils import run_bass_kernel_spmd

F32, BF16, I32 = mybir.dt.float32, mybir.dt.bfloat16, mybir.dt.int32
ALU = mybir.AluOpType
AF = mybir.ActivationFunctionType
NEG = -30000.0
D = 1024
SEQ = 8192
CH = 2048
NE = 2176
NT = 17
DFF = 2816
DIN = 8864
OFF_Q, OFF_KV, OFF_KR, OFF_DIL, OFF_MEMQ = 384, 640, 672, 672 + 4608, 672 + 4608 + 512
DIL_D = (1, 4, 16)
EPS = 1e-6
TWO_PI = 6.283185307179586
(C_GPRE, C_GMEM, C_GFFN, C_QN, C_KVN, C_BG, C_CW, C_CB, C_VB, C_ZERO, C_INVF, C_DVB0, C_DVB1, C_UFLAG,
 NCV) = (0, 8, 16, 24, 27, 29, 53, 185, 229, 232, 233, 249, 270, 291, 292)
GROUPS = [(0, 128), (128, 512), (640, 512), (1152, 512), (1664, 512)]
DEBUG = False


class Sched:
    CE = ('pe', 'act', 'dve', 'pool')

    def __init__(s, nc, sems, dsems):
        s.eng = {'pe': nc.tensor, 'act': nc.scalar, 'dve': nc.vector, 'pool': nc.gpsimd, 'sp': nc.sync}
        s.prog = {e: [] for e in s.eng}
        s.sem = sems
        s.dsems = dsems
        s.tick = {e: 0 for e in s.CE}
        s.seen = {e: {} for e in s.eng}
        s.bufs = {}
        s.dcount = {q: 0 for q in dsems}

    def _semh(s, k):
        return s.sem[k] if isinstance(k, str) else s.dsems[k[1]][k[2]]

    def _deps(s, eng, r, w):
        need = {}

        def add(k, v):
            if need.get(k, 0) < v:
                need[k] = v
        for key in r:
            b = s.bufs.get(key)
            if b and b['w']:
                add(*b['w'])
        for key in w:
            b = s.bufs.get(key)
            if b:
                if b['w'] and b['w'][0] != eng:
                    add(*b['w'])
                for k, v in b['r'].items():
                    if k != eng:
                        add(k, v)
        return need

    def _waits(s, q, need):
        for k, v in need.items():
            if s.seen[q].get(k, 0) >= v:
                continue
            s.seen[q][k] = v
            sem = s._semh(k)
            s.prog[q].append(lambda E, sem=sem, v=v: E.wait_ge(sem, v))

    def _record(s, ev, r, w):
        for key in r:
            b = s.bufs.setdefault(key, {'w': None, 'r': {}})
            if b['r'].get(ev[0], 0) < ev[1]:
                b['r'][ev[0]] = ev[1]
        for key in w:
            s.bufs[key] = {'w': ev, 'r': {}}

    def op(s, eng, fn, r=(), w=()):
        s._waits(eng, s._deps(eng, r, w))
        s.tick[eng] += 1
        sem = s.sem[eng]
        s.prog[eng].append(lambda E, fn=fn, sem=sem: fn(E).then_inc(sem, 1))
        s._record((eng, s.tick[eng]), r, w)

    def dma(s, q, fn, r=(), w=()):
        i = s.dcount[q]
        s.dcount[q] += 1
        ns = len(s.dsems[q])
        j, val = i % ns, 16 * (i // ns + 1)
        need = s._deps(None, r, w)
        if i >= ns:
            k = ('d', q, j)
            need[k] = max(need.get(k, 0), val - 16)
        s._waits(q, need)
        sem = s.dsems[q][j]
        s.prog[q].append(lambda E, fn=fn, sem=sem: fn(E).then_inc(sem, 16))
        s._record((('d', q, j), val), r, w)

    def barrier(s):
        for e in s.eng:
            need = {}
            for c in s.CE:
                if s.tick[c] > 0 and c != e:
                    need[c] = s.tick[c]
            for q in s.dsems:
                n, ns = s.dcount[q], len(s.dsems[q])
                for j in range(min(n, ns)):
                    need[('d', q, j)] = 16 * ((n - 1 - j) // ns + 1)
            s._waits(e, need)


def dil_geom(d):
    nq_tot = NE // d
    qb = []
    a = 0
    while a < nq_tot:
        qb.append((a, min(128, nq_tot - a)))
        a += 128
    kt = [(0, 128)] + [(128 + a0, n) for (a0, n) in qb]
    return qb, kt


def build_program():
    nc = bass.Bass("TRN2", target_bir_lowering=False, dynamic_dma_scratch_size=4096)

    def din(name, shape, dt=F32):
        return nc.dram_tensor(name, list(shape), dt, kind="ExternalInput").ap()
    xc = din("xc", [SEQ, D])
    posc = din("posc", [128, 64], I32)
    memx = din("memx", [256, D])
    cvd = din("cv", [128, NCV])
    gbcd = din("gbc", [2, D])
    dbd = din("dbias", [12, 128, 256])
    trid = din("tri", [128, 128])
    identd = din("ident", [128, 128])
    w_in = din("w_in", [D, DIN])
    w_uq = din("w_uq", [384, 768])
    w_ukv = din("w_ukv", [256, 1024])
    w_memkv = din("w_mem_kv", [D, 1024])
    w_br = [din("w_br_mla", [512, D]), din("w_br_dil", [512, D]), din("w_br_mem", [512, D])]
    w_o = din("w_o", [D, D])
    w_up = din("w_ffn_up", [D, 2 * DFF])
    w_dn = din("w_ffn_down", [DFF, D])
    outd = nc.dram_tensor("out", [CH, D], F32, kind="ExternalOutput").ap()
    wupb = nc.dram_tensor("wupb_scratch", [D, 2 * DFF], BF16).ap()
    wdnb = nc.dram_tensor("wdnb_scratch", [DFF, D], BF16).ap()
    dbg = {}
    if DEBUG:
        for nm in ("dbg_ydil", "dbg_ymla", "dbg_ymem", "dbg_merged"):
            dbg[nm] = nc.dram_tensor(nm, [128, 8 if nm == "dbg_merged" else 4, NE], BF16, kind="ExternalOutput").ap()
        dbg["dbg_h"] = nc.dram_tensor("dbg_h", [128, 8, NE], BF16, kind="ExternalOutput").ap()

    END = 212992
    cur = [4608]

    def sbt(name, shape, dt, at=None):
        esz = 4 if dt in (F32, I32) else 2
        n = 1
        for v in shape[1:]:
            n *= v
        nbytes = (n * esz + 63) // 64 * 64
        if at is None:
            off = cur[0]
            cur[0] += nbytes
        else:
            off = at[0]
            at[0] += nbytes
        assert off + nbytes <= END, (name, off, nbytes)
        return nc.alloc_sbuf_tensor_at(name, list(shape), dt, offset=off)

    ident = sbt("ident", [128, 128], BF16)
    ones = sbt("ones", [128, 128], BF16)
    tri = sbt("tri_sb", [128, 128], BF16)
    cv = sbt("cv_sb", [128, NCV], F32)
    gbc = sbt("gbc_sb", [128, 2, D], F32)
    cosT = sbt("cosT", [128, 64, 16], F32)
    sinT = sbt("sinT", [128, 64, 16], F32)
    cosQ = sbt("cosQ", [128, NT, 16], F32)
    sinQ = sbt("sinQ", [128, NT, 16], F32)
    small = sbt("small", [128, 64], F32)
    junk = sbt("junk", [128, D], BF16)
    hT_own = sbt("hT_own", [128, 8, NE], BF16)
    W_OFF = cur[0]
    wsl = [sbt(f"wslot{i}", [128, 4096], BF16) for i in range(4)]
    Y_OFF = cur[0]
    y_dilT = sbt("y_dilT", [128, 4, NE], BF16)
    y_mlaT = sbt("y_mlaT", [128, 4, NE], BF16)
    y_memT = sbt("y_memT", [128, 4, NE], BF16)
    T_OFF = cur[0]
    YB = 17408

    psf = nc.alloc_psum_tensor("psf", [128, 3584], F32)
    psb = nc.alloc_psum_tensor("psb", [128, 1024], BF16)

    def PF(b, n=512, off=0):
        return psf[:, b * 512 + off: b * 512 + off + n]

    def cvc(c, p=128):
        return cv[0:p, c:c + 1]

    es = ExitStack()
    sems = {e: es.enter_context(nc.semaphore(f"s_{e}")) for e in Sched.CE}
    dsems = {q: [es.enter_context(nc.semaphore(f"d_{q}{i}")) for i in range(8)] for q in ('sp', 'pool')}
    S = Sched(nc, sems, dsems)
    name_ctr = [0]

    wctr = [0]

    def wload(parts):
        i = wctr[0] % 4
        wctr[0] += 1
        views = []
        off = 0
        for src in parts:
            k, n = src.shape[1], src.shape[2]
            v = wsl[i][:, off:off + k * n].rearrange("p (k n) -> p k n", n=n)
            off += k * n
            assert off <= 4096
            S.dma('pool', lambda E, v=v, src=src: E.dma_start(out=v, in_=src), w=[('w', i)])
            views.append(v)
        return views, ('w', i)

    def wload_b(src, rkeys):
        i = wctr[0] % 4
        wctr[0] += 1
        k, n = src.shape[1], src.shape[2]
        v = wsl[i][:, 0:k * n].rearrange("p (k n) -> p k n", n=n)
        S.dma('sp', lambda E: E.dma_start(out=v, in_=src), r=rkeys, w=[('w', i)])
        return v, ('w', i)

    def wk(w, c0, n, kp=None):
        return w[:, c0:c0 + n].rearrange("(k p) n -> p k n", p=128)

    def mm(out, pairs, w, r):
        def f(E):
            last = None
            for i, (l, rr) in enumerate(pairs):
                last = E.matmul(out, lhsT=l, rhs=rr, start=(i == 0), stop=(i == len(pairs) - 1))
            return last
        S.op('pe', f, r=r, w=w)

    def act(out, in_, func, r, w, **kw):
        S.op('act', lambda E: E.activation(out=out, in_=in_, func=func, **kw), r=r, w=w)

    def rstd_from_ss(ss_ap, out_ap, tmp_ap, n_feat, rkeys, wkeys, tkeys):
        act(tmp_ap, ss_ap, AF.Ln, r=rkeys, w=tkeys, scale=1.0 / n_feat, bias=EPS)
        act(out_ap, tmp_ap, AF.Exp, r=tkeys, w=wkeys, scale=-0.5)

    def rms_T(tiles, gcol, dst, dkey, xhat, xkey):
        n = len(tiles)
        for i, (ap, key) in enumerate(tiles):
            S.op('act', lambda E, ap=ap, i=i: E.activation(out=junk[:, :], in_=ap, func=AF.Square,
                                                           accum_out=small[:, i:i + 1]),
                 r=[key], w=['junk', ('sm', i)])
        rstd_from_ss(small[:, 0:n], small[:, 16:16 + n], small[:, 8:8 + n], 1024.0,
                     [('sm', i) for i in range(n)], ['smr'], ['sml'])
        for i, (ap, key) in enumerate(tiles):
            S.op('dve', lambda E, ap=ap, i=i: E.tensor_scalar(out=xhat[:, i, :], in0=ap,
                                                              scalar1=small[:, 16 + i:17 + i], scalar2=None,
                                                              op0=ALU.mult),
                 r=[key, 'smr'], w=[(xkey, i)])
        for kc2 in range(4):
            def f(E, kc2=kc2):
                last = None
                for j in range(2):
                    kc = kc2 * 2 + j
                    for i in range(n):
                        last = E.transpose(psb[:, j * 512 + i * 128: j * 512 + (i + 1) * 128],
                                           xhat[:, i, kc * 128:(kc + 1) * 128], ident[:, :])
                return last
            S.op('pe', f, r=[(xkey, i) for i in range(n)] + ['ident'], w=['psb'])
            for j in range(2):
                kc = kc2 * 2 + j
                if j == 0:
                    act(dst(kc), psb[:, j * 512: j * 512 + n * 128], AF.Copy, r=['psb'], w=[dkey],
                        scale=cvc(gcol + kc))
                else:
                    S.op('dve', lambda E, kc=kc, j=j: E.tensor_scalar(out=dst(kc), in0=psb[:, j * 512: j * 512 + n * 128],
                                                                      scalar1=cvc(gcol + kc), scalar2=None, op0=ALU.mult),
                         r=['psb', 'cv'], w=[dkey])

    def rms_pipe(calls, gcol, xbufs, xhats, xname):
        def stageA(c):
            call = calls[c]
            par = c % 2
            base = par * 24
            n = len(call['srcs'])
            tiles = [xload(xbufs, src) for src in call['srcs']]
            for i, (ap, key) in enumerate(tiles):
                S.op('act', lambda E, ap=ap, i=i: E.activation(out=junk[:, :], in_=ap, func=AF.Square,
                                                               accum_out=small[:, base + i:base + i + 1]),
                     r=[key], w=['junk', ('sm', base + i)])
            rstd_from_ss(small[:, base:base + n], small[:, base + 16:base + 16 + n], small[:, base + 8:base + 8 + n],
                         1024.0, [('sm', base + i) for i in range(n)], [('smr', par)], [('sml', par)])
            for i, (ap, key) in enumerate(tiles):
                S.op('dve', lambda E, ap=ap, i=i: E.tensor_scalar(out=xhats[par][:, i, :], in0=ap,
                                                                  scalar1=small[:, base + 16 + i:base + 17 + i],
                                                                  scalar2=None, op0=ALU.mult),
                     r=[key, ('smr', par)], w=[(xname, par, i)])

        def stageB(c):
            call = calls[c]
            par = c % 2
            n = len(call['srcs'])
            dst, dkey = call['dst'], call['dkey']
            xh = xhats[par]
            for kc4 in range(2):
                def f(E, kc4=kc4):
                    last = None
                    for j in range(4):
                        kc = kc4 * 4 + j
                        for i in range(n):
                            last = E.transpose(psb[:, j * 256 + i * 128: j * 256 + (i + 1) * 128],
                                               xh[:, i, kc * 128:(kc + 1) * 128], ident[:, :])
                    return last
                S.op('pe', f, r=[(xname, par, i) for i in range(n)] + ['ident'], w=['psb'])
                for j in range(4):
                    kc = kc4 * 4 + j
                    S.op('dve', lambda E, kc=kc, j=j: E.tensor_scalar(out=dst(kc), in0=psb[:, j * 256: j * 256 + n * 128],
                                                                      scalar1=cvc(gcol + kc), scalar2=None,
                                                                      op0=ALU.mult), r=['psb', 'cv'], w=[dkey])
            if call.get('after'):
                call['after']()

        stageA(0)
        for c in range(len(calls)):
            if c + 1 < len(calls):
                stageA(c + 1)
            stageB(c)

    xctr = [0]

    def xload(xbufs, src):
        i = xctr[0] % len(xbufs)
        xctr[0] += 1
        dst_ = xbufs[i][:, :]
        S.dma('sp', lambda E: E.dma_start(out=dst_, in_=src), w=[('xb', i)])
        return dst_, ('xb', i)

    tp = [T_OFF]
    tmp32 = sbt("setup_tmp", [128, 128], F32, at=tp)
    S.dma('sp', lambda E: E.dma_start(out=cv[:, :], in_=cvd[:, :]), w=['cv'])
    S.dma('sp', lambda E: E.dma_start(out=gbc[:, :, :], in_=gbcd.partition_broadcast(128)), w=['gbc'])
    S.dma('pool', lambda E: E.dma_start(out=ident[:, :], in_=identd[:, :]), w=['ident'])
    S.dma('pool', lambda E: E.dma_start(out=tri[:, :], in_=trid[:, :]), w=['tri'])
    S.op('dve', lambda E: E.memset(ones[:, :], 1.0), w=['ones'])
    posi = sbt("posi", [128, 64], I32, at=tp)
    posf = sbt("posf", [128, 64], F32, at=tp)
    ang = sbt("ang", [128, 64, 16], F32, at=tp)
    kf = sbt("kf", [128, 64, 16], F32, at=tp)
    ki = sbt("ki", [128, 64, 16], I32, at=tp)
    S.dma('sp', lambda E: E.dma_start(out=posi[:, :], in_=posc[:, :]), w=['posi'])
    S.op('dve', lambda E: E.tensor_copy(out=posf[:, :], in_=posi[:, :]), r=['posi'], w=['posf'])
    S.op('dve', lambda E: E.tensor_tensor(out=ang[:, :, :], in0=posf[:, :].unsqueeze(2).to_broadcast([128, 64, 16]),
                                          in1=cv[:, C_INVF:C_INVF + 16].unsqueeze(1).to_broadcast([128, 64, 16]),
                                          op=ALU.mult), r=['posf', 'cv'], w=['ang'])
    for tab, shift in ((sinT, 0.0), (cosT, np.pi / 2)):
        S.op('dve', lambda E, shift=shift: E.tensor_scalar(out=kf[:, :, :], in0=ang[:, :, :], scalar1=float(shift),
                                                           scalar2=float(1.0 / TWO_PI), op0=ALU.add, op1=ALU.mult),
             r=['ang'], w=['kf'])
        S.op('dve', lambda E: E.tensor_copy(out=ki[:, :, :], in_=kf[:, :, :]), r=['kf'], w=['ki'])
        S.op('dve', lambda E: E.tensor_copy(out=kf[:, :, :], in_=ki[:, :, :]), r=['ki'], w=['kf'])
        S.op('dve', lambda E: E.scalar_tensor_tensor(out=kf[:, :, :], in0=kf[:, :, :], scalar=float(-TWO_PI),
                                                     in1=ang[:, :, :], op0=ALU.mult, op1=ALU.add),
             r=['kf', 'ang'], w=['kf'])
        S.op('dve', lambda E, shift=shift: E.tensor_scalar(out=kf[:, :, :], in0=kf[:, :, :], scalar1=float(shift),
                                                           scalar2=3.1415925, op0=ALU.add, op1=ALU.min),
             r=['kf'], w=['kf'])
        S.op('dve', lambda E: E.tensor_scalar(out=kf[:, :, :], in0=kf[:, :, :], scalar1=-3.1415925, scalar2=None,
                                              op0=ALU.max), r=['kf'], w=['kf'])
        act(tab[:, :, :], kf[:, :, :], AF.Sin, r=['kf'], w=['rope'])
    qs = 96.0 ** -0.5
    S.op('dve', lambda E: E.tensor_scalar(out=cosQ[:, :, :], in0=cosT[:, 47:64, :], scalar1=qs, scalar2=None,
                                          op0=ALU.mult), r=['rope'], w=['ropeq'])
    S.op('dve', lambda E: E.tensor_scalar(out=sinQ[:, :, :], in0=sinT[:, 47:64, :], scalar1=qs, scalar2=None,
                                          op0=ALU.mult), r=['rope'], w=['ropeq'])
    S.barrier()

    tp = [Y_OFF + YB]
    xbufA = [sbt(f"xbufA{i}", [128, D], F32, at=tp) for i in range(2)]
    xhatA = sbt("xhatA", [128, 4, D], BF16, at=tp)
    hT_halo = sbt("hT_halo", [128, 8, 2048], BF16, at=tp)
    QTd = sbt("QTd", [128, NE], BF16, at=tp)
    KTd = sbt("KTd", [128, 4224], BF16, at=tp)
    Vd = sbt("Vd", [128, 48, 128], BF16, at=tp)
    accO = sbt("accO", [128, NE], F32, at=tp)
    accD = sbt("accD", [128, NE], F32, at=tp)
    PTd = [sbt(f"PTd{i}", [128, 256], BF16, at=tp) for i in range(4)]
    ssb = [sbt(f"ssb{i}", [128, 256], F32, at=tp) for i in range(2)]
    dbt = [sbt(f"dbt{i}", [128, 256], F32, at=tp) for i in range(2)]

    def prep_all(row0, ntiles, dstbuf, dkey):
        t = 0
        while t < ntiles:
            n = min(2, ntiles - t)
            tiles = [xload(xbufA, xc[row0 + (t + i) * 128: row0 + (t + i + 1) * 128, :]) for i in range(n)]
            t0 = t
            rms_T(tiles, C_GPRE, lambda kc, t0=t0, n=n: dstbuf[:, kc, t0 * 128:(t0 + n) * 128], dkey, xhatA, 'xhA')
            t += n

    xhatsA = [xhatA[:, 0:2, :], xhatA[:, 2:4, :]]
    callsA = []
    for (row0, ntiles, dstbuf, dkey) in ((3968, 16, hT_halo, 'hTh'), (6016, NT, hT_own, 'hT')):
        t = 0
        while t < ntiles:
            n = min(2, ntiles - t)
            callsA.append(dict(srcs=[xc[row0 + (t + i) * 128: row0 + (t + i + 1) * 128, :] for i in range(n)],
                               dst=(lambda kc, t=t, n=n, dstbuf=dstbuf: dstbuf[:, kc, t * 128:(t + n) * 128]), dkey=dkey))
            t += n
    xbufA4 = xbufA + [accO[:, 0:D], accO[:, D:2 * D]]
    rms_pipe(callsA, C_GPRE, xbufA4, xhatsA, 'xhA')
    S.barrier()
    if DEBUG:
        S.dma('sp', lambda E: E.dma_start(out=dbg["dbg_h"][:, :, :], in_=hT_own[:, :, :]), r=['hT'])

    dscale = 128.0 ** -0.5
    bank_rr = [0]

    def nb():
        bank_rr[0] = (bank_rr[0] + 1) % 7
        return bank_rr[0]

    for hd in range(4):
        for g in range(3):
            d = DIL_D[g]
            gi = g * 4 + hd
            qb, kt = dil_geom(d)
            nql = NE // d
            nkl = 128 + nql
            c0 = OFF_KR + g * 512 + hd * 128
            (wq_, wk_, wv_), wkey = wload([wk(w_in, c0, 128), wk(w_in, c0 + 1536, 128), wk(w_in, c0 + 3072, 128)])
            bi = gi % 2
            S.dma('sp', lambda E, bi=bi, gi=gi: E.dma_start(out=dbt[bi][:, :], in_=dbd[gi, :, :]), w=[('dbt', bi)])
            QV = QTd[:, :].rearrange("p (r l) -> p r l", r=d)
            KV = KTd[:, 0:d * nkl].rearrange("p (r l) -> p r l", r=d)
            for (e0, n) in GROUPS:
                for which, wv3, dstv, loff, sc in ((0, wq_, QV, 0, dscale), (1, wk_, KV, 128, 1.0)):
                    b = nb()
                    mm(PF(b, n), [(wv3[:, kc, :], hT_own[:, kc, e0:e0 + n]) for kc in range(8)],
                       w=[('pf', b)], r=[wkey, 'hT'])
                    act(dstv[:, :, loff + e0 // d: loff + (e0 + n) // d],
                        PF(b, n).rearrange("p (l r) -> p r l", r=d), AF.Copy,
                        r=[('pf', b)], w=['QTd' if which == 0 else 'KTd'], scale=sc)
            h0 = 2048 - 128 * d
            while h0 < 2048:
                n = min(512, 2048 - h0)
                b = nb()
                mm(PF(b, n), [(wk_[:, kc, :], hT_halo[:, kc, h0:h0 + n]) for kc in range(8)],
                   w=[('pf', b)], r=[wkey, 'hTh'])
                l0 = (h0 - (2048 - 128 * d)) // d
                act(KV[:, :, l0: l0 + n // d], PF(b, n).rearrange("p (l r) -> p r l", r=d), AF.Copy,
                    r=[('pf', b)], w=['KTd'])
                h0 += n
            ntile = len(kt)
            for r_ in range(d):
                for j, (a, nk) in enumerate(kt):
                    if j == 0:
                        s0 = 2048 - 128 * d + r_
                        src = lambda kc, s0=s0, nk=nk: hT_halo[:, kc, s0: s0 + (nk - 1) * d + 1: d]
                        rk = 'hTh'
                    else:
                        s0 = (a - 128) * d + r_
                        src = lambda kc, s0=s0, nk=nk: hT_own[:, kc, s0: s0 + (nk - 1) * d + 1: d]
                        rk = 'hT'
                    b = nb()
                    mm(psf[0:nk, b * 512: b * 512 + 128], [(src(kc), wv_[:, kc, :]) for kc in range(8)],
                       w=[('pf', b)], r=[wkey, rk])
                    ti = r_ * ntile + j
                    S.op('dve', lambda E, nk=nk, b=b, ti=ti: E.tensor_copy(out=Vd[0:nk, ti, :],
                                                                         in_=psf[0:nk, b * 512: b * 512 + 128]),
                         r=[('pf', b)], w=['Vd'])
            for r_ in range(d):
                info = {}

                def dil_pv(m, r_=r_, info=info, d=d, g=g, qb=qb, ntile=ntile):
                    qa, qn = qb[m]
                    pp, pcol, _ = info[m]
                    pi, _, nk = info[m + 1]
                    b2 = nb()
                    tprev = r_ * ntile + m
                    tcur = r_ * ntile + m + 1

                    def f(E):
                        o = psf[:, b2 * 512: b2 * 512 + qn]
                        dd = psf[:, b2 * 512 + 128: b2 * 512 + 128 + qn]
                        E.matmul(o, lhsT=Vd[:, tprev, :], rhs=PTd[pp][:, pcol:pcol + qn], start=True, stop=False)
                        E.matmul(o, lhsT=Vd[0:nk, tcur, :], rhs=PTd[pi][0:nk, 0:qn], start=False, stop=True)
                        E.matmul(dd, lhsT=ones[:, :], rhs=PTd[pp][:, pcol:pcol + qn], start=False, stop=False,
                                 skip_group_check=True)
                        return E.matmul(dd, lhsT=ones[0:nk, :], rhs=PTd[pi][0:nk, 0:qn], start=False, stop=True,
                                        skip_group_check=True)
                    S.op('pe', f, r=['Vd', ('PTd', pp), ('PTd', pi), 'ones'], w=[('pf', b2)])
                    e_s = qa * d + r_
                    e_e = e_s + (qn - 1) * d + 1
                    for accb, off, akey in ((accO, 0, 'accO'), (accD, 128, 'accD')):
                        if g == 0:
                            S.op('dve', lambda E, accb=accb, off=off: E.tensor_copy(
                                out=accb[:, e_s:e_e:d], in_=psf[:, b2 * 512 + off: b2 * 512 + off + qn]),
                                r=[('pf', b2)], w=[akey])
                        else:
                            S.op('dve', lambda E, accb=accb, off=off: E.tensor_tensor(
                                out=accb[:, e_s:e_e:d], in0=psf[:, b2 * 512 + off: b2 * 512 + off + qn],
                                in1=accb[:, e_s:e_e:d], op=ALU.add), r=[('pf', b2), akey], w=[akey])

                for j, (a, nk) in enumerate(kt):
                    has_diag = j >= 1
                    has_prev = j < len(qb)
                    ncol = (qb[j - 1][1] if has_diag else 0) + (qb[j][1] if has_prev else 0)
                    qlo = qb[j - 1][0] if has_diag else qb[j][0]
                    boff = 0 if has_diag else 128
                    b = nb()
                    mm(psf[0:nk, b * 512: b * 512 + ncol], [(KV[:, r_, a:a + nk], QV[:, r_, qlo:qlo + ncol])],
                       w=[('pf', b)], r=['KTd', 'QTd'])
                    si = j % 2
                    S.op('dve', lambda E, nk=nk, b=b, ncol=ncol, si=si, boff=boff, bi=bi: E.tensor_tensor(
                        out=ssb[si][0:nk, 0:ncol], in0=psf[0:nk, b * 512: b * 512 + ncol],
                        in1=dbt[bi][0:nk, boff:boff + ncol], op=ALU.add),
                        r=[('pf', b), ('dbt', bi)], w=[('ssb', si)])
                    ridx = (0, 1, 5)[g] + r_
                    bcol = C_DVB0 + ridx if j == 0 else (C_DVB1 + ridx if j == 1 else C_ZERO)
                    pi = j % 4
                    act(PTd[pi][0:nk, 0:ncol], ssb[si][0:nk, 0:ncol], AF.Exp, r=[('ssb', si), 'cv'],
                        w=[('PTd', pi)], bias=cvc(bcol, nk))
                    info[j] = (pi, (qb[j - 1][1] if has_diag else 0), nk)
                    if j >= 2:
                        dil_pv(j - 2)
                dil_pv(len(kt) - 2)
        S.op('dve', lambda E: E.tensor_scalar(out=accD[:, :], in0=accD[:, :], scalar1=1e-30, scalar2=None,
                                              op0=ALU.max), r=['accD'], w=['accD'])
        S.op('dve', lambda E: E.reciprocal(out=accD[:, :], in_=accD[:, :]), r=['accD'], w=['accD'])
        S.op('dve', lambda E, hd=hd: E.tensor_tensor(out=y_dilT[:, hd, :], in0=accO[:, :], in1=accD[:, :],
                                                     op=ALU.mult), r=['accO', 'accD'], w=['ydil'])
    if DEBUG:
        S.dma('sp', lambda E: E.dma_start(out=dbg["dbg_ydil"][:, :, :], in_=y_dilT[:, :, :]), r=['ydil'])
    S.barrier()


    ckvT = nc.alloc_sbuf_tensor_at("ckvT", [128, 2, SEQ], BF16, offset=W_OFF)
    tp = [Y_OFF + 2 * YB]
    KT = sbt("KT", [128, SEQ], BF16, at=tp)
    cqT = sbt("cqT", [128, 3, NE], BF16, at=tp)
    Wuq = sbt("Wuq", [128, 3, 768], BF16, at=tp)
    Wukv = sbt("Wukv", [128, 2, 1024], BF16, at=tp)
    OV = tp[0]
    tp = [OV]
    xbufB = [nc.alloc_sbuf_tensor_at(f"xbufB{i}", [128, D], F32, offset=Y_OFF + YB + i * 4096) for i in range(4)]
    Wq = sbt("Wq", [128, 8, 384], BF16, at=tp)
    xhatB = sbt("xhatB", [128, 4, D], BF16, at=tp)
    hTgs = [sbt(f"hTg{i}", [128, 8, 512], BF16, at=tp) for i in range(2)]
    Wkvr = sbt("Wkvr", [128, 8, 288], BF16, at=tp)
    sq = sbt("sq", [128, 3, 512], BF16, at=tp)
    rbc = sbt("rbc", [128, 512], F32, at=tp)
    kpe = sbt("kpe", [128, 4, 96], BF16, at=tp)
    kt4 = [sbt(f"kt4_{i}", [128, 4, 16], F32, at=tp) for i in range(4)]

    S.dma('pool', lambda E: E.dma_start(out=Wkvr[:, :, :], in_=wk(w_in, OFF_Q, 288)), w=['Wkvr'])
    S.dma('pool', lambda E: E.dma_start(out=Wq[:, :, :], in_=wk(w_in, 0, 384)), w=['Wq'])
    S.dma('pool', lambda E: E.dma_start(out=Wuq[:, :, :], in_=w_uq.rearrange("(k p) n -> p k n", p=128)), w=['Wuq'])
    S.dma('pool', lambda E: E.dma_start(out=Wukv[:, :, :], in_=w_ukv.rearrange("(k p) n -> p k n", p=128)),
          w=['Wukv'])
    S.op('dve', lambda E: E.memset(kpe[:, :, :], 0.0), w=['kpe'])

    def latent_norm(banks, n, nchunk, gcol, dstf, dkey):
        for kc in range(nchunk):
            act(sq[:, kc, 0:n], PF(banks[kc], n), AF.Square, r=[('pf', banks[kc])], w=[('sq', kc)])
        bs = nb()
        mm(PF(bs, n), [(ones[:, :], sq[:, kc, 0:n]) for kc in range(nchunk)], w=[('pf', bs)],
           r=['ones'] + [('sq', kc) for kc in range(nchunk)])
        act(rbc[:, 0:n], PF(bs, n), AF.Ln, r=[('pf', bs)], w=['rbc'], scale=1.0 / (128 * nchunk), bias=EPS)
        act(rbc[:, 0:n], rbc[:, 0:n], AF.Exp, r=['rbc'], w=['rbc'], scale=-0.5)
        for kc in range(nchunk):
            S.op('dve', lambda E, kc=kc: E.scalar_tensor_tensor(out=dstf(kc), in0=PF(banks[kc], n),
                                                                scalar=cvc(gcol + kc), in1=rbc[:, 0:n],
                                                                op0=ALU.mult, op1=ALU.mult),
                 r=[('pf', banks[kc]), 'rbc', 'cv'], w=[dkey])

    def rope_tm(src3, cos3, sin3, dst3, nt, rkeys, wkey):
        a, b_ = src3[:, :, 0:16], src3[:, :, 16:32]
        t = [k[:, 0:nt, :] for k in kt4]
        S.op('dve', lambda E: E.tensor_tensor(out=t[0], in0=a, in1=cos3, op=ALU.mult), r=rkeys, w=[('kt4', 0)])
        S.op('dve', lambda E: E.tensor_tensor(out=t[1], in0=b_, in1=sin3, op=ALU.mult), r=rkeys, w=[('kt4', 1)])
        S.op('dve', lambda E: E.tensor_tensor(out=dst3[:, :, 0:16], in0=t[0], in1=t[1], op=ALU.subtract),
             r=[('kt4', 0), ('kt4', 1)], w=[wkey])
        S.op('dve', lambda E: E.tensor_tensor(out=t[2], in0=b_, in1=cos3, op=ALU.mult), r=rkeys, w=[('kt4', 2)])
        S.op('dve', lambda E: E.tensor_tensor(out=t[3], in0=a, in1=sin3, op=ALU.mult), r=rkeys, w=[('kt4', 3)])
        S.op('dve', lambda E: E.tensor_tensor(out=dst3[:, :, 16:32], in0=t[2], in1=t[3], op=ALU.add),
             r=[('kt4', 2), ('kt4', 3)], w=[wkey])

    xctr[0] = 0

    lat = {}

    def ctx_latent1(tg):
        hTg = hTgs[tg % 2]
        hk = ('hTg', tg % 2)
        banks = [nb(), nb()]
        for kc2 in range(2):
            mm(PF(banks[kc2]), [(Wkvr[:, kc, kc2 * 128:(kc2 + 1) * 128], hTg[:, kc, :]) for kc in range(8)],
               w=[('pf', banks[kc2])], r=['Wkvr', hk])
        bk = nb()

        def fk(E):
            last = None
            for i in range(4):
                for kc in range(8):
                    last = E.matmul(psf[:, bk * 512 + i * 32: bk * 512 + (i + 1) * 32],
                                    lhsT=hTg[:, kc, i * 128:(i + 1) * 128], rhs=Wkvr[:, kc, 256:288],
                                    start=(kc == 0), stop=(kc == 7))
            return last
        S.op('pe', fk, r=['Wkvr', hk], w=[('pf', bk)])
        for kc in range(2):
            act(sq[:, kc, 0:512], PF(banks[kc]), AF.Square, r=[('pf', banks[kc])], w=[('sq', kc)])
        src3 = psf[:, bk * 512: bk * 512 + 128].rearrange("p (t c) -> p t c", c=32)
        rope_tm(src3, cosT[:, tg * 4:(tg + 1) * 4, :], sinT[:, tg * 4:(tg + 1) * 4, :], kpe[:, :, 64:96], 4,
                [('pf', bk), 'rope'], 'kpe')
        lat[tg] = banks

    def ctx_latent2(tg):
        banks = lat[tg]
        n = 512
        bs = nb()
        mm(PF(bs, n), [(ones[:, :], sq[:, kc, 0:n]) for kc in range(2)], w=[('pf', bs)],
           r=['ones'] + [('sq', kc) for kc in range(2)])
        act(rbc[:, 0:n], PF(bs, n), AF.Ln, r=[('pf', bs)], w=['rbc'], scale=1.0 / 256, bias=EPS)
        act(rbc[:, 0:n], rbc[:, 0:n], AF.Exp, r=['rbc'], w=['rbc'], scale=-0.5)
        for kc in range(2):
            S.op('dve', lambda E, kc=kc: E.scalar_tensor_tensor(out=ckvT[:, kc, tg * 512:(tg + 1) * 512],
                                                                in0=PF(banks[kc], n), scalar=cvc(C_KVN + kc),
                                                                in1=rbc[:, 0:n], op0=ALU.mult, op1=ALU.mult),
                 r=[('pf', banks[kc]), 'rbc', 'cv'], w=['ckvT'])

        def ft(E):
            last = None
            for i in range(4):
                last = E.transpose(psb[0:96, i * 128:(i + 1) * 128], kpe[:, i, :], ident[:, :])
            return last
        S.op('pe', ft, r=['kpe', 'ident'], w=['psb'])
        act(KT[64:96, tg * 512:(tg + 1) * 512], psb[64:96, 0:512], AF.Copy, r=['psb'], w=['KTr'])

    xhatsB = [xhatB[:, 0:2, :], xhatB[:, 2:4, :]]
    callsB = []
    for tg in range(16):
        for hh in range(2):
            callsB.append(dict(
                srcs=[xc[(tg * 4 + hh * 2 + i) * 128:(tg * 4 + hh * 2 + i + 1) * 128, :] for i in range(2)],
                dst=(lambda kc, tg=tg, hh=hh: hTgs[tg % 2][:, kc, hh * 256:(hh + 1) * 256]), dkey=('hTg', tg % 2),
                after=((lambda tg=tg: ctx_latent1(tg)) if hh == 1 else
                       ((lambda tg=tg: ctx_latent2(tg - 1)) if tg >= 1 else None))))
    rms_pipe(callsB, C_GPRE, xbufB, xhatsB, 'xhB')
    ctx_latent2(15)
    for (e0, n) in GROUPS:
        banks = [nb(), nb(), nb()]
        for kc3 in range(3):
            mm(PF(banks[kc3], n), [(Wq[:, kc, kc3 * 128:(kc3 + 1) * 128], hT_own[:, kc, e0:e0 + n]) for kc in range(8)],
               w=[('pf', banks[kc3])], r=['Wq', 'hT'])
        latent_norm(banks, n, 3, C_QN, lambda kc, e0=e0, n=n: cqT[:, kc, e0:e0 + n], 'cqT')
    S.barrier()

    for c in range(4):
        S.dma('pool', lambda E, c=c: E.dma_start(out=wupb[:, c * 1408:(c + 1) * 1408], in_=w_up[:, c * 1408:(c + 1) * 1408]),
              w=[('wupb', c)])
    for r_ in range(2):
        S.dma('pool', lambda E, r_=r_: E.dma_start(out=wdnb[r_ * 1408:(r_ + 1) * 1408, :], in_=w_dn[r_ * 1408:(r_ + 1) * 1408, :]),
              w=[('wdnb', r_)])
    tp = [OV]
    Vts = [sbt(f"Vt{i}", [128, 64, 65], BF16, at=tp) for i in range(2)]
    QTs = [sbt(f"QT{i}", [128, NE], BF16, at=tp) for i in range(2)]
    qtm = sbt("qtm", [128, NT, 96], BF16, at=tp)
    ypair = sbt("ypair", [128, NT, 128], BF16, at=tp)
    PTm = [sbt(f"PTm{i}", [128, 1024], BF16, at=tp) for i in range(3)]
    rden = sbt("rden", [128, 8], F32, at=tp)
    kt4 = [sbt(f"kq4_{i}", [128, 5, 16], F32, at=tp) for i in range(4)]
    for Vt_ in Vts:
        S.op('dve', lambda E, Vt_=Vt_: E.memset(Vt_[:, :, 64:65], 1.0), w=['Vones'])
    evac_rr = [0]

    def evac(out, in_, r, w, **kw):
        evac_rr[0] += 1
        if not kw:
            S.op('dve', lambda E: E.tensor_copy(out=out, in_=in_), r=r, w=w)
        else:
            act(out, in_, AF.Copy, r=r, w=w, **kw)

    pctr = [0]
    for h in range(8):
        for tg in range(16):
            b = nb()
            mm(psf[0:64, b * 512:(b + 1) * 512],
               [(Wukv[:, kc, h * 128: h * 128 + 64], ckvT[:, kc, tg * 512:(tg + 1) * 512]) for kc in range(2)],
               w=[('pf', b)], r=['Wukv', 'ckvT'])
            evac(KT[0:64, tg * 512:(tg + 1) * 512], psf[0:64, b * 512:(b + 1) * 512], r=[('pf', b)], w=['KTn'])
        def build_vq(h, bankf):
            Vt, QT = Vts[h % 2], QTs[h % 2]
            vk, qk = ('Vt', h % 2), ('QT', h % 2)
            chunks = []

            def vchunk(t8):
                b = bankf()

                def fv(E):
                    last = None
                    for i in range(8):
                        for kc in range(2):
                            last = E.matmul(psf[:, b * 512 + i * 64: b * 512 + (i + 1) * 64],
                                            lhsT=ckvT[:, kc, (t8 * 8 + i) * 128:(t8 * 8 + i + 1) * 128],
                                            rhs=Wukv[:, kc, h * 128 + 64: h * 128 + 128], start=(kc == 0), stop=(kc == 1))
                    return last
                S.op('pe', fv, r=['Wukv', 'ckvT'], w=[('pf', b)])
                evac(Vt[:, t8 * 8:(t8 + 1) * 8, 0:64], PF(b).rearrange("p (t e) -> p t e", e=64), r=[('pf', b)], w=[vk])

            def qchunk(t0):
                nt = min(5, NT - t0)
                b = bankf()

                def fq(E):
                    last = None
                    for i in range(nt):
                        for kc in range(3):
                            last = E.matmul(psf[:, b * 512 + i * 96: b * 512 + (i + 1) * 96],
                                            lhsT=cqT[:, kc, (t0 + i) * 128:(t0 + i + 1) * 128],
                                            rhs=Wuq[:, kc, h * 96:(h + 1) * 96], start=(kc == 0), stop=(kc == 2))
                    return last
                S.op('pe', fq, r=['Wuq', 'cqT'], w=[('pf', b)])
                p3 = psf[:, b * 512: b * 512 + nt * 96].rearrange("p (t c) -> p t c", c=96)
                S.op('dve', lambda E: E.tensor_scalar(out=qtm[:, t0:t0 + nt, 0:64], in0=p3[:, :, 0:64],
                                                      scalar1=qs, scalar2=None, op0=ALU.mult),
                     r=[('pf', b)], w=['qtm'])
                rope_tm(p3[:, :, 64:96], cosQ[:, t0:t0 + nt, :], sinQ[:, t0:t0 + nt, :], qtm[:, t0:t0 + nt, 64:96], nt,
                        [('pf', b), 'ropeq'], 'qtm')

            def tchunk(t0):
                nt = min(8, NT - t0)

                def ftq(E):
                    last = None
                    for i in range(nt):
                        last = E.transpose(psb[0:96, i * 128:(i + 1) * 128], qtm[:, t0 + i, :], ident[:, :])
                    return last
                S.op('pe', ftq, r=['qtm', 'ident'], w=['psb'])
                S.op('dve', lambda E: E.tensor_copy(out=QT[0:96, t0 * 128:(t0 + nt) * 128],
                                                    in_=psb[0:96, 0:nt * 128]), r=['psb'], w=[qk])
            for t8 in range(8):
                chunks.append(lambda t8=t8: vchunk(t8))
            for t0 in range(0, NT, 5):
                chunks.append(lambda t0=t0: qchunk(t0))
            for t0 in range(0, NT, 8):
                chunks.append(lambda t0=t0: tchunk(t0))
            return chunks

        if h == 0:
            for ch in build_vq(0, nb):
                ch()
        Vt, QT = Vts[h % 2], QTs[h % 2]
        vk, qk = ('Vt', h % 2), ('QT', h % 2)
        items = []
        for gq, (e0, n) in enumerate(GROUPS):
            tq0, ntq = e0 // 128, n // 128
            kfirst = 47 + tq0
            per = 1024 // n
            units = []
            kb = 0
            while kb < kfirst:
                lim = min(kfirst, (kb // 16 + 1) * 16, kb + per)
                units.append((list(range(kb, lim)), None))
                kb = lim
            for m in range(ntq):
                units.append(([kfirst + m], m))
            for ui, (kbs, m) in enumerate(units):
                items.append(dict(e0=e0, n=n, tq0=tq0, ntq=ntq, ob=4 + (gq % 2), kbs=kbs, m=m, first=(ui == 0),
                                  last=(ui == len(units) - 1)))

        def emit_S(it, i):
            sp, n, e0, m = ((pctr[0] + i) % 2) * 2, it['n'], it['e0'], it['m']
            if m is None:
                def fs(E, it=it, sp=sp, n=n, e0=e0, QT=QT):
                    last = None
                    for idx, kbk in enumerate(it['kbs']):
                        last = E.matmul(psf[:, sp * 512 + idx * n: sp * 512 + (idx + 1) * n],
                                        lhsT=KT[0:96, kbk * 128:(kbk + 1) * 128], rhs=QT[0:96, e0:e0 + n],
                                        start=True, stop=True)
                    return last
                S.op('pe', fs, r=['KTn', 'KTr', qk], w=[('pf', sp), ('pf', sp + 1)])
            else:
                kbk = it['kbs'][0]
                cols = n - 128 * m
                mm(psf[:, sp * 512: sp * 512 + cols],
                   [(KT[0:96, kbk * 128:(kbk + 1) * 128], QT[0:96, e0 + 128 * m:e0 + n])],
                   w=[('pf', sp), ('pf', sp + 1)], r=['KTn', 'KTr', qk])

        def emit_A(it, i):
            sp, pi = ((pctr[0] + i) % 2) * 2, (pctr[0] + i) % 3
            n, m = it['n'], it['m']
            kb0 = it['kbs'][0]
            tot = len(it['kbs']) * n if m is None else n - 128 * m
            act(PTm[pi][:, 0:tot], psf[:, sp * 512: sp * 512 + tot], AF.Exp,
                r=[('pf', sp), ('pf', sp + 1), 'cv'], w=[('PTm', pi)], bias=cvc(C_VB + min(kb0 // 16, 3)))
            if m is not None:
                S.op('dve', lambda E, pi=pi: E.tensor_tensor(out=PTm[pi][:, 0:128], in0=PTm[pi][:, 0:128],
                                                             in1=tri[:, :], op=ALU.mult),
                     r=[('PTm', pi), 'tri'], w=[('PTm', pi)])

        def emit_PV(it, i):
            pi = (pctr[0] + i) % 3
            n, m, ntq, ob, tq0 = it['n'], it['m'], it['ntq'], it['ob'], it['tq0']
            if m is None:
                blocks = [(idx * n, kbk, list(range(ntq))) for idx, kbk in enumerate(it['kbs'])]
            else:
                blocks = [(0, it['kbs'][0], list(range(m, ntq)))]

            def pv(E, pi=pi, blocks=blocks, ob=ob, first=it['first'], Vt=Vt):
                last = None
                st = first
                for (col0, kbk, qts) in blocks:
                    for qt in qts:
                        last = E.matmul(psf[:, ob * 512 + qt * 65: ob * 512 + qt * 65 + 65],
                                        lhsT=PTm[pi][:, col0 + (qt - qts[0]) * 128: col0 + (qt - qts[0] + 1) * 128],
                                        rhs=Vt[:, kbk, :], start=st, stop=False, skip_group_check=True)
                        st = False
                return last
            S.op('pe', pv, r=[('PTm', pi), vk, 'Vones'], w=[('pf', ob)])
            if it['last']:
                o3 = psf[:, ob * 512: ob * 512 + ntq * 65].rearrange("p (t c) -> p t c", c=65)
                S.op('dve', lambda E, o3=o3, ntq=ntq: E.tensor_scalar(out=rden[:, 0:ntq], in0=o3[:, :, 64],
                                                                      scalar1=1e-30, scalar2=None, op0=ALU.max),
                     r=[('pf', ob)], w=['rden'])
                S.op('dve', lambda E, ntq=ntq: E.reciprocal(out=rden[:, 0:ntq], in_=rden[:, 0:ntq]),
                     r=['rden'], w=['rden'])
                hoff = (h % 2) * 64
                S.op('dve', lambda E, o3=o3, ntq=ntq, tq0=tq0, hoff=hoff: E.tensor_tensor(
                    out=ypair[:, tq0:tq0 + ntq, hoff:hoff + 64], in0=o3[:, :, 0:64],
                    in1=rden[:, 0:ntq].unsqueeze(2).to_broadcast([128, ntq, 64]), op=ALU.mult),
                    r=[('pf', ob), 'rden'], w=['ypair'])

        for i, it in enumerate(items):
            emit_S(it, i)
            emit_A(it, i)
            if i >= 1:
                emit_PV(items[i - 1], i - 1)
            if i == 12 and h < 7:
                pending = build_vq(h + 1, lambda: 6)
            if i > 12 and h < 7 and pending:
                pending.pop(0)()
        emit_PV(items[-1], len(items) - 1)
        while h < 7 and pending:
            pending.pop(0)()
        pctr[0] += len(items)
        if h % 2 == 1:
            for t0 in range(0, NT, 8):
                nt = min(8, NT - t0)

                def fty(E, t0=t0, nt=nt):
                    last = None
                    for i in range(nt):
                        last = E.transpose(psb[:, i * 128:(i + 1) * 128], ypair[:, t0 + i, :], ident[:, :])
                    return last
                S.op('pe', fty, r=['ypair', 'ident'], w=['psb'])
                S.op('dve', lambda E, t0=t0, nt=nt, h=h: E.tensor_copy(out=y_mlaT[:, h // 2, t0 * 128:(t0 + nt) * 128],
                                                                       in_=psb[:, 0:nt * 128]), r=['psb'], w=['ymla'])
    if DEBUG:
        S.dma('sp', lambda E: E.dma_start(out=dbg["dbg_ymla"][:, :, :], in_=y_mlaT[:, :, :]), r=['ymla'])
    S.barrier()


    MERG = END - 34816
    mergedT = nc.alloc_sbuf_tensor_at("mergedT", [128, 8, NE], BF16, offset=MERG)
    tp = [T_OFF]
    memT = sbt("memT", [128, 8, 256], BF16, at=tp)
    KmT = sbt("KmT", [128, 4, 256], BF16, at=tp)
    VmT = sbt("VmT", [128, 2, 512], BF16, at=tp)
    OVC = tp[0]
    xbufC = [sbt(f"xbufC{i}", [128, D], F32, at=tp) for i in range(2)]
    xhatC = sbt("xhatC", [128, 2, D], BF16, at=tp)
    assert tp[0] <= MERG
    xctr[0] = 0
    tiles = [xload(xbufC, memx[i * 128:(i + 1) * 128, :]) for i in range(2)]
    rms_T(tiles, C_GMEM, lambda kc: memT[:, kc, :], 'memT', xhatC, 'xhC')
    (wkm,), kkm = wload([wk(w_memkv, 0, 512)])
    for h in range(4):
        b = nb()
        mm(PF(b, 256), [(wkm[:, kc, h * 128:(h + 1) * 128], memT[:, kc, :]) for kc in range(8)],
           w=[('pf', b)], r=[kkm, 'memT'])
        act(KmT[:, h, :], PF(b, 256), AF.Copy, r=[('pf', b)], w=['KmT'])
    (wvm,), kvm = wload([wk(w_memkv, 512, 512)])
    for mt in range(2):
        b = nb()
        mm(PF(b), [(memT[:, kc, mt * 128:(mt + 1) * 128], wvm[:, kc, :]) for kc in range(8)],
           w=[('pf', b)], r=[kvm, 'memT'])
        act(VmT[:, mt, :], PF(b), AF.Copy, r=[('pf', b)], w=['VmT'])
    S.barrier()
    tp = [OVC]
    qm = sbt("qm", [128, 512], BF16, at=tp)
    PTc = sbt("PTc", [128, 2, 512], BF16, at=tp)
    recc = sbt("recc", [128, 512], F32, at=tp)
    gsb = [sbt(f"gsb{i}", [128, 512], BF16, at=tp) for i in range(3)]
    mtc = [sbt(f"mtc{i}", [128, 512], F32, at=tp) for i in range(2)]
    assert tp[0] <= MERG
    (wqm,), kqm = wload([wk(w_in, OFF_DIL, 512)])
    qmB = sbt("qmB", [128, 512], BF16, at=tp)
    PTcB = sbt("PTcB", [128, 2, 512], BF16, at=tp)
    assert tp[0] <= MERG
    qm2, PTc2 = [qm, qmB], [PTc, PTcB]
    itsC = [(h, e0, n) for h in range(4) for (e0, n) in GROUPS]

    def c_s1(i):
        h, e0, n = itsC[i]
        b = nb()
        mm(PF(b, n), [(wqm[:, kc, h * 128:(h + 1) * 128], hT_own[:, kc, e0:e0 + n]) for kc in range(8)],
           w=[('pf', b)], r=[kqm, 'hT'])
        act(qm2[i % 2][:, 0:n], PF(b, n), AF.Copy, r=[('pf', b)], w=[('qm', i % 2)], scale=dscale)

    def c_s2(i):
        h, e0, n = itsC[i]
        for mt in range(2):
            b = nb()
            mm(PF(b, n), [(KmT[:, h, mt * 128:(mt + 1) * 128], qm2[i % 2][:, 0:n])], w=[('pf', b)],
               r=['KmT', ('qm', i % 2)])
            act(PTc2[i % 2][:, mt, 0:n], PF(b, n), AF.Exp, r=[('pf', b)], w=[('PTc', i % 2, mt)])

    def c_s3(i):
        h, e0, n = itsC[i]
        P_ = PTc2[i % 2]
        pk = [('PTc', i % 2, 0), ('PTc', i % 2, 1)]
        bo, bd = nb(), nb()
        mm(PF(bo, n), [(VmT[:, mt, h * 128:(h + 1) * 128], P_[:, mt, 0:n]) for mt in range(2)],
           w=[('pf', bo)], r=['VmT'] + pk)
        mm(PF(bd, n), [(ones[:, :], P_[:, mt, 0:n]) for mt in range(2)], w=[('pf', bd)], r=['ones'] + pk)
        S.op('dve', lambda E: E.reciprocal(out=recc[:, 0:n], in_=PF(bd, n)), r=[('pf', bd)], w=['recc'])
        S.op('dve', lambda E: E.tensor_tensor(out=y_memT[:, h, e0:e0 + n], in0=PF(bo, n), in1=recc[:, 0:n],
                                              op=ALU.mult), r=[('pf', bo), 'recc'], w=['ymem'])
    NC_ = len(itsC)
    c_s1(0)
    c_s1(1)
    c_s2(0)
    for i in range(NC_):
        if i + 2 < NC_:
            c_s1(i + 2)
        if i + 1 < NC_:
            c_s2(i + 1)
        c_s3(i)
    if DEBUG:
        S.dma('sp', lambda E: E.dma_start(out=dbg["dbg_ymem"][:, :, :], in_=y_memT[:, :, :]), r=['ymem'])
    yT = [y_mlaT, y_dilT, y_memT]
    ykeys = ['ymla', 'ydil', 'ymem']
    for oc in range(8):
        wg, kg = wload([wk(w_in, OFF_MEMQ + br * 1024 + oc * 128, 128) for br in range(3)])
        wb_, kb_ = wload([w_br[br][:, oc * 128:(oc + 1) * 128].rearrange("(k p) n -> p k n", p=128) for br in range(3)])
        for (e0, n) in GROUPS:
            for br in range(3):
                bg = nb()
                mm(PF(bg, n), [(wg[br][:, kc, :], hT_own[:, kc, e0:e0 + n]) for kc in range(8)],
                   w=[('pf', bg)], r=[kg, 'hT'])
                act(gsb[br][:, 0:n], PF(bg, n), AF.Sigmoid, r=[('pf', bg), 'cv'], w=[('gsb', br)],
                    bias=cvc(C_BG + br * 8 + oc))
                bb = nb()
                mm(PF(bb, n), [(wb_[br][:, kc, :], yT[br][:, kc, e0:e0 + n]) for kc in range(4)],
                   w=[('pf', bb)], r=[kb_, ykeys[br]])
                di = 0 if br == 0 else 1
                S.op('dve', lambda E, bb=bb, n=n, br=br, di=di: E.tensor_tensor(out=mtc[di][:, 0:n], in0=PF(bb, n),
                                                                               in1=gsb[br][:, 0:n], op=ALU.mult),
                     r=[('pf', bb), ('gsb', br)], w=[('mtc', di)])
                if br == 1:
                    S.op('dve', lambda E, n=n: E.tensor_tensor(out=mtc[0][:, 0:n], in0=mtc[0][:, 0:n],
                                                               in1=mtc[1][:, 0:n], op=ALU.add),
                         r=[('mtc', 0), ('mtc', 1)], w=[('mtc', 0)])
                if br == 2:
                    S.op('dve', lambda E, n=n, oc=oc, e0=e0: E.tensor_tensor(out=mergedT[:, oc, e0:e0 + n],
                                                                            in0=mtc[0][:, 0:n], in1=mtc[1][:, 0:n],
                                                                            op=ALU.add),
                         r=[('mtc', 0), ('mtc', 1)], w=['merged'])
    if DEBUG:
        S.dma('sp', lambda E: E.dma_start(out=dbg["dbg_merged"][:, :, :], in_=mergedT[:, :, :]), r=['merged'])
    S.barrier()

    x1 = nc.alloc_sbuf_tensor_at("x1", [128, 16, D], F32, offset=Y_OFF)
    tp = [Y_OFF + 65536]
    xbufD = [sbt("xbufD0", [128, D], F32, at=tp)]
    tmpD = sbt("tmpD", [128, D], F32, at=tp)
    x1pre = sbt("x1pre", [128, D], F32, at=tp)
    xhatD = sbt("xhatD", [128, 1, D], BF16, at=tp)
    assert tp[0] <= MERG
    h2T = hT_own
    (wo0,), ko0 = wload([wk(w_o, 0, 512)])
    (wo1,), ko1 = wload([wk(w_o, 512, 512)])
    xctr[0] = 0
    dstD = {}

    def d_s1(t):
        p = t % 3
        for half, wo_, ko in ((0, wo0, ko0), (1, wo1, ko1)):
            mm(psf[:, p * 1024 + half * 512: p * 1024 + (half + 1) * 512],
               [(mergedT[:, kc, t * 128:(t + 1) * 128], wo_[:, kc, :]) for kc in range(8)],
               w=[('pf', 2 * p + half)], r=[ko, 'merged'])
        pk = [('pf', 2 * p), ('pf', 2 * p + 1)]
        pfull = psf[:, p * 1024:(p + 1) * 1024]
        S.op('act', lambda E: E.activation(out=junk[:, :], in_=pfull, func=AF.Square,
                                           accum_out=small[:, 48:49]), r=pk, w=['junk', ('sm', 48)])
        rstd_from_ss(small[:, 48:49], small[:, 50:51], small[:, 49:50], 1024.0, [('sm', 48)], [('sm', 50)], [('sm', 49)])
        xa, xk = xload(xbufD, xc[6016 + t * 128: 6016 + (t + 1) * 128, :])
        S.op('dve', lambda E: E.scalar_tensor_tensor(out=tmpD[:, :], in0=pfull, scalar=small[:, 50:51],
                                                     in1=gbc[:, 0, :], op0=ALU.mult, op1=ALU.mult),
             r=pk + [('sm', 50), 'gbc'], w=['tmpD'])
        dst = x1pre[:, :] if t == 0 else x1[:, t - 1, :]
        dk = ('x1', t)
        S.op('dve', lambda E: E.tensor_tensor(out=dst, in0=tmpD[:, :], in1=xa, op=ALU.add),
             r=['tmpD', xk], w=[dk])
        dstD[t] = (dst, dk)

    def d_s2(t):
        rms_T([dstD[t]], C_GFFN, lambda kc: h2T[:, kc, t * 128:(t + 1) * 128], 'h2T', xhatD, 'xhD')

    d_s1(0)
    for t in range(NT):
        if t + 1 < NT:
            d_s1(t + 1)
        d_s2(t)
    S.barrier()

    tp = [Y_OFF + 65536]
    gbuf = sbt("gbuf", [128, 22, 512], BF16, at=tp)
    ubuf = [sbt(f"ubuf{i}", [128, 528], F32, at=tp) for i in range(4)]
    zt = [sbt(f"zt{i}", [128, 512], F32, at=tp) for i in range(4)]
    sgs = [sbt(f"sg{i}", [128, 512], F32, at=tp) for i in range(2)]
    carry = sbt("carry", [128, 44, 2], F32, at=tp)
    fo0 = sbt("fo0", [128, 4, 512], F32, at=tp)
    tmpE = zt[0]
    eb = [0]

    def nbe():
        eb[0] = (eb[0] + 1) % 3
        return eb[0]
    for j in range(4):
        e0 = 128 + 512 * j
        for fc0 in range(0, 22, 4):
            nf = min(4, 22 - fc0)
            wupk = [('wupb', c) for c in range(4)]
            wgt, kgt = wload_b(wk(wupb, fc0 * 128, nf * 128), wupk)
            wvt, kvt = wload_b(wk(wupb, DFF + fc0 * 128, nf * 128), wupk)
            for fi in range(nf):
                fc = fc0 + fi
                if j > 0:
                    for fcn in ([0, 1] if fc == 0 else ([fc + 1] if fc + 1 < 22 else [])):
                        for half_ in range(2):
                            S.op('dve', lambda E, fcn=fcn, half_=half_: E.tensor_copy(
                                out=ubuf[half_ * 2 + fcn % 2][:, 0:2], in_=carry[:, half_ * 22 + fcn, :]),
                                r=[('carry', half_ * 22 + fcn)], w=[('ubc', half_ * 2 + fcn % 2)])
                for half, wt, kw_ in ((0, wgt, kgt), (1, wvt, kvt)):
                    ci = half * 22 + fc
                    b = nbe()
                    mm(PF(b), [(wt[:, kc, fi * 128:(fi + 1) * 128], h2T[:, kc, e0:e0 + 512]) for kc in range(8)],
                       w=[('pf', b)], r=[kw_, 'h2T'])
                    ub = ubuf[half * 2 + fc % 2]
                    uk = ('ub', half * 2 + fc % 2)
                    uck = ('ubc', half * 2 + fc % 2)
                    ztb = zt[half * 2 + fc % 2]
                    if j == 0:
                        pb_ = 3 + (ci % 4)
                        mm(psf[:, pb_ * 512: pb_ * 512 + 2],
                           [(wt[:, kc, fi * 128:(fi + 1) * 128], h2T[:, kc, 126:128]) for kc in range(8)],
                           w=[('pf', pb_)], r=[kw_, 'h2T'])
                        S.op('dve', lambda E, ub=ub, pb_=pb_: E.tensor_scalar(out=ub[:, 0:2], in0=psf[:, pb_ * 512: pb_ * 512 + 2],
                                                                             scalar1=cvc(C_UFLAG), scalar2=None, op0=ALU.mult),
                             r=[('pf', pb_), 'cv'], w=[uck])
                    act(ub[:, 2:514], PF(b), AF.Copy, r=[('pf', b)], w=[uk])
                    S.op('dve', lambda E, ub=ub, ci=ci: E.tensor_copy(out=carry[:, ci, :], in_=ub[:, 512:514]),
                         r=[uk], w=[('carry', ci)])
                    zk = ('zt', half * 2 + fc % 2)
                    act(ztb[:, :], ub[:, 0:512], AF.Identity, r=[uk, uck, 'cv'], w=[zk], scale=cvc(C_CW + ci),
                        bias=cvc(C_CB + ci))
                    S.op('dve', lambda E, ub=ub, ci=ci, ztb=ztb: E.scalar_tensor_tensor(
                        out=ztb[:, :], in0=ub[:, 1:513], scalar=cvc(C_CW + 44 + ci), in1=ztb[:, :],
                        op0=ALU.mult, op1=ALU.add), r=[uk, uck, zk, 'cv'], w=[zk])
                    S.op('dve', lambda E, ub=ub, ci=ci, ztb=ztb: E.scalar_tensor_tensor(
                        out=ztb[:, :], in0=ub[:, 2:514], scalar=cvc(C_CW + 88 + ci), in1=ztb[:, :],
                        op0=ALU.mult, op1=ALU.add), r=[uk, zk, 'cv'], w=[zk])
                sg = sgs[fc % 2]
                sk = ('sg', fc % 2)
                zg, zv = zt[fc % 2], zt[2 + fc % 2]
                act(sg[:, :], zg[:, :], AF.Silu, r=[('zt', fc % 2)], w=[sk])
                S.op('pool', lambda E, fc=fc, sg=sg, zv=zv: E.tensor_tensor(out=gbuf[:, fc, :], in0=sg[:, :], in1=zv[:, :],
                                                                          op=ALU.mult),
                     r=[sk, ('zt', 2 + fc % 2)], w=['gbuf'])
        for half in range(2):
            for (f0, nfk) in ((0, 8), (8, 8), (16, 6)):
                wd, kd = wload_b(wdnb[f0 * 128:(f0 + nfk) * 128, half * 512:(half + 1) * 512]
                                 .rearrange("(k p) n -> p k n", p=128), [('wdnb', 0), ('wdnb', 1)])
                for tt in range(4):
                    def fd(E, wd=wd, f0=f0, nfk=nfk, tt=tt):
                        last = None
                        for k in range(nfk):
                            last = E.matmul(PF(3 + tt), lhsT=gbuf[:, f0 + k, tt * 128:(tt + 1) * 128], rhs=wd[:, k, :],
                                            start=(f0 == 0 and k == 0), stop=(f0 == 16 and k == nfk - 1))
                        return last
                    S.op('pe', fd, r=[kd, 'gbuf'], w=[('pf', 3 + tt)])
            for tt in range(4):
                col = 40 + half * 4 + tt
                S.op('act', lambda E, tt=tt, col=col: E.activation(out=junk[:, 0:512], in_=PF(3 + tt), func=AF.Square,
                                                                   accum_out=small[:, col:col + 1]),
                     r=[('pf', 3 + tt)], w=['junk', ('sm', col)])
                if half == 0:
                    act(fo0[:, tt, :], PF(3 + tt), AF.Copy, r=[('pf', 3 + tt)], w=[('fo0', tt)])
            if half == 1:
                S.op('dve', lambda E: E.tensor_tensor(out=small[:, 48:52], in0=small[:, 40:44], in1=small[:, 44:48],
                                                      op=ALU.add), r=[('sm', c) for c in range(40, 48)], w=[('sm', 48)])
                rstd_from_ss(small[:, 48:52], small[:, 56:60], small[:, 52:56], 1024.0, [('sm', 48)], [('sm', 56)],
                             [('sm', 52)])
                for tt in range(4):
                    xt = x1[:, 4 * j + tt, :]
                    xk = ('x1', 4 * j + tt + 1)
                    rs = small[:, 56 + tt:57 + tt]
                    S.op('dve', lambda E, tt=tt, rs=rs: E.scalar_tensor_tensor(
                        out=fo0[:, tt, :], in0=fo0[:, tt, :], scalar=rs, in1=gbc[:, 1, 0:512], op0=ALU.mult,
                        op1=ALU.mult), r=[('fo0', tt), ('sm', 56), 'gbc'], w=[('fo0', tt)])
                    S.op('dve', lambda E, tt=tt, xt=xt: E.tensor_tensor(out=xt[:, 0:512], in0=xt[:, 0:512],
                                                                        in1=fo0[:, tt, :], op=ALU.add),
                         r=[('fo0', tt), xk], w=[xk])
                    S.op('dve', lambda E, tt=tt, rs=rs: E.scalar_tensor_tensor(
                        out=tmpE[:, :], in0=PF(3 + tt), scalar=rs, in1=gbc[:, 1, 512:1024], op0=ALU.mult,
                        op1=ALU.mult), r=[('pf', 3 + tt), ('sm', 56), 'gbc'], w=[('zt', 0)])
                    S.op('dve', lambda E, tt=tt, xt=xt: E.tensor_tensor(out=xt[:, 512:1024], in0=xt[:, 512:1024],
                                                                        in1=tmpE[:, :], op=ALU.add),
                         r=[('zt', 0), xk], w=[xk])
                    row = (4 * j + tt) * 128
                    S.dma('sp', lambda E, row=row, xt=xt: E.dma_start(out=outd[row:row + 128, :], in_=xt), r=[xk])

    return nc, es, S


def _finish(nc, es, S):
    S.barrier()
    with nc.Block() as block:
        @block.tensor
        def _(E):
            for f in S.prog['pe']:
                f(E)

        @block.scalar
        def _(E):
            for f in S.prog['act']:
                f(E)

        @block.vector
        def _(E):
            for f in S.prog['dve']:
                f(E)

        @block.gpsimd
        def _(E):
            for f in S.prog['pool']:
                f(E)

        @block.sync
        def _(E):
            for f in S.prog['sp']:
                f(E)
    es.close()
    return nc


def host_inputs(inputs):
    import ml_dtypes
    x = np.asarray(inputs["x"], np.float32)
    mem = np.asarray(inputs["mem"], np.float32)
    pos = np.asarray(inputs["positions"], np.int32)

    def P(k):
        return np.asarray(inputs[k], np.float32)[0]
    shared = {
        "gbc": np.stack([P("g_post_mix"), P("g_post_ffn")]).astype(np.float32),
        "tri": np.triu(np.ones((128, 128), np.float32)),
        "ident": np.eye(128, dtype=np.float32),
        "w_in": P("w_in"), "w_uq": P("w_uq"), "w_ukv": P("w_ukv"), "w_mem_kv": P("w_mem_kv"),
        "w_br_mla": P("w_br_mla"), "w_br_dil": P("w_br_dil"), "w_br_mem": P("w_br_mem"), "w_o": P("w_o"),
        "w_ffn_up": P("w_ffn_up"), "w_ffn_down": P("w_ffn_down"),
    }
    db = np.zeros((12, 128, 256), np.float32)
    slopes = np.exp2(-8.0 * np.arange(1, 13, dtype=np.float32) / 12).reshape(4, 3).T
    k = np.arange(128)[:, None]
    i = np.arange(128)[None, :]
    for g in range(3):
        for hd in range(4):
            a = slopes[g, hd] * DIL_D[g]
            dist = i - k
            db[g * 4 + hd, :, 0:128] = np.where(dist >= 0, -a * dist, NEG)
            dist2 = 128 + i - k
            db[g * 4 + hd, :, 128:256] = np.where(k >= i, -a * dist2, NEG)
    shared["dbias"] = db

    def cols(v, n):
        return v.reshape(n, 128).T
    cv0 = np.zeros((128, NCV), np.float32)
    cv0[:, C_GPRE:C_GPRE + 8] = cols(P("g_pre_mix"), 8)
    cv0[:, C_GMEM:C_GMEM + 8] = cols(P("g_mem"), 8)
    cv0[:, C_GFFN:C_GFFN + 8] = cols(P("g_pre_ffn"), 8)
    cv0[:, C_QN:C_QN + 3] = cols(P("mla_q_norm"), 3)
    cv0[:, C_KVN:C_KVN + 2] = cols(P("mla_kv_norm"), 2)
    cv0[:, C_BG:C_BG + 24] = cols(P("b_gate"), 24)
    cw = P("conv_w")
    for j in range(3):
        cv0[:, C_CW + j * 44: C_CW + (j + 1) * 44] = cols(cw[j], 44)
    cv0[:, C_CB:C_CB + 44] = cols(P("conv_b"), 44)
    cv0[:, C_INVF:C_INVF + 16] = (np.float32(10000.0) ** (-np.arange(16, dtype=np.float32) / np.float32(16)))[None, :]
    in_maps = []
    for c in range(8):
        b, q = c // 4, c % 4
        xcx = np.zeros((SEQ, D), np.float32)
        pc = np.zeros((SEQ,), np.int32)
        lo = 6144 - CH * q
        xcx[lo:] = x[b, 0: CH * (q + 1)]
        pc[lo:] = pos[b, 0: CH * (q + 1)]
        cvq = cv0.copy()
        for cidx in range(3):
            cvq[:, C_VB + cidx] = 0.0 if (cidx + q) >= 3 else NEG
        ridx = 0
        for g in range(3):
            d = DIL_D[g]
            for r_ in range(d):
                kk = np.arange(128)
                tau0 = 3968 + (2048 - 128 * d) + kk * d + r_
                cvq[:, C_DVB0 + ridx] = np.where(tau0 >= lo, 0.0, NEG)
                tau1 = 3968 + 2048 + kk * d + r_
                cvq[:, C_DVB1 + ridx] = np.where(tau1 >= lo, 0.0, NEG)
                ridx += 1
        cvq[:, C_UFLAG] = 1.0 if q > 0 else 0.0
        m = dict(shared)
        m["xc"] = xcx
        m["posc"] = np.ascontiguousarray(pc.reshape(64, 128).T)
        m["memx"] = np.ascontiguousarray(mem[b])
        m["cv"] = cvq
        in_maps.append(m)
    return in_maps


def kernel(**inputs):
    in_maps = host_inputs(inputs)
    nc, es, S = build_program()
    nc = _finish(nc, es, S)
    res = run_bass_kernel_spmd(nc, in_maps, core_ids=list(range(8)))
    out = np.zeros((2, SEQ, D), np.float32)
    for c in range(8):
        b, q = c // 4, c % 4
        out[b, q * CH:(q + 1) * CH] = np.asarray(res.results[c]["out"], np.float32)
    return out
```

```python
from contextlib import ExitStack
import numpy as np
import concourse.bass as bass
import concourse.mybir as mybir
from concourse.bass_utils import run_bass_kernel_spmd

F32, BF16, I32 = mybir.dt.float32, mybir.dt.bfloat16, mybir.dt.int32
ALU = mybir.AluOpType
AF = mybir.ActivationFunctionType
NEG = -30000.0
D = 1024
SEQ = 8192
CH = 2048
NE = 2176
NT = 17
DFF = 2816
DIN = 8864
OFF_Q, OFF_KV, OFF_KR, OFF_DIL, OFF_MEMQ = 384, 640, 672, 672 + 4608, 672 + 4608 + 512
DIL_D = (1, 4, 16)
EPS = 1e-6
TWO_PI = 6.283185307179586
(C_GPRE, C_GMEM, C_GFFN, C_QN, C_KVN, C_BG, C_CW, C_CB, C_VB, C_ZERO, C_INVF, C_DVB0, C_DVB1, C_UFLAG,
 NCV) = (0, 8, 16, 24, 27, 29, 53, 185, 229, 232, 233, 249, 270, 291, 292)
GROUPS = [(0, 128), (128, 512), (640, 512), (1152, 512), (1664, 512)]
DEBUG = False


class Sched:
    CE = ('pe', 'act', 'dve', 'pool')

    def __init__(s, nc, sems, dsems):
        s.eng = {'pe': nc.tensor, 'act': nc.scalar, 'dve': nc.vector, 'pool': nc.gpsimd, 'sp': nc.sync}
        s.prog = {e: [] for e in s.eng}
        s.sem = sems
        s.dsems = dsems
        s.tick = {e: 0 for e in s.CE}
        s.seen = {e: {} for e in s.eng}
        s.bufs = {}
        s.dcount = {q: 0 for q in dsems}

    def _semh(s, k):
        return s.sem[k] if isinstance(k, str) else s.dsems[k[1]][k[2]]

    def _deps(s, eng, r, w):
        need = {}

        def add(k, v):
            if need.get(k, 0) < v:
                need[k] = v
        for key in r:
            b = s.bufs.get(key)
            if b and b['w']:
                add(*b['w'])
        for key in w:
            b = s.bufs.get(key)
            if b:
                if b['w'] and b['w'][0] != eng:
                    add(*b['w'])
                for k, v in b['r'].items():
                    if k != eng:
                        add(k, v)
        return need

    def _waits(s, q, need):
        for k, v in need.items():
            if s.seen[q].get(k, 0) >= v:
                continue
            s.seen[q][k] = v
            sem = s._semh(k)
            s.prog[q].append(lambda E, sem=sem, v=v: E.wait_ge(sem, v))

    def _record(s, ev, r, w):
        for key in r:
            b = s.bufs.setdefault(key, {'w': None, 'r': {}})
            if b['r'].get(ev[0], 0) < ev[1]:
                b['r'][ev[0]] = ev[1]
        for key in w:
            s.bufs[key] = {'w': ev, 'r': {}}

    def op(s, eng, fn, r=(), w=()):
        s._waits(eng, s._deps(eng, r, w))
        s.tick[eng] += 1
        sem = s.sem[eng]
        s.prog[eng].append(lambda E, fn=fn, sem=sem: fn(E).then_inc(sem, 1))
        s._record((eng, s.tick[eng]), r, w)

    def dma(s, q, fn, r=(), w=()):
        i = s.dcount[q]
        s.dcount[q] += 1
        ns = len(s.dsems[q])
        j, val = i % ns, 16 * (i // ns + 1)
        need = s._deps(None, r, w)
        if i >= ns:
            k = ('d', q, j)
            need[k] = max(need.get(k, 0), val - 16)
        s._waits(q, need)
        sem = s.dsems[q][j]
        s.prog[q].append(lambda E, fn=fn, sem=sem: fn(E).then_inc(sem, 16))
        s._record((('d', q, j), val), r, w)

    def barrier(s):
        for e in s.eng:
            need = {}
            for c in s.CE:
                if s.tick[c] > 0 and c != e:
                    need[c] = s.tick[c]
            for q in s.dsems:
                n, ns = s.dcount[q], len(s.dsems[q])
                for j in range(min(n, ns)):
                    need[('d', q, j)] = 16 * ((n - 1 - j) // ns + 1)
            s._waits(e, need)


def dil_geom(d):
    nq_tot = NE // d
    qb = []
    a = 0
    while a < nq_tot:
        qb.append((a, min(128, nq_tot - a)))
        a += 128
    kt = [(0, 128)] + [(128 + a0, n) for (a0, n) in qb]
    return qb, kt


def build_program():
    nc = bass.Bass("TRN2", target_bir_lowering=False, dynamic_dma_scratch_size=4096)

    def din(name, shape, dt=F32):
        return nc.dram_tensor(name, list(shape), dt, kind="ExternalInput").ap()
    xc = din("xc", [SEQ, D])
    posc = din("posc", [128, 64], I32)
    memx = din("memx", [256, D])
    cvd = din("cv", [128, NCV])
    gbcd = din("gbc", [2, D])
    dbd = din("dbias", [12, 128, 256])
    trid = din("tri", [128, 128])
    identd = din("ident", [128, 128])
    w_in = din("w_in", [D, DIN])
    w_uq = din("w_uq", [384, 768])
    w_ukv = din("w_ukv", [256, 1024])
    w_memkv = din("w_mem_kv", [D, 1024])
    w_br = [din("w_br_mla", [512, D]), din("w_br_dil", [512, D]), din("w_br_mem", [512, D])]
    w_o = din("w_o", [D, D])
    w_up = din("w_ffn_up", [D, 2 * DFF])
    w_dn = din("w_ffn_down", [DFF, D])
    outd = nc.dram_tensor("out", [CH, D], F32, kind="ExternalOutput").ap()
    wupb = nc.dram_tensor("wupb_scratch", [D, 2 * DFF], BF16).ap()
    wdnb = nc.dram_tensor("wdnb_scratch", [DFF, D], BF16).ap()
    dbg = {}
    if DEBUG:
        for nm in ("dbg_ydil", "dbg_ymla", "dbg_ymem", "dbg_merged"):
            dbg[nm] = nc.dram_tensor(nm, [128, 8 if nm == "dbg_merged" else 4, NE], BF16, kind="ExternalOutput").ap()
        dbg["dbg_h"] = nc.dram_tensor("dbg_h", [128, 8, NE], BF16, kind="ExternalOutput").ap()

    END = 212992
    cur = [4608]

    def sbt(name, shape, dt, at=None):
        esz = 4 if dt in (F32, I32) else 2
        n = 1
        for v in shape[1:]:
            n *= v
        nbytes = (n * esz + 63) // 64 * 64
        if at is None:
            off = cur[0]
            cur[0] += nbytes
        else:
            off = at[0]
            at[0] += nbytes
        assert off + nbytes <= END, (name, off, nbytes)
        return nc.alloc_sbuf_tensor_at(name, list(shape), dt, offset=off)

    ident = sbt("ident", [128, 128], BF16)
    ones = sbt("ones", [128, 128], BF16)
    tri = sbt("tri_sb", [128, 128], BF16)
    cv = sbt("cv_sb", [128, NCV], F32)
    gbc = sbt("gbc_sb", [128, 2, D], F32)
    cosT = sbt("cosT", [128, 64, 16], F32)
    sinT = sbt("sinT", [128, 64, 16], F32)
    cosQ = sbt("cosQ", [128, NT, 16], F32)
    sinQ = sbt("sinQ", [128, NT, 16], F32)
    small = sbt("small", [128, 64], F32)
    junk = sbt("junk", [128, D], BF16)
    hT_own = sbt("hT_own", [128, 8, NE], BF16)
    W_OFF = cur[0]
    wsl = [sbt(f"wslot{i}", [128, 4096], BF16) for i in range(4)]
    Y_OFF = cur[0]
    y_dilT = sbt("y_dilT", [128, 4, NE], BF16)
    y_mlaT = sbt("y_mlaT", [128, 4, NE], BF16)
    y_memT = sbt("y_memT", [128, 4, NE], BF16)
    T_OFF = cur[0]
    YB = 17408

    psf = nc.alloc_psum_tensor("psf", [128, 3584], F32)
    psb = nc.alloc_psum_tensor("psb", [128, 1024], BF16)

    def PF(b, n=512, off=0):
        return psf[:, b * 512 + off: b * 512 + off + n]

    def cvc(c, p=128):
        return cv[0:p, c:c + 1]

    es = ExitStack()
    sems = {e: es.enter_context(nc.semaphore(f"s_{e}")) for e in Sched.CE}
    dsems = {q: [es.enter_context(nc.semaphore(f"d_{q}{i}")) for i in range(8)] for q in ('sp', 'pool')}
    S = Sched(nc, sems, dsems)
    name_ctr = [0]

    wctr = [0]

    def wload(parts):
        i = wctr[0] % 4
        wctr[0] += 1
        views = []
        off = 0
        for src in parts:
            k, n = src.shape[1], src.shape[2]
            v = wsl[i][:, off:off + k * n].rearrange("p (k n) -> p k n", n=n)
            off += k * n
            assert off <= 4096
            S.dma('pool', lambda E, v=v, src=src: E.dma_start(out=v, in_=src), w=[('w', i)])
            views.append(v)
        return views, ('w', i)

    def wload_b(src, rkeys):
        i = wctr[0] % 4
        wctr[0] += 1
        k, n = src.shape[1], src.shape[2]
        v = wsl[i][:, 0:k * n].rearrange("p (k n) -> p k n", n=n)
        S.dma('sp', lambda E: E.dma_start(out=v, in_=src), r=rkeys, w=[('w', i)])
        return v, ('w', i)

    def wk(w, c0, n, kp=None):
        return w[:, c0:c0 + n].rearrange("(k p) n -> p k n", p=128)

    def mm(out, pairs, w, r):
        def f(E):
            last = None
            for i, (l, rr) in enumerate(pairs):
                last = E.matmul(out, lhsT=l, rhs=rr, start=(i == 0), stop=(i == len(pairs) - 1))
            return last
        S.op('pe', f, r=r, w=w)

    def act(out, in_, func, r, w, **kw):
        S.op('act', lambda E: E.activation(out=out, in_=in_, func=func, **kw), r=r, w=w)

    def rstd_from_ss(ss_ap, out_ap, tmp_ap, n_feat, rkeys, wkeys, tkeys):
        act(tmp_ap, ss_ap, AF.Ln, r=rkeys, w=tkeys, scale=1.0 / n_feat, bias=EPS)
        act(out_ap, tmp_ap, AF.Exp, r=tkeys, w=wkeys, scale=-0.5)

    def rms_T(tiles, gcol, dst, dkey, xhat, xkey):
        n = len(tiles)
        for i, (ap, key) in enumerate(tiles):
            S.op('act', lambda E, ap=ap, i=i: E.activation(out=junk[:, :], in_=ap, func=AF.Square,
                                                           accum_out=small[:, i:i + 1]),
                 r=[key], w=['junk', ('sm', i)])
        rstd_from_ss(small[:, 0:n], small[:, 16:16 + n], small[:, 8:8 + n], 1024.0,
                     [('sm', i) for i in range(n)], ['smr'], ['sml'])
        for i, (ap, key) in enumerate(tiles):
            S.op('dve', lambda E, ap=ap, i=i: E.tensor_scalar(out=xhat[:, i, :], in0=ap,
                                                              scalar1=small[:, 16 + i:17 + i], scalar2=None,
                                                              op0=ALU.mult),
                 r=[key, 'smr'], w=[(xkey, i)])
        per = 8 // n if n <= 2 else 2
        w_ = 1024 // per
        for kc0 in range(0, 8, per):
            def f(E, kc0=kc0):
                last = None
                for j in range(per):
                    for i in range(n):
                        last = E.transpose(psb[:, j * w_ + i * 128: j * w_ + (i + 1) * 128],
                                           xhat[:, i, (kc0 + j) * 128:(kc0 + j + 1) * 128], ident[:, :])
                return last
            S.op('pe', f, r=[(xkey, i) for i in range(n)] + ['ident'], w=['psb'])
            for j in range(per):
                kc = kc0 + j
                S.op('dve', lambda E, kc=kc, j=j: E.tensor_scalar(out=dst(kc), in0=psb[:, j * w_: j * w_ + n * 128],
                                                                  scalar1=cvc(gcol + kc), scalar2=None, op0=ALU.mult),
                     r=['psb', 'cv'], w=[dkey])

    def rms_pipe(calls, gcol, xbufs, xhats, xname):
        def stageA(c):
            call = calls[c]
            par = c % 2
            base = par * 24
            n = len(call['srcs'])
            tiles = [xload(xbufs, src) for src in call['srcs']]
            for i, (ap, key) in enumerate(tiles):
                S.op('act', lambda E, ap=ap, i=i: E.activation(out=junk[:, :], in_=ap, func=AF.Square,
                                                               accum_out=small[:, base + i:base + i + 1]),
                     r=[key], w=['junk', ('sm', base + i)])
            rstd_from_ss(small[:, base:base + n], small[:, base + 16:base + 16 + n], small[:, base + 8:base + 8 + n],
                         1024.0, [('sm', base + i) for i in range(n)], [('smr', par)], [('sml', par)])
            for i, (ap, key) in enumerate(tiles):
                S.op('dve', lambda E, ap=ap, i=i: E.tensor_scalar(out=xhats[par][:, i, :], in0=ap,
                                                                  scalar1=small[:, base + 16 + i:base + 17 + i],
                                                                  scalar2=None, op0=ALU.mult),
                     r=[key, ('smr', par)], w=[(xname, par, i)])

        def stageB(c):
            call = calls[c]
            par = c % 2
            n = len(call['srcs'])
            dst, dkey = call['dst'], call['dkey']
            xh = xhats[par]
            for kc4 in range(2):
                def f(E, kc4=kc4):
                    last = None
                    for j in range(4):
                        kc = kc4 * 4 + j
                        for i in range(n):
                            last = E.transpose(psb[:, j * 256 + i * 128: j * 256 + (i + 1) * 128],
                                               xh[:, i, kc * 128:(kc + 1) * 128], ident[:, :])
                    return last
                S.op('pe', f, r=[(xname, par, i) for i in range(n)] + ['ident'], w=['psb'])
                for j in range(4):
                    kc = kc4 * 4 + j
                    S.op('dve', lambda E, kc=kc, j=j: E.tensor_scalar(out=dst(kc), in0=psb[:, j * 256: j * 256 + n * 128],
                                                                      scalar1=cvc(gcol + kc), scalar2=None,
                                                                      op0=ALU.mult), r=['psb', 'cv'], w=[dkey])
            if call.get('after'):
                call['after']()

        stageA(0)
        for c in range(len(calls)):
            if c + 1 < len(calls):
                stageA(c + 1)
            stageB(c)

    xctr = [0]

    def xload(xbufs, src):
        i = xctr[0] % len(xbufs)
        xctr[0] += 1
        dst_ = xbufs[i][:, :]
        S.dma('sp', lambda E: E.dma_start(out=dst_, in_=src), w=[('xb', i)])
        return dst_, ('xb', i)

    tp = [T_OFF]
    tmp32 = sbt("setup_tmp", [128, 128], F32, at=tp)
    S.dma('sp', lambda E: E.dma_start(out=cv[:, :], in_=cvd[:, :]), w=['cv'])
    S.dma('sp', lambda E: E.dma_start(out=gbc[:, :, :], in_=gbcd.partition_broadcast(128)), w=['gbc'])
    S.dma('pool', lambda E: E.dma_start(out=ident[:, :], in_=identd[:, :]), w=['ident'])
    S.dma('pool', lambda E: E.dma_start(out=tri[:, :], in_=trid[:, :]), w=['tri'])
    S.op('dve', lambda E: E.memset(ones[:, :], 1.0), w=['ones'])
    posi = sbt("posi", [128, 64], I32, at=tp)
    posf = sbt("posf", [128, 64], F32, at=tp)
    ang = sbt("ang", [128, 64, 16], F32, at=tp)
    kf = sbt("kf", [128, 64, 16], F32, at=tp)
    ki = sbt("ki", [128, 64, 16], I32, at=tp)
    S.dma('sp', lambda E: E.dma_start(out=posi[:, :], in_=posc[:, :]), w=['posi'])
    S.op('dve', lambda E: E.tensor_copy(out=posf[:, :], in_=posi[:, :]), r=['posi'], w=['posf'])
    S.op('dve', lambda E: E.tensor_tensor(out=ang[:, :, :], in0=posf[:, :].unsqueeze(2).to_broadcast([128, 64, 16]),
                                          in1=cv[:, C_INVF:C_INVF + 16].unsqueeze(1).to_broadcast([128, 64, 16]),
                                          op=ALU.mult), r=['posf', 'cv'], w=['ang'])
    for tab, shift in ((sinT, 0.0), (cosT, np.pi / 2)):
        S.op('dve', lambda E, shift=shift: E.tensor_scalar(out=kf[:, :, :], in0=ang[:, :, :], scalar1=float(shift),
                                                           scalar2=float(1.0 / TWO_PI), op0=ALU.add, op1=ALU.mult),
             r=['ang'], w=['kf'])
        S.op('dve', lambda E: E.tensor_copy(out=ki[:, :, :], in_=kf[:, :, :]), r=['kf'], w=['ki'])
        S.op('dve', lambda E: E.tensor_copy(out=kf[:, :, :], in_=ki[:, :, :]), r=['ki'], w=['kf'])
        S.op('dve', lambda E: E.scalar_tensor_tensor(out=kf[:, :, :], in0=kf[:, :, :], scalar=float(-TWO_PI),
                                                     in1=ang[:, :, :], op0=ALU.mult, op1=ALU.add),
             r=['kf', 'ang'], w=['kf'])
        S.op('dve', lambda E, shift=shift: E.tensor_scalar(out=kf[:, :, :], in0=kf[:, :, :], scalar1=float(shift),
                                                           scalar2=3.1415925, op0=ALU.add, op1=ALU.min),
             r=['kf'], w=['kf'])
        S.op('dve', lambda E: E.tensor_scalar(out=kf[:, :, :], in0=kf[:, :, :], scalar1=-3.1415925, scalar2=None,
                                              op0=ALU.max), r=['kf'], w=['kf'])
        act(tab[:, :, :], kf[:, :, :], AF.Sin, r=['kf'], w=['rope'])
    qs = 96.0 ** -0.5
    S.op('dve', lambda E: E.tensor_scalar(out=cosQ[:, :, :], in0=cosT[:, 47:64, :], scalar1=qs, scalar2=None,
                                          op0=ALU.mult), r=['rope'], w=['ropeq'])
    S.op('dve', lambda E: E.tensor_scalar(out=sinQ[:, :, :], in0=sinT[:, 47:64, :], scalar1=qs, scalar2=None,
                                          op0=ALU.mult), r=['rope'], w=['ropeq'])
    S.barrier()

    tp = [Y_OFF + YB]
    xbufA = [sbt(f"xbufA{i}", [128, D], F32, at=tp) for i in range(2)]
    xhatA = sbt("xhatA", [128, 4, D], BF16, at=tp)
    hT_halo = sbt("hT_halo", [128, 8, 2048], BF16, at=tp)
    QTd = sbt("QTd", [128, NE], BF16, at=tp)
    KTd = sbt("KTd", [128, 4224], BF16, at=tp)
    Vd = sbt("Vd", [128, 48, 128], BF16, at=tp)
    accO = sbt("accO", [128, NE], F32, at=tp)
    accD = sbt("accD", [128, NE], F32, at=tp)
    PTd = [sbt(f"PTd{i}", [128, 256], BF16, at=tp) for i in range(4)]
    ssb = [sbt(f"ssb{i}", [128, 256], F32, at=tp) for i in range(2)]
    dbt = [sbt(f"dbt{i}", [128, 256], F32, at=tp) for i in range(2)]

    def prep_all(row0, ntiles, dstbuf, dkey):
        t = 0
        while t < ntiles:
            n = min(2, ntiles - t)
            tiles = [xload(xbufA, xc[row0 + (t + i) * 128: row0 + (t + i + 1) * 128, :]) for i in range(n)]
            t0 = t
            rms_T(tiles, C_GPRE, lambda kc, t0=t0, n=n: dstbuf[:, kc, t0 * 128:(t0 + n) * 128], dkey, xhatA, 'xhA')
            t += n

    xhatsA = [xhatA[:, 0:2, :], xhatA[:, 2:4, :]]
    callsA = []
    for (row0, ntiles, dstbuf, dkey) in ((3968, 16, hT_halo, 'hTh'), (6016, NT, hT_own, 'hT')):
        t = 0
        while t < ntiles:
            n = min(2, ntiles - t)
            callsA.append(dict(srcs=[xc[row0 + (t + i) * 128: row0 + (t + i + 1) * 128, :] for i in range(n)],
                               dst=(lambda kc, t=t, n=n, dstbuf=dstbuf: dstbuf[:, kc, t * 128:(t + n) * 128]), dkey=dkey))
            t += n
    xbufA4 = xbufA + [accO[:, 0:D], accO[:, D:2 * D]]
    rms_pipe(callsA, C_GPRE, xbufA4, xhatsA, 'xhA')
    S.barrier()
    if DEBUG:
        S.dma('sp', lambda E: E.dma_start(out=dbg["dbg_h"][:, :, :], in_=hT_own[:, :, :]), r=['hT'])

    dscale = 128.0 ** -0.5
    bank_rr = [0]

    def nb():
        bank_rr[0] = (bank_rr[0] + 1) % 7
        return bank_rr[0]

    for hd in range(4):
        for g in range(3):
            d = DIL_D[g]
            gi = g * 4 + hd
            qb, kt = dil_geom(d)
            nql = NE // d
            nkl = 128 + nql
            c0 = OFF_KR + g * 512 + hd * 128
            (wq_, wk_, wv_), wkey = wload([wk(w_in, c0, 128), wk(w_in, c0 + 1536, 128), wk(w_in, c0 + 3072, 128)])
            bi = gi % 2
            S.dma('sp', lambda E, bi=bi, gi=gi: E.dma_start(out=dbt[bi][:, :], in_=dbd[gi, :, :]), w=[('dbt', bi)])
            QV = QTd[:, :].rearrange("p (r l) -> p r l", r=d)
            KV = KTd[:, 0:d * nkl].rearrange("p (r l) -> p r l", r=d)
            for (e0, n) in GROUPS:
                for which, wv3, dstv, loff, sc in ((0, wq_, QV, 0, dscale), (1, wk_, KV, 128, 1.0)):
                    b = nb()
                    mm(PF(b, n), [(wv3[:, kc, :], hT_own[:, kc, e0:e0 + n]) for kc in range(8)],
                       w=[('pf', b)], r=[wkey, 'hT'])
                    act(dstv[:, :, loff + e0 // d: loff + (e0 + n) // d],
                        PF(b, n).rearrange("p (l r) -> p r l", r=d), AF.Copy,
                        r=[('pf', b)], w=['QTd' if which == 0 else 'KTd'], scale=sc)
            h0 = 2048 - 128 * d
            while h0 < 2048:
                n = min(512, 2048 - h0)
                b = nb()
                mm(PF(b, n), [(wk_[:, kc, :], hT_halo[:, kc, h0:h0 + n]) for kc in range(8)],
                   w=[('pf', b)], r=[wkey, 'hTh'])
                l0 = (h0 - (2048 - 128 * d)) // d
                act(KV[:, :, l0: l0 + n // d], PF(b, n).rearrange("p (l r) -> p r l", r=d), AF.Copy,
                    r=[('pf', b)], w=['KTd'])
                h0 += n
            ntile = len(kt)
            for r_ in range(d):
                for j, (a, nk) in enumerate(kt):
                    if j == 0:
                        s0 = 2048 - 128 * d + r_
                        src = lambda kc, s0=s0, nk=nk: hT_halo[:, kc, s0: s0 + (nk - 1) * d + 1: d]
                        rk = 'hTh'
                    else:
                        s0 = (a - 128) * d + r_
                        src = lambda kc, s0=s0, nk=nk: hT_own[:, kc, s0: s0 + (nk - 1) * d + 1: d]
                        rk = 'hT'
                    b = nb()
                    mm(psf[0:nk, b * 512: b * 512 + 128], [(src(kc), wv_[:, kc, :]) for kc in range(8)],
                       w=[('pf', b)], r=[wkey, rk])
                    ti = r_ * ntile + j
                    S.op('dve', lambda E, nk=nk, b=b, ti=ti: E.tensor_copy(out=Vd[0:nk, ti, :],
                                                                         in_=psf[0:nk, b * 512: b * 512 + 128]),
                         r=[('pf', b)], w=['Vd'])
            for r_ in range(d):
                info = {}

                def dil_pv(m, r_=r_, info=info, d=d, g=g, qb=qb, ntile=ntile):
                    qa, qn = qb[m]
                    pp, pcol, _ = info[m]
                    pi, _, nk = info[m + 1]
                    b2 = nb()
                    tprev = r_ * ntile + m
                    tcur = r_ * ntile + m + 1

                    def f(E):
                        o = psf[:, b2 * 512: b2 * 512 + qn]
                        dd = psf[:, b2 * 512 + 128: b2 * 512 + 128 + qn]
                        E.matmul(o, lhsT=Vd[:, tprev, :], rhs=PTd[pp][:, pcol:pcol + qn], start=True, stop=False)
                        E.matmul(o, lhsT=Vd[0:nk, tcur, :], rhs=PTd[pi][0:nk, 0:qn], start=False, stop=True)
                        E.matmul(dd, lhsT=ones[:, :], rhs=PTd[pp][:, pcol:pcol + qn], start=False, stop=False,
                                 skip_group_check=True)
                        return E.matmul(dd, lhsT=ones[0:nk, :], rhs=PTd[pi][0:nk, 0:qn], start=False, stop=True,
                                        skip_group_check=True)
                    S.op('pe', f, r=['Vd', ('PTd', pp), ('PTd', pi), 'ones'], w=[('pf', b2)])
                    e_s = qa * d + r_
                    e_e = e_s + (qn - 1) * d + 1
                    for accb, off, akey in ((accO, 0, 'accO'), (accD, 128, 'accD')):
                        if g == 0:
                            S.op('dve', lambda E, accb=accb, off=off: E.tensor_copy(
                                out=accb[:, e_s:e_e:d], in_=psf[:, b2 * 512 + off: b2 * 512 + off + qn]),
                                r=[('pf', b2)], w=[akey])
                        else:
                            S.op('dve', lambda E, accb=accb, off=off: E.tensor_tensor(
                                out=accb[:, e_s:e_e:d], in0=psf[:, b2 * 512 + off: b2 * 512 + off + qn],
                                in1=accb[:, e_s:e_e:d], op=ALU.add), r=[('pf', b2), akey], w=[akey])

                for j, (a, nk) in enumerate(kt):
                    has_diag = j >= 1
                    has_prev = j < len(qb)
                    ncol = (qb[j - 1][1] if has_diag else 0) + (qb[j][1] if has_prev else 0)
                    qlo = qb[j - 1][0] if has_diag else qb[j][0]
                    boff = 0 if has_diag else 128
                    b = nb()
                    mm(psf[0:nk, b * 512: b * 512 + ncol], [(KV[:, r_, a:a + nk], QV[:, r_, qlo:qlo + ncol])],
                       w=[('pf', b)], r=['KTd', 'QTd'])
                    si = j % 2
                    S.op('dve', lambda E, nk=nk, b=b, ncol=ncol, si=si, boff=boff, bi=bi: E.tensor_tensor(
                        out=ssb[si][0:nk, 0:ncol], in0=psf[0:nk, b * 512: b * 512 + ncol],
                        in1=dbt[bi][0:nk, boff:boff + ncol], op=ALU.add),
                        r=[('pf', b), ('dbt', bi)], w=[('ssb', si)])
                    ridx = (0, 1, 5)[g] + r_
                    bcol = C_DVB0 + ridx if j == 0 else (C_DVB1 + ridx if j == 1 else C_ZERO)
                    pi = j % 4
                    act(PTd[pi][0:nk, 0:ncol], ssb[si][0:nk, 0:ncol], AF.Exp, r=[('ssb', si), 'cv'],
                        w=[('PTd', pi)], bias=cvc(bcol, nk))
                    info[j] = (pi, (qb[j - 1][1] if has_diag else 0), nk)
                    if j >= 2:
                        dil_pv(j - 2)
                dil_pv(len(kt) - 2)
        S.op('dve', lambda E: E.tensor_scalar(out=accD[:, :], in0=accD[:, :], scalar1=1e-30, scalar2=None,
                                              op0=ALU.max), r=['accD'], w=['accD'])
        S.op('dve', lambda E: E.reciprocal(out=accD[:, :], in_=accD[:, :]), r=['accD'], w=['accD'])
        S.op('dve', lambda E, hd=hd: E.tensor_tensor(out=y_dilT[:, hd, :], in0=accO[:, :], in1=accD[:, :],
                                                     op=ALU.mult), r=['accO', 'accD'], w=['ydil'])
    if DEBUG:
        S.dma('sp', lambda E: E.dma_start(out=dbg["dbg_ydil"][:, :, :], in_=y_dilT[:, :, :]), r=['ydil'])
    S.barrier()


    ckvT = nc.alloc_sbuf_tensor_at("ckvT", [128, 2, SEQ], BF16, offset=W_OFF)
    tp = [Y_OFF + 2 * YB]
    KT = sbt("KT", [128, SEQ], BF16, at=tp)
    cqT = sbt("cqT", [128, 3, NE], BF16, at=tp)
    Wuq = sbt("Wuq", [128, 3, 768], BF16, at=tp)
    Wukv = sbt("Wukv", [128, 2, 1024], BF16, at=tp)
    OV = tp[0]
    tp = [OV]
    xbufB = [nc.alloc_sbuf_tensor_at(f"xbufB{i}", [128, D], F32, offset=Y_OFF + YB + i * 4096) for i in range(4)]
    Wq = sbt("Wq", [128, 8, 384], BF16, at=tp)
    xhatB = sbt("xhatB", [128, 4, D], BF16, at=tp)
    hTgs = [sbt(f"hTg{i}", [128, 8, 512], BF16, at=tp) for i in range(2)]
    Wkvr = sbt("Wkvr", [128, 8, 288], BF16, at=tp)
    sq = sbt("sq", [128, 3, 512], BF16, at=tp)
    rbc = sbt("rbc", [128, 512], F32, at=tp)
    kpe = sbt("kpe", [128, 4, 96], BF16, at=tp)
    kt4 = [sbt(f"kt4_{i}", [128, 4, 16], F32, at=tp) for i in range(4)]

    S.dma('pool', lambda E: E.dma_start(out=Wkvr[:, :, :], in_=wk(w_in, OFF_Q, 288)), w=['Wkvr'])
    S.dma('pool', lambda E: E.dma_start(out=Wq[:, :, :], in_=wk(w_in, 0, 384)), w=['Wq'])
    S.dma('pool', lambda E: E.dma_start(out=Wuq[:, :, :], in_=w_uq.rearrange("(k p) n -> p k n", p=128)), w=['Wuq'])
    S.dma('pool', lambda E: E.dma_start(out=Wukv[:, :, :], in_=w_ukv.rearrange("(k p) n -> p k n", p=128)),
          w=['Wukv'])
    S.op('dve', lambda E: E.memset(kpe[:, :, :], 0.0), w=['kpe'])

    def latent_norm(banks, n, nchunk, gcol, dstf, dkey):
        for kc in range(nchunk):
            act(sq[:, kc, 0:n], PF(banks[kc], n), AF.Square, r=[('pf', banks[kc])], w=[('sq', kc)])
        bs = nb()
        mm(PF(bs, n), [(ones[:, :], sq[:, kc, 0:n]) for kc in range(nchunk)], w=[('pf', bs)],
           r=['ones'] + [('sq', kc) for kc in range(nchunk)])
        act(rbc[:, 0:n], PF(bs, n), AF.Ln, r=[('pf', bs)], w=['rbc'], scale=1.0 / (128 * nchunk), bias=EPS)
        act(rbc[:, 0:n], rbc[:, 0:n], AF.Exp, r=['rbc'], w=['rbc'], scale=-0.5)
        for kc in range(nchunk):
            S.op('dve', lambda E, kc=kc: E.scalar_tensor_tensor(out=dstf(kc), in0=PF(banks[kc], n),
                                                                scalar=cvc(gcol + kc), in1=rbc[:, 0:n],
                                                                op0=ALU.mult, op1=ALU.mult),
                 r=[('pf', banks[kc]), 'rbc', 'cv'], w=[dkey])

    def rope_tm(src3, cos3, sin3, dst3, nt, rkeys, wkey):
        a, b_ = src3[:, :, 0:16], src3[:, :, 16:32]
        t = [k[:, 0:nt, :] for k in kt4]
        S.op('dve', lambda E: E.tensor_tensor(out=t[0], in0=a, in1=cos3, op=ALU.mult), r=rkeys, w=[('kt4', 0)])
        S.op('dve', lambda E: E.tensor_tensor(out=t[1], in0=b_, in1=sin3, op=ALU.mult), r=rkeys, w=[('kt4', 1)])
        S.op('dve', lambda E: E.tensor_tensor(out=dst3[:, :, 0:16], in0=t[0], in1=t[1], op=ALU.subtract),
             r=[('kt4', 0), ('kt4', 1)], w=[wkey])
        S.op('dve', lambda E: E.tensor_tensor(out=t[2], in0=b_, in1=cos3, op=ALU.mult), r=rkeys, w=[('kt4', 2)])
        S.op('dve', lambda E: E.tensor_tensor(out=t[3], in0=a, in1=sin3, op=ALU.mult), r=rkeys, w=[('kt4', 3)])
        S.op('dve', lambda E: E.tensor_tensor(out=dst3[:, :, 16:32], in0=t[2], in1=t[3], op=ALU.add),
             r=[('kt4', 2), ('kt4', 3)], w=[wkey])

    xctr[0] = 0

    lat = {}

    def ctx_latent1(tg):
        hTg = hTgs[tg % 2]
        hk = ('hTg', tg % 2)
        banks = [nb(), nb()]
        for kc2 in range(2):
            mm(PF(banks[kc2]), [(Wkvr[:, kc, kc2 * 128:(kc2 + 1) * 128], hTg[:, kc, :]) for kc in range(8)],
               w=[('pf', banks[kc2])], r=['Wkvr', hk])
        bk = nb()

        def fk(E):
            last = None
            for i in range(4):
                for kc in range(8):
                    last = E.matmul(psf[:, bk * 512 + i * 32: bk * 512 + (i + 1) * 32],
                                    lhsT=hTg[:, kc, i * 128:(i + 1) * 128], rhs=Wkvr[:, kc, 256:288],
                                    start=(kc == 0), stop=(kc == 7))
            return last
        S.op('pe', fk, r=['Wkvr', hk], w=[('pf', bk)])
        for kc in range(2):
            act(sq[:, kc, 0:512], PF(banks[kc]), AF.Square, r=[('pf', banks[kc])], w=[('sq', kc)])
        src3 = psf[:, bk * 512: bk * 512 + 128].rearrange("p (t c) -> p t c", c=32)
        rope_tm(src3, cosT[:, tg * 4:(tg + 1) * 4, :], sinT[:, tg * 4:(tg + 1) * 4, :], kpe[:, :, 64:96], 4,
                [('pf', bk), 'rope'], 'kpe')
        lat[tg] = banks

    def ctx_latent2(tg):
        banks = lat[tg]
        n = 512
        bs = nb()
        mm(PF(bs, n), [(ones[:, :], sq[:, kc, 0:n]) for kc in range(2)], w=[('pf', bs)],
           r=['ones'] + [('sq', kc) for kc in range(2)])
        act(rbc[:, 0:n], PF(bs, n), AF.Ln, r=[('pf', bs)], w=['rbc'], scale=1.0 / 256, bias=EPS)
        act(rbc[:, 0:n], rbc[:, 0:n], AF.Exp, r=['rbc'], w=['rbc'], scale=-0.5)
        for kc in range(2):
            S.op('dve', lambda E, kc=kc: E.scalar_tensor_tensor(out=ckvT[:, kc, tg * 512:(tg + 1) * 512],
                                                                in0=PF(banks[kc], n), scalar=cvc(C_KVN + kc),
                                                                in1=rbc[:, 0:n], op0=ALU.mult, op1=ALU.mult),
                 r=[('pf', banks[kc]), 'rbc', 'cv'], w=['ckvT'])

        def ft(E):
            last = None
            for i in range(4):
                last = E.transpose(psb[0:96, i * 128:(i + 1) * 128], kpe[:, i, :], ident[:, :])
            return last
        S.op('pe', ft, r=['kpe', 'ident'], w=['psb'])
        act(KT[64:96, tg * 512:(tg + 1) * 512], psb[64:96, 0:512], AF.Copy, r=['psb'], w=['KTr'])

    xhatsB = [xhatB[:, 0:2, :], xhatB[:, 2:4, :]]
    callsB = []
    for tg in range(16):
        for hh in range(2):
            callsB.append(dict(
                srcs=[xc[(tg * 4 + hh * 2 + i) * 128:(tg * 4 + hh * 2 + i + 1) * 128, :] for i in range(2)],
                dst=(lambda kc, tg=tg, hh=hh: hTgs[tg % 2][:, kc, hh * 256:(hh + 1) * 256]), dkey=('hTg', tg % 2),
                after=((lambda tg=tg: ctx_latent1(tg)) if hh == 1 else
                       ((lambda tg=tg: ctx_latent2(tg - 1)) if tg >= 1 else None))))
    rms_pipe(callsB, C_GPRE, xbufB, xhatsB, 'xhB')
    ctx_latent2(15)
    for (e0, n) in GROUPS:
        banks = [nb(), nb(), nb()]
        for kc3 in range(3):
            mm(PF(banks[kc3], n), [(Wq[:, kc, kc3 * 128:(kc3 + 1) * 128], hT_own[:, kc, e0:e0 + n]) for kc in range(8)],
               w=[('pf', banks[kc3])], r=['Wq', 'hT'])
        latent_norm(banks, n, 3, C_QN, lambda kc, e0=e0, n=n: cqT[:, kc, e0:e0 + n], 'cqT')
    S.barrier()

    for c in range(4):
        S.dma('pool', lambda E, c=c: E.dma_start(out=wupb[:, c * 1408:(c + 1) * 1408], in_=w_up[:, c * 1408:(c + 1) * 1408]),
              w=[('wupb', c)])
    for r_ in range(2):
        S.dma('pool', lambda E, r_=r_: E.dma_start(out=wdnb[r_ * 1408:(r_ + 1) * 1408, :], in_=w_dn[r_ * 1408:(r_ + 1) * 1408, :]),
              w=[('wdnb', r_)])
    tp = [OV]
    Vts = [sbt(f"Vt{i}", [128, 64, 65], BF16, at=tp) for i in range(2)]
    QTs = [sbt(f"QT{i}", [128, NE], BF16, at=tp) for i in range(2)]
    qtm = sbt("qtm", [128, NT, 96], BF16, at=tp)
    ypair = sbt("ypair", [128, NT, 128], BF16, at=tp)
    PTm = [sbt(f"PTm{i}", [128, 1024], BF16, at=tp) for i in range(3)]
    rden = sbt("rden", [128, 8], F32, at=tp)
    kt4 = [sbt(f"kq4_{i}", [128, 5, 16], F32, at=tp) for i in range(4)]
    for Vt_ in Vts:
        S.op('dve', lambda E, Vt_=Vt_: E.memset(Vt_[:, :, 64:65], 1.0), w=['Vones'])
    evac_rr = [0]

    def evac(out, in_, r, w, **kw):
        evac_rr[0] += 1
        if not kw:
            S.op('dve', lambda E: E.tensor_copy(out=out, in_=in_), r=r, w=w)
        else:
            act(out, in_, AF.Copy, r=r, w=w, **kw)

    pctr = [0]
    for h in range(8):
        for tg in range(16):
            b = nb()
            mm(psf[0:64, b * 512:(b + 1) * 512],
               [(Wukv[:, kc, h * 128: h * 128 + 64], ckvT[:, kc, tg * 512:(tg + 1) * 512]) for kc in range(2)],
               w=[('pf', b)], r=['Wukv', 'ckvT'])
            evac(KT[0:64, tg * 512:(tg + 1) * 512], psf[0:64, b * 512:(b + 1) * 512], r=[('pf', b)], w=['KTn'])
        def build_vq(h, bankf):
            Vt, QT = Vts[h % 2], QTs[h % 2]
            vk, qk = ('Vt', h % 2), ('QT', h % 2)
            chunks = []

            def vchunk(t8):
                b = bankf()

                def fv(E):
                    last = None
                    for i in range(8):
                        for kc in range(2):
                            last = E.matmul(psf[:, b * 512 + i * 64: b * 512 + (i + 1) * 64],
                                            lhsT=ckvT[:, kc, (t8 * 8 + i) * 128:(t8 * 8 + i + 1) * 128],
                                            rhs=Wukv[:, kc, h * 128 + 64: h * 128 + 128], start=(kc == 0), stop=(kc == 1))
                    return last
                S.op('pe', fv, r=['Wukv', 'ckvT'], w=[('pf', b)])
                evac(Vt[:, t8 * 8:(t8 + 1) * 8, 0:64], PF(b).rearrange("p (t e) -> p t e", e=64), r=[('pf', b)], w=[vk])

            def qchunk(t0):
                nt = min(5, NT - t0)
                b = bankf()

                def fq(E):
                    last = None
                    for i in range(nt):
                        for kc in range(3):
                            last = E.matmul(psf[:, b * 512 + i * 96: b * 512 + (i + 1) * 96],
                                            lhsT=cqT[:, kc, (t0 + i) * 128:(t0 + i + 1) * 128],
                                            rhs=Wuq[:, kc, h * 96:(h + 1) * 96], start=(kc == 0), stop=(kc == 2))
                    return last
                S.op('pe', fq, r=['Wuq', 'cqT'], w=[('pf', b)])
                p3 = psf[:, b * 512: b * 512 + nt * 96].rearrange("p (t c) -> p t c", c=96)
                S.op('dve', lambda E: E.tensor_scalar(out=qtm[:, t0:t0 + nt, 0:64], in0=p3[:, :, 0:64],
                                                      scalar1=qs, scalar2=None, op0=ALU.mult),
                     r=[('pf', b)], w=['qtm'])
                rope_tm(p3[:, :, 64:96], cosQ[:, t0:t0 + nt, :], sinQ[:, t0:t0 + nt, :], qtm[:, t0:t0 + nt, 64:96], nt,
                        [('pf', b), 'ropeq'], 'qtm')

            def tchunk(t0):
                nt = min(8, NT - t0)

                def ftq(E):
                    last = None
                    for i in range(nt):
                        last = E.transpose(psb[0:96, i * 128:(i + 1) * 128], qtm[:, t0 + i, :], ident[:, :])
                    return last
                S.op('pe', ftq, r=['qtm', 'ident'], w=['psb'])
                S.op('dve', lambda E: E.tensor_copy(out=QT[0:96, t0 * 128:(t0 + nt) * 128],
                                                    in_=psb[0:96, 0:nt * 128]), r=['psb'], w=[qk])
            for t8 in range(8):
                chunks.append(lambda t8=t8: vchunk(t8))
            for t0 in range(0, NT, 5):
                chunks.append(lambda t0=t0: qchunk(t0))
            for t0 in range(0, NT, 8):
                chunks.append(lambda t0=t0: tchunk(t0))
            return chunks

        if h == 0:
            for ch in build_vq(0, nb):
                ch()
        Vt, QT = Vts[h % 2], QTs[h % 2]
        vk, qk = ('Vt', h % 2), ('QT', h % 2)
        items = []
        for gq, (e0, n) in enumerate(GROUPS):
            tq0, ntq = e0 // 128, n // 128
            kfirst = 47 + tq0
            per = 1024 // n
            units = []
            kb = 0
            while kb < kfirst:
                lim = min(kfirst, (kb // 16 + 1) * 16, kb + per)
                units.append((list(range(kb, lim)), None))
                kb = lim
            for m in range(ntq):
                units.append(([kfirst + m], m))
            for ui, (kbs, m) in enumerate(units):
                items.append(dict(e0=e0, n=n, tq0=tq0, ntq=ntq, ob=4 + (gq % 2), kbs=kbs, m=m, first=(ui == 0),
                                  last=(ui == len(units) - 1)))

        def emit_S(it, i):
            sp, n, e0, m = ((pctr[0] + i) % 2) * 2, it['n'], it['e0'], it['m']
            if m is None:
                def fs(E, it=it, sp=sp, n=n, e0=e0, QT=QT):
                    last = None
                    for idx, kbk in enumerate(it['kbs']):
                        last = E.matmul(psf[:, sp * 512 + idx * n: sp * 512 + (idx + 1) * n],
                                        lhsT=KT[0:96, kbk * 128:(kbk + 1) * 128], rhs=QT[0:96, e0:e0 + n],
                                        start=True, stop=True)
                    return last
                S.op('pe', fs, r=['KTn', 'KTr', qk], w=[('pf', sp), ('pf', sp + 1)])
            else:
                kbk = it['kbs'][0]
                cols = n - 128 * m
                mm(psf[:, sp * 512: sp * 512 + cols],
                   [(KT[0:96, kbk * 128:(kbk + 1) * 128], QT[0:96, e0 + 128 * m:e0 + n])],
                   w=[('pf', sp), ('pf', sp + 1)], r=['KTn', 'KTr', qk])

        def emit_A(it, i):
            sp, pi = ((pctr[0] + i) % 2) * 2, (pctr[0] + i) % 3
            n, m = it['n'], it['m']
            kb0 = it['kbs'][0]
            tot = len(it['kbs']) * n if m is None else n - 128 * m
            act(PTm[pi][:, 0:tot], psf[:, sp * 512: sp * 512 + tot], AF.Exp,
                r=[('pf', sp), ('pf', sp + 1), 'cv'], w=[('PTm', pi)], bias=cvc(C_VB + min(kb0 // 16, 3)))
            if m is not None:
                S.op('dve', lambda E, pi=pi: E.tensor_tensor(out=PTm[pi][:, 0:128], in0=PTm[pi][:, 0:128],
                                                             in1=tri[:, :], op=ALU.mult),
                     r=[('PTm', pi), 'tri'], w=[('PTm', pi)])

        def emit_PV(it, i):
            pi = (pctr[0] + i) % 3
            n, m, ntq, ob, tq0 = it['n'], it['m'], it['ntq'], it['ob'], it['tq0']
            if m is None:
                blocks = [(idx * n, kbk, list(range(ntq))) for idx, kbk in enumerate(it['kbs'])]
            else:
                blocks = [(0, it['kbs'][0], list(range(m, ntq)))]

            def pv(E, pi=pi, blocks=blocks, ob=ob, first=it['first'], Vt=Vt):
                last = None
                st = first
                for (col0, kbk, qts) in blocks:
                    for qt in qts:
                        last = E.matmul(psf[:, ob * 512 + qt * 65: ob * 512 + qt * 65 + 65],
                                        lhsT=PTm[pi][:, col0 + (qt - qts[0]) * 128: col0 + (qt - qts[0] + 1) * 128],
                                        rhs=Vt[:, kbk, :], start=st, stop=False, skip_group_check=True)
                        st = False
                return last
            S.op('pe', pv, r=[('PTm', pi), vk, 'Vones'], w=[('pf', ob)])
            if it['last']:
                o3 = psf[:, ob * 512: ob * 512 + ntq * 65].rearrange("p (t c) -> p t c", c=65)
                S.op('dve', lambda E, o3=o3, ntq=ntq: E.tensor_scalar(out=rden[:, 0:ntq], in0=o3[:, :, 64],
                                                                      scalar1=1e-30, scalar2=None, op0=ALU.max),
                     r=[('pf', ob)], w=['rden'])
                S.op('dve', lambda E, ntq=ntq: E.reciprocal(out=rden[:, 0:ntq], in_=rden[:, 0:ntq]),
                     r=['rden'], w=['rden'])
                hoff = (h % 2) * 64
                S.op('dve', lambda E, o3=o3, ntq=ntq, tq0=tq0, hoff=hoff: E.tensor_tensor(
                    out=ypair[:, tq0:tq0 + ntq, hoff:hoff + 64], in0=o3[:, :, 0:64],
                    in1=rden[:, 0:ntq].unsqueeze(2).to_broadcast([128, ntq, 64]), op=ALU.mult),
                    r=[('pf', ob), 'rden'], w=['ypair'])

        for i, it in enumerate(items):
            emit_S(it, i)
            emit_A(it, i)
            if i >= 1:
                emit_PV(items[i - 1], i - 1)
            if i == 12 and h < 7:
                pending = build_vq(h + 1, lambda: 6)
            if i > 12 and h < 7 and pending:
                pending.pop(0)()
        emit_PV(items[-1], len(items) - 1)
        while h < 7 and pending:
            pending.pop(0)()
        pctr[0] += len(items)
        if h % 2 == 1:
            for t0 in range(0, NT, 8):
                nt = min(8, NT - t0)

                def fty(E, t0=t0, nt=nt):
                    last = None
                    for i in range(nt):
                        last = E.transpose(psb[:, i * 128:(i + 1) * 128], ypair[:, t0 + i, :], ident[:, :])
                    return last
                S.op('pe', fty, r=['ypair', 'ident'], w=['psb'])
                S.op('dve', lambda E, t0=t0, nt=nt, h=h: E.tensor_copy(out=y_mlaT[:, h // 2, t0 * 128:(t0 + nt) * 128],
                                                                       in_=psb[:, 0:nt * 128]), r=['psb'], w=['ymla'])
    if DEBUG:
        S.dma('sp', lambda E: E.dma_start(out=dbg["dbg_ymla"][:, :, :], in_=y_mlaT[:, :, :]), r=['ymla'])
    S.barrier()


    MERG = END - 34816
    mergedT = nc.alloc_sbuf_tensor_at("mergedT", [128, 8, NE], BF16, offset=MERG)
    tp = [T_OFF]
    memT = sbt("memT", [128, 8, 256], BF16, at=tp)
    KmT = sbt("KmT", [128, 4, 256], BF16, at=tp)
    VmT = sbt("VmT", [128, 2, 512], BF16, at=tp)
    OVC = tp[0]
    xbufC = [sbt(f"xbufC{i}", [128, D], F32, at=tp) for i in range(2)]
    xhatC = sbt("xhatC", [128, 2, D], BF16, at=tp)
    assert tp[0] <= MERG
    xctr[0] = 0
    tiles = [xload(xbufC, memx[i * 128:(i + 1) * 128, :]) for i in range(2)]
    rms_T(tiles, C_GMEM, lambda kc: memT[:, kc, :], 'memT', xhatC, 'xhC')
    (wkm,), kkm = wload([wk(w_memkv, 0, 512)])
    for h in range(4):
        b = nb()
        mm(PF(b, 256), [(wkm[:, kc, h * 128:(h + 1) * 128], memT[:, kc, :]) for kc in range(8)],
           w=[('pf', b)], r=[kkm, 'memT'])
        act(KmT[:, h, :], PF(b, 256), AF.Copy, r=[('pf', b)], w=['KmT'])
    (wvm,), kvm = wload([wk(w_memkv, 512, 512)])
    for mt in range(2):
        b = nb()
        mm(PF(b), [(memT[:, kc, mt * 128:(mt + 1) * 128], wvm[:, kc, :]) for kc in range(8)],
           w=[('pf', b)], r=[kvm, 'memT'])
        act(VmT[:, mt, :], PF(b), AF.Copy, r=[('pf', b)], w=['VmT'])
    S.barrier()
    tp = [OVC]
    qm = sbt("qm", [128, 512], BF16, at=tp)
    PTc = sbt("PTc", [128, 2, 512], BF16, at=tp)
    recc = sbt("recc", [128, 512], F32, at=tp)
    gsb = [sbt(f"gsb{i}", [128, 512], BF16, at=tp) for i in range(3)]
    mtc = [sbt(f"mtc{i}", [128, 512], F32, at=tp) for i in range(2)]
    assert tp[0] <= MERG
    (wqm,), kqm = wload([wk(w_in, OFF_DIL, 512)])
    qmB = sbt("qmB", [128, 512], BF16, at=tp)
    PTcB = sbt("PTcB", [128, 2, 512], BF16, at=tp)
    assert tp[0] <= MERG
    qm2, PTc2 = [qm, qmB], [PTc, PTcB]
    itsC = [(h, e0, n) for h in range(4) for (e0, n) in GROUPS]

    def c_s1(i):
        h, e0, n = itsC[i]
        b = nb()
        mm(PF(b, n), [(wqm[:, kc, h * 128:(h + 1) * 128], hT_own[:, kc, e0:e0 + n]) for kc in range(8)],
           w=[('pf', b)], r=[kqm, 'hT'])
        act(qm2[i % 2][:, 0:n], PF(b, n), AF.Copy, r=[('pf', b)], w=[('qm', i % 2)], scale=dscale)

    def c_s2(i):
        h, e0, n = itsC[i]
        for mt in range(2):
            b = nb()
            mm(PF(b, n), [(KmT[:, h, mt * 128:(mt + 1) * 128], qm2[i % 2][:, 0:n])], w=[('pf', b)],
               r=['KmT', ('qm', i % 2)])
            act(PTc2[i % 2][:, mt, 0:n], PF(b, n), AF.Exp, r=[('pf', b)], w=[('PTc', i % 2, mt)])

    def c_s3(i):
        h, e0, n = itsC[i]
        P_ = PTc2[i % 2]
        pk = [('PTc', i % 2, 0), ('PTc', i % 2, 1)]
        bo, bd = nb(), nb()
        mm(PF(bo, n), [(VmT[:, mt, h * 128:(h + 1) * 128], P_[:, mt, 0:n]) for mt in range(2)],
           w=[('pf', bo)], r=['VmT'] + pk)
        mm(PF(bd, n), [(ones[:, :], P_[:, mt, 0:n]) for mt in range(2)], w=[('pf', bd)], r=['ones'] + pk)
        S.op('dve', lambda E: E.reciprocal(out=recc[:, 0:n], in_=PF(bd, n)), r=[('pf', bd)], w=['recc'])
        S.op('dve', lambda E: E.tensor_tensor(out=y_memT[:, h, e0:e0 + n], in0=PF(bo, n), in1=recc[:, 0:n],
                                              op=ALU.mult), r=[('pf', bo), 'recc'], w=['ymem'])
    NC_ = len(itsC)
    c_s1(0)
    c_s1(1)
    c_s2(0)
    for i in range(NC_):
        if i + 2 < NC_:
            c_s1(i + 2)
        if i + 1 < NC_:
            c_s2(i + 1)
        c_s3(i)
    if DEBUG:
        S.dma('sp', lambda E: E.dma_start(out=dbg["dbg_ymem"][:, :, :], in_=y_memT[:, :, :]), r=['ymem'])
    yT = [y_mlaT, y_dilT, y_memT]
    ykeys = ['ymla', 'ydil', 'ymem']
    for oc in range(8):
        wg, kg = wload([wk(w_in, OFF_MEMQ + br * 1024 + oc * 128, 128) for br in range(3)])
        wb_, kb_ = wload([w_br[br][:, oc * 128:(oc + 1) * 128].rearrange("(k p) n -> p k n", p=128) for br in range(3)])
        for (e0, n) in GROUPS:
            for br in range(3):
                bg = nb()
                mm(PF(bg, n), [(wg[br][:, kc, :], hT_own[:, kc, e0:e0 + n]) for kc in range(8)],
                   w=[('pf', bg)], r=[kg, 'hT'])
                act(gsb[br][:, 0:n], PF(bg, n), AF.Sigmoid, r=[('pf', bg), 'cv'], w=[('gsb', br)],
                    bias=cvc(C_BG + br * 8 + oc))
                bb = nb()
                mm(PF(bb, n), [(wb_[br][:, kc, :], yT[br][:, kc, e0:e0 + n]) for kc in range(4)],
                   w=[('pf', bb)], r=[kb_, ykeys[br]])
                di = 0 if br == 0 else 1
                S.op('dve', lambda E, bb=bb, n=n, br=br, di=di: E.tensor_tensor(out=mtc[di][:, 0:n], in0=PF(bb, n),
                                                                               in1=gsb[br][:, 0:n], op=ALU.mult),
                     r=[('pf', bb), ('gsb', br)], w=[('mtc', di)])
                if br == 1:
                    S.op('dve', lambda E, n=n: E.tensor_tensor(out=mtc[0][:, 0:n], in0=mtc[0][:, 0:n],
                                                               in1=mtc[1][:, 0:n], op=ALU.add),
                         r=[('mtc', 0), ('mtc', 1)], w=[('mtc', 0)])
                if br == 2:
                    S.op('dve', lambda E, n=n, oc=oc, e0=e0: E.tensor_tensor(out=mergedT[:, oc, e0:e0 + n],
                                                                            in0=mtc[0][:, 0:n], in1=mtc[1][:, 0:n],
                                                                            op=ALU.add),
                         r=[('mtc', 0), ('mtc', 1)], w=['merged'])
    if DEBUG:
        S.dma('sp', lambda E: E.dma_start(out=dbg["dbg_merged"][:, :, :], in_=mergedT[:, :, :]), r=['merged'])
    S.barrier()

    x1 = nc.alloc_sbuf_tensor_at("x1", [128, 16, D], F32, offset=Y_OFF)
    tp = [Y_OFF + 65536]
    xbufD = [sbt("xbufD0", [128, D], F32, at=tp)]
    tmpD = sbt("tmpD", [128, D], F32, at=tp)
    x1pre = sbt("x1pre", [128, D], F32, at=tp)
    xhatD = sbt("xhatD", [128, 1, D], BF16, at=tp)
    assert tp[0] <= MERG
    h2T = hT_own
    (wo0,), ko0 = wload([wk(w_o, 0, 512)])
    (wo1,), ko1 = wload([wk(w_o, 512, 512)])
    xctr[0] = 0
    dstD = {}

    def d_s1a(t):
        p = t % 3
        c0 = 48 + 4 * p
        for half, wo_, ko in ((0, wo0, ko0), (1, wo1, ko1)):
            mm(psf[:, p * 1024 + half * 512: p * 1024 + (half + 1) * 512],
               [(mergedT[:, kc, t * 128:(t + 1) * 128], wo_[:, kc, :]) for kc in range(8)],
               w=[('pf', 2 * p + half)], r=[ko, 'merged'])
        pk = [('pf', 2 * p), ('pf', 2 * p + 1)]
        pfull = psf[:, p * 1024:(p + 1) * 1024]
        S.op('act', lambda E: E.activation(out=junk[:, :], in_=pfull, func=AF.Square,
                                           accum_out=small[:, c0:c0 + 1]), r=pk, w=['junk', ('sm', c0)])
        rstd_from_ss(small[:, c0:c0 + 1], small[:, c0 + 2:c0 + 3], small[:, c0 + 1:c0 + 2], 1024.0,
                     [('sm', c0)], [('sm', c0 + 2)], [('sm', c0 + 1)])

    def d_s1b(t):
        p = t % 3
        c0 = 48 + 4 * p
        pk = [('pf', 2 * p), ('pf', 2 * p + 1)]
        pfull = psf[:, p * 1024:(p + 1) * 1024]
        xa, xk = xload(xbufD, xc[6016 + t * 128: 6016 + (t + 1) * 128, :])
        S.op('dve', lambda E: E.scalar_tensor_tensor(out=tmpD[:, :], in0=pfull, scalar=small[:, c0 + 2:c0 + 3],
                                                     in1=gbc[:, 0, :], op0=ALU.mult, op1=ALU.mult),
             r=pk + [('sm', c0 + 2), 'gbc'], w=['tmpD'])
        dst = x1pre[:, :] if t == 0 else x1[:, t - 1, :]
        dk = ('x1', t)
        S.op('dve', lambda E: E.tensor_tensor(out=dst, in0=tmpD[:, :], in1=xa, op=ALU.add),
             r=['tmpD', xk], w=[dk])
        dstD[t] = (dst, dk)

    def d_s2(t):
        rms_T([dstD[t]], C_GFFN, lambda kc: h2T[:, kc, t * 128:(t + 1) * 128], 'h2T', xhatD, 'xhD')

    d_s1a(0)
    d_s1a(1)
    d_s1b(0)
    for t in range(NT):
        if t + 2 < NT:
            d_s1a(t + 2)
        if t + 1 < NT:
            d_s1b(t + 1)
        d_s2(t)
    S.barrier()

    tp = [Y_OFF + 65536]
    gbuf = sbt("gbuf", [128, 22, 512], BF16, at=tp)
    ubuf = [sbt(f"ubuf{i}", [128, 528], F32, at=tp) for i in range(4)]
    zt = [sbt(f"zt{i}", [128, 512], F32, at=tp) for i in range(4)]
    sgs = [sbt(f"sg{i}", [128, 512], F32, at=tp) for i in range(2)]
    carry = sbt("carry", [128, 44, 2], F32, at=tp)
    fo0 = sbt("fo0", [128, 4, 512], F32, at=tp)
    tmpE = zt[0]
    eb = [0]

    def nbe():
        eb[0] = (eb[0] + 1) % 3
        return eb[0]
    for j in range(4):
        e0 = 128 + 512 * j
        for fc0 in range(0, 22, 4):
            nf = min(4, 22 - fc0)
            wupk = [('wupb', c) for c in range(4)]
            wgt, kgt = wload_b(wk(wupb, fc0 * 128, nf * 128), wupk)
            wvt, kvt = wload_b(wk(wupb, DFF + fc0 * 128, nf * 128), wupk)
            for fi in range(nf):
                fc = fc0 + fi
                if j > 0:
                    for fcn in ([0, 1] if fc == 0 else ([fc + 1] if fc + 1 < 22 else [])):
                        for half_ in range(2):
                            S.op('dve', lambda E, fcn=fcn, half_=half_: E.tensor_copy(
                                out=ubuf[half_ * 2 + fcn % 2][:, 0:2], in_=carry[:, half_ * 22 + fcn, :]),
                                r=[('carry', half_ * 22 + fcn)], w=[('ubc', half_ * 2 + fcn % 2)])
                for half, wt, kw_ in ((0, wgt, kgt), (1, wvt, kvt)):
                    ci = half * 22 + fc
                    b = nbe()
                    mm(PF(b), [(wt[:, kc, fi * 128:(fi + 1) * 128], h2T[:, kc, e0:e0 + 512]) for kc in range(8)],
                       w=[('pf', b)], r=[kw_, 'h2T'])
                    ub = ubuf[half * 2 + fc % 2]
                    uk = ('ub', half * 2 + fc % 2)
                    uck = ('ubc', half * 2 + fc % 2)
                    ztb = zt[half * 2 + fc % 2]
                    if j == 0:
                        pb_ = 3 + (ci % 4)
                        mm(psf[:, pb_ * 512: pb_ * 512 + 2],
                           [(wt[:, kc, fi * 128:(fi + 1) * 128], h2T[:, kc, 126:128]) for kc in range(8)],
                           w=[('pf', pb_)], r=[kw_, 'h2T'])
                        S.op('dve', lambda E, ub=ub, pb_=pb_: E.tensor_scalar(out=ub[:, 0:2], in0=psf[:, pb_ * 512: pb_ * 512 + 2],
                                                                             scalar1=cvc(C_UFLAG), scalar2=None, op0=ALU.mult),
                             r=[('pf', pb_), 'cv'], w=[uck])
                    act(ub[:, 2:514], PF(b), AF.Copy, r=[('pf', b)], w=[uk])
                    S.op('dve', lambda E, ub=ub, ci=ci: E.tensor_copy(out=carry[:, ci, :], in_=ub[:, 512:514]),
                         r=[uk], w=[('carry', ci)])
                    zk = ('zt', half * 2 + fc % 2)
                    act(ztb[:, :], ub[:, 0:512], AF.Identity, r=[uk, uck, 'cv'], w=[zk], scale=cvc(C_CW + ci),
                        bias=cvc(C_CB + ci))
                    S.op('dve', lambda E, ub=ub, ci=ci, ztb=ztb: E.scalar_tensor_tensor(
                        out=ztb[:, :], in0=ub[:, 1:513], scalar=cvc(C_CW + 44 + ci), in1=ztb[:, :],
                        op0=ALU.mult, op1=ALU.add), r=[uk, uck, zk, 'cv'], w=[zk])
                    S.op('dve', lambda E, ub=ub, ci=ci, ztb=ztb: E.scalar_tensor_tensor(
                        out=ztb[:, :], in0=ub[:, 2:514], scalar=cvc(C_CW + 88 + ci), in1=ztb[:, :],
                        op0=ALU.mult, op1=ALU.add), r=[uk, zk, 'cv'], w=[zk])
                sg = sgs[fc % 2]
                sk = ('sg', fc % 2)
                zg, zv = zt[fc % 2], zt[2 + fc % 2]
                act(sg[:, :], zg[:, :], AF.Silu, r=[('zt', fc % 2)], w=[sk])
                S.op('pool', lambda E, fc=fc, sg=sg, zv=zv: E.tensor_tensor(out=gbuf[:, fc, :], in0=sg[:, :], in1=zv[:, :],
                                                                          op=ALU.mult),
                     r=[sk, ('zt', 2 + fc % 2)], w=['gbuf'])
        for half in range(2):
            for (f0, nfk) in ((0, 8), (8, 8), (16, 6)):
                wd, kd = wload_b(wdnb[f0 * 128:(f0 + nfk) * 128, half * 512:(half + 1) * 512]
                                 .rearrange("(k p) n -> p k n", p=128), [('wdnb', 0), ('wdnb', 1)])
                for tt in range(4):
                    def fd(E, wd=wd, f0=f0, nfk=nfk, tt=tt):
                        last = None
                        for k in range(nfk):
                            last = E.matmul(PF(3 + tt), lhsT=gbuf[:, f0 + k, tt * 128:(tt + 1) * 128], rhs=wd[:, k, :],
                                            start=(f0 == 0 and k == 0), stop=(f0 == 16 and k == nfk - 1))
                        return last
                    S.op('pe', fd, r=[kd, 'gbuf'], w=[('pf', 3 + tt)])
            for tt in range(4):
                col = 40 + half * 4 + tt
                S.op('act', lambda E, tt=tt, col=col: E.activation(out=junk[:, 0:512], in_=PF(3 + tt), func=AF.Square,
                                                                   accum_out=small[:, col:col + 1]),
                     r=[('pf', 3 + tt)], w=['junk', ('sm', col)])
                if half == 0:
                    act(fo0[:, tt, :], PF(3 + tt), AF.Copy, r=[('pf', 3 + tt)], w=[('fo0', tt)])
            if half == 1:
                S.op('dve', lambda E: E.tensor_tensor(out=small[:, 48:52], in0=small[:, 40:44], in1=small[:, 44:48],
                                                      op=ALU.add), r=[('sm', c) for c in range(40, 48)], w=[('sm', 48)])
                rstd_from_ss(small[:, 48:52], small[:, 56:60], small[:, 52:56], 1024.0, [('sm', 48)], [('sm', 56)],
                             [('sm', 52)])
                for tt in range(4):
                    xt = x1[:, 4 * j + tt, :]
                    xk = ('x1', 4 * j + tt + 1)
                    rs = small[:, 56 + tt:57 + tt]
                    S.op('dve', lambda E, tt=tt, rs=rs: E.scalar_tensor_tensor(
                        out=fo0[:, tt, :], in0=fo0[:, tt, :], scalar=rs, in1=gbc[:, 1, 0:512], op0=ALU.mult,
                        op1=ALU.mult), r=[('fo0', tt), ('sm', 56), 'gbc'], w=[('fo0', tt)])
                    S.op('dve', lambda E, tt=tt, xt=xt: E.tensor_tensor(out=xt[:, 0:512], in0=xt[:, 0:512],
                                                                        in1=fo0[:, tt, :], op=ALU.add),
                         r=[('fo0', tt), xk], w=[xk])
                    S.op('dve', lambda E, tt=tt, rs=rs: E.scalar_tensor_tensor(
                        out=tmpE[:, :], in0=PF(3 + tt), scalar=rs, in1=gbc[:, 1, 512:1024], op0=ALU.mult,
                        op1=ALU.mult), r=[('pf', 3 + tt), ('sm', 56), 'gbc'], w=[('zt', 0)])
                    S.op('dve', lambda E, tt=tt, xt=xt: E.tensor_tensor(out=xt[:, 512:1024], in0=xt[:, 512:1024],
                                                                        in1=tmpE[:, :], op=ALU.add),
                         r=[('zt', 0), xk], w=[xk])
                    row = (4 * j + tt) * 128
                    S.dma('sp', lambda E, row=row, xt=xt: E.dma_start(out=outd[row:row + 128, :], in_=xt), r=[xk])

    return nc, es, S


def _finish(nc, es, S):
    S.barrier()
    with nc.Block() as block:
        @block.tensor
        def _(E):
            for f in S.prog['pe']:
                f(E)

        @block.scalar
        def _(E):
            for f in S.prog['act']:
                f(E)

        @block.vector
        def _(E):
            for f in S.prog['dve']:
                f(E)

        @block.gpsimd
        def _(E):
            for f in S.prog['pool']:
                f(E)

        @block.sync
        def _(E):
            for f in S.prog['sp']:
                f(E)
    es.close()
    return nc


def host_inputs(inputs):
    import ml_dtypes
    x = np.asarray(inputs["x"], np.float32)
    mem = np.asarray(inputs["mem"], np.float32)
    pos = np.asarray(inputs["positions"], np.int32)

    def P(k):
        return np.asarray(inputs[k], np.float32)[0]
    shared = {
        "gbc": np.stack([P("g_post_mix"), P("g_post_ffn")]).astype(np.float32),
        "tri": np.triu(np.ones((128, 128), np.float32)),
        "ident": np.eye(128, dtype=np.float32),
        "w_in": P("w_in"), "w_uq": P("w_uq"), "w_ukv": P("w_ukv"), "w_mem_kv": P("w_mem_kv"),
        "w_br_mla": P("w_br_mla"), "w_br_dil": P("w_br_dil"), "w_br_mem": P("w_br_mem"), "w_o": P("w_o"),
        "w_ffn_up": P("w_ffn_up"), "w_ffn_down": P("w_ffn_down"),
    }
    db = np.zeros((12, 128, 256), np.float32)
    slopes = np.exp2(-8.0 * np.arange(1, 13, dtype=np.float32) / 12).reshape(4, 3).T
    k = np.arange(128)[:, None]
    i = np.arange(128)[None, :]
    for g in range(3):
        for hd in range(4):
            a = slopes[g, hd] * DIL_D[g]
            dist = i - k
            db[g * 4 + hd, :, 0:128] = np.where(dist >= 0, -a * dist, NEG)
            dist2 = 128 + i - k
            db[g * 4 + hd, :, 128:256] = np.where(k >= i, -a * dist2, NEG)
    shared["dbias"] = db

    def cols(v, n):
        return v.reshape(n, 128).T
    cv0 = np.zeros((128, NCV), np.float32)
    cv0[:, C_GPRE:C_GPRE + 8] = cols(P("g_pre_mix"), 8)
    cv0[:, C_GMEM:C_GMEM + 8] = cols(P("g_mem"), 8)
    cv0[:, C_GFFN:C_GFFN + 8] = cols(P("g_pre_ffn"), 8)
    cv0[:, C_QN:C_QN + 3] = cols(P("mla_q_norm"), 3)
    cv0[:, C_KVN:C_KVN + 2] = cols(P("mla_kv_norm"), 2)
    cv0[:, C_BG:C_BG + 24] = cols(P("b_gate"), 24)
    cw = P("conv_w")
    for j in range(3):
        cv0[:, C_CW + j * 44: C_CW + (j + 1) * 44] = cols(cw[j], 44)
    cv0[:, C_CB:C_CB + 44] = cols(P("conv_b"), 44)
    cv0[:, C_INVF:C_INVF + 16] = (np.float32(10000.0) ** (-np.arange(16, dtype=np.float32) / np.float32(16)))[None, :]
    in_maps = []
    for c in range(8):
        b, q = c // 4, c % 4
        xcx = np.zeros((SEQ, D), np.float32)
        pc = np.zeros((SEQ,), np.int32)
        lo = 6144 - CH * q
        xcx[lo:] = x[b, 0: CH * (q + 1)]
        pc[lo:] = pos[b, 0: CH * (q + 1)]
        cvq = cv0.copy()
        for cidx in range(3):
            cvq[:, C_VB + cidx] = 0.0 if (cidx + q) >= 3 else NEG
        ridx = 0
        for g in range(3):
            d = DIL_D[g]
            for r_ in range(d):
                kk = np.arange(128)
                tau0 = 3968 + (2048 - 128 * d) + kk * d + r_
                cvq[:, C_DVB0 + ridx] = np.where(tau0 >= lo, 0.0, NEG)
                tau1 = 3968 + 2048 + kk * d + r_
                cvq[:, C_DVB1 + ridx] = np.where(tau1 >= lo, 0.0, NEG)
                ridx += 1
        cvq[:, C_UFLAG] = 1.0 if q > 0 else 0.0
        m = dict(shared)
        m["xc"] = xcx
        m["posc"] = np.ascontiguousarray(pc.reshape(64, 128).T)
        m["memx"] = np.ascontiguousarray(mem[b])
        m["cv"] = cvq
        in_maps.append(m)
    return in_maps


def kernel(**inputs):
    in_maps = host_inputs(inputs)
    nc, es, S = build_program()
    nc = _finish(nc, es, S)
    res = run_bass_kernel_spmd(nc, in_maps, core_ids=list(range(8)))
    out = np.zeros((2, SEQ, D), np.float32)
    for c in range(8):
        b, q = c // 4, c % 4
        out[b, q * CH:(q + 1) * CH] = np.asarray(res.results[c]["out"], np.float32)
    return out
```

```python
from contextlib import ExitStack
import numpy as np
import concourse.bass as bass
import concourse.mybir as mybir
from concourse.bass_utils import run_bass_kernel_spmd

F32, BF16, I32 = mybir.dt.float32, mybir.dt.bfloat16, mybir.dt.int32
ALU = mybir.AluOpType
AF = mybir.ActivationFunctionType
NEG = -30000.0
D = 1024
SEQ = 8192
CH = 2048
NE = 2176
NT = 17
DFF = 2816
DIN = 8864
OFF_Q, OFF_KV, OFF_KR, OFF_DIL, OFF_MEMQ = 384, 640, 672, 672 + 4608, 672 + 4608 + 512
DIL_D = (1, 4, 16)
EPS = 1e-6
TWO_PI = 6.283185307179586
(C_GPRE, C_GMEM, C_GFFN, C_QN, C_KVN, C_BG, C_CW, C_CB, C_VB, C_ZERO, C_INVF, C_DVB0, C_DVB1, C_UFLAG,
 NCV) = (0, 8, 16, 24, 27, 29, 53, 185, 229, 232, 233, 249, 270, 291, 292)
GROUPS = [(0, 128), (128, 512), (640, 512), (1152, 512), (1664, 512)]
DEBUG = False


class Sched:
    CE = ('pe', 'act', 'dve', 'pool')

    def __init__(s, nc, sems, dsems):
        s.eng = {'pe': nc.tensor, 'act': nc.scalar, 'dve': nc.vector, 'pool': nc.gpsimd, 'sp': nc.sync}
        s.prog = {e: [] for e in s.eng}
        s.sem = sems
        s.dsems = dsems
        s.tick = {e: 0 for e in s.CE}
        s.seen = {e: {} for e in s.eng}
        s.bufs = {}
        s.dcount = {q: 0 for q in dsems}

    def _semh(s, k):
        return s.sem[k] if isinstance(k, str) else s.dsems[k[1]][k[2]]

    def _deps(s, eng, r, w):
        need = {}

        def add(k, v):
            if need.get(k, 0) < v:
                need[k] = v
        for key in r:
            b = s.bufs.get(key)
            if b and b['w']:
                add(*b['w'])
        for key in w:
            b = s.bufs.get(key)
            if b:
                if b['w'] and b['w'][0] != eng:
                    add(*b['w'])
                for k, v in b['r'].items():
                    if k != eng:
                        add(k, v)
        return need

    def _waits(s, q, need):
        for k, v in need.items():
            if s.seen[q].get(k, 0) >= v:
                continue
            s.seen[q][k] = v
            sem = s._semh(k)
            s.prog[q].append(lambda E, sem=sem, v=v: E.wait_ge(sem, v))

    def _record(s, ev, r, w):
        for key in r:
            b = s.bufs.setdefault(key, {'w': None, 'r': {}})
            if b['r'].get(ev[0], 0) < ev[1]:
                b['r'][ev[0]] = ev[1]
        for key in w:
            s.bufs[key] = {'w': ev, 'r': {}}

    def op(s, eng, fn, r=(), w=()):
        s._waits(eng, s._deps(eng, r, w))
        s.tick[eng] += 1
        sem = s.sem[eng]
        s.prog[eng].append(lambda E, fn=fn, sem=sem: fn(E).then_inc(sem, 1))
        s._record((eng, s.tick[eng]), r, w)

    def dma(s, q, fn, r=(), w=()):
        i = s.dcount[q]
        s.dcount[q] += 1
        ns = len(s.dsems[q])
        j, val = i % ns, 16 * (i // ns + 1)
        need = s._deps(None, r, w)
        if i >= ns:
            k = ('d', q, j)
            need[k] = max(need.get(k, 0), val - 16)
        s._waits(q, need)
        sem = s.dsems[q][j]
        s.prog[q].append(lambda E, fn=fn, sem=sem: fn(E).then_inc(sem, 16))
        s._record((('d', q, j), val), r, w)

    def barrier(s):
        for e in s.eng:
            need = {}
            for c in s.CE:
                if s.tick[c] > 0 and c != e:
                    need[c] = s.tick[c]
            for q in s.dsems:
                n, ns = s.dcount[q], len(s.dsems[q])
                for j in range(min(n, ns)):
                    need[('d', q, j)] = 16 * ((n - 1 - j) // ns + 1)
            s._waits(e, need)


def dil_geom(d):
    nq_tot = NE // d
    qb = []
    a = 0
    while a < nq_tot:
        qb.append((a, min(128, nq_tot - a)))
        a += 128
    kt = [(0, 128)] + [(128 + a0, n) for (a0, n) in qb]
    return qb, kt


def build_program():
    nc = bass.Bass("TRN2", target_bir_lowering=False, dynamic_dma_scratch_size=4096)

    def din(name, shape, dt=F32):
        return nc.dram_tensor(name, list(shape), dt, kind="ExternalInput").ap()
    xc = din("xc", [SEQ, D])
    posc = din("posc", [128, 64], I32)
    memx = din("memx", [256, D])
    cvd = din("cv", [128, NCV])
    gbcd = din("gbc", [2, D])
    dbd = din("dbias", [12, 128, 256])
    trid = din("tri", [128, 128])
    identd = din("ident", [128, 128])
    w_in = din("w_in", [D, DIN])
    w_uq = din("w_uq", [384, 768])
    w_ukv = din("w_ukv", [256, 1024])
    w_memkv = din("w_mem_kv", [D, 1024])
    w_br = [din("w_br_mla", [512, D]), din("w_br_dil", [512, D]), din("w_br_mem", [512, D])]
    w_o = din("w_o", [D, D])
    w_up = din("w_ffn_up", [D, 2 * DFF])
    w_dn = din("w_ffn_down", [DFF, D])
    outd = nc.dram_tensor("out", [CH, D], F32, kind="ExternalOutput").ap()
    wupb = nc.dram_tensor("wupb_scratch", [D, 2 * DFF], BF16).ap()
    wdnb = nc.dram_tensor("wdnb_scratch", [DFF, D], BF16).ap()
    dbg = {}
    if DEBUG:
        for nm in ("dbg_ydil", "dbg_ymla", "dbg_ymem", "dbg_merged"):
            dbg[nm] = nc.dram_tensor(nm, [128, 8 if nm == "dbg_merged" else 4, NE], BF16, kind="ExternalOutput").ap()
        dbg["dbg_h"] = nc.dram_tensor("dbg_h", [128, 8, NE], BF16, kind="ExternalOutput").ap()

    END = 212992
    cur = [4608]

    def sbt(name, shape, dt, at=None):
        esz = 4 if dt in (F32, I32) else 2
        n = 1
        for v in shape[1:]:
            n *= v
        nbytes = (n * esz + 63) // 64 * 64
        if at is None:
            off = cur[0]
            cur[0] += nbytes
        else:
            off = at[0]
            at[0] += nbytes
        assert off + nbytes <= END, (name, off, nbytes)
        return nc.alloc_sbuf_tensor_at(name, list(shape), dt, offset=off)

    ident = sbt("ident", [128, 128], BF16)
    ones = sbt("ones", [128, 128], BF16)
    tri = sbt("tri_sb", [128, 128], BF16)
    cv = sbt("cv_sb", [128, NCV], F32)
    gbc = sbt("gbc_sb", [128, 2, D], F32)
    cosT = sbt("cosT", [128, 64, 16], F32)
    sinT = sbt("sinT", [128, 64, 16], F32)
    cosQ = sbt("cosQ", [128, NT, 16], F32)
    sinQ = sbt("sinQ", [128, NT, 16], F32)
    small = sbt("small", [128, 64], F32)
    junk = sbt("junk", [128, D], BF16)
    hT_own = sbt("hT_own", [128, 8, NE], BF16)
    W_OFF = cur[0]
    wsl = [sbt(f"wslot{i}", [128, 4096], BF16) for i in range(4)]
    Y_OFF = cur[0]
    y_dilT = sbt("y_dilT", [128, 4, NE], BF16)
    y_mlaT = sbt("y_mlaT", [128, 4, NE], BF16)
    y_memT = sbt("y_memT", [128, 4, NE], BF16)
    T_OFF = cur[0]
    YB = 17408

    psf = nc.alloc_psum_tensor("psf", [128, 3584], F32)
    psb = nc.alloc_psum_tensor("psb", [128, 1024], BF16)

    def PF(b, n=512, off=0):
        return psf[:, b * 512 + off: b * 512 + off + n]

    def cvc(c, p=128):
        return cv[0:p, c:c + 1]

    es = ExitStack()
    sems = {e: es.enter_context(nc.semaphore(f"s_{e}")) for e in Sched.CE}
    dsems = {q: [es.enter_context(nc.semaphore(f"d_{q}{i}")) for i in range(8)] for q in ('sp', 'pool')}
    S = Sched(nc, sems, dsems)
    name_ctr = [0]

    wctr = [0]

    def wload(parts):
        i = wctr[0] % 4
        wctr[0] += 1
        views = []
        off = 0
        for src in parts:
            k, n = src.shape[1], src.shape[2]
            v = wsl[i][:, off:off + k * n].rearrange("p (k n) -> p k n", n=n)
            off += k * n
            assert off <= 4096
            S.dma('pool', lambda E, v=v, src=src: E.dma_start(out=v, in_=src), w=[('w', i)])
            views.append(v)
        return views, ('w', i)

    def wload_b(src, rkeys):
        i = wctr[0] % 4
        wctr[0] += 1
        k, n = src.shape[1], src.shape[2]
        v = wsl[i][:, 0:k * n].rearrange("p (k n) -> p k n", n=n)
        S.dma('sp', lambda E: E.dma_start(out=v, in_=src), r=rkeys, w=[('w', i)])
        return v, ('w', i)

    def wk(w, c0, n, kp=None):
        return w[:, c0:c0 + n].rearrange("(k p) n -> p k n", p=128)

    def mm(out, pairs, w, r):
        def f(E):
            last = None
            for i, (l, rr) in enumerate(pairs):
                last = E.matmul(out, lhsT=l, rhs=rr, start=(i == 0), stop=(i == len(pairs) - 1))
            return last
        S.op('pe', f, r=r, w=w)

    def act(out, in_, func, r, w, **kw):
        S.op('act', lambda E: E.activation(out=out, in_=in_, func=func, **kw), r=r, w=w)

    def rstd_from_ss(ss_ap, out_ap, tmp_ap, n_feat, rkeys, wkeys, tkeys):
        act(tmp_ap, ss_ap, AF.Ln, r=rkeys, w=tkeys, scale=1.0 / n_feat, bias=EPS)
        act(out_ap, tmp_ap, AF.Exp, r=tkeys, w=wkeys, scale=-0.5)

    def rms_T(tiles, gcol, dst, dkey, xhat, xkey):
        n = len(tiles)
        for i, (ap, key) in enumerate(tiles):
            S.op('act', lambda E, ap=ap, i=i: E.activation(out=junk[:, :], in_=ap, func=AF.Square,
                                                           accum_out=small[:, i:i + 1]),
                 r=[key], w=['junk', ('sm', i)])
        rstd_from_ss(small[:, 0:n], small[:, 16:16 + n], small[:, 8:8 + n], 1024.0,
                     [('sm', i) for i in range(n)], ['smr'], ['sml'])
        for i, (ap, key) in enumerate(tiles):
            S.op('dve', lambda E, ap=ap, i=i: E.tensor_scalar(out=xhat[:, i, :], in0=ap,
                                                              scalar1=small[:, 16 + i:17 + i], scalar2=None,
                                                              op0=ALU.mult),
                 r=[key, 'smr'], w=[(xkey, i)])
        per = 8 // n if n <= 2 else 2
        w_ = 1024 // per
        for kc0 in range(0, 8, per):
            def f(E, kc0=kc0):
                last = None
                for j in range(per):
                    for i in range(n):
                        last = E.transpose(psb[:, j * w_ + i * 128: j * w_ + (i + 1) * 128],
                                           xhat[:, i, (kc0 + j) * 128:(kc0 + j + 1) * 128], ident[:, :])
                return last
            S.op('pe', f, r=[(xkey, i) for i in range(n)] + ['ident'], w=['psb'])
            for j in range(per):
                kc = kc0 + j
                S.op('dve', lambda E, kc=kc, j=j: E.tensor_scalar(out=dst(kc), in0=psb[:, j * w_: j * w_ + n * 128],
                                                                  scalar1=cvc(gcol + kc), scalar2=None, op0=ALU.mult),
                     r=['psb', 'cv'], w=[dkey])

    def rms_pipe(calls, gcol, xbufs, xhats, xname):
        def stageA(c):
            call = calls[c]
            par = c % 2
            base = par * 24
            n = len(call['srcs'])
            tiles = [xload(xbufs, src) for src in call['srcs']]
            for i, (ap, key) in enumerate(tiles):
                S.op('act', lambda E, ap=ap, i=i: E.activation(out=junk[:, :], in_=ap, func=AF.Square,
                                                               accum_out=small[:, base + i:base + i + 1]),
                     r=[key], w=['junk', ('sm', base + i)])
            rstd_from_ss(small[:, base:base + n], small[:, base + 16:base + 16 + n], small[:, base + 8:base + 8 + n],
                         1024.0, [('sm', base + i) for i in range(n)], [('smr', par)], [('sml', par)])
            for i, (ap, key) in enumerate(tiles):
                S.op('dve', lambda E, ap=ap, i=i: E.tensor_scalar(out=xhats[par][:, i, :], in0=ap,
                                                                  scalar1=small[:, base + 16 + i:base + 17 + i],
                                                                  scalar2=None, op0=ALU.mult),
                     r=[key, ('smr', par)], w=[(xname, par, i)])

        def stageB(c):
            call = calls[c]
            par = c % 2
            n = len(call['srcs'])
            dst, dkey = call['dst'], call['dkey']
            xh = xhats[par]
            for kc4 in range(2):
                def f(E, kc4=kc4):
                    last = None
                    for j in range(4):
                        kc = kc4 * 4 + j
                        for i in range(n):
                            last = E.transpose(psb[:, j * 256 + i * 128: j * 256 + (i + 1) * 128],
                                               xh[:, i, kc * 128:(kc + 1) * 128], ident[:, :])
                    return last
                S.op('pe', f, r=[(xname, par, i) for i in range(n)] + ['ident'], w=['psb'])
                for j in range(4):
                    kc = kc4 * 4 + j
                    S.op('dve', lambda E, kc=kc, j=j: E.tensor_scalar(out=dst(kc), in0=psb[:, j * 256: j * 256 + n * 128],
                                                                      scalar1=cvc(gcol + kc), scalar2=None,
                                                                      op0=ALU.mult), r=['psb', 'cv'], w=[dkey])
            if call.get('after'):
                call['after']()

        stageA(0)
        for c in range(len(calls)):
            if c + 1 < len(calls):
                stageA(c + 1)
            stageB(c)

    xctr = [0]

    def xload(xbufs, src):
        i = xctr[0] % len(xbufs)
        xctr[0] += 1
        dst_ = xbufs[i][:, :]
        S.dma('sp', lambda E: E.dma_start(out=dst_, in_=src), w=[('xb', i)])
        return dst_, ('xb', i)

    tp = [T_OFF]
    tmp32 = sbt("setup_tmp", [128, 128], F32, at=tp)
    S.dma('sp', lambda E: E.dma_start(out=cv[:, :], in_=cvd[:, :]), w=['cv'])
    S.dma('sp', lambda E: E.dma_start(out=gbc[:, :, :], in_=gbcd.partition_broadcast(128)), w=['gbc'])
    S.dma('pool', lambda E: E.dma_start(out=ident[:, :], in_=identd[:, :]), w=['ident'])
    S.dma('pool', lambda E: E.dma_start(out=tri[:, :], in_=trid[:, :]), w=['tri'])
    S.op('dve', lambda E: E.memset(ones[:, :], 1.0), w=['ones'])
    posi = sbt("posi", [128, 64], I32, at=tp)
    posf = sbt("posf", [128, 64], F32, at=tp)
    ang = sbt("ang", [128, 64, 16], F32, at=tp)
    kf = sbt("kf", [128, 64, 16], F32, at=tp)
    ki = sbt("ki", [128, 64, 16], I32, at=tp)
    S.dma('sp', lambda E: E.dma_start(out=posi[:, :], in_=posc[:, :]), w=['posi'])
    S.op('dve', lambda E: E.tensor_copy(out=posf[:, :], in_=posi[:, :]), r=['posi'], w=['posf'])
    S.op('dve', lambda E: E.tensor_tensor(out=ang[:, :, :], in0=posf[:, :].unsqueeze(2).to_broadcast([128, 64, 16]),
                                          in1=cv[:, C_INVF:C_INVF + 16].unsqueeze(1).to_broadcast([128, 64, 16]),
                                          op=ALU.mult), r=['posf', 'cv'], w=['ang'])
    for tab, shift in ((sinT, 0.0), (cosT, np.pi / 2)):
        S.op('dve', lambda E, shift=shift: E.tensor_scalar(out=kf[:, :, :], in0=ang[:, :, :], scalar1=float(shift),
                                                           scalar2=float(1.0 / TWO_PI), op0=ALU.add, op1=ALU.mult),
             r=['ang'], w=['kf'])
        S.op('dve', lambda E: E.tensor_copy(out=ki[:, :, :], in_=kf[:, :, :]), r=['kf'], w=['ki'])
        S.op('dve', lambda E: E.tensor_copy(out=kf[:, :, :], in_=ki[:, :, :]), r=['ki'], w=['kf'])
        S.op('dve', lambda E: E.scalar_tensor_tensor(out=kf[:, :, :], in0=kf[:, :, :], scalar=float(-TWO_PI),
                                                     in1=ang[:, :, :], op0=ALU.mult, op1=ALU.add),
             r=['kf', 'ang'], w=['kf'])
        S.op('dve', lambda E, shift=shift: E.tensor_scalar(out=kf[:, :, :], in0=kf[:, :, :], scalar1=float(shift),
                                                           scalar2=3.1415925, op0=ALU.add, op1=ALU.min),
             r=['kf'], w=['kf'])
        S.op('dve', lambda E: E.tensor_scalar(out=kf[:, :, :], in0=kf[:, :, :], scalar1=-3.1415925, scalar2=None,
                                              op0=ALU.max), r=['kf'], w=['kf'])
        act(tab[:, :, :], kf[:, :, :], AF.Sin, r=['kf'], w=['rope'])
    qs = 96.0 ** -0.5
    S.op('dve', lambda E: E.tensor_scalar(out=cosQ[:, :, :], in0=cosT[:, 47:64, :], scalar1=qs, scalar2=None,
                                          op0=ALU.mult), r=['rope'], w=['ropeq'])
    S.op('dve', lambda E: E.tensor_scalar(out=sinQ[:, :, :], in0=sinT[:, 47:64, :], scalar1=qs, scalar2=None,
                                          op0=ALU.mult), r=['rope'], w=['ropeq'])
    S.barrier()

    tp = [Y_OFF + YB]
    xbufA = [sbt(f"xbufA{i}", [128, D], F32, at=tp) for i in range(2)]
    xhatA = sbt("xhatA", [128, 4, D], BF16, at=tp)
    hT_halo = sbt("hT_halo", [128, 8, 2048], BF16, at=tp)
    QTd = sbt("QTd", [128, NE], BF16, at=tp)
    KTd = sbt("KTd", [128, 4224], BF16, at=tp)
    Vd = sbt("Vd", [128, 48, 128], BF16, at=tp)
    accO = sbt("accO", [128, NE], F32, at=tp)
    accD = sbt("accD", [128, NE], F32, at=tp)
    PTd = [sbt(f"PTd{i}", [128, 256], BF16, at=tp) for i in range(4)]
    ssb = [sbt(f"ssb{i}", [128, 256], F32, at=tp) for i in range(2)]
    dbt = [sbt(f"dbt{i}", [128, 256], F32, at=tp) for i in range(2)]

    def prep_all(row0, ntiles, dstbuf, dkey):
        t = 0
        while t < ntiles:
            n = min(2, ntiles - t)
            tiles = [xload(xbufA, xc[row0 + (t + i) * 128: row0 + (t + i + 1) * 128, :]) for i in range(n)]
            t0 = t
            rms_T(tiles, C_GPRE, lambda kc, t0=t0, n=n: dstbuf[:, kc, t0 * 128:(t0 + n) * 128], dkey, xhatA, 'xhA')
            t += n

    xhatsA = [xhatA[:, 0:2, :], xhatA[:, 2:4, :]]
    callsA = []
    for (row0, ntiles, dstbuf, dkey) in ((3968, 16, hT_halo, 'hTh'), (6016, NT, hT_own, 'hT')):
        t = 0
        while t < ntiles:
            n = min(2, ntiles - t)
            callsA.append(dict(srcs=[xc[row0 + (t + i) * 128: row0 + (t + i + 1) * 128, :] for i in range(n)],
                               dst=(lambda kc, t=t, n=n, dstbuf=dstbuf: dstbuf[:, kc, t * 128:(t + n) * 128]), dkey=dkey))
            t += n
    xbufA4 = xbufA + [accO[:, 0:D], accO[:, D:2 * D]]
    rms_pipe(callsA, C_GPRE, xbufA4, xhatsA, 'xhA')
    S.barrier()
    if DEBUG:
        S.dma('sp', lambda E: E.dma_start(out=dbg["dbg_h"][:, :, :], in_=hT_own[:, :, :]), r=['hT'])

    dscale = 128.0 ** -0.5
    bank_rr = [0]

    def nb():
        bank_rr[0] = (bank_rr[0] + 1) % 7
        return bank_rr[0]

    for hd in range(4):
        for g in range(3):
            d = DIL_D[g]
            gi = g * 4 + hd
            qb, kt = dil_geom(d)
            nql = NE // d
            nkl = 128 + nql
            c0 = OFF_KR + g * 512 + hd * 128
            (wq_, wk_, wv_), wkey = wload([wk(w_in, c0, 128), wk(w_in, c0 + 1536, 128), wk(w_in, c0 + 3072, 128)])
            bi = gi % 2
            S.dma('sp', lambda E, bi=bi, gi=gi: E.dma_start(out=dbt[bi][:, :], in_=dbd[gi, :, :]), w=[('dbt', bi)])
            QV = QTd[:, :].rearrange("p (r l) -> p r l", r=d)
            KV = KTd[:, 0:d * nkl].rearrange("p (r l) -> p r l", r=d)
            for (e0, n) in GROUPS:
                for which, wv3, dstv, loff, sc in ((0, wq_, QV, 0, dscale), (1, wk_, KV, 128, 1.0)):
                    b = nb()
                    mm(PF(b, n), [(wv3[:, kc, :], hT_own[:, kc, e0:e0 + n]) for kc in range(8)],
                       w=[('pf', b)], r=[wkey, 'hT'])
                    act(dstv[:, :, loff + e0 // d: loff + (e0 + n) // d],
                        PF(b, n).rearrange("p (l r) -> p r l", r=d), AF.Copy,
                        r=[('pf', b)], w=['QTd' if which == 0 else 'KTd'], scale=sc)
            h0 = 2048 - 128 * d
            while h0 < 2048:
                n = min(512, 2048 - h0)
                b = nb()
                mm(PF(b, n), [(wk_[:, kc, :], hT_halo[:, kc, h0:h0 + n]) for kc in range(8)],
                   w=[('pf', b)], r=[wkey, 'hTh'])
                l0 = (h0 - (2048 - 128 * d)) // d
                act(KV[:, :, l0: l0 + n // d], PF(b, n).rearrange("p (l r) -> p r l", r=d), AF.Copy,
                    r=[('pf', b)], w=['KTd'])
                h0 += n
            ntile = len(kt)
            for r_ in range(d):
                for j, (a, nk) in enumerate(kt):
                    if j == 0:
                        s0 = 2048 - 128 * d + r_
                        src = lambda kc, s0=s0, nk=nk: hT_halo[:, kc, s0: s0 + (nk - 1) * d + 1: d]
                        rk = 'hTh'
                    else:
                        s0 = (a - 128) * d + r_
                        src = lambda kc, s0=s0, nk=nk: hT_own[:, kc, s0: s0 + (nk - 1) * d + 1: d]
                        rk = 'hT'
                    b = nb()
                    mm(psf[0:nk, b * 512: b * 512 + 128], [(src(kc), wv_[:, kc, :]) for kc in range(8)],
                       w=[('pf', b)], r=[wkey, rk])
                    ti = r_ * ntile + j
                    S.op('dve', lambda E, nk=nk, b=b, ti=ti: E.tensor_copy(out=Vd[0:nk, ti, :],
                                                                         in_=psf[0:nk, b * 512: b * 512 + 128]),
                         r=[('pf', b)], w=['Vd'])
            for r_ in range(d):
                info = {}

                def dil_pv(m, r_=r_, info=info, d=d, g=g, qb=qb, ntile=ntile):
                    qa, qn = qb[m]
                    pp, pcol, _ = info[m]
                    pi, _, nk = info[m + 1]
                    b2 = nb()
                    tprev = r_ * ntile + m
                    tcur = r_ * ntile + m + 1

                    def f(E):
                        o = psf[:, b2 * 512: b2 * 512 + qn]
                        dd = psf[:, b2 * 512 + 128: b2 * 512 + 128 + qn]
                        E.matmul(o, lhsT=Vd[:, tprev, :], rhs=PTd[pp][:, pcol:pcol + qn], start=True, stop=False)
                        E.matmul(o, lhsT=Vd[0:nk, tcur, :], rhs=PTd[pi][0:nk, 0:qn], start=False, stop=True)
                        E.matmul(dd, lhsT=ones[:, :], rhs=PTd[pp][:, pcol:pcol + qn], start=False, stop=False,
                                 skip_group_check=True)
                        return E.matmul(dd, lhsT=ones[0:nk, :], rhs=PTd[pi][0:nk, 0:qn], start=False, stop=True,
                                        skip_group_check=True)
                    S.op('pe', f, r=['Vd', ('PTd', pp), ('PTd', pi), 'ones'], w=[('pf', b2)])
                    e_s = qa * d + r_
                    e_e = e_s + (qn - 1) * d + 1
                    for accb, off, akey in ((accO, 0, 'accO'), (accD, 128, 'accD')):
                        if g == 0:
                            S.op('dve', lambda E, accb=accb, off=off: E.tensor_copy(
                                out=accb[:, e_s:e_e:d], in_=psf[:, b2 * 512 + off: b2 * 512 + off + qn]),
                                r=[('pf', b2)], w=[akey])
                        else:
                            S.op('dve', lambda E, accb=accb, off=off: E.tensor_tensor(
                                out=accb[:, e_s:e_e:d], in0=psf[:, b2 * 512 + off: b2 * 512 + off + qn],
                                in1=accb[:, e_s:e_e:d], op=ALU.add), r=[('pf', b2), akey], w=[akey])

                for j, (a, nk) in enumerate(kt):
                    has_diag = j >= 1
                    has_prev = j < len(qb)
                    ncol = (qb[j - 1][1] if has_diag else 0) + (qb[j][1] if has_prev else 0)
                    qlo = qb[j - 1][0] if has_diag else qb[j][0]
                    boff = 0 if has_diag else 128
                    b = nb()
                    mm(psf[0:nk, b * 512: b * 512 + ncol], [(KV[:, r_, a:a + nk], QV[:, r_, qlo:qlo + ncol])],
                       w=[('pf', b)], r=['KTd', 'QTd'])
                    si = j % 2
                    S.op('dve', lambda E, nk=nk, b=b, ncol=ncol, si=si, boff=boff, bi=bi: E.tensor_tensor(
                        out=ssb[si][0:nk, 0:ncol], in0=psf[0:nk, b * 512: b * 512 + ncol],
                        in1=dbt[bi][0:nk, boff:boff + ncol], op=ALU.add),
                        r=[('pf', b), ('dbt', bi)], w=[('ssb', si)])
                    ridx = (0, 1, 5)[g] + r_
                    bcol = C_DVB0 + ridx if j == 0 else (C_DVB1 + ridx if j == 1 else C_ZERO)
                    pi = j % 4
                    act(PTd[pi][0:nk, 0:ncol], ssb[si][0:nk, 0:ncol], AF.Exp, r=[('ssb', si), 'cv'],
                        w=[('PTd', pi)], bias=cvc(bcol, nk))
                    info[j] = (pi, (qb[j - 1][1] if has_diag else 0), nk)
                    if j >= 2:
                        dil_pv(j - 2)
                dil_pv(len(kt) - 2)
        S.op('dve', lambda E: E.tensor_scalar(out=accD[:, :], in0=accD[:, :], scalar1=1e-30, scalar2=None,
                                              op0=ALU.max), r=['accD'], w=['accD'])
        S.op('dve', lambda E: E.reciprocal(out=accD[:, :], in_=accD[:, :]), r=['accD'], w=['accD'])
        S.op('dve', lambda E, hd=hd: E.tensor_tensor(out=y_dilT[:, hd, :], in0=accO[:, :], in1=accD[:, :],
                                                     op=ALU.mult), r=['accO', 'accD'], w=['ydil'])
    if DEBUG:
        S.dma('sp', lambda E: E.dma_start(out=dbg["dbg_ydil"][:, :, :], in_=y_dilT[:, :, :]), r=['ydil'])
    S.barrier()


    ckvT = nc.alloc_sbuf_tensor_at("ckvT", [128, 2, SEQ], BF16, offset=W_OFF)
    tp = [Y_OFF + 2 * YB]
    KT = sbt("KT", [128, SEQ], BF16, at=tp)
    cqT = sbt("cqT", [128, 3, NE], BF16, at=tp)
    Wuq = sbt("Wuq", [128, 3, 768], BF16, at=tp)
    Wukv = sbt("Wukv", [128, 2, 1024], BF16, at=tp)
    OV = tp[0]
    tp = [OV]
    xbufB = [nc.alloc_sbuf_tensor_at(f"xbufB{i}", [128, D], F32, offset=Y_OFF + YB + i * 4096) for i in range(4)]
    Wq = sbt("Wq", [128, 8, 384], BF16, at=tp)
    xhatB = sbt("xhatB", [128, 4, D], BF16, at=tp)
    hTgs = [sbt(f"hTg{i}", [128, 8, 512], BF16, at=tp) for i in range(2)]
    Wkvr = sbt("Wkvr", [128, 8, 288], BF16, at=tp)
    sq = sbt("sq", [128, 3, 512], BF16, at=tp)
    rbc = sbt("rbc", [128, 512], F32, at=tp)
    kpe = sbt("kpe", [128, 4, 96], BF16, at=tp)
    kt4 = [sbt(f"kt4_{i}", [128, 4, 16], F32, at=tp) for i in range(4)]

    S.dma('pool', lambda E: E.dma_start(out=Wkvr[:, :, :], in_=wk(w_in, OFF_Q, 288)), w=['Wkvr'])
    S.dma('pool', lambda E: E.dma_start(out=Wq[:, :, :], in_=wk(w_in, 0, 384)), w=['Wq'])
    S.dma('pool', lambda E: E.dma_start(out=Wuq[:, :, :], in_=w_uq.rearrange("(k p) n -> p k n", p=128)), w=['Wuq'])
    S.dma('pool', lambda E: E.dma_start(out=Wukv[:, :, :], in_=w_ukv.rearrange("(k p) n -> p k n", p=128)),
          w=['Wukv'])
    S.op('dve', lambda E: E.memset(kpe[:, :, :], 0.0), w=['kpe'])

    def latent_norm(banks, n, nchunk, gcol, dstf, dkey):
        for kc in range(nchunk):
            act(sq[:, kc, 0:n], PF(banks[kc], n), AF.Square, r=[('pf', banks[kc])], w=[('sq', kc)])
        bs = nb()
        mm(PF(bs, n), [(ones[:, :], sq[:, kc, 0:n]) for kc in range(nchunk)], w=[('pf', bs)],
           r=['ones'] + [('sq', kc) for kc in range(nchunk)])
        act(rbc[:, 0:n], PF(bs, n), AF.Ln, r=[('pf', bs)], w=['rbc'], scale=1.0 / (128 * nchunk), bias=EPS)
        act(rbc[:, 0:n], rbc[:, 0:n], AF.Exp, r=['rbc'], w=['rbc'], scale=-0.5)
        for kc in range(nchunk):
            S.op('dve', lambda E, kc=kc: E.scalar_tensor_tensor(out=dstf(kc), in0=PF(banks[kc], n),
                                                                scalar=cvc(gcol + kc), in1=rbc[:, 0:n],
                                                                op0=ALU.mult, op1=ALU.mult),
                 r=[('pf', banks[kc]), 'rbc', 'cv'], w=[dkey])

    def rope_tm(src3, cos3, sin3, dst3, nt, rkeys, wkey):
        a, b_ = src3[:, :, 0:16], src3[:, :, 16:32]
        t = [k[:, 0:nt, :] for k in kt4]
        S.op('dve', lambda E: E.tensor_tensor(out=t[0], in0=a, in1=cos3, op=ALU.mult), r=rkeys, w=[('kt4', 0)])
        S.op('dve', lambda E: E.tensor_tensor(out=t[1], in0=b_, in1=sin3, op=ALU.mult), r=rkeys, w=[('kt4', 1)])
        S.op('dve', lambda E: E.tensor_tensor(out=dst3[:, :, 0:16], in0=t[0], in1=t[1], op=ALU.subtract),
             r=[('kt4', 0), ('kt4', 1)], w=[wkey])
        S.op('dve', lambda E: E.tensor_tensor(out=t[2], in0=b_, in1=cos3, op=ALU.mult), r=rkeys, w=[('kt4', 2)])
        S.op('dve', lambda E: E.tensor_tensor(out=t[3], in0=a, in1=sin3, op=ALU.mult), r=rkeys, w=[('kt4', 3)])
        S.op('dve', lambda E: E.tensor_tensor(out=dst3[:, :, 16:32], in0=t[2], in1=t[3], op=ALU.add),
             r=[('kt4', 2), ('kt4', 3)], w=[wkey])

    xctr[0] = 0

    lat = {}

    def ctx_latent1(tg):
        hTg = hTgs[tg % 2]
        hk = ('hTg', tg % 2)
        banks = [nb(), nb()]
        for kc2 in range(2):
            mm(PF(banks[kc2]), [(Wkvr[:, kc, kc2 * 128:(kc2 + 1) * 128], hTg[:, kc, :]) for kc in range(8)],
               w=[('pf', banks[kc2])], r=['Wkvr', hk])
        bk = nb()

        def fk(E):
            last = None
            for i in range(4):
                for kc in range(8):
                    last = E.matmul(psf[:, bk * 512 + i * 32: bk * 512 + (i + 1) * 32],
                                    lhsT=hTg[:, kc, i * 128:(i + 1) * 128], rhs=Wkvr[:, kc, 256:288],
                                    start=(kc == 0), stop=(kc == 7))
            return last
        S.op('pe', fk, r=['Wkvr', hk], w=[('pf', bk)])
        src3 = psf[:, bk * 512: bk * 512 + 128].rearrange("p (t c) -> p t c", c=32)
        rope_tm(src3, cosT[:, tg * 4:(tg + 1) * 4, :], sinT[:, tg * 4:(tg + 1) * 4, :], kpe[:, :, 64:96], 4,
                [('pf', bk), 'rope'], 'kpe')
        lat[tg] = banks

    def ctx_latent2(tg):
        banks = lat[tg]
        n = 512
        for kc in range(2):
            act(sq[:, kc, 0:512], PF(banks[kc]), AF.Square, r=[('pf', banks[kc])], w=[('sq', kc)])
        bs = nb()
        mm(PF(bs, n), [(ones[:, :], sq[:, kc, 0:n]) for kc in range(2)], w=[('pf', bs)],
           r=['ones'] + [('sq', kc) for kc in range(2)])
        act(rbc[:, 0:n], PF(bs, n), AF.Ln, r=[('pf', bs)], w=['rbc'], scale=1.0 / 256, bias=EPS)
        act(rbc[:, 0:n], rbc[:, 0:n], AF.Exp, r=['rbc'], w=['rbc'], scale=-0.5)
        for kc in range(2):
            S.op('dve', lambda E, kc=kc: E.scalar_tensor_tensor(out=ckvT[:, kc, tg * 512:(tg + 1) * 512],
                                                                in0=PF(banks[kc], n), scalar=cvc(C_KVN + kc),
                                                                in1=rbc[:, 0:n], op0=ALU.mult, op1=ALU.mult),
                 r=[('pf', banks[kc]), 'rbc', 'cv'], w=['ckvT'])

        def ft(E):
            last = None
            for i in range(4):
                last = E.transpose(psb[0:96, i * 128:(i + 1) * 128], kpe[:, i, :], ident[:, :])
            return last
        S.op('pe', ft, r=['kpe', 'ident'], w=['psb'])
        S.op('dve', lambda E: E.tensor_copy(out=KT[64:96, tg * 512:(tg + 1) * 512], in_=psb[64:96, 0:512]),
             r=['psb'], w=['KTr'])

    xhatsB = [xhatB[:, 0:2, :], xhatB[:, 2:4, :]]
    callsB = []
    for tg in range(16):
        for hh in range(2):
            callsB.append(dict(
                srcs=[xc[(tg * 4 + hh * 2 + i) * 128:(tg * 4 + hh * 2 + i + 1) * 128, :] for i in range(2)],
                dst=(lambda kc, tg=tg, hh=hh: hTgs[tg % 2][:, kc, hh * 256:(hh + 1) * 256]), dkey=('hTg', tg % 2),
                after=((lambda tg=tg: ctx_latent1(tg)) if hh == 1 else
                       ((lambda tg=tg: ctx_latent2(tg - 1)) if tg >= 1 else None))))
    rms_pipe(callsB, C_GPRE, xbufB, xhatsB, 'xhB')
    ctx_latent2(15)
    for (e0, n) in GROUPS:
        banks = [nb(), nb(), nb()]
        for kc3 in range(3):
            mm(PF(banks[kc3], n), [(Wq[:, kc, kc3 * 128:(kc3 + 1) * 128], hT_own[:, kc, e0:e0 + n]) for kc in range(8)],
               w=[('pf', banks[kc3])], r=['Wq', 'hT'])
        latent_norm(banks, n, 3, C_QN, lambda kc, e0=e0, n=n: cqT[:, kc, e0:e0 + n], 'cqT')
    S.barrier()

    for c in range(4):
        S.dma('pool', lambda E, c=c: E.dma_start(out=wupb[:, c * 1408:(c + 1) * 1408], in_=w_up[:, c * 1408:(c + 1) * 1408]),
              w=[('wupb', c)])
    for r_ in range(2):
        S.dma('pool', lambda E, r_=r_: E.dma_start(out=wdnb[r_ * 1408:(r_ + 1) * 1408, :], in_=w_dn[r_ * 1408:(r_ + 1) * 1408, :]),
              w=[('wdnb', r_)])
    tp = [OV]
    Vts = [sbt(f"Vt{i}", [128, 64, 65], BF16, at=tp) for i in range(2)]
    QTs = [sbt(f"QT{i}", [128, NE], BF16, at=tp) for i in range(2)]
    qtm = sbt("qtm", [128, NT, 96], BF16, at=tp)
    ypair = sbt("ypair", [128, NT, 128], BF16, at=tp)
    PTm = [sbt(f"PTm{i}", [128, 1024], BF16, at=tp) for i in range(3)]
    rden = sbt("rden", [128, 8], F32, at=tp)
    kt4 = [sbt(f"kq4_{i}", [128, 5, 16], F32, at=tp) for i in range(4)]
    for Vt_ in Vts:
        S.op('dve', lambda E, Vt_=Vt_: E.memset(Vt_[:, :, 64:65], 1.0), w=['Vones'])
    evac_rr = [0]

    def evac(out, in_, r, w, **kw):
        evac_rr[0] += 1
        if not kw:
            S.op('dve', lambda E: E.tensor_copy(out=out, in_=in_), r=r, w=w)
        else:
            act(out, in_, AF.Copy, r=r, w=w, **kw)

    pctr = [0]
    for h in range(8):
        for tg in range(16):
            b = nb()
            mm(psf[0:64, b * 512:(b + 1) * 512],
               [(Wukv[:, kc, h * 128: h * 128 + 64], ckvT[:, kc, tg * 512:(tg + 1) * 512]) for kc in range(2)],
               w=[('pf', b)], r=['Wukv', 'ckvT'])
            evac(KT[0:64, tg * 512:(tg + 1) * 512], psf[0:64, b * 512:(b + 1) * 512], r=[('pf', b)], w=['KTn'])
        def build_vq(h, bankf):
            Vt, QT = Vts[h % 2], QTs[h % 2]
            vk, qk = ('Vt', h % 2), ('QT', h % 2)
            chunks = []

            def vchunk(t8):
                b = bankf()

                def fv(E):
                    last = None
                    for i in range(8):
                        for kc in range(2):
                            last = E.matmul(psf[:, b * 512 + i * 64: b * 512 + (i + 1) * 64],
                                            lhsT=ckvT[:, kc, (t8 * 8 + i) * 128:(t8 * 8 + i + 1) * 128],
                                            rhs=Wukv[:, kc, h * 128 + 64: h * 128 + 128], start=(kc == 0), stop=(kc == 1))
                    return last
                S.op('pe', fv, r=['Wukv', 'ckvT'], w=[('pf', b)])
                evac(Vt[:, t8 * 8:(t8 + 1) * 8, 0:64], PF(b).rearrange("p (t e) -> p t e", e=64), r=[('pf', b)], w=[vk])

            def qchunk(t0):
                nt = min(5, NT - t0)
                b = bankf()

                def fq(E):
                    last = None
                    for i in range(nt):
                        for kc in range(3):
                            last = E.matmul(psf[:, b * 512 + i * 96: b * 512 + (i + 1) * 96],
                                            lhsT=cqT[:, kc, (t0 + i) * 128:(t0 + i + 1) * 128],
                                            rhs=Wuq[:, kc, h * 96:(h + 1) * 96], start=(kc == 0), stop=(kc == 2))
                    return last
                S.op('pe', fq, r=['Wuq', 'cqT'], w=[('pf', b)])
                p3 = psf[:, b * 512: b * 512 + nt * 96].rearrange("p (t c) -> p t c", c=96)
                S.op('dve', lambda E: E.tensor_scalar(out=qtm[:, t0:t0 + nt, 0:64], in0=p3[:, :, 0:64],
                                                      scalar1=qs, scalar2=None, op0=ALU.mult),
                     r=[('pf', b)], w=['qtm'])
                rope_tm(p3[:, :, 64:96], cosQ[:, t0:t0 + nt, :], sinQ[:, t0:t0 + nt, :], qtm[:, t0:t0 + nt, 64:96], nt,
                        [('pf', b), 'ropeq'], 'qtm')

            def tchunk(t0):
                nt = min(8, NT - t0)

                def ftq(E):
                    last = None
                    for i in range(nt):
                        last = E.transpose(psb[0:96, i * 128:(i + 1) * 128], qtm[:, t0 + i, :], ident[:, :])
                    return last
                S.op('pe', ftq, r=['qtm', 'ident'], w=['psb'])
                S.op('dve', lambda E: E.tensor_copy(out=QT[0:96, t0 * 128:(t0 + nt) * 128],
                                                    in_=psb[0:96, 0:nt * 128]), r=['psb'], w=[qk])
            for t8 in range(8):
                chunks.append(lambda t8=t8: vchunk(t8))
            for t0 in range(0, NT, 5):
                chunks.append(lambda t0=t0: qchunk(t0))
            for t0 in range(0, NT, 8):
                chunks.append(lambda t0=t0: tchunk(t0))
            return chunks

        if h == 0:
            for ch in build_vq(0, nb):
                ch()
        Vt, QT = Vts[h % 2], QTs[h % 2]
        vk, qk = ('Vt', h % 2), ('QT', h % 2)
        items = []
        for gq, (e0, n) in enumerate(GROUPS):
            tq0, ntq = e0 // 128, n // 128
            kfirst = 47 + tq0
            per = 1024 // n
            units = []
            kb = 0
            while kb < kfirst:
                lim = min(kfirst, (kb // 16 + 1) * 16, kb + per)
                units.append((list(range(kb, lim)), None))
                kb = lim
            for m in range(ntq):
                units.append(([kfirst + m], m))
            for ui, (kbs, m) in enumerate(units):
                items.append(dict(e0=e0, n=n, tq0=tq0, ntq=ntq, ob=4 + (gq % 2), kbs=kbs, m=m, first=(ui == 0),
                                  last=(ui == len(units) - 1)))

        def emit_S(it, i):
            sp, n, e0, m = ((pctr[0] + i) % 2) * 2, it['n'], it['e0'], it['m']
            if m is None:
                def fs(E, it=it, sp=sp, n=n, e0=e0, QT=QT):
                    last = None
                    for idx, kbk in enumerate(it['kbs']):
                        last = E.matmul(psf[:, sp * 512 + idx * n: sp * 512 + (idx + 1) * n],
                                        lhsT=KT[0:96, kbk * 128:(kbk + 1) * 128], rhs=QT[0:96, e0:e0 + n],
                                        start=True, stop=True)
                    return last
                S.op('pe', fs, r=['KTn', 'KTr', qk], w=[('pf', sp), ('pf', sp + 1)])
            else:
                kbk = it['kbs'][0]
                cols = n - 128 * m
                mm(psf[:, sp * 512: sp * 512 + cols],
                   [(KT[0:96, kbk * 128:(kbk + 1) * 128], QT[0:96, e0 + 128 * m:e0 + n])],
                   w=[('pf', sp), ('pf', sp + 1)], r=['KTn', 'KTr', qk])

        def emit_A(it, i):
            sp, pi = ((pctr[0] + i) % 2) * 2, (pctr[0] + i) % 3
            n, m = it['n'], it['m']
            kb0 = it['kbs'][0]
            tot = len(it['kbs']) * n if m is None else n - 128 * m
            act(PTm[pi][:, 0:tot], psf[:, sp * 512: sp * 512 + tot], AF.Exp,
                r=[('pf', sp), ('pf', sp + 1), 'cv'], w=[('PTm', pi)], bias=cvc(C_VB + min(kb0 // 16, 3)))
            if m is not None:
                S.op('dve', lambda E, pi=pi: E.tensor_tensor(out=PTm[pi][:, 0:128], in0=PTm[pi][:, 0:128],
                                                             in1=tri[:, :], op=ALU.mult),
                     r=[('PTm', pi), 'tri'], w=[('PTm', pi)])

        def emit_PV(it, i):
            pi = (pctr[0] + i) % 3
            n, m, ntq, ob, tq0 = it['n'], it['m'], it['ntq'], it['ob'], it['tq0']
            if m is None:
                blocks = [(idx * n, kbk, list(range(ntq))) for idx, kbk in enumerate(it['kbs'])]
            else:
                blocks = [(0, it['kbs'][0], list(range(m, ntq)))]

            def pv(E, pi=pi, blocks=blocks, ob=ob, first=it['first'], Vt=Vt):
                last = None
                st = first
                for (col0, kbk, qts) in blocks:
                    for qt in qts:
                        last = E.matmul(psf[:, ob * 512 + qt * 65: ob * 512 + qt * 65 + 65],
                                        lhsT=PTm[pi][:, col0 + (qt - qts[0]) * 128: col0 + (qt - qts[0] + 1) * 128],
                                        rhs=Vt[:, kbk, :], start=st, stop=False, skip_group_check=True)
                        st = False
                return last
            S.op('pe', pv, r=[('PTm', pi), vk, 'Vones'], w=[('pf', ob)])
            if it['last']:
                o3 = psf[:, ob * 512: ob * 512 + ntq * 65].rearrange("p (t c) -> p t c", c=65)
                S.op('dve', lambda E, o3=o3, ntq=ntq: E.tensor_scalar(out=rden[:, 0:ntq], in0=o3[:, :, 64],
                                                                      scalar1=1e-30, scalar2=None, op0=ALU.max),
                     r=[('pf', ob)], w=['rden'])
                S.op('dve', lambda E, ntq=ntq: E.reciprocal(out=rden[:, 0:ntq], in_=rden[:, 0:ntq]),
                     r=['rden'], w=['rden'])
                hoff = (h % 2) * 64
                S.op('dve', lambda E, o3=o3, ntq=ntq, tq0=tq0, hoff=hoff: E.tensor_tensor(
                    out=ypair[:, tq0:tq0 + ntq, hoff:hoff + 64], in0=o3[:, :, 0:64],
                    in1=rden[:, 0:ntq].unsqueeze(2).to_broadcast([128, ntq, 64]), op=ALU.mult),
                    r=[('pf', ob), 'rden'], w=['ypair'])

        for i, it in enumerate(items):
            emit_S(it, i)
            emit_A(it, i)
            if i >= 1:
                emit_PV(items[i - 1], i - 1)
            if i == 12 and h < 7:
                pending = build_vq(h + 1, lambda: 6)
            if i > 12 and h < 7 and pending:
                pending.pop(0)()
        emit_PV(items[-1], len(items) - 1)
        while h < 7 and pending:
            pending.pop(0)()
        pctr[0] += len(items)
        if h % 2 == 1:
            for t0 in range(0, NT, 8):
                nt = min(8, NT - t0)

                def fty(E, t0=t0, nt=nt):
                    last = None
                    for i in range(nt):
                        last = E.transpose(psb[:, i * 128:(i + 1) * 128], ypair[:, t0 + i, :], ident[:, :])
                    return last
                S.op('pe', fty, r=['ypair', 'ident'], w=['psb'])
                S.op('dve', lambda E, t0=t0, nt=nt, h=h: E.tensor_copy(out=y_mlaT[:, h // 2, t0 * 128:(t0 + nt) * 128],
                                                                       in_=psb[:, 0:nt * 128]), r=['psb'], w=['ymla'])
    if DEBUG:
        S.dma('sp', lambda E: E.dma_start(out=dbg["dbg_ymla"][:, :, :], in_=y_mlaT[:, :, :]), r=['ymla'])
    S.barrier()


    MERG = END - 34816
    mergedT = nc.alloc_sbuf_tensor_at("mergedT", [128, 8, NE], BF16, offset=MERG)
    tp = [T_OFF]
    memT = sbt("memT", [128, 8, 256], BF16, at=tp)
    KmT = sbt("KmT", [128, 4, 256], BF16, at=tp)
    VmT = sbt("VmT", [128, 2, 512], BF16, at=tp)
    OVC = tp[0]
    xbufC = [sbt(f"xbufC{i}", [128, D], F32, at=tp) for i in range(2)]
    xhatC = sbt("xhatC", [128, 2, D], BF16, at=tp)
    assert tp[0] <= MERG
    xctr[0] = 0
    tiles = [xload(xbufC, memx[i * 128:(i + 1) * 128, :]) for i in range(2)]
    rms_T(tiles, C_GMEM, lambda kc: memT[:, kc, :], 'memT', xhatC, 'xhC')
    (wkm,), kkm = wload([wk(w_memkv, 0, 512)])
    for h in range(4):
        b = nb()
        mm(PF(b, 256), [(wkm[:, kc, h * 128:(h + 1) * 128], memT[:, kc, :]) for kc in range(8)],
           w=[('pf', b)], r=[kkm, 'memT'])
        act(KmT[:, h, :], PF(b, 256), AF.Copy, r=[('pf', b)], w=['KmT'])
    (wvm,), kvm = wload([wk(w_memkv, 512, 512)])
    for mt in range(2):
        b = nb()
        mm(PF(b), [(memT[:, kc, mt * 128:(mt + 1) * 128], wvm[:, kc, :]) for kc in range(8)],
           w=[('pf', b)], r=[kvm, 'memT'])
        act(VmT[:, mt, :], PF(b), AF.Copy, r=[('pf', b)], w=['VmT'])
    S.barrier()
    tp = [OVC]
    qm = sbt("qm", [128, 512], BF16, at=tp)
    PTc = sbt("PTc", [128, 2, 512], BF16, at=tp)
    recc = sbt("recc", [128, 512], F32, at=tp)
    gsb = [sbt(f"gsb{i}", [128, 512], BF16, at=tp) for i in range(3)]
    mtc = [sbt(f"mtc{i}", [128, 512], F32, at=tp) for i in range(2)]
    assert tp[0] <= MERG
    (wqm,), kqm = wload([wk(w_in, OFF_DIL, 512)])
    qmB = sbt("qmB", [128, 512], BF16, at=tp)
    PTcB = sbt("PTcB", [128, 2, 512], BF16, at=tp)
    assert tp[0] <= MERG
    qm2, PTc2 = [qm, qmB], [PTc, PTcB]
    itsC = [(h, e0, n) for h in range(4) for (e0, n) in GROUPS]

    def c_s1(i):
        h, e0, n = itsC[i]
        b = nb()
        mm(PF(b, n), [(wqm[:, kc, h * 128:(h + 1) * 128], hT_own[:, kc, e0:e0 + n]) for kc in range(8)],
           w=[('pf', b)], r=[kqm, 'hT'])
        act(qm2[i % 2][:, 0:n], PF(b, n), AF.Copy, r=[('pf', b)], w=[('qm', i % 2)], scale=dscale)

    def c_s2(i):
        h, e0, n = itsC[i]
        for mt in range(2):
            b = nb()
            mm(PF(b, n), [(KmT[:, h, mt * 128:(mt + 1) * 128], qm2[i % 2][:, 0:n])], w=[('pf', b)],
               r=['KmT', ('qm', i % 2)])
            act(PTc2[i % 2][:, mt, 0:n], PF(b, n), AF.Exp, r=[('pf', b)], w=[('PTc', i % 2, mt)])

    def c_s3(i):
        h, e0, n = itsC[i]
        P_ = PTc2[i % 2]
        pk = [('PTc', i % 2, 0), ('PTc', i % 2, 1)]
        bo, bd = nb(), nb()
        mm(PF(bo, n), [(VmT[:, mt, h * 128:(h + 1) * 128], P_[:, mt, 0:n]) for mt in range(2)],
           w=[('pf', bo)], r=['VmT'] + pk)
        mm(PF(bd, n), [(ones[:, :], P_[:, mt, 0:n]) for mt in range(2)], w=[('pf', bd)], r=['ones'] + pk)
        S.op('dve', lambda E: E.reciprocal(out=recc[:, 0:n], in_=PF(bd, n)), r=[('pf', bd)], w=['recc'])
        S.op('dve', lambda E: E.tensor_tensor(out=y_memT[:, h, e0:e0 + n], in0=PF(bo, n), in1=recc[:, 0:n],
                                              op=ALU.mult), r=[('pf', bo), 'recc'], w=['ymem'])
    NC_ = len(itsC)
    c_s1(0)
    c_s1(1)
    c_s2(0)
    for i in range(NC_):
        if i + 2 < NC_:
            c_s1(i + 2)
        if i + 1 < NC_:
            c_s2(i + 1)
        c_s3(i)
    if DEBUG:
        S.dma('sp', lambda E: E.dma_start(out=dbg["dbg_ymem"][:, :, :], in_=y_memT[:, :, :]), r=['ymem'])
    yT = [y_mlaT, y_dilT, y_memT]
    ykeys = ['ymla', 'ydil', 'ymem']
    for oc in range(8):
        wg, kg = wload([wk(w_in, OFF_MEMQ + br * 1024 + oc * 128, 128) for br in range(3)])
        wb_, kb_ = wload([w_br[br][:, oc * 128:(oc + 1) * 128].rearrange("(k p) n -> p k n", p=128) for br in range(3)])
        for (e0, n) in GROUPS:
            for br in range(3):
                bg = nb()
                mm(PF(bg, n), [(wg[br][:, kc, :], hT_own[:, kc, e0:e0 + n]) for kc in range(8)],
                   w=[('pf', bg)], r=[kg, 'hT'])
                act(gsb[br][:, 0:n], PF(bg, n), AF.Sigmoid, r=[('pf', bg), 'cv'], w=[('gsb', br)],
                    bias=cvc(C_BG + br * 8 + oc))
                bb = nb()
                mm(PF(bb, n), [(wb_[br][:, kc, :], yT[br][:, kc, e0:e0 + n]) for kc in range(4)],
                   w=[('pf', bb)], r=[kb_, ykeys[br]])
                di = 0 if br == 0 else 1
                S.op('dve', lambda E, bb=bb, n=n, br=br, di=di: E.tensor_tensor(out=mtc[di][:, 0:n], in0=PF(bb, n),
                                                                               in1=gsb[br][:, 0:n], op=ALU.mult),
                     r=[('pf', bb), ('gsb', br)], w=[('mtc', di)])
                if br == 1:
                    S.op('dve', lambda E, n=n: E.tensor_tensor(out=mtc[0][:, 0:n], in0=mtc[0][:, 0:n],
                                                               in1=mtc[1][:, 0:n], op=ALU.add),
                         r=[('mtc', 0), ('mtc', 1)], w=[('mtc', 0)])
                if br == 2:
                    S.op('dve', lambda E, n=n, oc=oc, e0=e0: E.tensor_tensor(out=mergedT[:, oc, e0:e0 + n],
                                                                            in0=mtc[0][:, 0:n], in1=mtc[1][:, 0:n],
                                                                            op=ALU.add),
                         r=[('mtc', 0), ('mtc', 1)], w=['merged'])
    if DEBUG:
        S.dma('sp', lambda E: E.dma_start(out=dbg["dbg_merged"][:, :, :], in_=mergedT[:, :, :]), r=['merged'])
    S.barrier()

    x1 = nc.alloc_sbuf_tensor_at("x1", [128, 16, D], F32, offset=Y_OFF)
    tp = [Y_OFF + 65536]
    xbufD = [sbt("xbufD0", [128, D], F32, at=tp)]
    tmpD = sbt("tmpD", [128, D], F32, at=tp)
    x1pre = sbt("x1pre", [128, D], F32, at=tp)
    xhatD = sbt("xhatD", [128, 1, D], BF16, at=tp)
    assert tp[0] <= MERG
    h2T = hT_own
    (wo0,), ko0 = wload([wk(w_o, 0, 512)])
    (wo1,), ko1 = wload([wk(w_o, 512, 512)])
    xctr[0] = 0
    dstD = {}

    def d_s1a(t):
        p = t % 3
        c0 = 48 + 4 * p
        for half, wo_, ko in ((0, wo0, ko0), (1, wo1, ko1)):
            mm(psf[:, p * 1024 + half * 512: p * 1024 + (half + 1) * 512],
               [(mergedT[:, kc, t * 128:(t + 1) * 128], wo_[:, kc, :]) for kc in range(8)],
               w=[('pf', 2 * p + half)], r=[ko, 'merged'])
        pk = [('pf', 2 * p), ('pf', 2 * p + 1)]
        pfull = psf[:, p * 1024:(p + 1) * 1024]
        S.op('act', lambda E: E.activation(out=junk[:, :], in_=pfull, func=AF.Square,
                                           accum_out=small[:, c0:c0 + 1]), r=pk, w=['junk', ('sm', c0)])
        rstd_from_ss(small[:, c0:c0 + 1], small[:, c0 + 2:c0 + 3], small[:, c0 + 1:c0 + 2], 1024.0,
                     [('sm', c0)], [('sm', c0 + 2)], [('sm', c0 + 1)])

    def d_s1b(t):
        p = t % 3
        c0 = 48 + 4 * p
        pk = [('pf', 2 * p), ('pf', 2 * p + 1)]
        pfull = psf[:, p * 1024:(p + 1) * 1024]
        xa, xk = xload(xbufD, xc[6016 + t * 128: 6016 + (t + 1) * 128, :])
        S.op('dve', lambda E: E.scalar_tensor_tensor(out=tmpD[:, :], in0=pfull, scalar=small[:, c0 + 2:c0 + 3],
                                                     in1=gbc[:, 0, :], op0=ALU.mult, op1=ALU.mult),
             r=pk + [('sm', c0 + 2), 'gbc'], w=['tmpD'])
        dst = x1pre[:, :] if t == 0 else x1[:, t - 1, :]
        dk = ('x1', t)
        S.op('dve', lambda E: E.tensor_tensor(out=dst, in0=tmpD[:, :], in1=xa, op=ALU.add),
             r=['tmpD', xk], w=[dk])
        dstD[t] = (dst, dk)

    def d_s2(t):
        rms_T([dstD[t]], C_GFFN, lambda kc: h2T[:, kc, t * 128:(t + 1) * 128], 'h2T', xhatD, 'xhD')

    d_s1a(0)
    d_s1a(1)
    d_s1b(0)
    for t in range(NT):
        if t + 2 < NT:
            d_s1a(t + 2)
        if t + 1 < NT:
            d_s1b(t + 1)
        d_s2(t)
    S.barrier()

    tp = [Y_OFF + 65536]
    gbuf = sbt("gbuf", [128, 22, 512], BF16, at=tp)
    ubuf = [sbt(f"ubuf{i}", [128, 528], F32, at=tp) for i in range(4)]
    zt = [sbt(f"zt{i}", [128, 512], F32, at=tp) for i in range(4)]
    sgs = [sbt(f"sg{i}", [128, 512], F32, at=tp) for i in range(2)]
    carry = sbt("carry", [128, 44, 2], F32, at=tp)
    fo0 = sbt("fo0", [128, 4, 512], F32, at=tp)
    tmpE = zt[0]
    eb = [0]

    def nbe():
        eb[0] = (eb[0] + 1) % 3
        return eb[0]
    for j in range(4):
        e0 = 128 + 512 * j
        for fc0 in range(0, 22, 4):
            nf = min(4, 22 - fc0)
            wupk = [('wupb', c) for c in range(4)]
            wgt, kgt = wload_b(wk(wupb, fc0 * 128, nf * 128), wupk)
            wvt, kvt = wload_b(wk(wupb, DFF + fc0 * 128, nf * 128), wupk)
            for fi in range(nf):
                fc = fc0 + fi
                if j > 0:
                    for fcn in ([0, 1] if fc == 0 else ([fc + 1] if fc + 1 < 22 else [])):
                        for half_ in range(2):
                            S.op('dve', lambda E, fcn=fcn, half_=half_: E.tensor_copy(
                                out=ubuf[half_ * 2 + fcn % 2][:, 0:2], in_=carry[:, half_ * 22 + fcn, :]),
                                r=[('carry', half_ * 22 + fcn)], w=[('ubc', half_ * 2 + fcn % 2)])
                for half, wt, kw_ in ((0, wgt, kgt), (1, wvt, kvt)):
                    ci = half * 22 + fc
                    b = nbe()
                    mm(PF(b), [(wt[:, kc, fi * 128:(fi + 1) * 128], h2T[:, kc, e0:e0 + 512]) for kc in range(8)],
                       w=[('pf', b)], r=[kw_, 'h2T'])
                    ub = ubuf[half * 2 + fc % 2]
                    uk = ('ub', half * 2 + fc % 2)
                    uck = ('ubc', half * 2 + fc % 2)
                    ztb = zt[half * 2 + fc % 2]
                    if j == 0:
                        pb_ = 3 + (ci % 4)
                        mm(psf[:, pb_ * 512: pb_ * 512 + 2],
                           [(wt[:, kc, fi * 128:(fi + 1) * 128], h2T[:, kc, 126:128]) for kc in range(8)],
                           w=[('pf', pb_)], r=[kw_, 'h2T'])
                        S.op('dve', lambda E, ub=ub, pb_=pb_: E.tensor_scalar(out=ub[:, 0:2], in0=psf[:, pb_ * 512: pb_ * 512 + 2],
                                                                             scalar1=cvc(C_UFLAG), scalar2=None, op0=ALU.mult),
                             r=[('pf', pb_), 'cv'], w=[uck])
                    act(ub[:, 2:514], PF(b), AF.Copy, r=[('pf', b)], w=[uk])
                    S.op('dve', lambda E, ub=ub, ci=ci: E.tensor_copy(out=carry[:, ci, :], in_=ub[:, 512:514]),
                         r=[uk], w=[('carry', ci)])
                    zk = ('zt', half * 2 + fc % 2)
                    act(ztb[:, :], ub[:, 0:512], AF.Identity, r=[uk, uck, 'cv'], w=[zk], scale=cvc(C_CW + ci),
                        bias=cvc(C_CB + ci))
                    S.op('dve', lambda E, ub=ub, ci=ci, ztb=ztb: E.scalar_tensor_tensor(
                        out=ztb[:, :], in0=ub[:, 1:513], scalar=cvc(C_CW + 44 + ci), in1=ztb[:, :],
                        op0=ALU.mult, op1=ALU.add), r=[uk, uck, zk, 'cv'], w=[zk])
                    S.op('dve', lambda E, ub=ub, ci=ci, ztb=ztb: E.scalar_tensor_tensor(
                        out=ztb[:, :], in0=ub[:, 2:514], scalar=cvc(C_CW + 88 + ci), in1=ztb[:, :],
                        op0=ALU.mult, op1=ALU.add), r=[uk, zk, 'cv'], w=[zk])
                sg = sgs[fc % 2]
                sk = ('sg', fc % 2)
                zg, zv = zt[fc % 2], zt[2 + fc % 2]
                act(sg[:, :], zg[:, :], AF.Silu, r=[('zt', fc % 2)], w=[sk])
                S.op('pool', lambda E, fc=fc, sg=sg, zv=zv: E.tensor_tensor(out=gbuf[:, fc, :], in0=sg[:, :], in1=zv[:, :],
                                                                          op=ALU.mult),
                     r=[sk, ('zt', 2 + fc % 2)], w=['gbuf'])
        for half in range(2):
            for (f0, nfk) in ((0, 8), (8, 8), (16, 6)):
                wd, kd = wload_b(wdnb[f0 * 128:(f0 + nfk) * 128, half * 512:(half + 1) * 512]
                                 .rearrange("(k p) n -> p k n", p=128), [('wdnb', 0), ('wdnb', 1)])
                for tt in range(4):
                    def fd(E, wd=wd, f0=f0, nfk=nfk, tt=tt):
                        last = None
                        for k in range(nfk):
                            last = E.matmul(PF(3 + tt), lhsT=gbuf[:, f0 + k, tt * 128:(tt + 1) * 128], rhs=wd[:, k, :],
                                            start=(f0 == 0 and k == 0), stop=(f0 == 16 and k == nfk - 1))
                        return last
                    S.op('pe', fd, r=[kd, 'gbuf'], w=[('pf', 3 + tt)])
            for tt in range(4):
                col = 40 + half * 4 + tt
                S.op('act', lambda E, tt=tt, col=col: E.activation(out=junk[:, 0:512], in_=PF(3 + tt), func=AF.Square,
                                                                   accum_out=small[:, col:col + 1]),
                     r=[('pf', 3 + tt)], w=['junk', ('sm', col)])
                if half == 0:
                    act(fo0[:, tt, :], PF(3 + tt), AF.Copy, r=[('pf', 3 + tt)], w=[('fo0', tt)])
            if half == 1:
                S.op('dve', lambda E: E.tensor_tensor(out=small[:, 48:52], in0=small[:, 40:44], in1=small[:, 44:48],
                                                      op=ALU.add), r=[('sm', c) for c in range(40, 48)], w=[('sm', 48)])
                rstd_from_ss(small[:, 48:52], small[:, 56:60], small[:, 52:56], 1024.0, [('sm', 48)], [('sm', 56)],
                             [('sm', 52)])
                for tt in range(4):
                    xt = x1[:, 4 * j + tt, :]
                    xk = ('x1', 4 * j + tt + 1)
                    rs = small[:, 56 + tt:57 + tt]
                    S.op('dve', lambda E, tt=tt, rs=rs: E.scalar_tensor_tensor(
                        out=fo0[:, tt, :], in0=fo0[:, tt, :], scalar=rs, in1=gbc[:, 1, 0:512], op0=ALU.mult,
                        op1=ALU.mult), r=[('fo0', tt), ('sm', 56), 'gbc'], w=[('fo0', tt)])
                    S.op('dve', lambda E, tt=tt, xt=xt: E.tensor_tensor(out=xt[:, 0:512], in0=xt[:, 0:512],
                                                                        in1=fo0[:, tt, :], op=ALU.add),
                         r=[('fo0', tt), xk], w=[xk])
                    S.op('dve', lambda E, tt=tt, rs=rs: E.scalar_tensor_tensor(
                        out=tmpE[:, :], in0=PF(3 + tt), scalar=rs, in1=gbc[:, 1, 512:1024], op0=ALU.mult,
                        op1=ALU.mult), r=[('pf', 3 + tt), ('sm', 56), 'gbc'], w=[('zt', 0)])
                    S.op('dve', lambda E, tt=tt, xt=xt: E.tensor_tensor(out=xt[:, 512:1024], in0=xt[:, 512:1024],
                                                                        in1=tmpE[:, :], op=ALU.add),
                         r=[('zt', 0), xk], w=[xk])
                    row = (4 * j + tt) * 128
                    S.dma('sp', lambda E, row=row, xt=xt: E.dma_start(out=outd[row:row + 128, :], in_=xt), r=[xk])

    return nc, es, S


def _finish(nc, es, S):
    S.barrier()
    with nc.Block() as block:
        @block.tensor
        def _(E):
            for f in S.prog['pe']:
                f(E)

        @block.scalar
        def _(E):
            for f in S.prog['act']:
                f(E)

        @block.vector
        def _(E):
            for f in S.prog['dve']:
                f(E)

        @block.gpsimd
        def _(E):
            for f in S.prog['pool']:
                f(E)

        @block.sync
        def _(E):
            for f in S.prog['sp']:
                f(E)
    es.close()
    return nc


def host_inputs(inputs):
    x = np.asarray(inputs["x"], np.float32)
    mem = np.asarray(inputs["mem"], np.float32)
    pos = np.asarray(inputs["positions"], np.int32)

    def P(k):
        return np.asarray(inputs[k], np.float32)[0]
    shared = {
        "gbc": np.stack([P("g_post_mix"), P("g_post_ffn")]).astype(np.float32),
        "tri": np.triu(np.ones((128, 128), np.float32)),
        "ident": np.eye(128, dtype=np.float32),
        "w_in": P("w_in"), "w_uq": P("w_uq"), "w_ukv": P("w_ukv"), "w_mem_kv": P("w_mem_kv"),
        "w_br_mla": P("w_br_mla"), "w_br_dil": P("w_br_dil"), "w_br_mem": P("w_br_mem"), "w_o": P("w_o"),
        "w_ffn_up": P("w_ffn_up"), "w_ffn_down": P("w_ffn_down"),
    }
    db = np.zeros((12, 128, 256), np.float32)
    slopes = np.exp2(-8.0 * np.arange(1, 13, dtype=np.float32) / 12).reshape(4, 3).T
    k = np.arange(128)[:, None]
    i = np.arange(128)[None, :]
    for g in range(3):
        for hd in range(4):
            a = slopes[g, hd] * DIL_D[g]
            dist = i - k
            db[g * 4 + hd, :, 0:128] = np.where(dist >= 0, -a * dist, NEG)
            dist2 = 128 + i - k
            db[g * 4 + hd, :, 128:256] = np.where(k >= i, -a * dist2, NEG)
    shared["dbias"] = db

    def cols(v, n):
        return v.reshape(n, 128).T
    cv0 = np.zeros((128, NCV), np.float32)
    cv0[:, C_GPRE:C_GPRE + 8] = cols(P("g_pre_mix"), 8)
    cv0[:, C_GMEM:C_GMEM + 8] = cols(P("g_mem"), 8)
    cv0[:, C_GFFN:C_GFFN + 8] = cols(P("g_pre_ffn"), 8)
    cv0[:, C_QN:C_QN + 3] = cols(P("mla_q_norm"), 3)
    cv0[:, C_KVN:C_KVN + 2] = cols(P("mla_kv_norm"), 2)
    cv0[:, C_BG:C_BG + 24] = cols(P("b_gate"), 24)
    cw = P("conv_w")
    for j in range(3):
        cv0[:, C_CW + j * 44: C_CW + (j + 1) * 44] = cols(cw[j], 44)
    cv0[:, C_CB:C_CB + 44] = cols(P("conv_b"), 44)
    cv0[:, C_INVF:C_INVF + 16] = (np.float32(10000.0) ** (-np.arange(16, dtype=np.float32) / np.float32(16)))[None, :]
    in_maps = []
    for c in range(8):
        b, q = c // 4, c % 4
        xcx = np.zeros((SEQ, D), np.float32)
        pc = np.zeros((SEQ,), np.int32)
        lo = 6144 - CH * q
        xcx[lo:] = x[b, 0: CH * (q + 1)]
        pc[lo:] = pos[b, 0: CH * (q + 1)]
        cvq = cv0.copy()
        for cidx in range(3):
            cvq[:, C_VB + cidx] = 0.0 if (cidx + q) >= 3 else NEG
        ridx = 0
        for g in range(3):
            d = DIL_D[g]
            for r_ in range(d):
                kk = np.arange(128)
                tau0 = 3968 + (2048 - 128 * d) + kk * d + r_
                cvq[:, C_DVB0 + ridx] = np.where(tau0 >= lo, 0.0, NEG)
                tau1 = 3968 + 2048 + kk * d + r_
                cvq[:, C_DVB1 + ridx] = np.where(tau1 >= lo, 0.0, NEG)
                ridx += 1
        cvq[:, C_UFLAG] = 1.0 if q > 0 else 0.0
        m = dict(shared)
        m["xc"] = xcx
        m["posc"] = np.ascontiguousarray(pc.reshape(64, 128).T)
        m["memx"] = np.ascontiguousarray(mem[b])
        m["cv"] = cvq
        in_maps.append(m)
    return in_maps


def kernel(**inputs):
    in_maps = host_inputs(inputs)
    nc, es, S = build_program()
    nc = _finish(nc, es, S)
    res = run_bass_kernel_spmd(nc, in_maps, core_ids=list(range(8)))
    out = np.zeros((2, SEQ, D), np.float32)
    for c in range(8):
        b, q = c // 4, c % 4
        out[b, q * CH:(q + 1) * CH] = np.asarray(res.results[c]["out"], np.float32)
    return out
```
